# Optimizing a Trainium2 kernel written in Bass

```python
import jax, jax.numpy as jnp
from jax import lax
import numpy as np

D_MODEL = 2048
BATCH = 16
SEQ = 256
DEPTH = 1
DEC_BATCH = 4
DEC_SEQ = 2048
PAST_LEN = 256

GRID_W = 64
D_FF = 5504
N_MOD = 9
GLA_HEADS = 4
GLA_DK = 256
GLA_DV = 512
GLA_QK_DIM = 1024
GLA_V_DIM = 2048
GLA_GATE_RANK = 16
GLA_TAU = 16.0
GLA_CHUNK = 64
MLA_HEADS = 16
MLA_NOPE = 128
MLA_ROPE = 64
MLA_V = 128
Q_LORA = 512
KV_LORA = 512
ROPE_THETA = 10000.0
Q_BLOCK = 128
NORM_EPS = 1e-6
PROJ_SPLITS = (1024, 1024, 2048, 2048, 16, 16, 512, 512, 64, 2048, 2048)
D_IN_PROJ = 11360

kernel_name = "hybrid_gla_mla_diffusion_step"


def rms_norm(x, g):
    xf = x.astype(jnp.float32)
    y = xf * lax.rsqrt(jnp.mean(xf * xf, axis=-1, keepdims=True) + NORM_EPS)
    return (y * g.astype(jnp.float32)).astype(x.dtype)


def swiglu(h, w_in, w_out):
    g, u = jnp.split(h @ w_in, 2, axis=-1)
    return (jax.nn.silu(g) * u) @ w_out


def axial_rope_tables(n_tokens):
    rows = n_tokens // GRID_W
    row = jnp.repeat(jnp.arange(rows), GRID_W).astype(jnp.float32)
    col = jnp.tile(jnp.arange(GRID_W), rows).astype(jnp.float32)
    half = MLA_ROPE // 2
    inv_freq = ROPE_THETA ** (-jnp.arange(0, half, 2, dtype=jnp.float32) / half)
    ang = jnp.stack([row[:, None] * inv_freq, col[:, None] * inv_freq], axis=1)
    ang = ang[:, :, None, :]
    return jnp.cos(ang), jnp.sin(ang)


def apply_axial_rope(x, cos, sin):
    xp = x.astype(jnp.float32).reshape(x.shape[:-1] + (2, 2, MLA_ROPE // 4))
    x1, x2 = xp[..., 0:1, :], xp[..., 1:2, :]
    out = jnp.concatenate([x1 * cos - x2 * sin, x2 * cos + x1 * sin], axis=-2)
    return out.reshape(x.shape).astype(x.dtype)


def gla_chunked(q, k, v, log_a, s0):
    B, T, H, K = q.shape
    V = v.shape[-1]
    n = T // GLA_CHUNK
    f32 = jnp.float32

    def chunks(t):
        return t.astype(f32).reshape(B, n, GLA_CHUNK, H, t.shape[-1])

    qc, kc, vc = chunks(q), chunks(k), chunks(v)
    b = jnp.cumsum(chunks(log_a), axis=2)
    b_last = b[:, :, -1:]
    q_dec = qc * jnp.exp(b)
    k_inv = kc * jnp.exp(-b)
    k_end = kc * jnp.exp(b_last - b)
    lower = jnp.tril(jnp.ones((GLA_CHUNK, GLA_CHUNK), dtype=bool))
    a = jnp.einsum("bnihk,bnjhk->bnhij", q_dec, k_inv)
    a = jnp.where(lower, a, 0.0)
    o_intra = jnp.einsum("bnhij,bnjhv->bnihv", a, vc)

    def step(S, xs):
        q_n, k_n, v_n, dec_n = xs
        o_n = jnp.einsum("bihk,bhkv->bihv", q_n, S)
        S = jnp.exp(dec_n)[..., None] * S + jnp.einsum("bjhk,bjhv->bhkv", k_n, v_n)
        return S, o_n

    xs = (q_dec.swapaxes(0, 1), k_end.swapaxes(0, 1), vc.swapaxes(0, 1), b_last[:, :, 0].swapaxes(0, 1))
    s_fin, o_inter = lax.scan(step, s0.astype(f32), xs)
    o = o_intra + o_inter.swapaxes(0, 1)
    return o.reshape(B, T, H, V).astype(v.dtype), s_fin.astype(s0.dtype)


def mla_attention(q_nope, q_rope, k_nope, k_rope, v):
    B, T, H, _ = q_nope.shape
    nb = T // Q_BLOCK
    scale = (MLA_NOPE + MLA_ROPE) ** -0.5
    qn = q_nope.reshape(B, nb, Q_BLOCK, H, MLA_NOPE).swapaxes(0, 1)
    qr = q_rope.reshape(B, nb, Q_BLOCK, H, MLA_ROPE).swapaxes(0, 1)

    def block(args):
        qn_b, qr_b = args
        s = jnp.einsum("bqhd,bkhd->bhqk", qn_b, k_nope) + jnp.einsum("bqhr,bkr->bhqk", qr_b, k_rope)
        p = jax.nn.softmax(s.astype(jnp.float32) * scale, axis=-1)
        return jnp.einsum("bhqk,bkhd->bqhd", p.astype(v.dtype), v)

    o = lax.map(block, (qn, qr))
    return o.swapaxes(0, 1).reshape(B, T, H * MLA_V)


def token_mixer(h, ctx, w_in, w_gla_alpha, b_gla_alpha, gla_norm, w_gla_out,
                q_norm, kv_norm, w_uq, w_ukv, w_mla_out, w_out):
    B, T, _ = h.shape
    proj = h @ w_in
    (q_g, k_g, v_g, r_g, a_f, a_b, cq, ckv, k_rope, g_a, g_b) = jnp.split(
        proj, np.cumsum(PROJ_SPLITS)[:-1].tolist(), axis=-1)

    q_g = q_g.reshape(B, T, GLA_HEADS, GLA_DK) * (GLA_DK ** -0.5)
    k_g = k_g.reshape(B, T, GLA_HEADS, GLA_DK)
    v_g = v_g.reshape(B, T, GLA_HEADS, GLA_DV)
    la_f = (jax.nn.log_sigmoid((a_f @ w_gla_alpha[0] + b_gla_alpha[0]).astype(jnp.float32)) / GLA_TAU
            ).reshape(B, T, GLA_HEADS, GLA_DK)
    la_b = (jax.nn.log_sigmoid((a_b @ w_gla_alpha[1] + b_gla_alpha[1]).astype(jnp.float32)) / GLA_TAU
            ).reshape(B, T, GLA_HEADS, GLA_DK)

    cq = rms_norm(cq, q_norm)
    q = (cq @ w_uq).reshape(B, T, MLA_HEADS, MLA_NOPE + MLA_ROPE)
    q_nope, q_rope = q[..., :MLA_NOPE], q[..., MLA_NOPE:]
    ckv = rms_norm(ckv, kv_norm)

    if ctx is None:
        s0_f = jnp.zeros((B, GLA_HEADS, GLA_DK, GLA_DV), h.dtype)
        s0_b = s0_f
        ckv_all, krope_all = ckv, k_rope
    else:
        ckv_ctx, krope_ctx, s0_f, s0_b = ctx
        cos, sin = axial_rope_tables(T)
        q_rope = apply_axial_rope(q_rope, cos[:, None], sin[:, None])
        k_rope = apply_axial_rope(k_rope, cos, sin)
        ckv_all = jnp.concatenate([ckv_ctx.astype(h.dtype), ckv], axis=1)
        krope_all = jnp.concatenate([krope_ctx.astype(h.dtype), k_rope], axis=1)

    kv = (ckv_all @ w_ukv).reshape(B, ckv_all.shape[1], MLA_HEADS, MLA_NOPE + MLA_V)
    k_nope, v_m = kv[..., :MLA_NOPE], kv[..., MLA_NOPE:]
    y_mla = mla_attention(q_nope, q_rope, k_nope, krope_all, v_m) @ w_mla_out

    o_f, s_f = gla_chunked(q_g, k_g, v_g, la_f, s0_f)
    o_b, s_b = gla_chunked(jnp.flip(q_g, 1), jnp.flip(k_g, 1), jnp.flip(v_g, 1), jnp.flip(la_b, 1), s0_b)
    o = rms_norm(o_f + jnp.flip(o_b, 1), gla_norm).reshape(B, T, GLA_V_DIM) * jax.nn.silu(r_g)
    y_gla = o @ w_gla_out

    y = (jax.nn.sigmoid(g_a) * y_gla + jax.nn.sigmoid(g_b) * y_mla) @ w_out
    return y, (ckv, k_rope, s_f, s_b)


def trunk_layer(x, mod, ctx, norm_gains, w_ffn1_in, w_ffn1_out, w_ffn2_in, w_ffn2_out, mix_w):
    sh1, sc1, gt1, sh2, sc2, gt2, sh3, sc3, gt3 = jnp.split(mod, N_MOD, axis=-1)
    h = rms_norm(x, norm_gains[0]) * (1 + sc1) + sh1
    x = x + 0.5 * gt1 * rms_norm(swiglu(h, w_ffn1_in, w_ffn1_out), norm_gains[1])
    h = rms_norm(x, norm_gains[2]) * (1 + sc2) + sh2
    y, ctx_out = token_mixer(h, ctx, *mix_w)
    x = x + gt2 * rms_norm(y, norm_gains[3])
    h = rms_norm(x, norm_gains[4]) * (1 + sc3) + sh3
    x = x + 0.5 * gt3 * rms_norm(swiglu(h, w_ffn2_in, w_ffn2_out), norm_gains[5])
    return x, ctx_out


def setup_inputs(seed: int = 0) -> dict:
    key = jax.random.key(seed)
    ks = jax.random.split(key, 32)
    f32 = jnp.float32

    def nrm(k, shape, scale):
        return jax.random.normal(k, shape, f32) * scale

    H_QK = MLA_NOPE + MLA_ROPE
    return {
        "x_prompt": nrm(ks[0], (BATCH, SEQ, D_MODEL), 1.0),
        "x_sample": nrm(ks[1], (DEC_BATCH, DEC_SEQ, D_MODEL), 1.0),
        "cache_ckv": nrm(ks[2], (DEC_BATCH, DEPTH, PAST_LEN, KV_LORA), 1.0),
        "cache_krope": nrm(ks[3], (DEC_BATCH, DEPTH, PAST_LEN, MLA_ROPE), 1.0),
        "state_gla_fwd": nrm(ks[4], (DEC_BATCH, DEPTH, GLA_HEADS, GLA_DK, GLA_DV), 0.5),
        "state_gla_bwd": nrm(ks[5], (DEC_BATCH, DEPTH, GLA_HEADS, GLA_DK, GLA_DV), 0.5),
        "c": nrm(ks[6], (DEC_BATCH, D_MODEL), 1.0),
        "c_ctx": nrm(ks[7], (D_MODEL,), 1.0),
        "w_ada": nrm(ks[8], (DEPTH, D_MODEL, N_MOD * D_MODEL), 0.5 * D_MODEL ** -0.5),
        "b_ada": nrm(ks[9], (DEPTH, N_MOD * D_MODEL), 0.01),
        "norm_gains": 1.0 + nrm(ks[10], (DEPTH, 6, D_MODEL), 0.05),
        "w_ffn1_in": nrm(ks[11], (DEPTH, D_MODEL, 2 * D_FF), D_MODEL ** -0.5),
        "w_ffn1_out": nrm(ks[12], (DEPTH, D_FF, D_MODEL), D_FF ** -0.5),
        "w_ffn2_in": nrm(ks[13], (DEPTH, D_MODEL, 2 * D_FF), D_MODEL ** -0.5),
        "w_ffn2_out": nrm(ks[14], (DEPTH, D_FF, D_MODEL), D_FF ** -0.5),
        "w_in": nrm(ks[15], (DEPTH, D_MODEL, D_IN_PROJ), D_MODEL ** -0.5),
        "w_gla_alpha": nrm(ks[16], (DEPTH, 2, GLA_GATE_RANK, GLA_QK_DIM), GLA_GATE_RANK ** -0.5),
        "b_gla_alpha": nrm(ks[17], (DEPTH, 2, GLA_QK_DIM), 0.1),
        "gla_norm": 1.0 + nrm(ks[18], (DEPTH, GLA_DV), 0.05),
        "w_gla_out": nrm(ks[19], (DEPTH, GLA_V_DIM, D_MODEL), GLA_V_DIM ** -0.5),
        "q_norm": 1.0 + nrm(ks[20], (DEPTH, Q_LORA), 0.05),
        "kv_norm": 1.0 + nrm(ks[21], (DEPTH, KV_LORA), 0.05),
        "w_uq": nrm(ks[22], (DEPTH, Q_LORA, MLA_HEADS * H_QK), Q_LORA ** -0.5),
        "w_ukv": nrm(ks[23], (DEPTH, KV_LORA, MLA_HEADS * (MLA_NOPE + MLA_V)), KV_LORA ** -0.5),
        "w_mla_out": nrm(ks[24], (DEPTH, MLA_HEADS * MLA_V, D_MODEL), (MLA_HEADS * MLA_V) ** -0.5),
        "w_out": nrm(ks[25], (DEPTH, D_MODEL, D_MODEL), D_MODEL ** -0.5),
    }


def reference(x_prompt, x_sample, cache_ckv, cache_krope, state_gla_fwd, state_gla_bwd, c, c_ctx,
              w_ada, b_ada, norm_gains, w_ffn1_in, w_ffn1_out, w_ffn2_in, w_ffn2_out,
              w_in, w_gla_alpha, b_gla_alpha, gla_norm, w_gla_out,
              q_norm, kv_norm, w_uq, w_ukv, w_mla_out, w_out):
    xp, xs = x_prompt, x_sample
    new_ckv, new_krope, new_sf, new_sb = [], [], [], []
    for l in range(DEPTH):
        mix_w = (w_in[l], w_gla_alpha[l], b_gla_alpha[l], gla_norm[l], w_gla_out[l],
                 q_norm[l], kv_norm[l], w_uq[l], w_ukv[l], w_mla_out[l], w_out[l])
        ffn_w = (norm_gains[l], w_ffn1_in[l], w_ffn1_out[l], w_ffn2_in[l], w_ffn2_out[l])
        mod_ctx = (jax.nn.silu(c_ctx) @ w_ada[l] + b_ada[l])[None, None, :]
        mod_lat = (jax.nn.silu(c) @ w_ada[l] + b_ada[l])[:, None, :]
        xp, (ckv, krope, s_f, s_b) = trunk_layer(xp, mod_ctx, None, *ffn_w, mix_w)
        new_ckv.append(ckv)
        new_krope.append(krope)
        new_sf.append(s_f)
        new_sb.append(s_b)
        ctx = (cache_ckv[:, l], cache_krope[:, l], state_gla_fwd[:, l], state_gla_bwd[:, l])
        xs, _ = trunk_layer(xs, mod_lat, ctx, *ffn_w, mix_w)
    return (xp, xs, jnp.stack(new_ckv, axis=1), jnp.stack(new_krope, axis=1),
            jnp.stack(new_sf, axis=1), jnp.stack(new_sb, axis=1))
```

```python
import numpy as np
from contextlib import ExitStack
import concourse.bass as bass
import concourse.mybir as mybir
from concourse.bass_utils import run_bass_kernel_spmd

F32 = mybir.dt.float32
BF16 = mybir.dt.bfloat16
AF = mybir.ActivationFunctionType
ALU = mybir.AluOpType
AX = mybir.AxisListType

ENGINES = ["pe", "act", "dve", "pool", "sp"]


class Buf:
    def __init__(self, name, nparts=1):
        self.name = name
        self.n = nparts
        self.w = [None] * nparts
        self.r = [[] for _ in range(nparts)]
        self.sem = [None] * nparts

    def parts(self, p):
        if p is None:
            return range(self.n)
        if isinstance(p, int):
            return (p,)
        return p


class V:
    def __init__(self, ap, *deps):
        self.ap = ap
        self.deps = list(deps)


class _Op:
    __slots__ = ("fn", "waits", "is_dma", "dsem", "clock", "marked")

    def __init__(self, fn, waits, is_dma, dsem):
        self.fn = fn
        self.waits = waits
        self.is_dma = is_dma
        self.dsem = dsem
        self.clock = None
        self.marked = False


class Sched:
    def __init__(self, n_dma_sems=80):
        self.ops = {e: [] for e in ENGINES}
        self.known = {e: {} for e in ENGINES}
        self.n_dma = n_dma_sems
        self.dma_count = [0] * n_dma_sems
        self.n_sw = 24
        self.dma_next_sw = 0
        self.dma_next = self.n_sw
        self.dma_clock = {}
        self.snap = {e: None for e in ENGINES}
        self.sem_bufs = []

    def _sem_for(self, buf, part, queue):
        if buf.sem[part] is None:
            buf.sem[part] = {}
            self.sem_bufs.append(buf)
        d = buf.sem[part]
        kind = "sw" if queue == "pool" else "hw"
        if kind not in d:
            if kind == "sw":
                if self.dma_next_sw >= self.n_sw:
                    raise RuntimeError("out of sw dma semaphores")
                d[kind] = self.dma_next_sw
                self.dma_next_sw += 1
            else:
                if self.dma_next >= self.n_dma:
                    raise RuntimeError("out of hw dma semaphores")
                d[kind] = self.dma_next
                self.dma_next += 1
        return d[kind]

    def _collect(self, engine, reads, writes):
        deps = {}

        def add(ev):
            if ev is None:
                return
            k, v = ev
            if k == "pe" and engine == "pe":
                return
            if deps.get(k, -1) < v:
                deps[k] = v

        for b, p in reads:
            for i in b.parts(p):
                add(b.w[i])
        for b, p in writes:
            for i in b.parts(p):
                add(b.w[i])
                for ev in b.r[i]:
                    add(ev)
        return deps

    def _update(self, ev, reads, writes):
        for b, p in reads:
            for i in b.parts(p):
                lst = b.r[i]
                for j, (k, v) in enumerate(lst):
                    if k == ev[0]:
                        lst[j] = ev
                        break
                else:
                    lst.append(ev)
        for b, p in writes:
            for i in b.parts(p):
                b.w[i] = ev
                b.r[i] = []

    def _merge_clock(self, engine, clock):
        kn = self.known[engine]
        for k, v in clock.items():
            if kn.get(k, -1) < v:
                kn[k] = v
        self.snap[engine] = None

    def _snapshot(self, engine):
        if self.snap[engine] is None:
            self.snap[engine] = dict(self.known[engine])
        return self.snap[engine]

    def _make_waits(self, engine, deps):
        waits = []
        kn = self.known[engine]
        for k, v in deps.items():
            if isinstance(k, tuple):
                if kn.get(k, -1) >= v:
                    continue
                v = max(v, self.dma_count[k[1]])
                waits.append((k, v))
                kn[k] = v
                self.snap[engine] = None
                clk = self.dma_clock.get((k[1], v))
                if clk:
                    self._merge_clock(engine, clk)
            else:
                if kn.get(k, -1) >= v:
                    continue
                waits.append((k, v))
                kn[k] = v
                self.snap[engine] = None
                op = self.ops[k][v]
                op.marked = True
                if op.clock:
                    self._merge_clock(engine, op.clock)
        return waits

    def op(self, engine, fn, reads=(), writes=()):
        deps = self._collect(engine, reads, writes)
        waits = self._make_waits(engine, deps)
        o = _Op(fn, waits, False, None)
        o.clock = self._snapshot(engine)
        idx = len(self.ops[engine])
        self.ops[engine].append(o)
        self._update((engine, idx), reads, writes)
        return idx

    def dma(self, queue, fn, reads, writes, semof):
        deps = self._collect(queue, reads, writes)
        waits = self._make_waits(queue, deps)
        si = self._sem_for(semof[0], semof[1], queue)
        self.dma_count[si] += 16
        val = self.dma_count[si]
        o = _Op(fn, waits, True, si)
        self.ops[queue].append(o)
        self.dma_clock[(si, val)] = self._snapshot(queue)
        self._update((("d", si), val), reads, writes)

    def barrier(self):
        last = {}
        for e in ("pe", "act", "dve", "pool"):
            for i in range(len(self.ops[e]) - 1, -1, -1):
                if not self.ops[e][i].is_dma and self.ops[e][i].fn is not None:
                    last[e] = i
                    break
        for e in ENGINES:
            deps = {}
            for f, i in last.items():
                if f != e:
                    deps[f] = i
                elif e in ("act", "dve", "pool"):
                    deps[f] = i
            for si in range(self.n_dma):
                if self.dma_count[si] > 0:
                    deps[("d", si)] = self.dma_count[si]
            waits = self._make_waits(e, deps)
            if waits:
                o = _Op(None, waits, False, None)
                o.clock = self._snapshot(e)
                self.ops[e].append(o)

    def reset_sems(self):
        self.dma_next = self.n_sw
        self.dma_next_sw = 0
        for b in self.sem_bufs:
            b.sem = [None] * b.n
        self.sem_bufs = []

    def emit(self, nc, block, esems, dsems):
        vals = {}
        for e in ENGINES:
            c = 0
            m = {}
            for i, o in enumerate(self.ops[e]):
                if o.marked:
                    c += 1
                    m[i] = c
            vals[e] = m

        def run(e, eng):
            for i, o in enumerate(self.ops[e]):
                for k, v in o.waits:
                    if isinstance(k, tuple):
                        eng.wait_ge(dsems[k[1]], v)
                    else:
                        eng.wait_ge(esems[k], vals[k][v])
                if o.fn is None:
                    continue
                ins = o.fn(eng)
                if o.is_dma:
                    ins.then_inc(dsems[o.dsem], 16)
                elif o.marked:
                    ins.then_inc(esems[e], 1)

        @block.tensor
        def _(eng):
            run("pe", eng)

        @block.scalar
        def _(eng):
            run("act", eng)

        @block.vector
        def _(eng):
            run("dve", eng)

        @block.gpsimd
        def _(eng):
            run("pool", eng)

        @block.sync
        def _(eng):
            run("sp", eng)

    def stats(self):
        return {e: len(self.ops[e]) for e in ENGINES}


class Arena:
    def __init__(self, t, nwords):
        self.t = t
        self.n = nwords
        self.off = 0
        self.marks = []

    def alloc(self, free_shape, dtype, parts=128):
        nel = int(np.prod(free_shape))
        nbytes = nel * (2 if dtype == BF16 else 4)
        nw = (nbytes + 3) // 4
        nw = (nw + 15) // 16 * 16
        if self.off + nw > self.n:
            raise RuntimeError(f"arena overflow: need {nw} words at {self.off} of {self.n}")
        ap = self.t[0:parts, self.off:self.off + nw]
        self.off += nw
        if dtype == BF16:
            ap = ap.bitcast(BF16)[:, 0:nel]
        else:
            ap = ap[:, 0:nel]
        if len(free_shape) == 2:
            ap = ap.rearrange("p (a b) -> p a b", b=free_shape[1])
        elif len(free_shape) == 3:
            ap = ap.rearrange("p (a b c) -> p a b c", b=free_shape[1], c=free_shape[2])
        return ap

    def push(self):
        self.marks.append(self.off)

    def pop(self):
        self.off = self.marks.pop()


D = 2048
DFF = 5504
NFC = 43
ATT_SCALE = 192 ** -0.5
EPS = 1e-6
NEG_BIG = -30000.0
ARENA_WORDS = 53184


class TT:
    def __init__(self, ap, name, nparts=1):
        self.ap = ap
        self.buf = Buf(name, nparts)

    def v(self, part=None, idx=None):
        ap = self.ap if idx is None else self.ap[idx]
        return V(ap, (self.buf, part))


def build(NSEG, upto=99, debug=False):
    T = NSEG * 256
    NT = T // 128
    NK = T + 256
    NKT = NK // 128
    nc = bass.Bass("TRN2", target_bir_lowering=False)

    def din(name, shape):
        return nc.dram_tensor(name, list(shape), F32, kind="ExternalInput").ap()

    def dout(name, shape):
        return nc.dram_tensor(name, list(shape), F32, kind="ExternalOutput").ap()

    def dscr(name, shape, dt):
        if debug:
            return nc.dram_tensor(name, list(shape), dt, kind="ExternalOutput").ap()
        return nc.dram_tensor(name, list(shape), dt).ap()

    x_in = din("x", [T, D])
    cvec = din("cvec", [128, 16])
    cckv = din("cckv", [256, 512])
    ckr = din("ckr", [256, 64])
    s0 = din("s0", [2, 4, 256, 512])
    flag_in = din("flag", [128, 1])
    cosT_in = din("cosT", [64, T])
    sinT_in = din("sinT", [64, T])
    cosK_in = din("cosK", [T, 64])
    sinK_in = din("sinK", [T, 64])
    mq_in = din("mq", [16, T])
    mk_in = din("mk", [16, NK])
    consts_in = din("consts", [128, 6, 128])
    w_ada = din("w_ada", [D, 9 * D])
    b_ada = din("b_ada", [1, 9 * D])
    gains = din("norm_gains", [6, D])
    w_f1i = din("w_ffn1_in", [D, 2 * DFF])
    w_f1o = din("w_ffn1_out", [DFF, D])
    w_f2i = din("w_ffn2_in", [D, 2 * DFF])
    w_f2o = din("w_ffn2_out", [DFF, D])
    w_inp = din("w_in", [D, 11360])
    w_alpha = din("w_gla_alpha", [2, 16, 1024])
    b_alpha = din("b_gla_alpha", [2, 1024])
    gla_norm = din("gla_norm", [1, 512])
    w_go = din("w_gla_out", [D, D])
    q_norm = din("q_norm", [1, 512])
    kv_norm = din("kv_norm", [1, 512])
    w_uq = din("w_uq", [512, 3072])
    w_ukv = din("w_ukv", [512, 4096])
    w_mo = din("w_mla_out", [D, D])
    w_o = din("w_out", [D, D])

    y_out = dout("y", [T, D])
    nckv_out = dout("nckv", [T, 512])
    nkr_out = dout("nkr", [T, 64])
    sf_out = dout("sf", [NSEG, 4, 256, 512])
    sb_out = dout("sb", [NSEG, 4, 256, 512])

    modS = TT(dscr("modS", [9, 128, D], F32), "modS", 9)
    X1 = TT(dscr("X1", [T, D], F32), "X1", NT)
    X2 = TT(dscr("X2", [T, D], F32), "X2", NT)
    QD = TT(dscr("QD", [2, NT, 128, 8, 128], BF16), "QD", 2 * NT)
    KI = TT(dscr("KI", [2, NT, 128, 8, 128], BF16), "KI", 2 * NT)
    KE = TT(dscr("KE", [2, NT, 128, 1024], BF16), "KE", 2 * NT)
    ED = TT(dscr("ED", [2, NT, 128, 8], F32), "ED", 2 * NT)
    VV = TT(dscr("VV", [T, D], BF16), "VV", NT)
    RR = TT(dscr("RR", [T, D], BF16), "RR", NT)
    GAT = TT(dscr("GAT", [16, 128, T], BF16), "GAT", 1)
    GBT = TT(dscr("GBT", [16, 128, T], BF16), "GBT", 1)
    OF = TT(dscr("OF", [T, D], F32), "OF", NT)
    OGT = TT(dscr("OGT", [128, 16, T], BF16), "OGT", 1)
    OMT = TT(dscr("OMT", [128, 16, T], BF16), "OMT", 1)

    S = Sched(90)
    es = ExitStack()
    arena_t = es.enter_context(nc.sbuf_tensor("arena", [128, ARENA_WORDS], F32))
    ps = es.enter_context(nc.psum_tensor("ps", [128, 4096], F32))
    esems = {e: es.enter_context(nc.semaphore(f"s_{e}")) for e in ["pe", "act", "dve", "pool"]}
    dsems = [es.enter_context(nc.semaphore(f"d{i}")) for i in range(90)]
    block = es.enter_context(nc.Block())
    A = Arena(arena_t, ARENA_WORDS)
    PB = Buf("psum", 8)
    bank_rr = [0]

    def nb():
        i = bank_rr[0]
        bank_rr[0] = (i + 1) % 8
        return i

    def bank(i, w=512, rows=128, off=0):
        return V(ps[0:rows, i * 512 + off:i * 512 + off + w], (PB, i))

    def bank_bf(i):
        return ps[:, i * 512:(i + 1) * 512].bitcast(BF16)

    def mm(o, l, r, start=True, stop=True):
        S.op("pe", lambda e: e.matmul(o.ap, lhsT=l.ap, rhs=r.ap, start=start, stop=stop), l.deps + r.deps, o.deps)

    def act(o, i, func, bias=None, scale=None, accum=None):
        kw = {}
        reads = list(i.deps)
        writes = list(o.deps)
        if bias is not None:
            if isinstance(bias, V):
                kw["bias"] = bias.ap
                reads += bias.deps
            else:
                kw["bias"] = bias
        if scale is not None:
            if isinstance(scale, V):
                kw["scale"] = scale.ap
                reads += scale.deps
            else:
                kw["scale"] = scale
        if accum is not None:
            kw["accum_out"] = accum.ap
            writes += accum.deps
        S.op("act", lambda e: e.activation(out=o.ap, in_=i.ap, func=func, **kw), reads, writes)

    def tt(eng, o, a, b, op):
        S.op(eng, lambda e: e.tensor_tensor(out=o.ap, in0=a.ap, in1=b.ap, op=op), a.deps + b.deps, o.deps)

    def stt(eng, o, a, sc, b, op0, op1):
        reads = a.deps + b.deps
        if isinstance(sc, V):
            reads = reads + sc.deps
            scv = sc.ap
        else:
            scv = sc
        S.op(eng, lambda e: e.scalar_tensor_tensor(out=o.ap, in0=a.ap, scalar=scv, in1=b.ap, op0=op0, op1=op1), reads, o.deps)

    def ts(eng, o, a, s1, op0, s2=None, op1=None):
        reads = list(a.deps)
        if isinstance(s1, V):
            reads += s1.deps
            s1 = s1.ap
        if isinstance(s2, V):
            reads += s2.deps
            s2 = s2.ap
        if op1 is None:
            S.op(eng, lambda e: e.tensor_scalar(out=o.ap, in0=a.ap, scalar1=s1, scalar2=None, op0=op0), reads, o.deps)
        else:
            S.op(eng, lambda e: e.tensor_scalar(out=o.ap, in0=a.ap, scalar1=s1, scalar2=s2, op0=op0, op1=op1), reads, o.deps)

    def cp(eng, o, i, scale=None):
        if eng == "act":
            act(o, i, AF.Copy, scale=scale)
        else:
            S.op(eng, lambda e: e.tensor_copy(out=o.ap, in_=i.ap), i.deps, o.deps)

    def dma(q, o, i, semof):
        S.dma(q, lambda e: e.dma_start(out=o.ap, in_=i.ap), i.deps, o.deps, semof)

    def rmax(o, i):
        S.op("dve", lambda e: e.reduce_max(out=o.ap, in_=i.ap, axis=AX.X), i.deps, o.deps)

    def rsum(o, i):
        S.op("dve", lambda e: e.reduce_sum(out=o.ap, in_=i.ap, axis=AX.X), i.deps, o.deps)

    def recip(o, i):
        S.op("dve", lambda e: e.reciprocal(out=o.ap, in_=i.ap), i.deps, o.deps)

    def stage_end():
        S.barrier()
        S.reset_sems()

    cst = TT(A.alloc([6, 128], F32), "cst")
    dma("sp", cst.v(), V(consts_in), (cst.buf, 0))
    identb = TT(A.alloc([128], BF16), "identb")
    onesb = TT(A.alloc([128], BF16), "onesb")
    cp("dve", identb.v(), cst.v(None, np.s_[:, 0, :]))
    cp("dve", onesb.v(), cst.v(None, np.s_[:, 5, :]))
    Lf = cst.v(None, np.s_[:, 1, :])
    Lb = cst.v(None, np.s_[:, 2, :])
    Uf = cst.v(None, np.s_[:, 3, :])
    Ub = cst.v(None, np.s_[:, 4, :])
    ones_col = cst.v(None, np.s_[:, 5, 0:1])
    flag = TT(A.alloc([1], F32), "flag")
    dma("sp", flag.v(), V(flag_in), (flag.buf, 0))
    stat = TT(A.alloc([64], F32), "stat", 64)
    stat_rr = [0]

    def scol():
        i = stat_rr[0]
        stat_rr[0] = (i + 1) % 64
        return i

    def sv(i, w=1):
        return V(stat.ap[:, i:i + w], (stat.buf, tuple(range(i, i + w))))

    def tr(o, i):
        S.op("pe", lambda e: e.transpose(o.ap, i.ap, identb.ap), i.deps + [(identb.buf, None)], o.deps)

    def rstd_from(src, n, junk):
        c0, c1, c2 = scol(), scol(), scol()
        act(junk, src, AF.Square, accum=sv(c0))
        act(sv(c1), sv(c0), AF.Ln, scale=1.0 / n, bias=EPS)
        act(sv(c2), sv(c1), AF.Exp, scale=-0.5)
        return sv(c2)

    def wslab(dst, src_cols):
        dma("pool", dst, V(src_cols.rearrange("(kc p) n -> p kc n", p=128)), (dst.deps[0][0], 0))

    cs_p = TT(A.alloc([16], F32), "cs_p")

    def make_crep():
        crep = TT(A.alloc([16, 128], BF16), "crep")
        for kc in range(16):
            ts("dve", crep.v(None, np.s_[:, kc, :]), onesb.v(), cs_p.v(None, np.s_[:, kc:kc + 1]), ALU.mult)
        return crep

    def s0_units(ms, ncol, wsl, brow, gt_, mt, crep):
        k = 0
        nct = D // ncol
        for mi, m in enumerate(ms):
            sub = m // 3
            kind = m % 3
            g = gt_[mi % len(gt_)]
            if kind == 1:
                dma("sp", g.v(), V(gains[2 * sub:2 * sub + 1, :].partition_broadcast(128)), (g.buf, 0))
            if kind == 2:
                dma("sp", g.v(), V(gains[2 * sub + 1:2 * sub + 2, :].partition_broadcast(128)), (g.buf, 0))
            mtile = mt[mi % len(mt)]
            for ct in range(nct):
                c0 = m * D + ct * ncol
                w = wsl[k % len(wsl)]
                br = brow[k % len(brow)]
                k += 1
                wslab(w.v(None, np.s_[:, :, 0:ncol]), w_ada[:, c0:c0 + ncol])
                dma("pool", br.v(None, np.s_[:, 0:ncol]), V(b_ada[0:1, c0:c0 + ncol]), (br.buf, 0))
                bi = nb()
                for kc in range(16):
                    mm(bank(bi, ncol), crep.v(None, np.s_[:, kc, :]), w.v(None, np.s_[:, kc, 0:ncol]), start=(kc == 0), stop=False)
                mm(bank(bi, ncol), onesb.v(None, np.s_[0:1, :]), br.v(None, np.s_[:, 0:ncol]), start=False, stop=True)
                cs_ = np.s_[:, ct * ncol:(ct + 1) * ncol]
                o = mtile.v(None, cs_)
                if kind == 0:
                    cp("act", o, bank(bi, ncol))
                elif kind == 1:
                    stt("dve", o, bank(bi, ncol), 1.0, g.v(None, cs_), ALU.add, ALU.mult)
                else:
                    coef = 1.0 if sub == 1 else 0.5
                    stt("dve", o, bank(bi, ncol), coef, g.v(None, cs_), ALU.mult, ALU.mult)
                yield
            dma("sp", modS.v(m, np.s_[m]), mtile.v(), (mtile.buf, 0))
            yield

    S0_FIRST = [0, 1, 2, 3, 4]
    S0_LATE = [5, 6, 7, 8]

    def stage0():
        A.push()
        cv = TT(A.alloc([16], F32), "cv")
        wsl = [TT(A.alloc([16, 512], BF16), f"s0w{i}") for i in range(3)]
        brow = [TT(A.alloc([512], BF16, parts=1), f"s0b{i}") for i in range(3)]
        gt_ = [TT(A.alloc([D], F32), f"s0g{i}") for i in range(2)]
        mt = [TT(A.alloc([D], F32), f"s0m{i}") for i in range(2)]
        dma("sp", cv.v(), V(cvec), (cv.buf, 0))
        act(cs_p.v(), cv.v(), AF.Silu)
        crep = make_crep()
        for _ in s0_units(S0_FIRST if upto >= 2 else list(range(9)), 512, wsl, brow, gt_, mt, crep):
            pass
        stage_end()
        A.pop()

    def prenorm_tiles(src, src_is_input, tiles, At, Bt, xt, hb, hT, hT_part_of, col_of, extra_reads=()):
        n = len(tiles)

        def load(i):
            t_ = tiles[i]
            xv = xt[i % 2].v()
            if src_is_input:
                dma("sp", xv, V(src[t_ * 128:(t_ + 1) * 128, :]), (xt[i % 2].buf, 0))
            else:
                dma("sp", xv, src.v(t_, np.s_[t_ * 128:(t_ + 1) * 128, :]), (xt[i % 2].buf, 0))

        def s1(i):
            xv = xt[i % 2].v()
            hv = hb[i % 2].v()
            r = rstd_from(xv, D, hv)
            stt("dve", xv, xv, r, At.v(), ALU.mult, ALU.mult)
            tt("dve" if i % 2 == 0 else "pool", hv, xv, Bt.v(), ALU.add)

        def s2(i):
            t_ = tiles[i]
            c0 = col_of(t_)
            for half in range(2):
                bi = nb()
                pv = bank_bf(bi)
                for q in range(8):
                    kc = half * 8 + q
                    tr(V(pv[:, q * 128:(q + 1) * 128], (PB, bi)), hb[i % 2].v(None, np.s_[:, kc * 128:(kc + 1) * 128]))
                dst = V(hT.ap[:, half * 8:half * 8 + 8, c0:c0 + 128], (hT.buf, hT_part_of(t_)))
                srcv = V(pv.rearrange("p (a b) -> p a b", b=128), (PB, bi), *extra_reads)
                cp("act" if half == 0 else "dve", dst, srcv)

        load(0)
        if n > 1:
            load(1)
        s1(0)
        for i in range(n):
            if i + 1 < n:
                s1(i + 1)
            s2(i)
            if i + 2 < n:
                load(i + 2)

    def postnorm_tiles(ytile_of, tiles, Gt, res_src, res_is_input, dst, dst_is_output, xt, junk, add_eng="dve"):
        n = len(tiles)

        def load(i):
            t_ = tiles[i]
            xv = xt[i % 2].v()
            if res_is_input:
                dma("sp", xv, V(res_src[t_ * 128:(t_ + 1) * 128, :]), (xt[i % 2].buf, 0))
            else:
                dma("sp", xv, res_src.v(t_, np.s_[t_ * 128:(t_ + 1) * 128, :]), (xt[i % 2].buf, 0))

        load(0)
        if n > 1:
            load(1)
        for i, t_ in enumerate(tiles):
            yv = ytile_of(i)
            xv = xt[i % 2].v()
            r = rstd_from(yv, D, junk.v())
            stt("dve", yv, yv, r, Gt.v(), ALU.mult, ALU.mult)
            tt(add_eng if i % 2 == 1 else "dve", xv, yv, xv, ALU.add)
            if dst_is_output:
                dma("sp", V(dst[t_ * 128:(t_ + 1) * 128, :]), xv, (xt[i % 2].buf, 0))
            else:
                dma("sp", dst.v(t_, np.s_[t_ * 128:(t_ + 1) * 128, :]), xv, (xt[i % 2].buf, 0))
            if i + 2 < n:
                load(i + 2)

    def ffn_stage(src, src_is_input, dst, dst_is_output, mbase, w1, w2):
        A.push()
        FB = min(1024, T)
        NFB = T // FB
        TB = FB // 128
        HB = min(512, FB)
        NH = FB // HB
        At = TT(A.alloc([D], F32), "ffA")
        Bt = TT(A.alloc([D], F32), "ffB")
        Gt = TT(A.alloc([D], F32), "ffG")
        dma("sp", Bt.v(), modS.v(mbase, np.s_[mbase]), (Bt.buf, 0))
        dma("sp", At.v(), modS.v(mbase + 1, np.s_[mbase + 1]), (At.buf, 0))
        dma("sp", Gt.v(), modS.v(mbase + 2, np.s_[mbase + 2]), (Gt.buf, 0))
        Rw = A.alloc([8192], F32)
        Rbuf = Buf("ffR", 4)
        hT = TT(Rw.bitcast(BF16)[:, 0:16 * FB].rearrange("p (a b) -> p a b", b=FB), "ffhT", TB)
        yv_ap = Rw.rearrange("p (a b) -> p a b", b=D)
        aT = TT(A.alloc([NFC, FB], BF16), "ffaT", NFC)
        w1s = [TT(A.alloc([16, 2, 128], BF16), f"ffw1_{i}") for i in range(2)]
        w2s = [TT(A.alloc([4, 512], BF16), f"ffw2_{i}") for i in range(3)]
        xt = [TT(A.alloc([D], F32), f"ffx{i}") for i in range(2)]
        hb = [TT(A.alloc([D], BF16), f"ffh{i}") for i in range(2)]
        sg = [TT(A.alloc([512], F32), f"ffsg{i}") for i in range(2)]
        junk = TT(A.alloc([D], BF16), "ffjunk")
        kw2 = 0
        for fb in range(NFB):
            tiles = [fb * TB + i for i in range(TB)]

            def ld_w1(j):
                w = w1s[j % 2]
                dma("pool", w.v(None, np.s_[:, :, 0, :]), V(w1[:, j * 128:(j + 1) * 128].rearrange("(kc p) n -> p kc n", p=128)), (w.buf, 0))
                dma("pool", w.v(None, np.s_[:, :, 1, :]), V(w1[:, DFF + j * 128:DFF + (j + 1) * 128].rearrange("(kc p) n -> p kc n", p=128)), (w.buf, 0))

            ld_w1(0)
            ld_w1(1)
            prenorm_tiles(src, src_is_input, tiles, At, Bt, xt, hb, hT, lambda t_: t_ - fb * TB, lambda t_: (t_ - fb * TB) * 128, extra_reads=((Rbuf, None),))
            for j in range(NFC):
                w = w1s[j % 2]
                pg = [nb() for _ in range(NH)]
                pu = [nb() for _ in range(NH)]
                for gi, pbs in ((0, pg), (1, pu)):
                    for kc in range(16):
                        for h_ in range(NH):
                            mm(bank(pbs[h_], HB), w.v(None, np.s_[:, kc, gi, :]), hT.v(None, np.s_[:, kc, h_ * HB:(h_ + 1) * HB]), start=(kc == 0), stop=(kc == 15))
                if j + 2 < NFC:
                    ld_w1(j + 2)
                for h_ in range(NH):
                    sgv = sg[h_ % 2].v(None, np.s_[:, 0:HB])
                    act(sgv, bank(pg[h_], HB), AF.Silu)
                    tt("dve", aT.v(j, np.s_[:, j, h_ * HB:(h_ + 1) * HB]), sgv, bank(pu[h_], HB), ALU.mult)
            for th in range(NH):
                nt4 = HB // 128
                for dt in range(4):
                    pys = [nb() for _ in range(nt4)]
                    j = 0
                    while j < NFC:
                        g = min(4, NFC - j)
                        w = w2s[kw2 % 3]
                        kw2 += 1
                        dma("pool", w.v(None, np.s_[:, 0:g, :]), V(w2[j * 128:(j + g) * 128, dt * 512:(dt + 1) * 512].rearrange("(a p) n -> p a n", p=128)), (w.buf, 0))
                        for a in range(g):
                            for t4 in range(nt4):
                                c0 = (th * nt4 + t4) * 128
                                mm(bank(pys[t4]), aT.v(j + a, np.s_[:, j + a, c0:c0 + 128]), w.v(None, np.s_[:, a, :]), start=(j + a == 0), stop=(j + a == NFC - 1))
                        j += g
                    for t4 in range(nt4):
                        cp("act", V(yv_ap[:, t4, dt * 512:(dt + 1) * 512], (Rbuf, t4)), bank(pys[t4]))
                tiles_h = [fb * TB + th * nt4 + t4 for t4 in range(nt4)]
                postnorm_tiles(lambda i: V(yv_ap[:, i, :], (Rbuf, i)), tiles_h, Gt, src, src_is_input, dst, dst_is_output, xt, junk,
                               add_eng=("pool" if th == NH - 1 else "dve"))
        stage_end()
        A.pop()

    def mixer_stages():
        BW = min(512, T)
        NBLK = T // BW
        A.push()
        cqnT = TT(A.alloc([4, T], BF16), "cqnT", NT)
        ckvT = TT(A.alloc([4, NK], BF16), "ckvT", NKT)
        krT = TT(A.alloc([NK], BF16), "krT", NKT + 1)
        GAT.buf = Buf("GAT", 16 * NBLK)
        GBT.buf = Buf("GBT", 16 * NBLK)
        OGT.buf = Buf("OGT", NT)
        OMT.buf = Buf("OMT", 16)
        QD.buf = Buf("QD", 4 * NT)
        KI.buf = Buf("KI", 4 * NT)
        KE.buf = Buf("KE", 4 * NT)
        ED.buf = Buf("ED", 4 * NT)

        A.push()
        h2T = TT(A.alloc([16, T], BF16), "h2T", NT)
        aT2 = [TT(A.alloc([T], BF16), "aTf"), TT(A.alloc([T], BF16), "aTb")]

        A.push()
        At = TT(A.alloc([D], F32), "m1A")
        Bt = TT(A.alloc([D], F32), "m1B")
        dma("sp", Bt.v(), modS.v(3, np.s_[3]), (Bt.buf, 0))
        dma("sp", At.v(), modS.v(4, np.s_[4]), (At.buf, 0))
        xt = [TT(A.alloc([D], F32), f"m1x{i}") for i in range(2)]
        hb = [TT(A.alloc([D], BF16), f"m1h{i}") for i in range(2)]
        Wsm = TT(A.alloc([16, 1120], BF16), "Wsm")
        wslab(Wsm.v(), w_inp[:, 6144:7264])
        qnb = TT(A.alloc([512], F32), "qnb")
        kvnb = TT(A.alloc([512], F32), "kvnb")
        dma("sp", qnb.v(), V(q_norm.partition_broadcast(128)), (qnb.buf, 0))
        dma("sp", kvnb.v(), V(kv_norm.partition_broadcast(128)), (kvnb.buf, 0))
        junk5 = TT(A.alloc([512], BF16), "junk5")
        cqb = [TT(A.alloc([512], BF16), f"cqb{i}") for i in range(2)]
        ckf = [TT(A.alloc([512], F32), f"ckf{i}") for i in range(2)]
        ckb = [TT(A.alloc([512], BF16), f"ckb{i}") for i in range(2)]
        krf = [TT(A.alloc([64], F32), f"krf{i}") for i in range(2)]
        krb = [TT(A.alloc([64], BF16), f"krb{i}") for i in range(2)]
        kt1 = [TT(A.alloc([64], F32), f"kt1{i}") for i in range(2)]
        kt2 = [TT(A.alloc([64], F32), f"kt2{i}") for i in range(2)]
        cosk = [TT(A.alloc([64], F32), f"cosk{i}") for i in range(2)]
        sink = [TT(A.alloc([64], F32), f"sink{i}") for i in range(2)]
        dma("pool", krT.v(NKT, np.s_[65:81, :]), V(mk_in), (krT.buf, NKT))
        for c2 in range(2):
            k = c2 % 2
            dma("sp", ckf[k].v(), V(cckv[c2 * 128:(c2 + 1) * 128, :]), (ckf[k].buf, 0))
            cp("act", ckb[k].v(), ckf[k].v())
            b = nb()
            for q in range(4):
                tr(V(bank_bf(b)[:, q * 128:(q + 1) * 128], (PB, b)), ckb[k].v(None, np.s_[:, q * 128:(q + 1) * 128]))
            cp("dve", ckvT.v(c2, np.s_[:, 0:4, c2 * 128:(c2 + 1) * 128]),
               V(bank_bf(b)[:, 0:512].rearrange("p (a b) -> p a b", b=128), (PB, b)))
            dma("sp", krf[k].v(), V(ckr[c2 * 128:(c2 + 1) * 128, :]), (krf[k].buf, 0))
            cp("act", krb[k].v(), krf[k].v())
            b = nb()
            tr(V(bank_bf(b)[0:64, 0:128], (PB, b)), krb[k].v())
            cp("dve", krT.v(c2, np.s_[0:64, c2 * 128:(c2 + 1) * 128]), V(bank_bf(b)[0:64, 0:128], (PB, b)))
        prenorm_tiles(X1, False, list(range(NT)), At, Bt, xt, hb, h2T, lambda t_: t_, lambda t_: t_ * 128)
        def m1a_A(t_):
            k = t_ % 2
            tc = np.s_[t_ * 128:(t_ + 1) * 128]
            dma("sp", cosk[k].v(), V(cosK_in[tc, :]), (cosk[k].buf, 0))
            dma("sp", sink[k].v(), V(sinK_in[tc, :]), (sink[k].buf, 0))
            b1 = nb()
            for kc in range(16):
                mm(bank(b1), h2T.v(t_, np.s_[:, kc, tc]), Wsm.v(None, np.s_[:, kc, 32:544]), start=(kc == 0), stop=(kc == 15))
            b3 = nb()
            for kc in range(16):
                mm(bank(b3), h2T.v(t_, np.s_[:, kc, tc]), Wsm.v(None, np.s_[:, kc, 544:1056]), start=(kc == 0), stop=(kc == 15))
            b5 = nb()
            for kc in range(16):
                mm(bank(b5, 64), h2T.v(t_, np.s_[:, kc, tc]), Wsm.v(None, np.s_[:, kc, 1056:1120]), start=(kc == 0), stop=(kc == 15))
            r = rstd_from(bank(b1), 512, junk5.v())
            stt("dve", cqb[k].v(), bank(b1), r, qnb.v(), ALU.mult, ALU.mult)
            r = rstd_from(bank(b3), 512, junk5.v())
            stt("dve", ckf[k].v(), bank(b3), r, kvnb.v(), ALU.mult, ALU.mult)
            dma("sp", V(nckv_out[tc, :]), ckf[k].v(), (ckf[k].buf, 0))
            cp("act", ckb[k].v(), ckf[k].v())
            cp("act", krf[k].v(), bank(b5, 64))
            dma("sp", V(nkr_out[tc, :]), krf[k].v(), (krf[k].buf, 0))
            tt("dve", kt1[k].v(), krf[k].v(), cosk[k].v(), ALU.mult)
            x4 = krf[k].ap.rearrange("p (a h j) -> p a h j", a=2, h=2)
            s4 = sink[k].ap.rearrange("p (a h j) -> p a h j", a=2, h=2)
            o4 = kt2[k].ap.rearrange("p (a h j) -> p a h j", a=2, h=2)
            for hh in range(2):
                tt("dve", V(o4[:, :, hh, :], (kt2[k].buf, None)), V(x4[:, :, 1 - hh, :], (krf[k].buf, None)),
                   V(s4[:, :, hh, :], (sink[k].buf, None)), ALU.mult)
            tt("dve", krb[k].v(), kt1[k].v(), kt2[k].v(), ALU.add)

        def m1a_B(t_):
            k = t_ % 2
            tc = np.s_[t_ * 128:(t_ + 1) * 128]
            kc0 = 256 + t_ * 128
            b2 = nb()
            for q in range(4):
                tr(V(bank_bf(b2)[:, q * 128:(q + 1) * 128], (PB, b2)), cqb[k].v(None, np.s_[:, q * 128:(q + 1) * 128]))
            for q in range(4):
                tr(V(bank_bf(b2)[:, 512 + q * 128:512 + (q + 1) * 128], (PB, b2)), ckb[k].v(None, np.s_[:, q * 128:(q + 1) * 128]))
            b6 = nb()
            tr(V(bank_bf(b6)[0:64, 0:128], (PB, b6)), krb[k].v())
            cp("act", cqnT.v(t_, np.s_[:, 0:4, tc]), V(bank_bf(b2)[:, 0:512].rearrange("p (a b) -> p a b", b=128), (PB, b2)))
            cp("dve", ckvT.v(2 + t_, np.s_[:, 0:4, kc0:kc0 + 128]), V(bank_bf(b2)[:, 512:1024].rearrange("p (a b) -> p a b", b=128), (PB, b2)))
            cp("dve", krT.v(2 + t_, np.s_[0:64, kc0:kc0 + 128]), V(bank_bf(b6)[0:64, 0:128], (PB, b6)))

        m1a_A(0)
        for t_ in range(NT):
            if t_ + 1 < NT:
                m1a_A(t_ + 1)
            m1a_B(t_)
        for blk in range(NBLK):
            cols = np.s_[blk * BW:(blk + 1) * BW]
            tparts = tuple(range(blk * (BW // 128), (blk + 1) * (BW // 128)))
            for d_ in range(2):
                b = nb()
                for kc in range(16):
                    mm(bank(b, BW, rows=16), Wsm.v(None, np.s_[:, kc, d_ * 16:(d_ + 1) * 16]),
                       V(h2T.ap[:, kc, cols], (h2T.buf, tparts)), start=(kc == 0), stop=(kc == 15))
                cp("act", aT2[d_].v(None, np.s_[0:16, cols]), bank(b, BW, rows=16))
        stage_end()
        A.pop()

        A.push()
        Wqk = TT(A.alloc([16, 1024], BF16), "Wqk", 2)
        wal = TT(A.alloc([2, 1024], BF16), "wal")
        bal = TT(A.alloc([2, 1024], BF16), "bal")
        dma("pool", wal.v(None, np.s_[0:16]), V(w_alpha.rearrange("d r n -> r d n")), (wal.buf, 0))
        dma("pool", bal.v(None, np.s_[0:1]), V(b_alpha.rearrange("(o d) n -> o d n", o=1)), (bal.buf, 0))
        qf = [TT(A.alloc([512], F32), f"qf{i}") for i in range(3)]
        kf = [TT(A.alloc([512], F32), f"kf{i}") for i in range(3)]
        ef = TT(A.alloc([512], F32), "ef")
        spf = [TT(A.alloc([512], F32), f"spf{i}") for i in range(6)]
        ebf = [TT(A.alloc([512], F32), f"ebf{i}") for i in range(2)]
        eif = [TT(A.alloc([512], F32), f"eif{i}") for i in range(2)]
        eef = [TT(A.alloc([512], F32), f"eef{i}") for i in range(2)]
        qdb = [TT(A.alloc([512], BF16), f"qdb{i}") for i in range(4)]
        kib = [TT(A.alloc([512], BF16), f"kib{i}") for i in range(4)]
        keb = [TT(A.alloc([512], BF16), f"keb{i}") for i in range(2)]
        qdT = [TT(A.alloc([4, 128], BF16), f"qdT{i}") for i in range(2)]
        kiT = [TT(A.alloc([4, 128], BF16), f"kiT{i}") for i in range(2)]
        edc = [TT(A.alloc([4], F32), f"edc{i}") for i in range(2)]
        items = [(ch, t_) for ch in range(2) for t_ in range(NT)]

        def ld_wqk(ch):
            c0 = ch * 512
            dma("pool", Wqk.v(0, np.s_[:, :, 0:512]), V(w_inp[:, c0:c0 + 512].rearrange("(kc p) n -> p kc n", p=128)), (Wqk.buf, 0))
            dma("pool", Wqk.v(1, np.s_[:, :, 512:1024]), V(w_inp[:, 1024 + c0:1024 + c0 + 512].rearrange("(kc p) n -> p kc n", p=128)), (Wqk.buf, 1))

        def p1(n):
            ch, t_ = items[n]
            c0 = ch * 512
            if t_ == 0:
                ld_wqk(ch)
            k = n % 3
            tc = np.s_[t_ * 128:(t_ + 1) * 128]
            bq = nb()
            for kc in range(16):
                mm(bank(bq), h2T.v(t_, np.s_[:, kc, tc]), Wqk.v(0, np.s_[:, kc, 0:512]), start=(kc == 0), stop=(kc == 15))
            cp("act", qf[k].v(), bank(bq), scale=1.0 / 16.0)
            bk = nb()
            for kc in range(16):
                mm(bank(bk), h2T.v(t_, np.s_[:, kc, tc]), Wqk.v(1, np.s_[:, kc, 512:1024]), start=(kc == 0), stop=(kc == 15))
            cp("act", kf[k].v(), bank(bk))
            for d_ in range(2):
                sp_ = spf[(n % 3) * 2 + d_]
                bz = nb()
                mm(bank(bz), aT2[d_].v(None, np.s_[0:16, tc]), wal.v(None, np.s_[0:16, d_, c0:c0 + 512]), start=True, stop=False)
                mm(bank(bz), onesb.v(None, np.s_[0:1, :]), bal.v(None, np.s_[0:1, d_, c0:c0 + 512]), start=False, stop=True)
                act(ef.v(), bank(bz), AF.Exp, scale=-1.0)
                act(sp_.v(), ef.v(), AF.Ln, bias=1.0)

        def p2(n):
            ch, t_ = items[n]
            c0 = ch * 512
            k = n % 3
            for d_ in range(2):
                sp_ = spf[(n % 3) * 2 + d_]
                kk = d_
                k4 = (n % 2) * 2 + d_
                part = (d_ * NT + t_) * 2 + ch
                Lm = Lf if d_ == 0 else Lb
                Um = Uf if d_ == 0 else Ub
                bB = nb()
                mm(bank(bB), Lm, sp_.v())
                bE = nb()
                mm(bank(bE), Um, sp_.v())
                bl = nb()
                for c in range(4):
                    mm(bank(bl, 1, off=c), sp_.v(None, np.s_[:, c * 128:(c + 1) * 128]), ones_col)
                act(ebf[kk].v(), bank(bB), AF.Exp, scale=-1.0 / 16.0)
                act(eif[kk].v(), bank(bB), AF.Exp, scale=1.0 / 16.0)
                act(eef[kk].v(), bank(bE), AF.Exp, scale=-1.0 / 16.0)
                act(edc[kk].v(), bank(bl, 4), AF.Exp, scale=-1.0 / 16.0)
                dma("sp", ED.v(part, np.s_[d_, t_, :, ch * 4:(ch + 1) * 4]), edc[kk].v(), (edc[kk].buf, 0))
                tt("dve", qdb[k4].v(), qf[k].v(), ebf[kk].v(), ALU.mult)
                tt("dve", kib[k4].v(), kf[k].v(), eif[kk].v(), ALU.mult)
                tt("pool", keb[kk].v(), kf[k].v(), eef[kk].v(), ALU.mult)
                dma("sp", KE.v(part, np.s_[d_, t_, :, c0:c0 + 512]), keb[kk].v(), (keb[kk].buf, 0))

        def p3(n):
            ch, t_ = items[n]
            for d_ in range(2):
                kk = d_
                k4 = (n % 2) * 2 + d_
                part = (d_ * NT + t_) * 2 + ch
                for (srcb, dstT, DR, eng) in ((qdb[k4], qdT[kk], QD, "act"), (kib[k4], kiT[kk], KI, "dve")):
                    bt = nb()
                    for q in range(4):
                        tr(V(bank_bf(bt)[:, q * 128:(q + 1) * 128], (PB, bt)), srcb.v(None, np.s_[:, q * 128:(q + 1) * 128]))
                    cp(eng, dstT.v(), V(bank_bf(bt)[:, 0:512].rearrange("p (a b) -> p a b", b=128), (PB, bt)))
                    dma("sp", DR.v(part, np.s_[d_, t_, :, ch * 4:(ch + 1) * 4, :]), dstT.v(), (dstT.buf, 0))

        NI = len(items)
        for n in range(NI + 2):
            if n < NI:
                p1(n)
            if 0 <= n - 1 < NI:
                p2(n - 1)
            if 0 <= n - 2 < NI:
                p3(n - 2)
        stage_end()
        A.pop()

        A.push()
        wsl = [TT(A.alloc([16, 512], BF16), f"m1cw{i}") for i in range(2)]
        vb = [TT(A.alloc([512], BF16), f"m1cv{i}") for i in range(4)]
        gw = [TT(A.alloc([16, 128], BF16), f"m1cg{i}") for i in range(2)]
        gtb = [TT(A.alloc([BW], BF16), f"m1cgt{i}") for i in range(4)]
        bg_w = [TT(A.alloc([16, 256], BF16), f"bgw{i}") for i in range(2)]
        bg_b = [TT(A.alloc([256], BF16, parts=1), f"bgb{i}") for i in range(2)]
        bg_g = [TT(A.alloc([D], F32), "bgg")]
        bg_m = [TT(A.alloc([D], F32), "bgm0")]
        bg = s0_units(S0_LATE, 256, bg_w, bg_b, bg_g, bg_m, make_crep())
        kq = 0
        kv_ = 0
        for (cbase, DR, fn) in ((2048, VV, AF.Copy), (4096, RR, AF.Silu)):
            for s4 in range(4):
                w = wsl[kq % 2]
                kq += 1
                wslab(w.v(), w_inp[:, cbase + s4 * 512:cbase + (s4 + 1) * 512])
                for t_ in range(NT):
                    tc = np.s_[t_ * 128:(t_ + 1) * 128]
                    b = nb()
                    for kc in range(16):
                        mm(bank(b), h2T.v(t_, np.s_[:, kc, tc]), w.v(None, np.s_[:, kc, :]), start=(kc == 0), stop=(kc == 15))
                    o = vb[kv_ % 4]
                    kv_ += 1
                    if fn == AF.Copy and (kv_ % 2 == 0):
                        cp("dve", o.v(), bank(b))
                    else:
                        act(o.v(), bank(b), fn)
                    dma("sp", DR.v(t_, np.s_[tc, s4 * 512:(s4 + 1) * 512]), o.v(), (o.buf, 0))
                    if t_ % 4 == 3:
                        next(bg, None)
        kg = 0
        ko = 0
        for (cbase, DR) in ((7264, GAT), (9312, GBT)):
            for dc in range(16):
                w = gw[kg % 2]
                kg += 1
                wslab(w.v(), w_inp[:, cbase + dc * 128:cbase + (dc + 1) * 128])
                for blk in range(NBLK):
                    cols = np.s_[blk * BW:(blk + 1) * BW]
                    tparts = tuple(range(blk * (BW // 128), (blk + 1) * (BW // 128)))
                    b = nb()
                    for kc in range(16):
                        mm(bank(b, BW), w.v(None, np.s_[:, kc, :]), V(h2T.ap[:, kc, cols], (h2T.buf, tparts)), start=(kc == 0), stop=(kc == 15))
                    o = gtb[ko % 4]
                    ko += 1
                    act(o.v(), bank(b, BW), AF.Sigmoid)
                    dma("sp", DR.v(dc * NBLK + blk, np.s_[dc, :, cols]), o.v(), (o.buf, 0))
                next(bg, None)
        for _ in bg:
            pass
        stage_end()
        A.pop()
        A.pop()

        A.push()
        Stl = [TT(A.alloc([8, 512], F32), f"St{i}", 8) for i in range(2)]
        edf = TT(A.alloc([8], F32), "edf")
        Sbf = TT(A.alloc([8, 512], BF16), "Sbf", 8)
        qd = [TT(A.alloc([8, 128], BF16), f"qd{i}") for i in range(3)]
        ki = [TT(A.alloc([8, 128], BF16), f"ki{i}") for i in range(3)]
        ke = [TT(A.alloc([1024], BF16), f"ke{i}") for i in range(3)]
        vt = [TT(A.alloc([D], BF16), f"vt{i}") for i in range(3)]
        ed = [TT(A.alloc([8], F32), f"ed{i}") for i in range(3)]
        ATs = [TT(A.alloc([512], BF16), f"ATs{i}") for i in range(2)]
        Acp = [TT(A.alloc([512], BF16), f"Acp{i}") for i in range(2)]
        maskf = [TT(A.alloc([512], F32), f"maskf{i}") for i in range(2)]
        for d_ in range(2):
            for h in range(4):
                cp("dve", maskf[d_].v(None, np.s_[:, h * 128:(h + 1) * 128]), Lf if d_ == 0 else Lb)
        ot = [TT(A.alloc([D], F32), f"ot{i}", 4) for i in range(2)]
        oft = [TT(A.alloc([D], F32), f"oft{i}") for i in range(3)]
        rt = [TT(A.alloc([D], BF16), f"rt{i}") for i in range(3)]
        ogb = [TT(A.alloc([D], BF16), f"ogb{i}", 4) for i in range(2)]
        ogTt = [TT(A.alloc([16, 128], BF16), f"ogTt{i}") for i in range(2)]
        gnb = TT(A.alloc([512], F32), "gnb")
        dma("sp", gnb.v(), V(gla_norm.partition_broadcast(128)), (gnb.buf, 0))
        tmpf = [TT(A.alloc([512], F32), f"m2tmp{i}") for i in range(4)]
        junk5 = TT(A.alloc([512], BF16), "m2junk")
        for d_ in range(2):
            cur = 0
            cross = False
            dma("sp", Stl[cur].v(None), V(s0[d_].rearrange("h (kc p) v -> p (h kc) v", p=128)), (Stl[cur].buf, 0))
            cp("act", Sbf.v(None), Stl[cur].v(None))
            order = list(range(NT)) if d_ == 0 else list(range(NT - 1, -1, -1))

            def load(i):
                t_ = order[i]
                k = i % 3
                tc = np.s_[t_ * 128:(t_ + 1) * 128]
                pr = ((d_ * NT + t_) * 2, (d_ * NT + t_) * 2 + 1)
                dma("sp", qd[k].v(), V(QD.ap[d_, t_], (QD.buf, pr)), (qd[k].buf, 0))
                dma("sp", ki[k].v(), V(KI.ap[d_, t_], (KI.buf, pr)), (ki[k].buf, 0))
                dma("sp", ke[k].v(), V(KE.ap[d_, t_], (KE.buf, pr)), (ke[k].buf, 0))
                dma("sp", ed[k].v(), V(ED.ap[d_, t_], (ED.buf, pr)), (ed[k].buf, 0))
                dma("sp", vt[k].v(), VV.v(t_, np.s_[tc, :]), (vt[k].buf, 0))
                if d_ == 1:
                    dma("sp", oft[k].v(), OF.v(t_, np.s_[tc, :]), (oft[k].buf, 0))
                    dma("sp", rt[k].v(), RR.v(t_, np.s_[tc, :]), (rt[k].buf, 0))

            load(0)
            if NT > 1:
                load(1)
            def emitA(i):
                k = i % 3
                k2 = i % 2
                ba = nb()
                for h in range(4):
                    for kc in range(2):
                        mm(bank(ba, 128, off=h * 128), ki[k].v(None, np.s_[:, h * 2 + kc, :]), qd[k].v(None, np.s_[:, h * 2 + kc, :]), start=(kc == 0), stop=(kc == 1))
                tt("dve", ATs[k2].v(), bank(ba), maskf[d_].v(), ALU.mult)

            emitA(0)
            kcp = 0
            for i, t_ in enumerate(order):
                k = i % 3
                k2 = i % 2
                tc = np.s_[t_ * 128:(t_ + 1) * 128]
                if i + 2 < NT:
                    load(i + 2)
                if i + 1 < NT:
                    emitA(i + 1)
                for hp in range(2):
                    bss = {}
                    for h in (2 * hp, 2 * hp + 1):
                        hc = np.s_[:, h * 512:(h + 1) * 512]
                        for kc in range(2):
                            c = h * 2 + kc
                            bs = nb()
                            bss[c] = bs
                            mm(bank(bs), ke[k].v(None, np.s_[:, c * 128:(c + 1) * 128]), vt[k].v(None, hc))
                    for h in (2 * hp, 2 * hp + 1):
                        hc = np.s_[:, h * 512:(h + 1) * 512]
                        bo = nb()
                        mm(bank(bo), ATs[k2].v(None, np.s_[:, h * 128:(h + 1) * 128]), vt[k].v(None, hc), start=True, stop=False)
                        for kc in range(2):
                            c = h * 2 + kc
                            mm(bank(bo), qd[k].v(None, np.s_[:, c, :]), Sbf.v(c, np.s_[:, c, :]), start=False, stop=(kc == 1))
                        if d_ == 0:
                            cp("act", ot[k2].v(h, hc), bank(bo))
                        else:
                            tt("dve", ot[k2].v(h, hc), bank(bo), oft[k].v(None, hc), ALU.add)
                    for h in (2 * hp, 2 * hp + 1):
                        for kc in range(2):
                            c = h * 2 + kc
                            Ssrc = Stl[cur]
                            Sdst = Stl[1 - cur] if cross else Stl[cur]
                            edv = edf.v(None, np.s_[:, c:c + 1]) if cross else ed[k].v(None, np.s_[:, c:c + 1])
                            stt("dve", Sdst.v(c, np.s_[:, c, :]), Ssrc.v(c, np.s_[:, c, :]), edv, bank(bss[c]), ALU.mult, ALU.add)
                            cp("act" if (d_ == 1 or kcp % 2 == 0) else "dve", Sbf.v(c, np.s_[:, c, :]), Sdst.v(c, np.s_[:, c, :]))
                            kcp += 1
                if d_ == 0:
                    dma("act", OF.v(t_, np.s_[tc, :]), ot[k2].v(None), (ot[k2].buf, 0))
                else:
                    for h in range(4):
                        hc = np.s_[:, h * 512:(h + 1) * 512]
                        r = rstd_from(ot[k2].v(h, hc), 512, junk5.v())
                        tm = tmpf[h % 4]
                        stt("dve", tm.v(), ot[k2].v(h, hc), r, gnb.v(), ALU.mult, ALU.mult)
                        tt("pool", ogb[k2].v(h, hc), tm.v(), rt[k].v(None, hc), ALU.mult)
                    for half in range(2):
                        bt = nb()
                        for q in range(8):
                            kc = half * 8 + q
                            tr(V(bank_bf(bt)[:, q * 128:(q + 1) * 128], (PB, bt)), ogb[k2].v(kc // 4, np.s_[:, kc * 128:(kc + 1) * 128]))
                        cp("act" if half == 0 else "dve", ogTt[k2].v(None, np.s_[:, half * 8:half * 8 + 8, :]),
                           V(bank_bf(bt).rearrange("p (a b) -> p a b", b=128), (PB, bt)))
                    dma("pool", OGT.v(t_, np.s_[:, :, tc]), ogTt[k2].v(), (ogTt[k2].buf, 0))
                end_seg = (t_ % 2 == 1) if d_ == 0 else (t_ % 2 == 0)
                if cross:
                    cur = 1 - cur
                    cross = False
                if end_seg:
                    seg = t_ // 2
                    dst = (sf_out if d_ == 0 else sb_out)[seg].rearrange("h (kc p) v -> p (h kc) v", p=128)
                    dma("act", V(dst), Stl[cur].v(None), (Stl[cur].buf, 1))
                    if i != NT - 1:
                        ts("dve", edf.v(), ed[(i + 1) % 3].v(), flag.v(None, np.s_[:, 0:1]), ALU.mult)
                        act(Sbf.v(None), Stl[cur].v(None), AF.Copy, scale=flag.v(None, np.s_[:, 0:1]))
                        cross = True
        stage_end()
        A.pop()

        A.push()
        QB = min(512, T)
        NQB = T // QB
        NQ4 = QB // 128
        kblocks = []
        ks = 0
        while ks < NK:
            kblocks.append((ks, min(512, NK - ks)))
            ks += 512
        omT = TT(A.alloc([16, T], BF16), "omT", 16)
        cosT = TT(A.alloc([T], F32), "cosT")
        sinT = TT(A.alloc([T], F32), "sinT")
        dma("sp", cosT.v(None, np.s_[0:64]), V(cosT_in), (cosT.buf, 0))
        dma("sp", sinT.v(None, np.s_[0:64]), V(sinT_in), (sinT.buf, 0))
        QrT = [TT(A.alloc([T], BF16), f"QrT{i}", 2 + NT) for i in range(2)]
        for i in range(2):
            dma("pool", QrT[i].v(1, np.s_[65:81, :]), V(mq_in), (QrT[i].buf, 1))
        S.op("dve", lambda e: e.memset(krT.ap[64:65, :], 1.0), [], [(krT.buf, NKT)])
        QnT = [TT(A.alloc([T], BF16), f"QnT{i}") for i in range(2)]
        KT = [TT(A.alloc([NK], BF16), f"KT{i}") for i in range(2)]
        Vh = [TT(A.alloc([NKT, 129], BF16), f"Vh{i}") for i in range(2)]
        for i in range(2):
            S.op("dve", lambda e, i=i: e.memset(Vh[i].ap[:, :, 128:129], 1.0), [], [(Vh[i].buf, None)])
        sel64 = TT(A.alloc([65], BF16), "sel64")
        S.op("dve", lambda e: e.memset(sel64.ap, 0.0), [], [(sel64.buf, None)])
        S.op("dve", lambda e: e.memset(sel64.ap[:, 64:65], 1.0), [], [(sel64.buf, None)])
        wkv = [TT(A.alloc([4, 256], BF16), f"wkv{i}") for i in range(2)]
        wq = [TT(A.alloc([4, 192], BF16), f"wq{i}") for i in range(2)]
        wqs = [TT(A.alloc([4, 64], BF16), f"wqs{i}") for i in range(2)]
        NPT = 4
        PT = [TT(A.alloc([QB], BF16), f"PT{i}") for i in range(NPT)]
        obt = [TT(A.alloc([128], BF16), f"obt{i}") for i in range(4)]
        dgb = [TT(A.alloc([128], BF16), f"dgb{i}") for i in range(4)]
        rt1 = TT(A.alloc([QB], F32), "rt1")
        rt2 = TT(A.alloc([QB], F32), "rt2")
        n4 = [0]
        n2 = [0]

        def nbs():
            i = n4[0]
            n4[0] = (i + 1) % 3
            return i

        def nbm():
            i = n2[0]
            n2[0] = (i + 1) % 4
            return i

        MB = 3

        def prep(h):
            k = h % 2
            dma("pool", wkv[k].v(), V(w_ukv[:, h * 256:(h + 1) * 256].rearrange("(kc p) n -> p kc n", p=128)), (wkv[k].buf, 0))
            dma("pool", wq[k].v(), V(w_uq[:, h * 192:(h + 1) * 192].rearrange("(kc p) n -> p kc n", p=128)), (wq[k].buf, 0))
            for a in range(2):
                for hh in range(2):
                    o0 = a * 32 + hh * 16
                    i0 = 128 + a * 32 + (1 - hh) * 16
                    cp("pool", wqs[k].v(None, np.s_[:, :, o0:o0 + 16]), wq[k].v(None, np.s_[:, :, i0:i0 + 16]))
            yield
            for (ks, kw) in kblocks:
                b = MB
                kparts = tuple(range(ks // 128, (ks + kw) // 128))
                for kc in range(4):
                    mm(bank(b, kw), wkv[k].v(None, np.s_[:, kc, 0:128]), V(ckvT.ap[:, kc, ks:ks + kw], (ckvT.buf, kparts)), start=(kc == 0), stop=(kc == 3))
                cp("dve", KT[k].v(None, np.s_[:, ks:ks + kw]), bank(b, kw))
                yield
            for g0 in range(0, NKT, 4):
                g = min(4, NKT - g0)
                b = MB
                for a in range(g):
                    kt_ = g0 + a
                    for kc in range(4):
                        mm(bank(b, 128, off=a * 128), ckvT.v(kt_, np.s_[:, kc, kt_ * 128:(kt_ + 1) * 128]), wkv[k].v(None, np.s_[:, kc, 128:256]), start=(kc == 0), stop=(kc == 3))
                cp("dve", Vh[k].v(None, np.s_[:, g0:g0 + g, 0:128]), V(ps[:, b * 512:b * 512 + g * 128].rearrange("p (a b) -> p a b", b=128), (PB, b)))
                yield
            for qb in range(NQB):
                cols = np.s_[qb * QB:(qb + 1) * QB]
                tparts = tuple(range(qb * NQ4, (qb + 1) * NQ4))
                b = MB
                for kc in range(4):
                    mm(bank(b, QB), wq[k].v(None, np.s_[:, kc, 0:128]), V(cqnT.ap[:, kc, cols], (cqnT.buf, tparts)), start=(kc == 0), stop=(kc == 3))
                cp("dve", QnT[k].v(None, np.s_[:, cols]), bank(b, QB))
                yield
                for kc in range(4):
                    mm(bank(b, QB, rows=64), wq[k].v(None, np.s_[:, kc, 128:192]), V(cqnT.ap[:, kc, cols], (cqnT.buf, tparts)), start=(kc == 0), stop=(kc == 3))
                tt("dve", rt1.v(None, np.s_[0:64, :]), bank(b, QB, rows=64), cosT.v(None, np.s_[0:64, cols]), ALU.mult)
                yield
                for kc in range(4):
                    mm(bank(b, QB, rows=64), wqs[k].v(None, np.s_[:, kc, :]), V(cqnT.ap[:, kc, cols], (cqnT.buf, tparts)), start=(kc == 0), stop=(kc == 3))
                tt("dve", rt2.v(None, np.s_[0:64, :]), bank(b, QB, rows=64), sinT.v(None, np.s_[0:64, cols]), ALU.mult)
                tt("dve", QrT[k].v(0, np.s_[0:64, cols]), rt1.v(None, np.s_[0:64, :]), rt2.v(None, np.s_[0:64, :]), ALU.add)
                yield
            for qb in range(NQB):
                cols = np.s_[qb * QB:(qb + 1) * QB]
                b = MB
                for q4 in range(NQ4):
                    qt = qb * NQ4 + q4
                    tc = np.s_[qt * 128:(qt + 1) * 128]
                    kd = np.s_[(2 + qt) * 128:(3 + qt) * 128]
                    mm(bank(b, 128, off=q4 * 128), QnT[k].v(None, np.s_[:, tc]), KT[k].v(None, np.s_[:, kd]), start=True, stop=False)
                    mm(bank(b, 128, off=q4 * 128), QrT[k].v(0, np.s_[0:64, tc]), krT.v(2 + qt, np.s_[0:64, kd]), start=False, stop=True)
                c0 = [scol() for _ in range(NQ4)]
                while c0[-1] != c0[0] + NQ4 - 1:
                    c0 = [scol() for _ in range(NQ4)]
                S.op("dve", lambda e, b=b, c=c0[0]: e.reduce_max(out=stat.ap[:, c:c + NQ4], in_=ps[:, b * 512:b * 512 + NQ4 * 128].rearrange("p (a b) -> p a b", b=128), axis=AX.X),
                     [(PB, b)], [(stat.buf, tuple(c0))])
                for q4 in range(NQ4):
                    ts("dve", dgb[q4].v(), identb.v(), sv(c0[q4]), ALU.mult, -1.0, ALU.mult)
                yield
                for q4 in range(NQ4):
                    mm(bank(b, 128, rows=65, off=q4 * 128), sel64.v(), dgb[q4].v())
                sparts = tuple(2 + qb * NQ4 + i for i in range(NQ4))
                cp("dve", V(QrT[k].ap[64:65, cols], (QrT[k].buf, sparts)), V(ps[64:65, b * 512:b * 512 + QB], (PB, b)))
                yield

        def run_all(gen):
            for _ in gen:
                pass

        kpt = 0
        LAG = 2
        run_all(prep(0))
        for h in range(16):
            k = h % 2
            bg = prep(h + 1) if h + 1 < 16 else iter(())
            tiles = [(qb, kt_) for qb in range(NQB) for kt_ in range(NKT)]
            pbuf = {}
            deferred = []

            def emit_qk(i):
                qb, kt_ = tiles[i]
                cols = np.s_[qb * QB:(qb + 1) * QB]
                qparts = (0, 1) + tuple(2 + qb * NQ4 + q for q in range(NQ4))
                kcs = np.s_[kt_ * 128:(kt_ + 1) * 128]
                b = nbs()
                mm(bank(b, QB), KT[k].v(None, np.s_[:, kcs]), QnT[k].v(None, np.s_[:, cols]), start=True, stop=False)
                mm(bank(b, QB), V(krT.ap[0:81, kcs], (krT.buf, (kt_, NKT))), V(QrT[k].ap[0:81, cols], (QrT[k].buf, qparts)), start=False, stop=True)
                pbuf[i] = b

            def emit_pv(i):
                nonlocal kpt
                qb, kt_ = tiles[i]
                b = pbuf.pop(i)
                p_ = PT[kpt % NPT]
                kpt += 1
                act(p_.v(), bank(b, QB), AF.Exp, scale=ATT_SCALE)
                for q4 in range(NQ4):
                    ob_ = 4 + q4
                    mm(V(ps[:, ob_ * 512:ob_ * 512 + 129], (PB, ob_)),
                       p_.v(None, np.s_[:, q4 * 128:(q4 + 1) * 128]), Vh[k].v(None, np.s_[:, kt_, :]),
                       start=(kt_ == 0), stop=(kt_ == NKT - 1))
                if kt_ == NKT - 1:
                    obs = []
                    for q4 in range(NQ4):
                        ob_ = 4 + q4
                        o0 = ob_ * 512
                        c_ri = scol()
                        recip(sv(c_ri), V(ps[:, o0 + 128:o0 + 129], (PB, ob_)))
                        ts("dve", obt[q4].v(), V(ps[:, o0:o0 + 128], (PB, ob_)), sv(c_ri), ALU.mult)

                    def fin(qb=qb):
                        for q4 in range(NQ4):
                            tr(V(bank_bf(MB)[:, q4 * 128:(q4 + 1) * 128], (PB, MB)), obt[q4].v())
                        c0_ = qb * QB
                        cp("dve", V(omT.ap[:, h, c0_:c0_ + QB], (omT.buf, h)), V(bank_bf(MB)[:, 0:QB], (PB, MB)))
                    deferred.append((i + 5, fin))

            n = len(tiles)
            for i in range(n + LAG):
                if i < n:
                    emit_qk(i)
                j = i - LAG
                if j >= 0:
                    emit_pv(j)
                while deferred and deferred[0][0] <= i:
                    deferred.pop(0)[1]()
                if i % 2 == 1:
                    next(bg, None)
            while deferred:
                deferred.pop(0)[1]()
            run_all(bg)
        for h in range(16):
            dma("sp", OMT.v(h, np.s_[:, h, :]), omT.v(h, np.s_[:, h, :]), (omT.buf, h))
        stage_end()
        A.pop()
        A.pop()

        A.push()
        FB = min(1024, T)
        NFB = T // FB
        HB = min(512, FB)
        NH = FB // HB
        nt4 = HB // 128
        Rw = A.alloc([8192], F32)
        Rbuf = Buf("m4R", 4)
        ogT_ap = Rw.bitcast(BF16)[:, 0:16 * FB].rearrange("p (a b) -> p a b", b=FB)
        y_ap = Rw.rearrange("p (a b) -> p a b", b=D)
        O2w = A.alloc([8192], F32)
        omTb = TT(O2w.bitcast(BF16)[:, 0:16 * FB].rearrange("p (a b) -> p a b", b=FB), "omTb", 4)
        y2_ap = O2w.rearrange("p (a b) -> p a b", b=D)
        mT = TT(A.alloc([16, FB], BF16), "mT", 16)
        wg = [TT(A.alloc([16, 128], BF16), f"wg{i}") for i in range(2)]
        wm = [TT(A.alloc([16, 128], BF16), f"wm{i}") for i in range(2)]
        wo = [TT(A.alloc([16, 512], BF16), f"wo{i}") for i in range(2)]
        gaT = [TT(A.alloc([FB], BF16), f"gaT{i}") for i in range(2)]
        gbT = [TT(A.alloc([FB], BF16), f"gbT{i}") for i in range(2)]
        G2 = TT(A.alloc([D], F32), "G2")
        dma("sp", G2.v(), modS.v(5, np.s_[5]), (G2.buf, 0))
        xt = [TT(A.alloc([D], F32), f"m4x{i}") for i in range(2)]
        junk = TT(A.alloc([D], BF16), "m4junk")
        t1 = [TT(A.alloc([HB], F32), f"m4t1{i}") for i in range(2)]
        t2 = [TT(A.alloc([HB], F32), f"m4t2{i}") for i in range(2)]
        kwo = 0
        for fb in range(NFB):
            cols = np.s_[fb * FB:(fb + 1) * FB]
            ttiles = tuple(range(fb * (FB // 128), (fb + 1) * (FB // 128)))
            gparts_of = lambda dc: tuple(dc * NBLK + bb for bb in range(fb * (FB // BW), (fb + 1) * (FB // BW)))
            dma("sp", V(ogT_ap, (Rbuf, None)), V(OGT.ap[:, :, cols], (OGT.buf, ttiles)), (Rbuf, 0))
            dma("sp", omTb.v(), V(OMT.ap[:, :, cols], (OMT.buf, None)), (omTb.buf, 0))
            for dc in range(16):
                k = dc % 2
                wslab(wg[k].v(), w_go[:, dc * 128:(dc + 1) * 128])
                wslab(wm[k].v(), w_mo[:, dc * 128:(dc + 1) * 128])
                dma("sp", gaT[k].v(), V(GAT.ap[dc, :, cols], (GAT.buf, gparts_of(dc))), (gaT[k].buf, 0))
                dma("sp", gbT[k].v(), V(GBT.ap[dc, :, cols], (GBT.buf, gparts_of(dc))), (gbT[k].buf, 0))
                for hh in range(NH):
                    hcs = np.s_[hh * HB:(hh + 1) * HB]
                    bg = nb()
                    for kc in range(16):
                        mm(bank(bg, HB), wg[k].v(None, np.s_[:, kc, :]), V(ogT_ap[:, kc, hcs], (Rbuf, None)), start=(kc == 0), stop=(kc == 15))
                    bm = nb()
                    for kc in range(16):
                        mm(bank(bm, HB), wm[k].v(None, np.s_[:, kc, :]), omTb.v(None, np.s_[:, kc, hcs]), start=(kc == 0), stop=(kc == 15))
                    tt("dve", t1[hh % 2].v(), bank(bg, HB), gaT[k].v(None, np.s_[:, hcs]), ALU.mult)
                    tt("dve", t2[hh % 2].v(), bank(bm, HB), gbT[k].v(None, np.s_[:, hcs]), ALU.mult)
                    tt("dve", mT.v(dc, np.s_[:, dc, hcs]), t1[hh % 2].v(), t2[hh % 2].v(), ALU.add)
            TBk = FB // 128

            def ytile(t8, sl=np.s_[:]):
                if t8 < 4:
                    return V(y_ap[:, t8, sl], (Rbuf, t8))
                return V(y2_ap[:, t8 - 4, sl], (omTb.buf, t8 - 4))

            for ct in range(4):
                w = wo[kwo % 2]
                kwo += 1
                wslab(w.v(), w_o[:, ct * 512:(ct + 1) * 512])
                pys = [nb() for _ in range(TBk)]
                for t8 in range(TBk):
                    c0 = t8 * 128
                    for kc in range(16):
                        mm(bank(pys[t8]), mT.v(kc, np.s_[:, kc, c0:c0 + 128]), w.v(None, np.s_[:, kc, :]), start=(kc == 0), stop=(kc == 15))
                    cp("act" if t8 % 2 == 0 else "dve", ytile(t8, np.s_[ct * 512:(ct + 1) * 512]), bank(pys[t8]))
            tiles_b = [fb * TBk + t8 for t8 in range(TBk)]
            postnorm_tiles(lambda i: ytile(i), tiles_b, G2, X1, False, X2, False, xt, junk, add_eng="pool")
        stage_end()
        A.pop()

    stage0()
    if upto >= 1:
        ffn_stage(x_in, True, X1, False, 0, w_f1i, w_f1o)
    if upto >= 2:
        mixer_stages()
    if upto >= 3:
        ffn_stage(X2, False, y_out, True, 6, w_f2i, w_f2o)
    S.barrier()
    S.emit(nc, block, esems, dsems)
    print("ops:", S.stats(), "arena max", A.off)
    es.close()
    return nc


WNAMES = ["w_ada", "b_ada", "norm_gains", "w_ffn1_in", "w_ffn1_out", "w_ffn2_in", "w_ffn2_out", "w_in",
          "w_gla_alpha", "b_gla_alpha", "gla_norm", "w_gla_out", "q_norm", "kv_norm", "w_uq", "w_ukv",
          "w_mla_out", "w_out"]


def consts_array():
    s = np.arange(128)[:, None]
    t = np.arange(128)[None, :]
    c = np.zeros((128, 6, 128), np.float32)
    c[:, 0] = (s == t)
    c[:, 1] = (s <= t)
    c[:, 2] = (s >= t)
    c[:, 3] = (s > t)
    c[:, 4] = (s < t)
    c[:, 5] = 1.0
    return c


def rope_tables(T, real):
    cosT = np.ones((64, T), np.float32)
    sinT = np.zeros((64, T), np.float32)
    if real:
        t = np.arange(T)
        pos = [(t // 64).astype(np.float32), (t % 64).astype(np.float32)]
        inv = (np.float32(10000.0) ** (-np.arange(0, 32, 2, dtype=np.float32) / np.float32(32))).astype(np.float32)
        for a in range(2):
            ang = pos[a][None, :] * inv[:, None]
            for hh in range(2):
                r0 = a * 32 + hh * 16
                cosT[r0:r0 + 16] = np.cos(ang)
                sinT[r0:r0 + 16] = np.sin(ang) * (-1.0 if hh == 0 else 1.0)
    return cosT, sinT, np.ascontiguousarray(cosT.T), np.ascontiguousarray(sinT.T)


def mask_rows(NSEG, kind):
    T = NSEG * 256
    NK = T + 256
    mq = np.zeros((16, T), np.float32)
    mk = np.zeros((16, NK), np.float32)
    for m in range(NSEG + 1):
        mk[m, m * 256:(m + 1) * 256] = 1.0
    if kind == "P":
        for s in range(NSEG):
            mq[:NSEG + 1, s * 256:(s + 1) * 256] = NEG_BIG
            mq[1 + s, s * 256:(s + 1) * 256] = 0.0
    return mq, mk


def core_inputs(full, NSEG, kind):
    T = NSEG * 256
    m = {}
    m["x"] = np.ascontiguousarray(full["x"], dtype=np.float32)
    m["cvec"] = np.ascontiguousarray(full["c"].reshape(16, 128).T)
    m["cckv"] = np.ascontiguousarray(full["cache_ckv"])
    m["ckr"] = np.ascontiguousarray(full["cache_krope"])
    m["s0"] = np.ascontiguousarray(np.stack([full["s0f"], full["s0b"]], axis=0))
    m["flag"] = np.full((128, 1), 1.0 if kind == "S" else 0.0, np.float32)
    cosT, sinT, cosK, sinK = rope_tables(T, kind == "S")
    m["cosT"], m["sinT"], m["cosK"], m["sinK"] = cosT, sinT, cosK, sinK
    mq, mk = mask_rows(NSEG, kind)
    m["mq"], m["mk"] = mq, mk
    m["consts"] = consts_array()
    for k in WNAMES:
        m[k] = full[k]
    return m


def make_test_inputs(rng, NSEG, kind):
    T = NSEG * 256
    f32 = np.float32

    def nrm(shape, scale):
        return (rng.standard_normal(shape, dtype=f32) * f32(scale)).astype(f32)

    DFF = 5504
    full = {
        "x": nrm((T, D), 1.0),
        "c": nrm((D,), 1.0),
        "w_ada": nrm((D, 9 * D), 0.5 * D ** -0.5),
        "b_ada": nrm((1, 9 * D), 0.01),
        "norm_gains": 1.0 + nrm((6, D), 0.05),
        "w_ffn1_in": nrm((D, 2 * DFF), D ** -0.5),
        "w_ffn1_out": nrm((DFF, D), DFF ** -0.5),
        "w_ffn2_in": nrm((D, 2 * DFF), D ** -0.5),
        "w_ffn2_out": nrm((DFF, D), DFF ** -0.5),
        "w_in": nrm((D, 11360), D ** -0.5),
        "w_gla_alpha": nrm((2, 16, 1024), 16 ** -0.5),
        "b_gla_alpha": nrm((2, 1024), 0.1),
        "gla_norm": 1.0 + nrm((1, 512), 0.05),
        "w_gla_out": nrm((D, D), D ** -0.5),
        "q_norm": 1.0 + nrm((1, 512), 0.05),
        "kv_norm": 1.0 + nrm((1, 512), 0.05),
        "w_uq": nrm((512, 3072), 512 ** -0.5),
        "w_ukv": nrm((512, 4096), 512 ** -0.5),
        "w_mla_out": nrm((D, D), D ** -0.5),
        "w_out": nrm((D, D), D ** -0.5),
    }
    if kind == "S":
        full["cache_ckv"] = nrm((256, 512), 1.0)
        full["cache_krope"] = nrm((256, 64), 1.0)
        full["s0f"] = nrm((4, 256, 512), 0.5)
        full["s0b"] = nrm((4, 256, 512), 0.5)
    else:
        full["cache_ckv"] = np.zeros((256, 512), f32)
        full["cache_krope"] = np.zeros((256, 64), f32)
        full["s0f"] = np.zeros((4, 256, 512), f32)
        full["s0b"] = np.zeros((4, 256, 512), f32)
    return full


_NC_CACHE = {}


def kernel(**inputs):
    NSEG = 8
    f32 = np.float32
    inp = {k: np.asarray(v) for k, v in inputs.items()}
    W = {
        "w_ada": inp["w_ada"][0], "b_ada": inp["b_ada"], "norm_gains": inp["norm_gains"][0],
        "w_ffn1_in": inp["w_ffn1_in"][0], "w_ffn1_out": inp["w_ffn1_out"][0],
        "w_ffn2_in": inp["w_ffn2_in"][0], "w_ffn2_out": inp["w_ffn2_out"][0],
        "w_in": inp["w_in"][0], "w_gla_alpha": inp["w_gla_alpha"][0], "b_gla_alpha": inp["b_gla_alpha"][0],
        "gla_norm": inp["gla_norm"], "w_gla_out": inp["w_gla_out"][0],
        "q_norm": inp["q_norm"], "kv_norm": inp["kv_norm"], "w_uq": inp["w_uq"][0], "w_ukv": inp["w_ukv"][0],
        "w_mla_out": inp["w_mla_out"][0], "w_out": inp["w_out"][0],
    }
    W = {k: np.ascontiguousarray(v, dtype=f32) for k, v in W.items()}
    in_maps = []
    for core in range(8):
        if core < 4:
            b = core
            full = dict(x=inp["x_sample"][b], c=inp["c"][b], cache_ckv=inp["cache_ckv"][b, 0],
                        cache_krope=inp["cache_krope"][b, 0], s0f=inp["state_gla_fwd"][b, 0],
                        s0b=inp["state_gla_bwd"][b, 0], **W)
            in_maps.append(core_inputs(full, NSEG, "S"))
        else:
            s0_ = 4 * (core - 4)
            xp = inp["x_prompt"][s0_:s0_ + 4].reshape(1024, D)
            x = np.concatenate([xp, np.zeros((1024, D), f32)], axis=0)
            full = dict(x=x, c=inp["c_ctx"], cache_ckv=np.zeros((256, 512), f32),
                        cache_krope=np.zeros((256, 64), f32), s0f=np.zeros((4, 256, 512), f32),
                        s0b=np.zeros((4, 256, 512), f32), **W)
            in_maps.append(core_inputs(full, NSEG, "P"))
    if NSEG not in _NC_CACHE:
        _NC_CACHE[NSEG] = build(NSEG)
    nc = _NC_CACHE[NSEG]
    res = run_bass_kernel_spmd(nc, in_maps, core_ids=list(range(8)))
    R = res.results
    y_prompt = np.zeros((16, 256, D), f32)
    y_sample = np.zeros((4, 2048, D), f32)
    new_ckv = np.zeros((16, 1, 256, 512), f32)
    new_krope = np.zeros((16, 1, 256, 64), f32)
    new_sf = np.zeros((16, 1, 4, 256, 512), f32)
    new_sb = np.zeros((16, 1, 4, 256, 512), f32)
    for core in range(8):
        r = R[core]
        if core < 4:
            y_sample[core] = np.asarray(r["y"])
        else:
            s0_ = 4 * (core - 4)
            y = np.asarray(r["y"]); ck = np.asarray(r["nckv"]); kr = np.asarray(r["nkr"])
            sf = np.asarray(r["sf"]); sb = np.asarray(r["sb"])
            for s in range(4):
                y_prompt[s0_ + s] = y[s * 256:(s + 1) * 256]
                new_ckv[s0_ + s, 0] = ck[s * 256:(s + 1) * 256]
                new_krope[s0_ + s, 0] = kr[s * 256:(s + 1) * 256]
                new_sf[s0_ + s, 0] = sf[s]
                new_sb[s0_ + s, 0] = sb[s]
    return (y_prompt, y_sample, new_ckv, new_krope, new_sf, new_sb)
```

```python
import numpy as np
from contextlib import ExitStack
import concourse.bass as bass
import concourse.mybir as mybir
from concourse.bass_utils import run_bass_kernel_spmd

F32 = mybir.dt.float32
BF16 = mybir.dt.bfloat16
AF = mybir.ActivationFunctionType
ALU = mybir.AluOpType
AX = mybir.AxisListType

ENGINES = ["pe", "act", "dve", "pool", "sp"]


class Buf:
    def __init__(self, name, nparts=1):
        self.name = name
        self.n = nparts
        self.w = [None] * nparts
        self.r = [[] for _ in range(nparts)]
        self.sem = [None] * nparts

    def parts(self, p):
        if p is None:
            return range(self.n)
        if isinstance(p, int):
            return (p,)
        return p


class V:
    def __init__(self, ap, *deps):
        self.ap = ap
        self.deps = list(deps)


class _Op:
    __slots__ = ("fn", "waits", "is_dma", "dsem", "clock", "marked")

    def __init__(self, fn, waits, is_dma, dsem):
        self.fn = fn
        self.waits = waits
        self.is_dma = is_dma
        self.dsem = dsem
        self.clock = None
        self.marked = False


class Sched:
    def __init__(self, n_dma_sems=80):
        self.ops = {e: [] for e in ENGINES}
        self.known = {e: {} for e in ENGINES}
        self.n_dma = n_dma_sems
        self.dma_count = [0] * n_dma_sems
        self.n_sw = 24
        self.dma_next_sw = 0
        self.dma_next = self.n_sw
        self.dma_clock = {}
        self.snap = {e: None for e in ENGINES}
        self.sem_bufs = []

    def _sem_for(self, buf, part, queue):
        if buf.sem[part] is None:
            buf.sem[part] = {}
            self.sem_bufs.append(buf)
        d = buf.sem[part]
        kind = "sw" if queue == "pool" else "hw"
        if kind not in d:
            if kind == "sw":
                if self.dma_next_sw >= self.n_sw:
                    raise RuntimeError("out of sw dma semaphores")
                d[kind] = self.dma_next_sw
                self.dma_next_sw += 1
            else:
                if self.dma_next >= self.n_dma:
                    raise RuntimeError("out of hw dma semaphores")
                d[kind] = self.dma_next
                self.dma_next += 1
        return d[kind]

    def _collect(self, engine, reads, writes):
        deps = {}

        def add(ev):
            if ev is None:
                return
            k, v = ev
            if k == "pe" and engine == "pe":
                return
            if deps.get(k, -1) < v:
                deps[k] = v

        for b, p in reads:
            for i in b.parts(p):
                add(b.w[i])
        for b, p in writes:
            for i in b.parts(p):
                add(b.w[i])
                for ev in b.r[i]:
                    add(ev)
        return deps

    def _update(self, ev, reads, writes):
        for b, p in reads:
            for i in b.parts(p):
                lst = b.r[i]
                for j, (k, v) in enumerate(lst):
                    if k == ev[0]:
                        lst[j] = ev
                        break
                else:
                    lst.append(ev)
        for b, p in writes:
            for i in b.parts(p):
                b.w[i] = ev
                b.r[i] = []

    def _merge_clock(self, engine, clock):
        kn = self.known[engine]
        for k, v in clock.items():
            if kn.get(k, -1) < v:
                kn[k] = v
        self.snap[engine] = None

    def _snapshot(self, engine):
        if self.snap[engine] is None:
            self.snap[engine] = dict(self.known[engine])
        return self.snap[engine]

    def _make_waits(self, engine, deps):
        waits = []
        kn = self.known[engine]
        for k, v in deps.items():
            if isinstance(k, tuple):
                if kn.get(k, -1) >= v:
                    continue
                v = max(v, self.dma_count[k[1]])
                waits.append((k, v))
                kn[k] = v
                self.snap[engine] = None
                clk = self.dma_clock.get((k[1], v))
                if clk:
                    self._merge_clock(engine, clk)
            else:
                if kn.get(k, -1) >= v:
                    continue
                waits.append((k, v))
                kn[k] = v
                self.snap[engine] = None
                op = self.ops[k][v]
                op.marked = True
                if op.clock:
                    self._merge_clock(engine, op.clock)
        return waits

    def op(self, engine, fn, reads=(), writes=()):
        deps = self._collect(engine, reads, writes)
        waits = self._make_waits(engine, deps)
        o = _Op(fn, waits, False, None)
        o.clock = self._snapshot(engine)
        idx = len(self.ops[engine])
        self.ops[engine].append(o)
        self._update((engine, idx), reads, writes)
        return idx

    def dma(self, queue, fn, reads, writes, semof):
        deps = self._collect(queue, reads, writes)
        waits = self._make_waits(queue, deps)
        si = self._sem_for(semof[0], semof[1], queue)
        self.dma_count[si] += 16
        val = self.dma_count[si]
        o = _Op(fn, waits, True, si)
        self.ops[queue].append(o)
        self.dma_clock[(si, val)] = self._snapshot(queue)
        self._update((("d", si), val), reads, writes)

    def barrier(self):
        last = {}
        for e in ("pe", "act", "dve", "pool"):
            for i in range(len(self.ops[e]) - 1, -1, -1):
                if not self.ops[e][i].is_dma and self.ops[e][i].fn is not None:
                    last[e] = i
                    break
        for e in ENGINES:
            deps = {}
            for f, i in last.items():
                if f != e:
                    deps[f] = i
                elif e in ("act", "dve", "pool"):
                    deps[f] = i
            for si in range(self.n_dma):
                if self.dma_count[si] > 0:
                    deps[("d", si)] = self.dma_count[si]
            waits = self._make_waits(e, deps)
            if waits:
                o = _Op(None, waits, False, None)
                o.clock = self._snapshot(e)
                self.ops[e].append(o)

    def reset_sems(self):
        self.dma_next = self.n_sw
        self.dma_next_sw = 0
        for b in self.sem_bufs:
            b.sem = [None] * b.n
        self.sem_bufs = []

    def emit(self, nc, block, esems, dsems):
        vals = {}
        for e in ENGINES:
            c = 0
            m = {}
            for i, o in enumerate(self.ops[e]):
                if o.marked:
                    c += 1
                    m[i] = c
            vals[e] = m

        def run(e, eng):
            for i, o in enumerate(self.ops[e]):
                for k, v in o.waits:
                    if isinstance(k, tuple):
                        eng.wait_ge(dsems[k[1]], v)
                    else:
                        eng.wait_ge(esems[k], vals[k][v])
                if o.fn is None:
                    continue
                ins = o.fn(eng)
                if o.is_dma:
                    ins.then_inc(dsems[o.dsem], 16)
                elif o.marked:
                    ins.then_inc(esems[e], 1)

        @block.tensor
        def _(eng):
            run("pe", eng)

        @block.scalar
        def _(eng):
            run("act", eng)

        @block.vector
        def _(eng):
            run("dve", eng)

        @block.gpsimd
        def _(eng):
            run("pool", eng)

        @block.sync
        def _(eng):
            run("sp", eng)

    def stats(self):
        return {e: len(self.ops[e]) for e in ENGINES}


class Arena:
    def __init__(self, t, nwords):
        self.t = t
        self.n = nwords
        self.off = 0
        self.marks = []

    def alloc(self, free_shape, dtype, parts=128):
        nel = int(np.prod(free_shape))
        nbytes = nel * (2 if dtype == BF16 else 4)
        nw = (nbytes + 3) // 4
        nw = (nw + 15) // 16 * 16
        if self.off + nw > self.n:
            raise RuntimeError(f"arena overflow: need {nw} words at {self.off} of {self.n}")
        ap = self.t[0:parts, self.off:self.off + nw]
        self.off += nw
        if dtype == BF16:
            ap = ap.bitcast(BF16)[:, 0:nel]
        else:
            ap = ap[:, 0:nel]
        if len(free_shape) == 2:
            ap = ap.rearrange("p (a b) -> p a b", b=free_shape[1])
        elif len(free_shape) == 3:
            ap = ap.rearrange("p (a b c) -> p a b c", b=free_shape[1], c=free_shape[2])
        return ap

    def push(self):
        self.marks.append(self.off)

    def pop(self):
        self.off = self.marks.pop()


D = 2048
DFF = 5504
NFC = 43
ATT_SCALE = 192 ** -0.5
EPS = 1e-6
NEG_BIG = -30000.0
ARENA_WORDS = 53184


class TT:
    def __init__(self, ap, name, nparts=1):
        self.ap = ap
        self.buf = Buf(name, nparts)

    def v(self, part=None, idx=None):
        ap = self.ap if idx is None else self.ap[idx]
        return V(ap, (self.buf, part))


def build(NSEG, upto=99, debug=False):
    T = NSEG * 256
    NT = T // 128
    NK = T + 256
    NKT = NK // 128
    nc = bass.Bass("TRN2", target_bir_lowering=False)

    def din(name, shape):
        return nc.dram_tensor(name, list(shape), F32, kind="ExternalInput").ap()

    def dout(name, shape):
        return nc.dram_tensor(name, list(shape), F32, kind="ExternalOutput").ap()

    def dscr(name, shape, dt):
        if debug:
            return nc.dram_tensor(name, list(shape), dt, kind="ExternalOutput").ap()
        return nc.dram_tensor(name, list(shape), dt).ap()

    x_in = din("x", [T, D])
    cvec = din("cvec", [128, 16])
    cckv = din("cckv", [256, 512])
    ckr = din("ckr", [256, 64])
    s0 = din("s0", [2, 4, 256, 512])
    flag_in = din("flag", [128, 1])
    cosT_in = din("cosT", [64, T])
    sinT_in = din("sinT", [64, T])
    cosK_in = din("cosK", [T, 64])
    sinK_in = din("sinK", [T, 64])
    mq_in = din("mq", [16, T])
    mk_in = din("mk", [16, NK])
    consts_in = din("consts", [128, 6, 128])
    w_ada = din("w_ada", [D, 9 * D])
    b_ada = din("b_ada", [1, 9 * D])
    gains = din("norm_gains", [6, D])
    w_f1i = din("w_ffn1_in", [D, 2 * DFF])
    w_f1o = din("w_ffn1_out", [DFF, D])
    w_f2i = din("w_ffn2_in", [D, 2 * DFF])
    w_f2o = din("w_ffn2_out", [DFF, D])
    w_inp = din("w_in", [D, 11360])
    w_alpha = din("w_gla_alpha", [2, 16, 1024])
    b_alpha = din("b_gla_alpha", [2, 1024])
    gla_norm = din("gla_norm", [1, 512])
    w_go = din("w_gla_out", [D, D])
    q_norm = din("q_norm", [1, 512])
    kv_norm = din("kv_norm", [1, 512])
    w_uq = din("w_uq", [512, 3072])
    w_ukv = din("w_ukv", [512, 4096])
    w_mo = din("w_mla_out", [D, D])
    w_o = din("w_out", [D, D])

    y_out = dout("y", [T, D])
    nckv_out = dout("nckv", [T, 512])
    nkr_out = dout("nkr", [T, 64])
    sf_out = dout("sf", [NSEG, 4, 256, 512])
    sb_out = dout("sb", [NSEG, 4, 256, 512])

    modS = TT(dscr("modS", [9, 128, D], F32), "modS", 9)
    X1 = TT(dscr("X1", [T, D], F32), "X1", NT)
    X2 = TT(dscr("X2", [T, D], F32), "X2", NT)
    QD = TT(dscr("QD", [2, NT, 128, 8, 128], BF16), "QD", 2 * NT)
    KI = TT(dscr("KI", [2, NT, 128, 8, 128], BF16), "KI", 2 * NT)
    KE = TT(dscr("KE", [2, NT, 128, 1024], BF16), "KE", 2 * NT)
    ED = TT(dscr("ED", [2, NT, 128, 8], F32), "ED", 2 * NT)
    VV = TT(dscr("VV", [T, D], BF16), "VV", NT)
    RR = TT(dscr("RR", [T, D], BF16), "RR", NT)
    GAT = TT(dscr("GAT", [16, 128, T], BF16), "GAT", 1)
    GBT = TT(dscr("GBT", [16, 128, T], BF16), "GBT", 1)
    OF = TT(dscr("OF", [T, D], F32), "OF", NT)
    OGT = TT(dscr("OGT", [128, 16, T], BF16), "OGT", 1)
    OMT = TT(dscr("OMT", [128, 16, T], BF16), "OMT", 1)

    S = Sched(90)
    es = ExitStack()
    arena_t = es.enter_context(nc.sbuf_tensor("arena", [128, ARENA_WORDS], F32))
    ps = es.enter_context(nc.psum_tensor("ps", [128, 4096], F32))
    esems = {e: es.enter_context(nc.semaphore(f"s_{e}")) for e in ["pe", "act", "dve", "pool"]}
    dsems = [es.enter_context(nc.semaphore(f"d{i}")) for i in range(90)]
    block = es.enter_context(nc.Block())
    A = Arena(arena_t, ARENA_WORDS)
    PB = Buf("psum", 8)
    bank_rr = [0]

    def nb():
        i = bank_rr[0]
        bank_rr[0] = (i + 1) % 8
        return i

    def bank(i, w=512, rows=128, off=0):
        return V(ps[0:rows, i * 512 + off:i * 512 + off + w], (PB, i))

    def bank_bf(i):
        return ps[:, i * 512:(i + 1) * 512].bitcast(BF16)

    def mm(o, l, r, start=True, stop=True):
        S.op("pe", lambda e: e.matmul(o.ap, lhsT=l.ap, rhs=r.ap, start=start, stop=stop), l.deps + r.deps, o.deps)

    def act(o, i, func, bias=None, scale=None, accum=None):
        kw = {}
        reads = list(i.deps)
        writes = list(o.deps)
        if bias is not None:
            if isinstance(bias, V):
                kw["bias"] = bias.ap
                reads += bias.deps
            else:
                kw["bias"] = bias
        if scale is not None:
            if isinstance(scale, V):
                kw["scale"] = scale.ap
                reads += scale.deps
            else:
                kw["scale"] = scale
        if accum is not None:
            kw["accum_out"] = accum.ap
            writes += accum.deps
        S.op("act", lambda e: e.activation(out=o.ap, in_=i.ap, func=func, **kw), reads, writes)

    def tt(eng, o, a, b, op):
        S.op(eng, lambda e: e.tensor_tensor(out=o.ap, in0=a.ap, in1=b.ap, op=op), a.deps + b.deps, o.deps)

    def stt(eng, o, a, sc, b, op0, op1):
        reads = a.deps + b.deps
        if isinstance(sc, V):
            reads = reads + sc.deps
            scv = sc.ap
        else:
            scv = sc
        S.op(eng, lambda e: e.scalar_tensor_tensor(out=o.ap, in0=a.ap, scalar=scv, in1=b.ap, op0=op0, op1=op1), reads, o.deps)

    def ts(eng, o, a, s1, op0, s2=None, op1=None):
        reads = list(a.deps)
        if isinstance(s1, V):
            reads += s1.deps
            s1 = s1.ap
        if isinstance(s2, V):
            reads += s2.deps
            s2 = s2.ap
        if op1 is None:
            S.op(eng, lambda e: e.tensor_scalar(out=o.ap, in0=a.ap, scalar1=s1, scalar2=None, op0=op0), reads, o.deps)
        else:
            S.op(eng, lambda e: e.tensor_scalar(out=o.ap, in0=a.ap, scalar1=s1, scalar2=s2, op0=op0, op1=op1), reads, o.deps)

    def cp(eng, o, i, scale=None):
        if eng == "act":
            act(o, i, AF.Copy, scale=scale)
        else:
            S.op(eng, lambda e: e.tensor_copy(out=o.ap, in_=i.ap), i.deps, o.deps)

    def dma(q, o, i, semof):
        S.dma(q, lambda e: e.dma_start(out=o.ap, in_=i.ap), i.deps, o.deps, semof)

    def rmax(o, i):
        S.op("dve", lambda e: e.reduce_max(out=o.ap, in_=i.ap, axis=AX.X), i.deps, o.deps)

    def rsum(o, i):
        S.op("dve", lambda e: e.reduce_sum(out=o.ap, in_=i.ap, axis=AX.X), i.deps, o.deps)

    def recip(o, i):
        S.op("dve", lambda e: e.reciprocal(out=o.ap, in_=i.ap), i.deps, o.deps)

    def stage_end():
        S.barrier()
        S.reset_sems()

    cst = TT(A.alloc([6, 128], F32), "cst")
    dma("sp", cst.v(), V(consts_in), (cst.buf, 0))
    identb = TT(A.alloc([128], BF16), "identb")
    onesb = TT(A.alloc([128], BF16), "onesb")
    cp("dve", identb.v(), cst.v(None, np.s_[:, 0, :]))
    cp("dve", onesb.v(), cst.v(None, np.s_[:, 5, :]))
    Lf = cst.v(None, np.s_[:, 1, :])
    Lb = cst.v(None, np.s_[:, 2, :])
    Uf = cst.v(None, np.s_[:, 3, :])
    Ub = cst.v(None, np.s_[:, 4, :])
    ones_col = cst.v(None, np.s_[:, 5, 0:1])
    flag = TT(A.alloc([1], F32), "flag")
    dma("sp", flag.v(), V(flag_in), (flag.buf, 0))
    stat = TT(A.alloc([64], F32), "stat", 64)
    stat_rr = [0]

    def scol():
        i = stat_rr[0]
        stat_rr[0] = (i + 1) % 64
        return i

    def sv(i, w=1):
        return V(stat.ap[:, i:i + w], (stat.buf, tuple(range(i, i + w))))

    def tr(o, i):
        S.op("pe", lambda e: e.transpose(o.ap, i.ap, identb.ap), i.deps + [(identb.buf, None)], o.deps)

    def rstd_from(src, n, junk):
        c0, c1, c2 = scol(), scol(), scol()
        act(junk, src, AF.Square, accum=sv(c0))
        act(sv(c1), sv(c0), AF.Ln, scale=1.0 / n, bias=EPS)
        act(sv(c2), sv(c1), AF.Exp, scale=-0.5)
        return sv(c2)

    def wslab(dst, src_cols):
        dma("pool", dst, V(src_cols.rearrange("(kc p) n -> p kc n", p=128)), (dst.deps[0][0], 0))

    cs_p = TT(A.alloc([16], F32), "cs_p")

    def make_crep():
        crep = TT(A.alloc([16, 128], BF16), "crep")
        for kc in range(16):
            ts("dve", crep.v(None, np.s_[:, kc, :]), onesb.v(), cs_p.v(None, np.s_[:, kc:kc + 1]), ALU.mult)
        return crep

    def s0_units(ms, ncol, wsl, brow, gt_, mt, crep):
        k = 0
        nct = D // ncol
        for mi, m in enumerate(ms):
            sub = m // 3
            kind = m % 3
            g = gt_[mi % len(gt_)]
            if kind == 1:
                dma("sp", g.v(), V(gains[2 * sub:2 * sub + 1, :].partition_broadcast(128)), (g.buf, 0))
            if kind == 2:
                dma("sp", g.v(), V(gains[2 * sub + 1:2 * sub + 2, :].partition_broadcast(128)), (g.buf, 0))
            mtile = mt[mi % len(mt)]
            for ct in range(nct):
                c0 = m * D + ct * ncol
                w = wsl[k % len(wsl)]
                br = brow[k % len(brow)]
                k += 1
                wslab(w.v(None, np.s_[:, :, 0:ncol]), w_ada[:, c0:c0 + ncol])
                dma("pool", br.v(None, np.s_[:, 0:ncol]), V(b_ada[0:1, c0:c0 + ncol]), (br.buf, 0))
                bi = nb()
                for kc in range(16):
                    mm(bank(bi, ncol), crep.v(None, np.s_[:, kc, :]), w.v(None, np.s_[:, kc, 0:ncol]), start=(kc == 0), stop=False)
                mm(bank(bi, ncol), onesb.v(None, np.s_[0:1, :]), br.v(None, np.s_[:, 0:ncol]), start=False, stop=True)
                cs_ = np.s_[:, ct * ncol:(ct + 1) * ncol]
                o = mtile.v(None, cs_)
                if kind == 0:
                    cp("act", o, bank(bi, ncol))
                elif kind == 1:
                    stt("dve", o, bank(bi, ncol), 1.0, g.v(None, cs_), ALU.add, ALU.mult)
                else:
                    coef = 1.0 if sub == 1 else 0.5
                    stt("dve", o, bank(bi, ncol), coef, g.v(None, cs_), ALU.mult, ALU.mult)
                yield
            dma("sp", modS.v(m, np.s_[m]), mtile.v(), (mtile.buf, 0))
            yield

    S0_FIRST = [0, 1, 2, 3, 4]
    S0_LATE = [5, 6, 7, 8]

    def stage0():
        A.push()
        cv = TT(A.alloc([16], F32), "cv")
        wsl = [TT(A.alloc([16, 512], BF16), f"s0w{i}") for i in range(3)]
        brow = [TT(A.alloc([512], BF16, parts=1), f"s0b{i}") for i in range(3)]
        gt_ = [TT(A.alloc([D], F32), f"s0g{i}") for i in range(2)]
        mt = [TT(A.alloc([D], F32), f"s0m{i}") for i in range(2)]
        dma("sp", cv.v(), V(cvec), (cv.buf, 0))
        act(cs_p.v(), cv.v(), AF.Silu)
        crep = make_crep()
        for _ in s0_units(S0_FIRST if upto >= 2 else list(range(9)), 512, wsl, brow, gt_, mt, crep):
            pass
        stage_end()
        A.pop()

    def prenorm_tiles(src, src_is_input, tiles, At, Bt, xt, hb, hT, hT_part_of, col_of, extra_reads=(), first_writes=()):
        n = len(tiles)

        def load(i):
            t_ = tiles[i]
            xv = xt[i % 2].v()
            if src_is_input:
                dma("sp", xv, V(src[t_ * 128:(t_ + 1) * 128, :]), (xt[i % 2].buf, 0))
            else:
                dma("sp", xv, src.v(t_, np.s_[t_ * 128:(t_ + 1) * 128, :]), (xt[i % 2].buf, 0))

        def s1(i):
            xv = xt[i % 2].v()
            hv = hb[i % 2].v()
            r = rstd_from(xv, D, hv)
            stt("dve", xv, xv, r, At.v(), ALU.mult, ALU.mult)
            tt("dve" if i % 2 == 0 else "pool", hv, xv, Bt.v(), ALU.add)

        def s2(i):
            t_ = tiles[i]
            c0 = col_of(t_)
            for half in range(2):
                bi = nb()
                pv = bank_bf(bi)
                for q in range(8):
                    kc = half * 8 + q
                    tr(V(pv[:, q * 128:(q + 1) * 128], (PB, bi)), hb[i % 2].v(None, np.s_[:, kc * 128:(kc + 1) * 128]))
                dst = V(hT.ap[:, half * 8:half * 8 + 8, c0:c0 + 128], (hT.buf, hT_part_of(t_)), *(first_writes if i == 0 else ()))
                srcv = V(pv.rearrange("p (a b) -> p a b", b=128), (PB, bi), *extra_reads)
                cp("act" if half == 0 else "dve", dst, srcv)

        load(0)
        if n > 1:
            load(1)
        s1(0)
        for i in range(n):
            if i + 1 < n:
                s1(i + 1)
            s2(i)
            if i + 2 < n:
                load(i + 2)

    def postnorm_tiles(ytile_of, tiles, Gt, res_src, res_is_input, dst, dst_is_output, xt, junk, add_eng="dve"):
        n = len(tiles)

        def load(i):
            t_ = tiles[i]
            xv = xt[i % 2].v()
            if res_is_input:
                dma("sp", xv, V(res_src[t_ * 128:(t_ + 1) * 128, :]), (xt[i % 2].buf, 0))
            else:
                dma("sp", xv, res_src.v(t_, np.s_[t_ * 128:(t_ + 1) * 128, :]), (xt[i % 2].buf, 0))

        load(0)
        if n > 1:
            load(1)
        for i, t_ in enumerate(tiles):
            yv = ytile_of(i)
            xv = xt[i % 2].v()
            r = rstd_from(yv, D, junk.v())
            stt("dve", yv, yv, r, Gt.v(), ALU.mult, ALU.mult)
            tt(add_eng if i % 2 == 1 else "dve", xv, yv, xv, ALU.add)
            if dst_is_output:
                dma("sp", V(dst[t_ * 128:(t_ + 1) * 128, :]), xv, (xt[i % 2].buf, 0))
            else:
                dma("sp", dst.v(t_, np.s_[t_ * 128:(t_ + 1) * 128, :]), xv, (xt[i % 2].buf, 0))
            if i + 2 < n:
                load(i + 2)

    def ffn_stage(src, src_is_input, dst, dst_is_output, mbase, w1, w2):
        A.push()
        FB = min(1024, T)
        NFB = T // FB
        TB = FB // 128
        HB = min(512, FB)
        NH = FB // HB
        At = TT(A.alloc([D], F32), "ffA")
        Bt = TT(A.alloc([D], F32), "ffB")
        Gt = TT(A.alloc([D], F32), "ffG")
        dma("sp", Bt.v(), modS.v(mbase, np.s_[mbase]), (Bt.buf, 0))
        dma("sp", At.v(), modS.v(mbase + 1, np.s_[mbase + 1]), (At.buf, 0))
        dma("sp", Gt.v(), modS.v(mbase + 2, np.s_[mbase + 2]), (Gt.buf, 0))
        Rw = A.alloc([8192], F32)
        Rbuf = Buf("ffR", 4)
        hT = TT(Rw.bitcast(BF16)[:, 0:16 * FB].rearrange("p (a b) -> p a b", b=FB), "ffhT", TB)
        yv_ap = Rw.rearrange("p (a b) -> p a b", b=D)
        aT = TT(A.alloc([NFC, FB], BF16), "ffaT", NFC)
        w1s = [TT(A.alloc([16, 2, 128], BF16), f"ffw1_{i}") for i in range(2)]
        w2s = [TT(A.alloc([4, 512], BF16), f"ffw2_{i}") for i in range(3)]
        xt = [TT(A.alloc([D], F32), f"ffx{i}") for i in range(2)]
        hb = [TT(A.alloc([D], BF16), f"ffh{i}") for i in range(2)]
        sg = [TT(A.alloc([512], F32), f"ffsg{i}") for i in range(2)]
        junk = TT(A.alloc([D], BF16), "ffjunk")
        kw2 = 0
        for fb in range(NFB):
            tiles = [fb * TB + i for i in range(TB)]

            def ld_w1(j):
                w = w1s[j % 2]
                dma("pool", w.v(None, np.s_[:, :, 0, :]), V(w1[:, j * 128:(j + 1) * 128].rearrange("(kc p) n -> p kc n", p=128)), (w.buf, 0))
                dma("pool", w.v(None, np.s_[:, :, 1, :]), V(w1[:, DFF + j * 128:DFF + (j + 1) * 128].rearrange("(kc p) n -> p kc n", p=128)), (w.buf, 0))

            ld_w1(0)
            ld_w1(1)
            prenorm_tiles(src, src_is_input, tiles, At, Bt, xt, hb, hT, lambda t_: t_ - fb * TB, lambda t_: (t_ - fb * TB) * 128, extra_reads=((Rbuf, None),), first_writes=((Rbuf, None),))
            for j in range(NFC):
                w = w1s[j % 2]
                pg = [nb() for _ in range(NH)]
                pu = [nb() for _ in range(NH)]
                for gi, pbs in ((0, pg), (1, pu)):
                    for kc in range(16):
                        for h_ in range(NH):
                            mm(bank(pbs[h_], HB), w.v(None, np.s_[:, kc, gi, :]), hT.v(None, np.s_[:, kc, h_ * HB:(h_ + 1) * HB]), start=(kc == 0), stop=(kc == 15))
                if j + 2 < NFC:
                    ld_w1(j + 2)
                for h_ in range(NH):
                    sgv = sg[h_ % 2].v(None, np.s_[:, 0:HB])
                    act(sgv, bank(pg[h_], HB), AF.Silu)
                    tt("dve", aT.v(j, np.s_[:, j, h_ * HB:(h_ + 1) * HB]), sgv, bank(pu[h_], HB), ALU.mult)
            for th in range(NH):
                nt4 = HB // 128
                for dt in range(4):
                    pys = [nb() for _ in range(nt4)]
                    j = 0
                    while j < NFC:
                        g = min(4, NFC - j)
                        w = w2s[kw2 % 3]
                        kw2 += 1
                        dma("pool", w.v(None, np.s_[:, 0:g, :]), V(w2[j * 128:(j + g) * 128, dt * 512:(dt + 1) * 512].rearrange("(a p) n -> p a n", p=128)), (w.buf, 0))
                        for a in range(g):
                            for t4 in range(nt4):
                                c0 = (th * nt4 + t4) * 128
                                mm(bank(pys[t4]), aT.v(j + a, np.s_[:, j + a, c0:c0 + 128]), w.v(None, np.s_[:, a, :]), start=(j + a == 0), stop=(j + a == NFC - 1))
                        j += g
                    for t4 in range(nt4):
                        cp("act", V(yv_ap[:, t4, dt * 512:(dt + 1) * 512], (Rbuf, t4)), bank(pys[t4]))
                tiles_h = [fb * TB + th * nt4 + t4 for t4 in range(nt4)]
                postnorm_tiles(lambda i: V(yv_ap[:, i, :], (Rbuf, i)), tiles_h, Gt, src, src_is_input, dst, dst_is_output, xt, junk,
                               add_eng=("pool" if th == NH - 1 else "dve"))
        stage_end()
        A.pop()

    def mixer_stages():
        BW = min(512, T)
        NBLK = T // BW
        A.push()
        cqnT = TT(A.alloc([4, T], BF16), "cqnT", NT)
        ckvT = TT(A.alloc([4, NK], BF16), "ckvT", NKT)
        krT = TT(A.alloc([NK], BF16), "krT", NKT + 1)
        GAT.buf = Buf("GAT", 16 * NBLK)
        GBT.buf = Buf("GBT", 16 * NBLK)
        OGT.buf = Buf("OGT", NT)
        OMT.buf = Buf("OMT", 16)
        QD.buf = Buf("QD", 4 * NT)
        KI.buf = Buf("KI", 4 * NT)
        KE.buf = Buf("KE", 4 * NT)
        ED.buf = Buf("ED", 4 * NT)

        A.push()
        h2T = TT(A.alloc([16, T], BF16), "h2T", NT)
        aT2 = [TT(A.alloc([T], BF16), "aTf"), TT(A.alloc([T], BF16), "aTb")]

        A.push()
        At = TT(A.alloc([D], F32), "m1A")
        Bt = TT(A.alloc([D], F32), "m1B")
        dma("sp", Bt.v(), modS.v(3, np.s_[3]), (Bt.buf, 0))
        dma("sp", At.v(), modS.v(4, np.s_[4]), (At.buf, 0))
        xt = [TT(A.alloc([D], F32), f"m1x{i}") for i in range(2)]
        hb = [TT(A.alloc([D], BF16), f"m1h{i}") for i in range(2)]
        Wsm = TT(A.alloc([16, 1120], BF16), "Wsm")
        wslab(Wsm.v(), w_inp[:, 6144:7264])
        qnb = TT(A.alloc([512], F32), "qnb")
        kvnb = TT(A.alloc([512], F32), "kvnb")
        dma("sp", qnb.v(), V(q_norm.partition_broadcast(128)), (qnb.buf, 0))
        dma("sp", kvnb.v(), V(kv_norm.partition_broadcast(128)), (kvnb.buf, 0))
        junk5 = TT(A.alloc([512], BF16), "junk5")
        cqb = [TT(A.alloc([512], BF16), f"cqb{i}") for i in range(2)]
        ckf = [TT(A.alloc([512], F32), f"ckf{i}") for i in range(2)]
        ckb = [TT(A.alloc([512], BF16), f"ckb{i}") for i in range(2)]
        krf = [TT(A.alloc([64], F32), f"krf{i}") for i in range(2)]
        krb = [TT(A.alloc([64], BF16), f"krb{i}") for i in range(2)]
        kt1 = [TT(A.alloc([64], F32), f"kt1{i}") for i in range(2)]
        kt2 = [TT(A.alloc([64], F32), f"kt2{i}") for i in range(2)]
        cosk = [TT(A.alloc([64], F32), f"cosk{i}") for i in range(2)]
        sink = [TT(A.alloc([64], F32), f"sink{i}") for i in range(2)]
        dma("pool", krT.v(NKT, np.s_[65:81, :]), V(mk_in), (krT.buf, NKT))
        for c2 in range(2):
            k = c2 % 2
            dma("sp", ckf[k].v(), V(cckv[c2 * 128:(c2 + 1) * 128, :]), (ckf[k].buf, 0))
            cp("act", ckb[k].v(), ckf[k].v())
            b = nb()
            for q in range(4):
                tr(V(bank_bf(b)[:, q * 128:(q + 1) * 128], (PB, b)), ckb[k].v(None, np.s_[:, q * 128:(q + 1) * 128]))
            cp("dve", ckvT.v(c2, np.s_[:, 0:4, c2 * 128:(c2 + 1) * 128]),
               V(bank_bf(b)[:, 0:512].rearrange("p (a b) -> p a b", b=128), (PB, b)))
            dma("sp", krf[k].v(), V(ckr[c2 * 128:(c2 + 1) * 128, :]), (krf[k].buf, 0))
            cp("act", krb[k].v(), krf[k].v())
            b = nb()
            tr(V(bank_bf(b)[0:64, 0:128], (PB, b)), krb[k].v())
            cp("dve", krT.v(c2, np.s_[0:64, c2 * 128:(c2 + 1) * 128]), V(bank_bf(b)[0:64, 0:128], (PB, b)))
        prenorm_tiles(X1, False, list(range(NT)), At, Bt, xt, hb, h2T, lambda t_: t_, lambda t_: t_ * 128)
        def m1a_A(t_):
            k = t_ % 2
            tc = np.s_[t_ * 128:(t_ + 1) * 128]
            dma("sp", cosk[k].v(), V(cosK_in[tc, :]), (cosk[k].buf, 0))
            dma("sp", sink[k].v(), V(sinK_in[tc, :]), (sink[k].buf, 0))
            b1 = nb()
            for kc in range(16):
                mm(bank(b1), h2T.v(t_, np.s_[:, kc, tc]), Wsm.v(None, np.s_[:, kc, 32:544]), start=(kc == 0), stop=(kc == 15))
            b3 = nb()
            for kc in range(16):
                mm(bank(b3), h2T.v(t_, np.s_[:, kc, tc]), Wsm.v(None, np.s_[:, kc, 544:1056]), start=(kc == 0), stop=(kc == 15))
            b5 = nb()
            for kc in range(16):
                mm(bank(b5, 64), h2T.v(t_, np.s_[:, kc, tc]), Wsm.v(None, np.s_[:, kc, 1056:1120]), start=(kc == 0), stop=(kc == 15))
            r = rstd_from(bank(b1), 512, junk5.v())
            stt("dve", cqb[k].v(), bank(b1), r, qnb.v(), ALU.mult, ALU.mult)
            r = rstd_from(bank(b3), 512, junk5.v())
            stt("dve", ckf[k].v(), bank(b3), r, kvnb.v(), ALU.mult, ALU.mult)
            dma("sp", V(nckv_out[tc, :]), ckf[k].v(), (ckf[k].buf, 0))
            cp("act", ckb[k].v(), ckf[k].v())
            cp("act", krf[k].v(), bank(b5, 64))
            dma("sp", V(nkr_out[tc, :]), krf[k].v(), (krf[k].buf, 0))
            tt("dve", kt1[k].v(), krf[k].v(), cosk[k].v(), ALU.mult)
            x4 = krf[k].ap.rearrange("p (a h j) -> p a h j", a=2, h=2)
            s4 = sink[k].ap.rearrange("p (a h j) -> p a h j", a=2, h=2)
            o4 = kt2[k].ap.rearrange("p (a h j) -> p a h j", a=2, h=2)
            for hh in range(2):
                tt("dve", V(o4[:, :, hh, :], (kt2[k].buf, None)), V(x4[:, :, 1 - hh, :], (krf[k].buf, None)),
                   V(s4[:, :, hh, :], (sink[k].buf, None)), ALU.mult)
            tt("dve", krb[k].v(), kt1[k].v(), kt2[k].v(), ALU.add)

        def m1a_B(t_):
            k = t_ % 2
            tc = np.s_[t_ * 128:(t_ + 1) * 128]
            kc0 = 256 + t_ * 128
            b2 = nb()
            for q in range(4):
                tr(V(bank_bf(b2)[:, q * 128:(q + 1) * 128], (PB, b2)), cqb[k].v(None, np.s_[:, q * 128:(q + 1) * 128]))
            for q in range(4):
                tr(V(bank_bf(b2)[:, 512 + q * 128:512 + (q + 1) * 128], (PB, b2)), ckb[k].v(None, np.s_[:, q * 128:(q + 1) * 128]))
            b6 = nb()
            tr(V(bank_bf(b6)[0:64, 0:128], (PB, b6)), krb[k].v())
            cp("act", cqnT.v(t_, np.s_[:, 0:4, tc]), V(bank_bf(b2)[:, 0:512].rearrange("p (a b) -> p a b", b=128), (PB, b2)))
            cp("dve", ckvT.v(2 + t_, np.s_[:, 0:4, kc0:kc0 + 128]), V(bank_bf(b2)[:, 512:1024].rearrange("p (a b) -> p a b", b=128), (PB, b2)))
            cp("dve", krT.v(2 + t_, np.s_[0:64, kc0:kc0 + 128]), V(bank_bf(b6)[0:64, 0:128], (PB, b6)))

        m1a_A(0)
        for t_ in range(NT):
            if t_ + 1 < NT:
                m1a_A(t_ + 1)
            m1a_B(t_)
        for blk in range(NBLK):
            cols = np.s_[blk * BW:(blk + 1) * BW]
            tparts = tuple(range(blk * (BW // 128), (blk + 1) * (BW // 128)))
            for d_ in range(2):
                b = nb()
                for kc in range(16):
                    mm(bank(b, BW, rows=16), Wsm.v(None, np.s_[:, kc, d_ * 16:(d_ + 1) * 16]),
                       V(h2T.ap[:, kc, cols], (h2T.buf, tparts)), start=(kc == 0), stop=(kc == 15))
                cp("act", aT2[d_].v(None, np.s_[0:16, cols]), bank(b, BW, rows=16))
        stage_end()
        A.pop()

        A.push()
        Wqk = TT(A.alloc([16, 1024], BF16), "Wqk", 2)
        wal = TT(A.alloc([2, 1024], BF16), "wal")
        bal = TT(A.alloc([2, 1024], BF16), "bal")
        dma("pool", wal.v(None, np.s_[0:16]), V(w_alpha.rearrange("d r n -> r d n")), (wal.buf, 0))
        dma("pool", bal.v(None, np.s_[0:1]), V(b_alpha.rearrange("(o d) n -> o d n", o=1)), (bal.buf, 0))
        qf = [TT(A.alloc([512], F32), f"qf{i}") for i in range(3)]
        kf = [TT(A.alloc([512], F32), f"kf{i}") for i in range(3)]
        ef = TT(A.alloc([512], F32), "ef")
        spf = [TT(A.alloc([512], F32), f"spf{i}") for i in range(6)]
        ebf = [TT(A.alloc([512], F32), f"ebf{i}") for i in range(2)]
        eif = [TT(A.alloc([512], F32), f"eif{i}") for i in range(2)]
        eef = [TT(A.alloc([512], F32), f"eef{i}") for i in range(2)]
        qdb = [TT(A.alloc([512], BF16), f"qdb{i}") for i in range(4)]
        kib = [TT(A.alloc([512], BF16), f"kib{i}") for i in range(4)]
        keb = [TT(A.alloc([512], BF16), f"keb{i}") for i in range(2)]
        qdT = [TT(A.alloc([4, 128], BF16), f"qdT{i}") for i in range(2)]
        kiT = [TT(A.alloc([4, 128], BF16), f"kiT{i}") for i in range(2)]
        edc = [TT(A.alloc([4], F32), f"edc{i}") for i in range(2)]
        items = [(ch, t_) for ch in range(2) for t_ in range(NT)]

        def ld_wqk(ch):
            c0 = ch * 512
            dma("pool", Wqk.v(0, np.s_[:, :, 0:512]), V(w_inp[:, c0:c0 + 512].rearrange("(kc p) n -> p kc n", p=128)), (Wqk.buf, 0))
            dma("pool", Wqk.v(1, np.s_[:, :, 512:1024]), V(w_inp[:, 1024 + c0:1024 + c0 + 512].rearrange("(kc p) n -> p kc n", p=128)), (Wqk.buf, 1))

        def p1(n):
            ch, t_ = items[n]
            c0 = ch * 512
            if t_ == 0:
                ld_wqk(ch)
            k = n % 3
            tc = np.s_[t_ * 128:(t_ + 1) * 128]
            bq = nb()
            for kc in range(16):
                mm(bank(bq), h2T.v(t_, np.s_[:, kc, tc]), Wqk.v(0, np.s_[:, kc, 0:512]), start=(kc == 0), stop=(kc == 15))
            cp("act", qf[k].v(), bank(bq), scale=1.0 / 16.0)
            bk = nb()
            for kc in range(16):
                mm(bank(bk), h2T.v(t_, np.s_[:, kc, tc]), Wqk.v(1, np.s_[:, kc, 512:1024]), start=(kc == 0), stop=(kc == 15))
            cp("act", kf[k].v(), bank(bk))
            for d_ in range(2):
                sp_ = spf[(n % 3) * 2 + d_]
                bz = nb()
                mm(bank(bz), aT2[d_].v(None, np.s_[0:16, tc]), wal.v(None, np.s_[0:16, d_, c0:c0 + 512]), start=True, stop=False)
                mm(bank(bz), onesb.v(None, np.s_[0:1, :]), bal.v(None, np.s_[0:1, d_, c0:c0 + 512]), start=False, stop=True)
                act(ef.v(), bank(bz), AF.Exp, scale=-1.0)
                act(sp_.v(), ef.v(), AF.Ln, bias=1.0)

        def p2(n):
            ch, t_ = items[n]
            c0 = ch * 512
            k = n % 3
            for d_ in range(2):
                sp_ = spf[(n % 3) * 2 + d_]
                kk = d_
                k4 = (n % 2) * 2 + d_
                part = (d_ * NT + t_) * 2 + ch
                Lm = Lf if d_ == 0 else Lb
                Um = Uf if d_ == 0 else Ub
                bB = nb()
                mm(bank(bB), Lm, sp_.v())
                bE = nb()
                mm(bank(bE), Um, sp_.v())
                bl = nb()
                for c in range(4):
                    mm(bank(bl, 1, off=c), sp_.v(None, np.s_[:, c * 128:(c + 1) * 128]), ones_col)
                act(ebf[kk].v(), bank(bB), AF.Exp, scale=-1.0 / 16.0)
                act(eif[kk].v(), bank(bB), AF.Exp, scale=1.0 / 16.0)
                act(eef[kk].v(), bank(bE), AF.Exp, scale=-1.0 / 16.0)
                act(edc[kk].v(), bank(bl, 4), AF.Exp, scale=-1.0 / 16.0)
                dma("sp", ED.v(part, np.s_[d_, t_, :, ch * 4:(ch + 1) * 4]), edc[kk].v(), (edc[kk].buf, 0))
                tt("dve", qdb[k4].v(), qf[k].v(), ebf[kk].v(), ALU.mult)
                tt("dve", kib[k4].v(), kf[k].v(), eif[kk].v(), ALU.mult)
                tt("pool", keb[kk].v(), kf[k].v(), eef[kk].v(), ALU.mult)
                dma("sp", KE.v(part, np.s_[d_, t_, :, c0:c0 + 512]), keb[kk].v(), (keb[kk].buf, 0))

        def p3(n):
            ch, t_ = items[n]
            for d_ in range(2):
                kk = d_
                k4 = (n % 2) * 2 + d_
                part = (d_ * NT + t_) * 2 + ch
                for (srcb, dstT, DR, eng) in ((qdb[k4], qdT[kk], QD, "act"), (kib[k4], kiT[kk], KI, "dve")):
                    bt = nb()
                    for q in range(4):
                        tr(V(bank_bf(bt)[:, q * 128:(q + 1) * 128], (PB, bt)), srcb.v(None, np.s_[:, q * 128:(q + 1) * 128]))
                    cp(eng, dstT.v(), V(bank_bf(bt)[:, 0:512].rearrange("p (a b) -> p a b", b=128), (PB, bt)))
                    dma("sp", DR.v(part, np.s_[d_, t_, :, ch * 4:(ch + 1) * 4, :]), dstT.v(), (dstT.buf, 0))

        NI = len(items)
        for n in range(NI + 2):
            if n < NI:
                p1(n)
            if 0 <= n - 1 < NI:
                p2(n - 1)
            if 0 <= n - 2 < NI:
                p3(n - 2)
        stage_end()
        A.pop()

        A.push()
        wsl = [TT(A.alloc([16, 512], BF16), f"m1cw{i}") for i in range(2)]
        vb = [TT(A.alloc([512], BF16), f"m1cv{i}") for i in range(4)]
        gw = [TT(A.alloc([16, 128], BF16), f"m1cg{i}") for i in range(2)]
        gtb = [TT(A.alloc([BW], BF16), f"m1cgt{i}") for i in range(4)]
        bg_w = [TT(A.alloc([16, 256], BF16), f"bgw{i}") for i in range(2)]
        bg_b = [TT(A.alloc([256], BF16, parts=1), f"bgb{i}") for i in range(2)]
        bg_g = [TT(A.alloc([D], F32), "bgg")]
        bg_m = [TT(A.alloc([D], F32), "bgm0")]
        bg = s0_units(S0_LATE, 256, bg_w, bg_b, bg_g, bg_m, make_crep())
        kq = 0
        kv_ = 0
        for (cbase, DR, fn) in ((2048, VV, AF.Copy), (4096, RR, AF.Silu)):
            for s4 in range(4):
                w = wsl[kq % 2]
                kq += 1
                wslab(w.v(), w_inp[:, cbase + s4 * 512:cbase + (s4 + 1) * 512])
                for t_ in range(NT):
                    tc = np.s_[t_ * 128:(t_ + 1) * 128]
                    b = nb()
                    for kc in range(16):
                        mm(bank(b), h2T.v(t_, np.s_[:, kc, tc]), w.v(None, np.s_[:, kc, :]), start=(kc == 0), stop=(kc == 15))
                    o = vb[kv_ % 4]
                    kv_ += 1
                    if fn == AF.Copy and (kv_ % 2 == 0):
                        cp("dve", o.v(), bank(b))
                    else:
                        act(o.v(), bank(b), fn)
                    dma("sp", DR.v(t_, np.s_[tc, s4 * 512:(s4 + 1) * 512]), o.v(), (o.buf, 0))
                    if t_ % 4 == 3:
                        next(bg, None)
        kg = 0
        ko = 0
        for (cbase, DR) in ((7264, GAT), (9312, GBT)):
            for dc in range(16):
                w = gw[kg % 2]
                kg += 1
                wslab(w.v(), w_inp[:, cbase + dc * 128:cbase + (dc + 1) * 128])
                for blk in range(NBLK):
                    cols = np.s_[blk * BW:(blk + 1) * BW]
                    tparts = tuple(range(blk * (BW // 128), (blk + 1) * (BW // 128)))
                    b = nb()
                    for kc in range(16):
                        mm(bank(b, BW), w.v(None, np.s_[:, kc, :]), V(h2T.ap[:, kc, cols], (h2T.buf, tparts)), start=(kc == 0), stop=(kc == 15))
                    o = gtb[ko % 4]
                    ko += 1
                    act(o.v(), bank(b, BW), AF.Sigmoid)
                    dma("sp", DR.v(dc * NBLK + blk, np.s_[dc, :, cols]), o.v(), (o.buf, 0))
                next(bg, None)
        for _ in bg:
            pass
        stage_end()
        A.pop()
        A.pop()

        A.push()
        Stl = [TT(A.alloc([8, 512], F32), f"St{i}", 8) for i in range(2)]
        edf = TT(A.alloc([8], F32), "edf")
        Sbf = TT(A.alloc([8, 512], BF16), "Sbf", 8)
        qd = [TT(A.alloc([8, 128], BF16), f"qd{i}") for i in range(3)]
        ki = [TT(A.alloc([8, 128], BF16), f"ki{i}") for i in range(3)]
        ke = [TT(A.alloc([1024], BF16), f"ke{i}") for i in range(3)]
        vt = [TT(A.alloc([D], BF16), f"vt{i}") for i in range(3)]
        ed = [TT(A.alloc([8], F32), f"ed{i}") for i in range(3)]
        ATs = [TT(A.alloc([512], BF16), f"ATs{i}") for i in range(2)]
        Acp = [TT(A.alloc([512], BF16), f"Acp{i}") for i in range(2)]
        maskf = [TT(A.alloc([512], F32), f"maskf{i}") for i in range(2)]
        for d_ in range(2):
            for h in range(4):
                cp("dve", maskf[d_].v(None, np.s_[:, h * 128:(h + 1) * 128]), Lf if d_ == 0 else Lb)
        ot = [TT(A.alloc([D], F32), f"ot{i}", 4) for i in range(2)]
        oft = [TT(A.alloc([D], F32), f"oft{i}") for i in range(3)]
        rt = [TT(A.alloc([D], BF16), f"rt{i}") for i in range(3)]
        ogb = [TT(A.alloc([D], BF16), f"ogb{i}", 4) for i in range(2)]
        ogTt = [TT(A.alloc([16, 128], BF16), f"ogTt{i}") for i in range(2)]
        gnb = TT(A.alloc([512], F32), "gnb")
        dma("sp", gnb.v(), V(gla_norm.partition_broadcast(128)), (gnb.buf, 0))
        tmpf = [TT(A.alloc([512], F32), f"m2tmp{i}") for i in range(4)]
        junk5 = TT(A.alloc([512], BF16), "m2junk")
        for d_ in range(2):
            cur = 0
            cross = False
            dma("sp", Stl[cur].v(None), V(s0[d_].rearrange("h (kc p) v -> p (h kc) v", p=128)), (Stl[cur].buf, 0))
            cp("act", Sbf.v(None), Stl[cur].v(None))
            order = list(range(NT)) if d_ == 0 else list(range(NT - 1, -1, -1))

            def load(i):
                t_ = order[i]
                k = i % 3
                tc = np.s_[t_ * 128:(t_ + 1) * 128]
                pr = ((d_ * NT + t_) * 2, (d_ * NT + t_) * 2 + 1)
                dma("sp", qd[k].v(), V(QD.ap[d_, t_], (QD.buf, pr)), (qd[k].buf, 0))
                dma("sp", ki[k].v(), V(KI.ap[d_, t_], (KI.buf, pr)), (ki[k].buf, 0))
                dma("sp", ke[k].v(), V(KE.ap[d_, t_], (KE.buf, pr)), (ke[k].buf, 0))
                dma("sp", ed[k].v(), V(ED.ap[d_, t_], (ED.buf, pr)), (ed[k].buf, 0))
                dma("sp", vt[k].v(), VV.v(t_, np.s_[tc, :]), (vt[k].buf, 0))
                if d_ == 1:
                    dma("sp", oft[k].v(), OF.v(t_, np.s_[tc, :]), (oft[k].buf, 0))
                    dma("sp", rt[k].v(), RR.v(t_, np.s_[tc, :]), (rt[k].buf, 0))

            load(0)
            if NT > 1:
                load(1)
            def emitA(i):
                k = i % 3
                k2 = i % 2
                ba = nb()
                for h in range(4):
                    for kc in range(2):
                        mm(bank(ba, 128, off=h * 128), ki[k].v(None, np.s_[:, h * 2 + kc, :]), qd[k].v(None, np.s_[:, h * 2 + kc, :]), start=(kc == 0), stop=(kc == 1))
                tt("dve", ATs[k2].v(), bank(ba), maskf[d_].v(), ALU.mult)

            emitA(0)
            kcp = 0
            for i, t_ in enumerate(order):
                k = i % 3
                k2 = i % 2
                tc = np.s_[t_ * 128:(t_ + 1) * 128]
                if i + 2 < NT:
                    load(i + 2)
                if i + 1 < NT:
                    emitA(i + 1)
                for hp in range(2):
                    bss = {}
                    for h in (2 * hp, 2 * hp + 1):
                        hc = np.s_[:, h * 512:(h + 1) * 512]
                        for kc in range(2):
                            c = h * 2 + kc
                            bs = nb()
                            bss[c] = bs
                            mm(bank(bs), ke[k].v(None, np.s_[:, c * 128:(c + 1) * 128]), vt[k].v(None, hc))
                    for h in (2 * hp, 2 * hp + 1):
                        hc = np.s_[:, h * 512:(h + 1) * 512]
                        bo = nb()
                        mm(bank(bo), ATs[k2].v(None, np.s_[:, h * 128:(h + 1) * 128]), vt[k].v(None, hc), start=True, stop=False)
                        for kc in range(2):
                            c = h * 2 + kc
                            mm(bank(bo), qd[k].v(None, np.s_[:, c, :]), Sbf.v(c, np.s_[:, c, :]), start=False, stop=(kc == 1))
                        if d_ == 0:
                            cp("act", ot[k2].v(h, hc), bank(bo))
                        else:
                            tt("dve", ot[k2].v(h, hc), bank(bo), oft[k].v(None, hc), ALU.add)
                    for h in (2 * hp, 2 * hp + 1):
                        for kc in range(2):
                            c = h * 2 + kc
                            Ssrc = Stl[cur]
                            Sdst = Stl[1 - cur] if cross else Stl[cur]
                            edv = edf.v(None, np.s_[:, c:c + 1]) if cross else ed[k].v(None, np.s_[:, c:c + 1])
                            stt("dve", Sdst.v(c, np.s_[:, c, :]), Ssrc.v(c, np.s_[:, c, :]), edv, bank(bss[c]), ALU.mult, ALU.add)
                            cp("act" if (d_ == 1 or kcp % 2 == 0) else "dve", Sbf.v(c, np.s_[:, c, :]), Sdst.v(c, np.s_[:, c, :]))
                            kcp += 1
                if d_ == 0:
                    dma("act", OF.v(t_, np.s_[tc, :]), ot[k2].v(None), (ot[k2].buf, 0))
                else:
                    for h in range(4):
                        hc = np.s_[:, h * 512:(h + 1) * 512]
                        r = rstd_from(ot[k2].v(h, hc), 512, junk5.v())
                        tm = tmpf[h % 4]
                        stt("dve", tm.v(), ot[k2].v(h, hc), r, gnb.v(), ALU.mult, ALU.mult)
                        tt("pool", ogb[k2].v(h, hc), tm.v(), rt[k].v(None, hc), ALU.mult)
                    for half in range(2):
                        bt = nb()
                        for q in range(8):
                            kc = half * 8 + q
                            tr(V(bank_bf(bt)[:, q * 128:(q + 1) * 128], (PB, bt)), ogb[k2].v(kc // 4, np.s_[:, kc * 128:(kc + 1) * 128]))
                        cp("act" if half == 0 else "dve", ogTt[k2].v(None, np.s_[:, half * 8:half * 8 + 8, :]),
                           V(bank_bf(bt).rearrange("p (a b) -> p a b", b=128), (PB, bt)))
                    dma("pool", OGT.v(t_, np.s_[:, :, tc]), ogTt[k2].v(), (ogTt[k2].buf, 0))
                end_seg = (t_ % 2 == 1) if d_ == 0 else (t_ % 2 == 0)
                if cross:
                    cur = 1 - cur
                    cross = False
                if end_seg:
                    seg = t_ // 2
                    dst = (sf_out if d_ == 0 else sb_out)[seg].rearrange("h (kc p) v -> p (h kc) v", p=128)
                    dma("act", V(dst), Stl[cur].v(None), (Stl[cur].buf, 1))
                    if i != NT - 1:
                        ts("dve", edf.v(), ed[(i + 1) % 3].v(), flag.v(None, np.s_[:, 0:1]), ALU.mult)
                        act(Sbf.v(None), Stl[cur].v(None), AF.Copy, scale=flag.v(None, np.s_[:, 0:1]))
                        cross = True
        stage_end()
        A.pop()

        A.push()
        QB = min(512, T)
        NQB = T // QB
        NQ4 = QB // 128
        kblocks = []
        ks = 0
        while ks < NK:
            kblocks.append((ks, min(512, NK - ks)))
            ks += 512
        omT = TT(A.alloc([16, T], BF16), "omT", 16)
        cosT = TT(A.alloc([T], F32), "cosT")
        sinT = TT(A.alloc([T], F32), "sinT")
        dma("sp", cosT.v(None, np.s_[0:64]), V(cosT_in), (cosT.buf, 0))
        dma("sp", sinT.v(None, np.s_[0:64]), V(sinT_in), (sinT.buf, 0))
        QrT = [TT(A.alloc([T], BF16), f"QrT{i}", 2 + NT) for i in range(2)]
        for i in range(2):
            dma("pool", QrT[i].v(1, np.s_[65:81, :]), V(mq_in), (QrT[i].buf, 1))
        S.op("dve", lambda e: e.memset(krT.ap[64:65, :], 1.0), [], [(krT.buf, NKT)])
        QnT = [TT(A.alloc([T], BF16), f"QnT{i}") for i in range(2)]
        KT = [TT(A.alloc([NK], BF16), f"KT{i}") for i in range(2)]
        Vh = [TT(A.alloc([NKT, 129], BF16), f"Vh{i}") for i in range(2)]
        for i in range(2):
            S.op("dve", lambda e, i=i: e.memset(Vh[i].ap[:, :, 128:129], 1.0), [], [(Vh[i].buf, None)])
        sel64 = TT(A.alloc([65], BF16), "sel64")
        S.op("dve", lambda e: e.memset(sel64.ap, 0.0), [], [(sel64.buf, None)])
        S.op("dve", lambda e: e.memset(sel64.ap[:, 64:65], 1.0), [], [(sel64.buf, None)])
        wkv = [TT(A.alloc([4, 256], BF16), f"wkv{i}") for i in range(2)]
        wq = [TT(A.alloc([4, 192], BF16), f"wq{i}") for i in range(2)]
        wqs = [TT(A.alloc([4, 64], BF16), f"wqs{i}") for i in range(2)]
        NPT = 4
        PT = [TT(A.alloc([QB], BF16), f"PT{i}") for i in range(NPT)]
        obt = [TT(A.alloc([128], BF16), f"obt{i}") for i in range(4)]
        dgb = [TT(A.alloc([128], BF16), f"dgb{i}") for i in range(4)]
        rt1 = TT(A.alloc([QB], F32), "rt1")
        rt2 = TT(A.alloc([QB], F32), "rt2")
        n4 = [0]
        n2 = [0]

        def nbs():
            i = n4[0]
            n4[0] = (i + 1) % 3
            return i

        def nbm():
            i = n2[0]
            n2[0] = (i + 1) % 4
            return i

        MB = 3

        def prep(h):
            k = h % 2
            dma("pool", wkv[k].v(), V(w_ukv[:, h * 256:(h + 1) * 256].rearrange("(kc p) n -> p kc n", p=128)), (wkv[k].buf, 0))
            dma("pool", wq[k].v(), V(w_uq[:, h * 192:(h + 1) * 192].rearrange("(kc p) n -> p kc n", p=128)), (wq[k].buf, 0))
            for a in range(2):
                for hh in range(2):
                    o0 = a * 32 + hh * 16
                    i0 = 128 + a * 32 + (1 - hh) * 16
                    cp("pool", wqs[k].v(None, np.s_[:, :, o0:o0 + 16]), wq[k].v(None, np.s_[:, :, i0:i0 + 16]))
            yield
            for (ks, kw) in kblocks:
                b = MB
                kparts = tuple(range(ks // 128, (ks + kw) // 128))
                for kc in range(4):
                    mm(bank(b, kw), wkv[k].v(None, np.s_[:, kc, 0:128]), V(ckvT.ap[:, kc, ks:ks + kw], (ckvT.buf, kparts)), start=(kc == 0), stop=(kc == 3))
                cp("dve", KT[k].v(None, np.s_[:, ks:ks + kw]), bank(b, kw))
                yield
            for g0 in range(0, NKT, 4):
                g = min(4, NKT - g0)
                b = MB
                for a in range(g):
                    kt_ = g0 + a
                    for kc in range(4):
                        mm(bank(b, 128, off=a * 128), ckvT.v(kt_, np.s_[:, kc, kt_ * 128:(kt_ + 1) * 128]), wkv[k].v(None, np.s_[:, kc, 128:256]), start=(kc == 0), stop=(kc == 3))
                cp("dve", Vh[k].v(None, np.s_[:, g0:g0 + g, 0:128]), V(ps[:, b * 512:b * 512 + g * 128].rearrange("p (a b) -> p a b", b=128), (PB, b)))
                yield
            for qb in range(NQB):
                cols = np.s_[qb * QB:(qb + 1) * QB]
                tparts = tuple(range(qb * NQ4, (qb + 1) * NQ4))
                b = MB
                for kc in range(4):
                    mm(bank(b, QB), wq[k].v(None, np.s_[:, kc, 0:128]), V(cqnT.ap[:, kc, cols], (cqnT.buf, tparts)), start=(kc == 0), stop=(kc == 3))
                cp("dve", QnT[k].v(None, np.s_[:, cols]), bank(b, QB))
                yield
                for kc in range(4):
                    mm(bank(b, QB, rows=64), wq[k].v(None, np.s_[:, kc, 128:192]), V(cqnT.ap[:, kc, cols], (cqnT.buf, tparts)), start=(kc == 0), stop=(kc == 3))
                tt("dve", rt1.v(None, np.s_[0:64, :]), bank(b, QB, rows=64), cosT.v(None, np.s_[0:64, cols]), ALU.mult)
                yield
                for kc in range(4):
                    mm(bank(b, QB, rows=64), wqs[k].v(None, np.s_[:, kc, :]), V(cqnT.ap[:, kc, cols], (cqnT.buf, tparts)), start=(kc == 0), stop=(kc == 3))
                tt("dve", rt2.v(None, np.s_[0:64, :]), bank(b, QB, rows=64), sinT.v(None, np.s_[0:64, cols]), ALU.mult)
                tt("dve", QrT[k].v(0, np.s_[0:64, cols]), rt1.v(None, np.s_[0:64, :]), rt2.v(None, np.s_[0:64, :]), ALU.add)
                yield
            for qb in range(NQB):
                cols = np.s_[qb * QB:(qb + 1) * QB]
                b = MB
                for q4 in range(NQ4):
                    qt = qb * NQ4 + q4
                    tc = np.s_[qt * 128:(qt + 1) * 128]
                    kd = np.s_[(2 + qt) * 128:(3 + qt) * 128]
                    mm(bank(b, 128, off=q4 * 128), QnT[k].v(None, np.s_[:, tc]), KT[k].v(None, np.s_[:, kd]), start=True, stop=False)
                    mm(bank(b, 128, off=q4 * 128), QrT[k].v(0, np.s_[0:64, tc]), krT.v(2 + qt, np.s_[0:64, kd]), start=False, stop=True)
                c0 = [scol() for _ in range(NQ4)]
                while c0[-1] != c0[0] + NQ4 - 1:
                    c0 = [scol() for _ in range(NQ4)]
                S.op("dve", lambda e, b=b, c=c0[0]: e.reduce_max(out=stat.ap[:, c:c + NQ4], in_=ps[:, b * 512:b * 512 + NQ4 * 128].rearrange("p (a b) -> p a b", b=128), axis=AX.X),
                     [(PB, b)], [(stat.buf, tuple(c0))])
                for q4 in range(NQ4):
                    ts("dve", dgb[q4].v(), identb.v(), sv(c0[q4]), ALU.mult, -1.0, ALU.mult)
                yield
                for q4 in range(NQ4):
                    mm(bank(b, 128, rows=65, off=q4 * 128), sel64.v(), dgb[q4].v())
                sparts = tuple(2 + qb * NQ4 + i for i in range(NQ4))
                cp("dve", V(QrT[k].ap[64:65, cols], (QrT[k].buf, sparts)), V(ps[64:65, b * 512:b * 512 + QB], (PB, b)))
                yield

        def run_all(gen):
            for _ in gen:
                pass

        kpt = 0
        LAG = 2
        run_all(prep(0))
        for h in range(16):
            k = h % 2
            bg = prep(h + 1) if h + 1 < 16 else iter(())
            tiles = [(qb, kt_) for qb in range(NQB) for kt_ in range(NKT)]
            pbuf = {}
            deferred = []

            def emit_qk(i):
                qb, kt_ = tiles[i]
                cols = np.s_[qb * QB:(qb + 1) * QB]
                qparts = (0, 1) + tuple(2 + qb * NQ4 + q for q in range(NQ4))
                kcs = np.s_[kt_ * 128:(kt_ + 1) * 128]
                b = nbs()
                mm(bank(b, QB), KT[k].v(None, np.s_[:, kcs]), QnT[k].v(None, np.s_[:, cols]), start=True, stop=False)
                mm(bank(b, QB), V(krT.ap[0:81, kcs], (krT.buf, (kt_, NKT))), V(QrT[k].ap[0:81, cols], (QrT[k].buf, qparts)), start=False, stop=True)
                pbuf[i] = b

            def emit_pv(i):
                nonlocal kpt
                qb, kt_ = tiles[i]
                b = pbuf.pop(i)
                p_ = PT[kpt % NPT]
                kpt += 1
                act(p_.v(), bank(b, QB), AF.Exp, scale=ATT_SCALE)
                for q4 in range(NQ4):
                    ob_ = 4 + q4
                    mm(V(ps[:, ob_ * 512:ob_ * 512 + 129], (PB, ob_)),
                       p_.v(None, np.s_[:, q4 * 128:(q4 + 1) * 128]), Vh[k].v(None, np.s_[:, kt_, :]),
                       start=(kt_ == 0), stop=(kt_ == NKT - 1))
                if kt_ == NKT - 1:
                    obs = []
                    for q4 in range(NQ4):
                        ob_ = 4 + q4
                        o0 = ob_ * 512
                        c_ri = scol()
                        recip(sv(c_ri), V(ps[:, o0 + 128:o0 + 129], (PB, ob_)))
                        ts("dve", obt[q4].v(), V(ps[:, o0:o0 + 128], (PB, ob_)), sv(c_ri), ALU.mult)

                    def fin(qb=qb):
                        for q4 in range(NQ4):
                            tr(V(bank_bf(MB)[:, q4 * 128:(q4 + 1) * 128], (PB, MB)), obt[q4].v())
                        c0_ = qb * QB
                        cp("dve", V(omT.ap[:, h, c0_:c0_ + QB], (omT.buf, h)), V(bank_bf(MB)[:, 0:QB], (PB, MB)))
                    deferred.append((i + 5, fin))

            n = len(tiles)
            for i in range(n + LAG):
                if i < n:
                    emit_qk(i)
                j = i - LAG
                if j >= 0:
                    emit_pv(j)
                while deferred and deferred[0][0] <= i:
                    deferred.pop(0)[1]()
                if i % 2 == 1:
                    next(bg, None)
            while deferred:
                deferred.pop(0)[1]()
            run_all(bg)
        for h in range(16):
            dma("sp", OMT.v(h, np.s_[:, h, :]), omT.v(h, np.s_[:, h, :]), (omT.buf, h))
        stage_end()
        A.pop()
        A.pop()

        A.push()
        FB = min(1024, T)
        NFB = T // FB
        HB = min(512, FB)
        NH = FB // HB
        nt4 = HB // 128
        Rw = A.alloc([8192], F32)
        Rbuf = Buf("m4R", 4)
        ogT_ap = Rw.bitcast(BF16)[:, 0:16 * FB].rearrange("p (a b) -> p a b", b=FB)
        y_ap = Rw.rearrange("p (a b) -> p a b", b=D)
        O2w = A.alloc([8192], F32)
        omTb = TT(O2w.bitcast(BF16)[:, 0:16 * FB].rearrange("p (a b) -> p a b", b=FB), "omTb", 4)
        y2_ap = O2w.rearrange("p (a b) -> p a b", b=D)
        mT = TT(A.alloc([16, FB], BF16), "mT", 16)
        wg = [TT(A.alloc([16, 128], BF16), f"wg{i}") for i in range(2)]
        wm = [TT(A.alloc([16, 128], BF16), f"wm{i}") for i in range(2)]
        wo = [TT(A.alloc([16, 512], BF16), f"wo{i}") for i in range(2)]
        gaT = [TT(A.alloc([FB], BF16), f"gaT{i}") for i in range(2)]
        gbT = [TT(A.alloc([FB], BF16), f"gbT{i}") for i in range(2)]
        G2 = TT(A.alloc([D], F32), "G2")
        dma("sp", G2.v(), modS.v(5, np.s_[5]), (G2.buf, 0))
        xt = [TT(A.alloc([D], F32), f"m4x{i}") for i in range(2)]
        junk = TT(A.alloc([D], BF16), "m4junk")
        t1 = [TT(A.alloc([HB], F32), f"m4t1{i}") for i in range(2)]
        t2 = [TT(A.alloc([HB], F32), f"m4t2{i}") for i in range(2)]
        kwo = 0
        for fb in range(NFB):
            cols = np.s_[fb * FB:(fb + 1) * FB]
            ttiles = tuple(range(fb * (FB // 128), (fb + 1) * (FB // 128)))
            gparts_of = lambda dc: tuple(dc * NBLK + bb for bb in range(fb * (FB // BW), (fb + 1) * (FB // BW)))
            dma("sp", V(ogT_ap, (Rbuf, None)), V(OGT.ap[:, :, cols], (OGT.buf, ttiles)), (Rbuf, 0))
            dma("sp", omTb.v(), V(OMT.ap[:, :, cols], (OMT.buf, None)), (omTb.buf, 0))
            for dc in range(16):
                k = dc % 2
                wslab(wg[k].v(), w_go[:, dc * 128:(dc + 1) * 128])
                wslab(wm[k].v(), w_mo[:, dc * 128:(dc + 1) * 128])
                dma("sp", gaT[k].v(), V(GAT.ap[dc, :, cols], (GAT.buf, gparts_of(dc))), (gaT[k].buf, 0))
                dma("sp", gbT[k].v(), V(GBT.ap[dc, :, cols], (GBT.buf, gparts_of(dc))), (gbT[k].buf, 0))
                for hh in range(NH):
                    hcs = np.s_[hh * HB:(hh + 1) * HB]
                    bg = nb()
                    for kc in range(16):
                        mm(bank(bg, HB), wg[k].v(None, np.s_[:, kc, :]), V(ogT_ap[:, kc, hcs], (Rbuf, None)), start=(kc == 0), stop=(kc == 15))
                    bm = nb()
                    for kc in range(16):
                        mm(bank(bm, HB), wm[k].v(None, np.s_[:, kc, :]), omTb.v(None, np.s_[:, kc, hcs]), start=(kc == 0), stop=(kc == 15))
                    tt("dve", t1[hh % 2].v(), bank(bg, HB), gaT[k].v(None, np.s_[:, hcs]), ALU.mult)
                    tt("dve", t2[hh % 2].v(), bank(bm, HB), gbT[k].v(None, np.s_[:, hcs]), ALU.mult)
                    tt("dve", mT.v(dc, np.s_[:, dc, hcs]), t1[hh % 2].v(), t2[hh % 2].v(), ALU.add)
            TBk = FB // 128

            def ytile(t8, sl=np.s_[:]):
                if t8 < 4:
                    return V(y_ap[:, t8, sl], (Rbuf, t8))
                return V(y2_ap[:, t8 - 4, sl], (omTb.buf, t8 - 4))

            for ct in range(4):
                w = wo[kwo % 2]
                kwo += 1
                wslab(w.v(), w_o[:, ct * 512:(ct + 1) * 512])
                pys = [nb() for _ in range(TBk)]
                for t8 in range(TBk):
                    c0 = t8 * 128
                    for kc in range(16):
                        mm(bank(pys[t8]), mT.v(kc, np.s_[:, kc, c0:c0 + 128]), w.v(None, np.s_[:, kc, :]), start=(kc == 0), stop=(kc == 15))
                    cp("act" if t8 % 2 == 0 else "dve", ytile(t8, np.s_[ct * 512:(ct + 1) * 512]), bank(pys[t8]))
            tiles_b = [fb * TBk + t8 for t8 in range(TBk)]
            postnorm_tiles(lambda i: ytile(i), tiles_b, G2, X1, False, X2, False, xt, junk, add_eng="pool")
        stage_end()
        A.pop()

    stage0()
    if upto >= 1:
        ffn_stage(x_in, True, X1, False, 0, w_f1i, w_f1o)
    if upto >= 2:
        mixer_stages()
    if upto >= 3:
        ffn_stage(X2, False, y_out, True, 6, w_f2i, w_f2o)
    S.barrier()
    S.emit(nc, block, esems, dsems)
    print("ops:", S.stats(), "arena max", A.off)
    es.close()
    return nc


WNAMES = ["w_ada", "b_ada", "norm_gains", "w_ffn1_in", "w_ffn1_out", "w_ffn2_in", "w_ffn2_out", "w_in",
          "w_gla_alpha", "b_gla_alpha", "gla_norm", "w_gla_out", "q_norm", "kv_norm", "w_uq", "w_ukv",
          "w_mla_out", "w_out"]


def consts_array():
    s = np.arange(128)[:, None]
    t = np.arange(128)[None, :]
    c = np.zeros((128, 6, 128), np.float32)
    c[:, 0] = (s == t)
    c[:, 1] = (s <= t)
    c[:, 2] = (s >= t)
    c[:, 3] = (s > t)
    c[:, 4] = (s < t)
    c[:, 5] = 1.0
    return c


def rope_tables(T, real):
    cosT = np.ones((64, T), np.float32)
    sinT = np.zeros((64, T), np.float32)
    if real:
        t = np.arange(T)
        pos = [(t // 64).astype(np.float32), (t % 64).astype(np.float32)]
        inv = (np.float32(10000.0) ** (-np.arange(0, 32, 2, dtype=np.float32) / np.float32(32))).astype(np.float32)
        for a in range(2):
            ang = pos[a][None, :] * inv[:, None]
            for hh in range(2):
                r0 = a * 32 + hh * 16
                cosT[r0:r0 + 16] = np.cos(ang)
                sinT[r0:r0 + 16] = np.sin(ang) * (-1.0 if hh == 0 else 1.0)
    return cosT, sinT, np.ascontiguousarray(cosT.T), np.ascontiguousarray(sinT.T)


def mask_rows(NSEG, kind):
    T = NSEG * 256
    NK = T + 256
    mq = np.zeros((16, T), np.float32)
    mk = np.zeros((16, NK), np.float32)
    for m in range(NSEG + 1):
        mk[m, m * 256:(m + 1) * 256] = 1.0
    if kind == "P":
        for s in range(NSEG):
            mq[:NSEG + 1, s * 256:(s + 1) * 256] = NEG_BIG
            mq[1 + s, s * 256:(s + 1) * 256] = 0.0
    return mq, mk


def core_inputs(full, NSEG, kind):
    T = NSEG * 256
    m = {}
    m["x"] = np.ascontiguousarray(full["x"], dtype=np.float32)
    m["cvec"] = np.ascontiguousarray(full["c"].reshape(16, 128).T)
    m["cckv"] = np.ascontiguousarray(full["cache_ckv"])
    m["ckr"] = np.ascontiguousarray(full["cache_krope"])
    m["s0"] = np.ascontiguousarray(np.stack([full["s0f"], full["s0b"]], axis=0))
    m["flag"] = np.full((128, 1), 1.0 if kind == "S" else 0.0, np.float32)
    cosT, sinT, cosK, sinK = rope_tables(T, kind == "S")
    m["cosT"], m["sinT"], m["cosK"], m["sinK"] = cosT, sinT, cosK, sinK
    mq, mk = mask_rows(NSEG, kind)
    m["mq"], m["mk"] = mq, mk
    m["consts"] = consts_array()
    for k in WNAMES:
        m[k] = full[k]
    return m


def make_test_inputs(rng, NSEG, kind):
    T = NSEG * 256
    f32 = np.float32

    def nrm(shape, scale):
        return (rng.standard_normal(shape, dtype=f32) * f32(scale)).astype(f32)

    DFF = 5504
    full = {
        "x": nrm((T, D), 1.0),
        "c": nrm((D,), 1.0),
        "w_ada": nrm((D, 9 * D), 0.5 * D ** -0.5),
        "b_ada": nrm((1, 9 * D), 0.01),
        "norm_gains": 1.0 + nrm((6, D), 0.05),
        "w_ffn1_in": nrm((D, 2 * DFF), D ** -0.5),
        "w_ffn1_out": nrm((DFF, D), DFF ** -0.5),
        "w_ffn2_in": nrm((D, 2 * DFF), D ** -0.5),
        "w_ffn2_out": nrm((DFF, D), DFF ** -0.5),
        "w_in": nrm((D, 11360), D ** -0.5),
        "w_gla_alpha": nrm((2, 16, 1024), 16 ** -0.5),
        "b_gla_alpha": nrm((2, 1024), 0.1),
        "gla_norm": 1.0 + nrm((1, 512), 0.05),
        "w_gla_out": nrm((D, D), D ** -0.5),
        "q_norm": 1.0 + nrm((1, 512), 0.05),
        "kv_norm": 1.0 + nrm((1, 512), 0.05),
        "w_uq": nrm((512, 3072), 512 ** -0.5),
        "w_ukv": nrm((512, 4096), 512 ** -0.5),
        "w_mla_out": nrm((D, D), D ** -0.5),
        "w_out": nrm((D, D), D ** -0.5),
    }
    if kind == "S":
        full["cache_ckv"] = nrm((256, 512), 1.0)
        full["cache_krope"] = nrm((256, 64), 1.0)
        full["s0f"] = nrm((4, 256, 512), 0.5)
        full["s0b"] = nrm((4, 256, 512), 0.5)
    else:
        full["cache_ckv"] = np.zeros((256, 512), f32)
        full["cache_krope"] = np.zeros((256, 64), f32)
        full["s0f"] = np.zeros((4, 256, 512), f32)
        full["s0b"] = np.zeros((4, 256, 512), f32)
    return full


_NC_CACHE = {}


def kernel(**inputs):
    NSEG = 8
    f32 = np.float32
    inp = {k: np.asarray(v) for k, v in inputs.items()}
    W = {
        "w_ada": inp["w_ada"][0], "b_ada": inp["b_ada"], "norm_gains": inp["norm_gains"][0],
        "w_ffn1_in": inp["w_ffn1_in"][0], "w_ffn1_out": inp["w_ffn1_out"][0],
        "w_ffn2_in": inp["w_ffn2_in"][0], "w_ffn2_out": inp["w_ffn2_out"][0],
        "w_in": inp["w_in"][0], "w_gla_alpha": inp["w_gla_alpha"][0], "b_gla_alpha": inp["b_gla_alpha"][0],
        "gla_norm": inp["gla_norm"], "w_gla_out": inp["w_gla_out"][0],
        "q_norm": inp["q_norm"], "kv_norm": inp["kv_norm"], "w_uq": inp["w_uq"][0], "w_ukv": inp["w_ukv"][0],
        "w_mla_out": inp["w_mla_out"][0], "w_out": inp["w_out"][0],
    }
    W = {k: np.ascontiguousarray(v, dtype=f32) for k, v in W.items()}
    in_maps = []
    for core in range(8):
        if core < 4:
            b = core
            full = dict(x=inp["x_sample"][b], c=inp["c"][b], cache_ckv=inp["cache_ckv"][b, 0],
                        cache_krope=inp["cache_krope"][b, 0], s0f=inp["state_gla_fwd"][b, 0],
                        s0b=inp["state_gla_bwd"][b, 0], **W)
            in_maps.append(core_inputs(full, NSEG, "S"))
        else:
            s0_ = 4 * (core - 4)
            xp = inp["x_prompt"][s0_:s0_ + 4].reshape(1024, D)
            x = np.concatenate([xp, np.zeros((1024, D), f32)], axis=0)
            full = dict(x=x, c=inp["c_ctx"], cache_ckv=np.zeros((256, 512), f32),
                        cache_krope=np.zeros((256, 64), f32), s0f=np.zeros((4, 256, 512), f32),
                        s0b=np.zeros((4, 256, 512), f32), **W)
            in_maps.append(core_inputs(full, NSEG, "P"))
    if NSEG not in _NC_CACHE:
        _NC_CACHE[NSEG] = build(NSEG)
    nc = _NC_CACHE[NSEG]
    res = run_bass_kernel_spmd(nc, in_maps, core_ids=list(range(8)))
    R = res.results
    y_prompt = np.zeros((16, 256, D), f32)
    y_sample = np.zeros((4, 2048, D), f32)
    new_ckv = np.zeros((16, 1, 256, 512), f32)
    new_krope = np.zeros((16, 1, 256, 64), f32)
    new_sf = np.zeros((16, 1, 4, 256, 512), f32)
    new_sb = np.zeros((16, 1, 4, 256, 512), f32)
    for core in range(8):
        r = R[core]
        if core < 4:
            y_sample[core] = np.asarray(r["y"])
        else:
            s0_ = 4 * (core - 4)
            y = np.asarray(r["y"]); ck = np.asarray(r["nckv"]); kr = np.asarray(r["nkr"])
            sf = np.asarray(r["sf"]); sb = np.asarray(r["sb"])
            for s in range(4):
                y_prompt[s0_ + s] = y[s * 256:(s + 1) * 256]
                new_ckv[s0_ + s, 0] = ck[s * 256:(s + 1) * 256]
                new_krope[s0_ + s, 0] = kr[s * 256:(s + 1) * 256]
                new_sf[s0_ + s, 0] = sf[s]
                new_sb[s0_ + s, 0] = sb[s]
    return (y_prompt, y_sample, new_ckv, new_krope, new_sf, new_sb)
```

```python
import numpy as np
from contextlib import ExitStack
import concourse.bass as bass
import concourse.mybir as mybir
from concourse.bass_utils import run_bass_kernel_spmd

F32 = mybir.dt.float32
BF16 = mybir.dt.bfloat16
AF = mybir.ActivationFunctionType
ALU = mybir.AluOpType
AX = mybir.AxisListType

ENGINES = ["pe", "act", "dve", "pool", "sp"]


class Buf:
    def __init__(self, name, nparts=1):
        self.name = name
        self.n = nparts
        self.w = [None] * nparts
        self.r = [[] for _ in range(nparts)]
        self.sem = [None] * nparts

    def parts(self, p):
        if p is None:
            return range(self.n)
        if isinstance(p, int):
            return (p,)
        return p


class V:
    def __init__(self, ap, *deps):
        self.ap = ap
        self.deps = list(deps)


class _Op:
    __slots__ = ("fn", "waits", "is_dma", "dsem", "clock", "marked")

    def __init__(self, fn, waits, is_dma, dsem):
        self.fn = fn
        self.waits = waits
        self.is_dma = is_dma
        self.dsem = dsem
        self.clock = None
        self.marked = False


class Sched:
    def __init__(self, n_dma_sems=80):
        self.ops = {e: [] for e in ENGINES}
        self.known = {e: {} for e in ENGINES}
        self.n_dma = n_dma_sems
        self.dma_count = [0] * n_dma_sems
        self.n_sw = 24
        self.dma_next_sw = 0
        self.dma_next = self.n_sw
        self.dma_clock = {}
        self.snap = {e: None for e in ENGINES}
        self.sem_bufs = []

    def _sem_for(self, buf, part, queue):
        if buf.sem[part] is None:
            buf.sem[part] = {}
            self.sem_bufs.append(buf)
        d = buf.sem[part]
        kind = "sw" if queue == "pool" else "hw"
        if kind not in d:
            if kind == "sw":
                if self.dma_next_sw >= self.n_sw:
                    raise RuntimeError("out of sw dma semaphores")
                d[kind] = self.dma_next_sw
                self.dma_next_sw += 1
            else:
                if self.dma_next >= self.n_dma:
                    raise RuntimeError("out of hw dma semaphores")
                d[kind] = self.dma_next
                self.dma_next += 1
        return d[kind]

    def _collect(self, engine, reads, writes):
        deps = {}

        def add(ev):
            if ev is None:
                return
            k, v = ev
            if k == "pe" and engine == "pe":
                return
            if deps.get(k, -1) < v:
                deps[k] = v

        for b, p in reads:
            for i in b.parts(p):
                add(b.w[i])
        for b, p in writes:
            for i in b.parts(p):
                add(b.w[i])
                for ev in b.r[i]:
                    add(ev)
        return deps

    def _update(self, ev, reads, writes):
        for b, p in reads:
            for i in b.parts(p):
                lst = b.r[i]
                for j, (k, v) in enumerate(lst):
                    if k == ev[0]:
                        lst[j] = ev
                        break
                else:
                    lst.append(ev)
        for b, p in writes:
            for i in b.parts(p):
                b.w[i] = ev
                b.r[i] = []

    def _merge_clock(self, engine, clock):
        kn = self.known[engine]
        for k, v in clock.items():
            if kn.get(k, -1) < v:
                kn[k] = v
        self.snap[engine] = None

    def _snapshot(self, engine):
        if self.snap[engine] is None:
            self.snap[engine] = dict(self.known[engine])
        return self.snap[engine]

    def _make_waits(self, engine, deps):
        waits = []
        kn = self.known[engine]
        for k, v in deps.items():
            if isinstance(k, tuple):
                if kn.get(k, -1) >= v:
                    continue
                v = max(v, self.dma_count[k[1]])
                waits.append((k, v))
                kn[k] = v
                self.snap[engine] = None
                clk = self.dma_clock.get((k[1], v))
                if clk:
                    self._merge_clock(engine, clk)
            else:
                if kn.get(k, -1) >= v:
                    continue
                waits.append((k, v))
                kn[k] = v
                self.snap[engine] = None
                op = self.ops[k][v]
                op.marked = True
                if op.clock:
                    self._merge_clock(engine, op.clock)
        return waits

    def op(self, engine, fn, reads=(), writes=()):
        deps = self._collect(engine, reads, writes)
        waits = self._make_waits(engine, deps)
        o = _Op(fn, waits, False, None)
        o.clock = self._snapshot(engine)
        idx = len(self.ops[engine])
        self.ops[engine].append(o)
        self._update((engine, idx), reads, writes)
        return idx

    def dma(self, queue, fn, reads, writes, semof):
        deps = self._collect(queue, reads, writes)
        waits = self._make_waits(queue, deps)
        si = self._sem_for(semof[0], semof[1], queue)
        self.dma_count[si] += 16
        val = self.dma_count[si]
        o = _Op(fn, waits, True, si)
        self.ops[queue].append(o)
        self.dma_clock[(si, val)] = self._snapshot(queue)
        self._update((("d", si), val), reads, writes)

    def barrier(self):
        last = {}
        for e in ("pe", "act", "dve", "pool"):
            for i in range(len(self.ops[e]) - 1, -1, -1):
                if not self.ops[e][i].is_dma and self.ops[e][i].fn is not None:
                    last[e] = i
                    break
        for e in ENGINES:
            deps = {}
            for f, i in last.items():
                if f != e:
                    deps[f] = i
                elif e in ("act", "dve", "pool"):
                    deps[f] = i
            for si in range(self.n_dma):
                if self.dma_count[si] > 0:
                    deps[("d", si)] = self.dma_count[si]
            waits = self._make_waits(e, deps)
            if waits:
                o = _Op(None, waits, False, None)
                o.clock = self._snapshot(e)
                self.ops[e].append(o)

    def reset_sems(self):
        self.dma_next = self.n_sw
        self.dma_next_sw = 0
        for b in self.sem_bufs:
            b.sem = [None] * b.n
        self.sem_bufs = []

    def emit(self, nc, block, esems, dsems):
        vals = {}
        for e in ENGINES:
            c = 0
            m = {}
            for i, o in enumerate(self.ops[e]):
                if o.marked:
                    c += 1
                    m[i] = c
            vals[e] = m

        def run(e, eng):
            for i, o in enumerate(self.ops[e]):
                for k, v in o.waits:
                    if isinstance(k, tuple):
                        eng.wait_ge(dsems[k[1]], v)
                    else:
                        eng.wait_ge(esems[k], vals[k][v])
                if o.fn is None:
                    continue
                ins = o.fn(eng)
                if o.is_dma:
                    ins.then_inc(dsems[o.dsem], 16)
                elif o.marked:
                    ins.then_inc(esems[e], 1)

        @block.tensor
        def _(eng):
            run("pe", eng)

        @block.scalar
        def _(eng):
            run("act", eng)

        @block.vector
        def _(eng):
            run("dve", eng)

        @block.gpsimd
        def _(eng):
            run("pool", eng)

        @block.sync
        def _(eng):
            run("sp", eng)

    def stats(self):
        return {e: len(self.ops[e]) for e in ENGINES}


class Arena:
    def __init__(self, t, nwords):
        self.t = t
        self.n = nwords
        self.off = 0
        self.marks = []

    def alloc(self, free_shape, dtype, parts=128):
        nel = int(np.prod(free_shape))
        nbytes = nel * (2 if dtype == BF16 else 4)
        nw = (nbytes + 3) // 4
        nw = (nw + 15) // 16 * 16
        if self.off + nw > self.n:
            raise RuntimeError(f"arena overflow: need {nw} words at {self.off} of {self.n}")
        ap = self.t[0:parts, self.off:self.off + nw]
        self.off += nw
        if dtype == BF16:
            ap = ap.bitcast(BF16)[:, 0:nel]
        else:
            ap = ap[:, 0:nel]
        if len(free_shape) == 2:
            ap = ap.rearrange("p (a b) -> p a b", b=free_shape[1])
        elif len(free_shape) == 3:
            ap = ap.rearrange("p (a b c) -> p a b c", b=free_shape[1], c=free_shape[2])
        return ap

    def push(self):
        self.marks.append(self.off)

    def pop(self):
        self.off = self.marks.pop()


D = 2048
DFF = 5504
NFC = 43
ATT_SCALE = 192 ** -0.5
EPS = 1e-6
NEG_BIG = -30000.0
ARENA_WORDS = 53184


class TT:
    def __init__(self, ap, name, nparts=1):
        self.ap = ap
        self.buf = Buf(name, nparts)

    def v(self, part=None, idx=None):
        ap = self.ap if idx is None else self.ap[idx]
        return V(ap, (self.buf, part))


def build(NSEG, upto=99, debug=False):
    T = NSEG * 256
    NT = T // 128
    NK = T + 256
    NKT = NK // 128
    nc = bass.Bass("TRN2", target_bir_lowering=False)

    def din(name, shape):
        return nc.dram_tensor(name, list(shape), F32, kind="ExternalInput").ap()

    def dout(name, shape):
        return nc.dram_tensor(name, list(shape), F32, kind="ExternalOutput").ap()

    def dscr(name, shape, dt):
        if debug:
            return nc.dram_tensor(name, list(shape), dt, kind="ExternalOutput").ap()
        return nc.dram_tensor(name, list(shape), dt).ap()

    x_in = din("x", [T, D])
    cvec = din("cvec", [128, 16])
    cckv = din("cckv", [256, 512])
    ckr = din("ckr", [256, 64])
    s0 = din("s0", [2, 4, 256, 512])
    flag_in = din("flag", [128, 1])
    cosT_in = din("cosT", [64, T])
    sinT_in = din("sinT", [64, T])
    cosK_in = din("cosK", [T, 64])
    sinK_in = din("sinK", [T, 64])
    mq_in = din("mq", [16, T])
    mk_in = din("mk", [16, NK])
    consts_in = din("consts", [128, 6, 128])
    w_ada = din("w_ada", [D, 9 * D])
    b_ada = din("b_ada", [1, 9 * D])
    gains = din("norm_gains", [6, D])
    w_f1i = din("w_ffn1_in", [D, 2 * DFF])
    w_f1o = din("w_ffn1_out", [DFF, D])
    w_f2i = din("w_ffn2_in", [D, 2 * DFF])
    w_f2o = din("w_ffn2_out", [DFF, D])
    w_inp = din("w_in", [D, 11360])
    w_alpha = din("w_gla_alpha", [2, 16, 1024])
    b_alpha = din("b_gla_alpha", [2, 1024])
    gla_norm = din("gla_norm", [1, 512])
    w_go = din("w_gla_out", [D, D])
    q_norm = din("q_norm", [1, 512])
    kv_norm = din("kv_norm", [1, 512])
    w_uq = din("w_uq", [512, 3072])
    w_ukv = din("w_ukv", [512, 4096])
    w_mo = din("w_mla_out", [D, D])
    w_o = din("w_out", [D, D])

    y_out = dout("y", [T, D])
    nckv_out = dout("nckv", [T, 512])
    nkr_out = dout("nkr", [T, 64])
    sf_out = dout("sf", [NSEG, 4, 256, 512])
    sb_out = dout("sb", [NSEG, 4, 256, 512])

    modS = TT(dscr("modS", [9, 128, D], F32), "modS", 9)
    X1 = TT(dscr("X1", [T, D], F32), "X1", NT)
    X2 = TT(dscr("X2", [T, D], F32), "X2", NT)
    QD = TT(dscr("QD", [2, NT, 128, 8, 128], BF16), "QD", 2 * NT)
    KI = TT(dscr("KI", [2, NT, 128, 8, 128], BF16), "KI", 2 * NT)
    KE = TT(dscr("KE", [2, NT, 128, 1024], BF16), "KE", 2 * NT)
    ED = TT(dscr("ED", [2, NT, 128, 8], F32), "ED", 2 * NT)
    VV = TT(dscr("VV", [T, D], BF16), "VV", NT)
    RR = TT(dscr("RR", [T, D], BF16), "RR", NT)
    GAT = TT(dscr("GAT", [16, 128, T], BF16), "GAT", 1)
    GBT = TT(dscr("GBT", [16, 128, T], BF16), "GBT", 1)
    OF = TT(dscr("OF", [T, D], F32), "OF", NT)
    OGT = TT(dscr("OGT", [128, 16, T], BF16), "OGT", 1)
    OMT = TT(dscr("OMT", [128, 16, T], BF16), "OMT", 1)

    S = Sched(90)
    es = ExitStack()
    arena_t = es.enter_context(nc.sbuf_tensor("arena", [128, ARENA_WORDS], F32))
    ps = es.enter_context(nc.psum_tensor("ps", [128, 4096], F32))
    esems = {e: es.enter_context(nc.semaphore(f"s_{e}")) for e in ["pe", "act", "dve", "pool"]}
    dsems = [es.enter_context(nc.semaphore(f"d{i}")) for i in range(90)]
    block = es.enter_context(nc.Block())
    A = Arena(arena_t, ARENA_WORDS)
    PB = Buf("psum", 8)
    bank_rr = [0]

    def nb():
        i = bank_rr[0]
        bank_rr[0] = (i + 1) % 8
        return i

    def bank(i, w=512, rows=128, off=0):
        return V(ps[0:rows, i * 512 + off:i * 512 + off + w], (PB, i))

    def bank_bf(i):
        return ps[:, i * 512:(i + 1) * 512].bitcast(BF16)

    def mm(o, l, r, start=True, stop=True):
        S.op("pe", lambda e: e.matmul(o.ap, lhsT=l.ap, rhs=r.ap, start=start, stop=stop), l.deps + r.deps, o.deps)

    def act(o, i, func, bias=None, scale=None, accum=None):
        kw = {}
        reads = list(i.deps)
        writes = list(o.deps)
        if bias is not None:
            if isinstance(bias, V):
                kw["bias"] = bias.ap
                reads += bias.deps
            else:
                kw["bias"] = bias
        if scale is not None:
            if isinstance(scale, V):
                kw["scale"] = scale.ap
                reads += scale.deps
            else:
                kw["scale"] = scale
        if accum is not None:
            kw["accum_out"] = accum.ap
            writes += accum.deps
        S.op("act", lambda e: e.activation(out=o.ap, in_=i.ap, func=func, **kw), reads, writes)

    def tt(eng, o, a, b, op):
        S.op(eng, lambda e: e.tensor_tensor(out=o.ap, in0=a.ap, in1=b.ap, op=op), a.deps + b.deps, o.deps)

    def stt(eng, o, a, sc, b, op0, op1):
        reads = a.deps + b.deps
        if isinstance(sc, V):
            reads = reads + sc.deps
            scv = sc.ap
        else:
            scv = sc
        S.op(eng, lambda e: e.scalar_tensor_tensor(out=o.ap, in0=a.ap, scalar=scv, in1=b.ap, op0=op0, op1=op1), reads, o.deps)

    def ts(eng, o, a, s1, op0, s2=None, op1=None):
        reads = list(a.deps)
        if isinstance(s1, V):
            reads += s1.deps
            s1 = s1.ap
        if isinstance(s2, V):
            reads += s2.deps
            s2 = s2.ap
        if op1 is None:
            S.op(eng, lambda e: e.tensor_scalar(out=o.ap, in0=a.ap, scalar1=s1, scalar2=None, op0=op0), reads, o.deps)
        else:
            S.op(eng, lambda e: e.tensor_scalar(out=o.ap, in0=a.ap, scalar1=s1, scalar2=s2, op0=op0, op1=op1), reads, o.deps)

    def cp(eng, o, i, scale=None):
        if eng == "act":
            act(o, i, AF.Copy, scale=scale)
        else:
            S.op(eng, lambda e: e.tensor_copy(out=o.ap, in_=i.ap), i.deps, o.deps)

    def dma(q, o, i, semof):
        S.dma(q, lambda e: e.dma_start(out=o.ap, in_=i.ap), i.deps, o.deps, semof)

    def rmax(o, i):
        S.op("dve", lambda e: e.reduce_max(out=o.ap, in_=i.ap, axis=AX.X), i.deps, o.deps)

    def rsum(o, i):
        S.op("dve", lambda e: e.reduce_sum(out=o.ap, in_=i.ap, axis=AX.X), i.deps, o.deps)

    def recip(o, i):
        S.op("dve", lambda e: e.reciprocal(out=o.ap, in_=i.ap), i.deps, o.deps)

    def stage_end():
        S.barrier()
        S.reset_sems()

    cst = TT(A.alloc([6, 128], F32), "cst")
    dma("sp", cst.v(), V(consts_in), (cst.buf, 0))
    identb = TT(A.alloc([128], BF16), "identb")
    onesb = TT(A.alloc([128], BF16), "onesb")
    cp("dve", identb.v(), cst.v(None, np.s_[:, 0, :]))
    cp("dve", onesb.v(), cst.v(None, np.s_[:, 5, :]))
    Lf = cst.v(None, np.s_[:, 1, :])
    Lb = cst.v(None, np.s_[:, 2, :])
    Uf = cst.v(None, np.s_[:, 3, :])
    Ub = cst.v(None, np.s_[:, 4, :])
    ones_col = cst.v(None, np.s_[:, 5, 0:1])
    flag = TT(A.alloc([1], F32), "flag")
    dma("sp", flag.v(), V(flag_in), (flag.buf, 0))
    stat = TT(A.alloc([64], F32), "stat", 64)
    stat_rr = [0]

    def scol():
        i = stat_rr[0]
        stat_rr[0] = (i + 1) % 64
        return i

    def sv(i, w=1):
        return V(stat.ap[:, i:i + w], (stat.buf, tuple(range(i, i + w))))

    def tr(o, i):
        S.op("pe", lambda e: e.transpose(o.ap, i.ap, identb.ap), i.deps + [(identb.buf, None)], o.deps)

    def rstd_from(src, n, junk):
        c0, c1, c2 = scol(), scol(), scol()
        act(junk, src, AF.Square, accum=sv(c0))
        act(sv(c1), sv(c0), AF.Ln, scale=1.0 / n, bias=EPS)
        act(sv(c2), sv(c1), AF.Exp, scale=-0.5)
        return sv(c2)

    def wslab(dst, src_cols):
        dma("pool", dst, V(src_cols.rearrange("(kc p) n -> p kc n", p=128)), (dst.deps[0][0], 0))

    cs_p = TT(A.alloc([16], F32), "cs_p")

    def make_crep():
        crep = TT(A.alloc([16, 128], BF16), "crep")
        for kc in range(16):
            ts("dve", crep.v(None, np.s_[:, kc, :]), onesb.v(), cs_p.v(None, np.s_[:, kc:kc + 1]), ALU.mult)
        return crep

    def s0_units(ms, ncol, wsl, brow, gt_, mt, crep):
        k = 0
        nct = D // ncol
        for mi, m in enumerate(ms):
            sub = m // 3
            kind = m % 3
            g = gt_[mi % len(gt_)]
            if kind == 1:
                dma("sp", g.v(), V(gains[2 * sub:2 * sub + 1, :].partition_broadcast(128)), (g.buf, 0))
            if kind == 2:
                dma("sp", g.v(), V(gains[2 * sub + 1:2 * sub + 2, :].partition_broadcast(128)), (g.buf, 0))
            mtile = mt[mi % len(mt)]
            for ct in range(nct):
                c0 = m * D + ct * ncol
                w = wsl[k % len(wsl)]
                br = brow[k % len(brow)]
                k += 1
                wslab(w.v(None, np.s_[:, :, 0:ncol]), w_ada[:, c0:c0 + ncol])
                dma("pool", br.v(None, np.s_[:, 0:ncol]), V(b_ada[0:1, c0:c0 + ncol]), (br.buf, 0))
                bi = nb()
                for kc in range(16):
                    mm(bank(bi, ncol), crep.v(None, np.s_[:, kc, :]), w.v(None, np.s_[:, kc, 0:ncol]), start=(kc == 0), stop=False)
                mm(bank(bi, ncol), onesb.v(None, np.s_[0:1, :]), br.v(None, np.s_[:, 0:ncol]), start=False, stop=True)
                cs_ = np.s_[:, ct * ncol:(ct + 1) * ncol]
                o = mtile.v(None, cs_)
                if kind == 0:
                    cp("act", o, bank(bi, ncol))
                elif kind == 1:
                    stt("dve", o, bank(bi, ncol), 1.0, g.v(None, cs_), ALU.add, ALU.mult)
                else:
                    coef = 1.0 if sub == 1 else 0.5
                    stt("dve", o, bank(bi, ncol), coef, g.v(None, cs_), ALU.mult, ALU.mult)
                yield
            dma("sp", modS.v(m, np.s_[m]), mtile.v(), (mtile.buf, 0))
            yield

    S0_FIRST = [0, 1, 2, 3, 4]
    S0_LATE = [5, 6, 7, 8]

    def stage0():
        A.push()
        cv = TT(A.alloc([16], F32), "cv")
        wsl = [TT(A.alloc([16, 512], BF16), f"s0w{i}") for i in range(3)]
        brow = [TT(A.alloc([512], BF16, parts=1), f"s0b{i}") for i in range(3)]
        gt_ = [TT(A.alloc([D], F32), f"s0g{i}") for i in range(2)]
        mt = [TT(A.alloc([D], F32), f"s0m{i}") for i in range(2)]
        dma("sp", cv.v(), V(cvec), (cv.buf, 0))
        act(cs_p.v(), cv.v(), AF.Silu)
        crep = make_crep()
        for _ in s0_units(S0_FIRST if upto >= 2 else list(range(9)), 512, wsl, brow, gt_, mt, crep):
            pass
        stage_end()
        A.pop()

    def prenorm_tiles(*a, **kw):
        for _ in prenorm_gen(*a, **kw):
            pass

    def prenorm_gen(src, src_is_input, tiles, At, Bt, xt, hb, hT, hT_part_of, col_of, extra_reads=(), first_writes=()):
        n = len(tiles)

        def load(i):
            t_ = tiles[i]
            xv = xt[i % 2].v()
            if src_is_input:
                dma("sp", xv, V(src[t_ * 128:(t_ + 1) * 128, :]), (xt[i % 2].buf, 0))
            else:
                dma("sp", xv, src.v(t_, np.s_[t_ * 128:(t_ + 1) * 128, :]), (xt[i % 2].buf, 0))

        def s1(i):
            xv = xt[i % 2].v()
            hv = hb[i % 2].v()
            r = rstd_from(xv, D, hv)
            stt("dve", xv, xv, r, At.v(), ALU.mult, ALU.mult)
            tt("dve" if i % 2 == 0 else "pool", hv, xv, Bt.v(), ALU.add)

        def s2(i):
            t_ = tiles[i]
            c0 = col_of(t_)
            for half in range(2):
                bi = nb()
                pv = bank_bf(bi)
                for q in range(8):
                    kc = half * 8 + q
                    tr(V(pv[:, q * 128:(q + 1) * 128], (PB, bi)), hb[i % 2].v(None, np.s_[:, kc * 128:(kc + 1) * 128]))
                dst = V(hT.ap[:, half * 8:half * 8 + 8, c0:c0 + 128], (hT.buf, hT_part_of(t_)), *(first_writes if i == 0 else ()))
                srcv = V(pv.rearrange("p (a b) -> p a b", b=128), (PB, bi), *extra_reads)
                cp("act" if half == 0 else "dve", dst, srcv)

        load(0)
        if n > 1:
            load(1)
        s1(0)
        for i in range(n):
            if i + 1 < n:
                s1(i + 1)
            s2(i)
            if i + 2 < n:
                load(i + 2)
            yield

    def postnorm_tiles(ytile_of, tiles, Gt, res_src, res_is_input, dst, dst_is_output, xt, junk, add_eng="dve"):
        n = len(tiles)

        def load(i):
            t_ = tiles[i]
            xv = xt[i % 2].v()
            if res_is_input:
                dma("sp", xv, V(res_src[t_ * 128:(t_ + 1) * 128, :]), (xt[i % 2].buf, 0))
            else:
                dma("sp", xv, res_src.v(t_, np.s_[t_ * 128:(t_ + 1) * 128, :]), (xt[i % 2].buf, 0))

        load(0)
        if n > 1:
            load(1)
        for i, t_ in enumerate(tiles):
            yv = ytile_of(i)
            xv = xt[i % 2].v()
            r = rstd_from(yv, D, junk.v())
            stt("dve", yv, yv, r, Gt.v(), ALU.mult, ALU.mult)
            tt(add_eng if i % 2 == 1 else "dve", xv, yv, xv, ALU.add)
            if dst_is_output:
                dma("sp", V(dst[t_ * 128:(t_ + 1) * 128, :]), xv, (xt[i % 2].buf, 0))
            else:
                dma("sp", dst.v(t_, np.s_[t_ * 128:(t_ + 1) * 128, :]), xv, (xt[i % 2].buf, 0))
            if i + 2 < n:
                load(i + 2)

    def ffn_stage(src, src_is_input, dst, dst_is_output, mbase, w1, w2):
        A.push()
        FB = min(1024, T)
        NFB = T // FB
        TB = FB // 128
        HB = min(512, FB)
        NH = FB // HB
        At = TT(A.alloc([D], F32), "ffA")
        Bt = TT(A.alloc([D], F32), "ffB")
        Gt = TT(A.alloc([D], F32), "ffG")
        dma("sp", Bt.v(), modS.v(mbase, np.s_[mbase]), (Bt.buf, 0))
        dma("sp", At.v(), modS.v(mbase + 1, np.s_[mbase + 1]), (At.buf, 0))
        dma("sp", Gt.v(), modS.v(mbase + 2, np.s_[mbase + 2]), (Gt.buf, 0))
        Rw = A.alloc([8192], F32)
        Rbuf = Buf("ffR", 4)
        hT = TT(Rw.bitcast(BF16)[:, 0:16 * FB].rearrange("p (a b) -> p a b", b=FB), "ffhT", TB)
        yv_ap = Rw.rearrange("p (a b) -> p a b", b=D)
        aT = TT(A.alloc([NFC, FB], BF16), "ffaT", NFC)
        w1s = [TT(A.alloc([16, 2, 128], BF16), f"ffw1_{i}") for i in range(2)]
        w2s = [TT(A.alloc([4, 512], BF16), f"ffw2_{i}") for i in range(3)]
        xt = [TT(A.alloc([D], F32), f"ffx{i}") for i in range(2)]
        hb = [TT(A.alloc([D], BF16), f"ffh{i}") for i in range(2)]
        sg = [TT(A.alloc([512], F32), f"ffsg{i}") for i in range(2)]
        junk = TT(A.alloc([D], BF16), "ffjunk")
        kw2 = 0
        for fb in range(NFB):
            tiles = [fb * TB + i for i in range(TB)]

            def ld_w1(j):
                w = w1s[j % 2]
                dma("pool", w.v(None, np.s_[:, :, 0, :]), V(w1[:, j * 128:(j + 1) * 128].rearrange("(kc p) n -> p kc n", p=128)), (w.buf, 0))
                dma("pool", w.v(None, np.s_[:, :, 1, :]), V(w1[:, DFF + j * 128:DFF + (j + 1) * 128].rearrange("(kc p) n -> p kc n", p=128)), (w.buf, 0))

            ld_w1(0)
            ld_w1(1)
            prenorm_tiles(src, src_is_input, tiles, At, Bt, xt, hb, hT, lambda t_: t_ - fb * TB, lambda t_: (t_ - fb * TB) * 128, extra_reads=((Rbuf, None),), first_writes=((Rbuf, None),))
            for j in range(NFC):
                w = w1s[j % 2]
                pg = [nb() for _ in range(NH)]
                pu = [nb() for _ in range(NH)]
                for gi, pbs in ((0, pg), (1, pu)):
                    for kc in range(16):
                        for h_ in range(NH):
                            mm(bank(pbs[h_], HB), w.v(None, np.s_[:, kc, gi, :]), hT.v(None, np.s_[:, kc, h_ * HB:(h_ + 1) * HB]), start=(kc == 0), stop=(kc == 15))
                if j + 2 < NFC:
                    ld_w1(j + 2)
                for h_ in range(NH):
                    sgv = sg[h_ % 2].v(None, np.s_[:, 0:HB])
                    act(sgv, bank(pg[h_], HB), AF.Silu)
                    tt("dve", aT.v(j, np.s_[:, j, h_ * HB:(h_ + 1) * HB]), sgv, bank(pu[h_], HB), ALU.mult)
            for th in range(NH):
                nt4 = HB // 128
                for dt in range(4):
                    pys = [nb() for _ in range(nt4)]
                    j = 0
                    while j < NFC:
                        g = min(4, NFC - j)
                        w = w2s[kw2 % 3]
                        kw2 += 1
                        dma("pool", w.v(None, np.s_[:, 0:g, :]), V(w2[j * 128:(j + g) * 128, dt * 512:(dt + 1) * 512].rearrange("(a p) n -> p a n", p=128)), (w.buf, 0))
                        for a in range(g):
                            for t4 in range(nt4):
                                c0 = (th * nt4 + t4) * 128
                                mm(bank(pys[t4]), aT.v(j + a, np.s_[:, j + a, c0:c0 + 128]), w.v(None, np.s_[:, a, :]), start=(j + a == 0), stop=(j + a == NFC - 1))
                        j += g
                    for t4 in range(nt4):
                        cp("act", V(yv_ap[:, t4, dt * 512:(dt + 1) * 512], (Rbuf, t4)), bank(pys[t4]))
                tiles_h = [fb * TB + th * nt4 + t4 for t4 in range(nt4)]
                postnorm_tiles(lambda i: V(yv_ap[:, i, :], (Rbuf, i)), tiles_h, Gt, src, src_is_input, dst, dst_is_output, xt, junk,
                               add_eng=("pool" if th == NH - 1 else "dve"))
        stage_end()
        A.pop()

    def mixer_stages():
        BW = min(512, T)
        NBLK = T // BW
        A.push()
        cqnT = TT(A.alloc([4, T], BF16), "cqnT", NT)
        ckvT = TT(A.alloc([4, NK], BF16), "ckvT", NKT)
        krT = TT(A.alloc([NK], BF16), "krT", NKT + 1)
        GAT.buf = Buf("GAT", 16 * NBLK)
        GBT.buf = Buf("GBT", 16 * NBLK)
        OGT.buf = Buf("OGT", NT)
        OMT.buf = Buf("OMT", 16)
        QD.buf = Buf("QD", 4 * NT)
        KI.buf = Buf("KI", 4 * NT)
        KE.buf = Buf("KE", 4 * NT)
        ED.buf = Buf("ED", 4 * NT)

        A.push()
        h2T = TT(A.alloc([16, T], BF16), "h2T", NT)
        aT2 = [TT(A.alloc([T], BF16), "aTf"), TT(A.alloc([T], BF16), "aTb")]

        A.push()
        At = TT(A.alloc([D], F32), "m1A")
        Bt = TT(A.alloc([D], F32), "m1B")
        dma("sp", Bt.v(), modS.v(3, np.s_[3]), (Bt.buf, 0))
        dma("sp", At.v(), modS.v(4, np.s_[4]), (At.buf, 0))
        xt = [TT(A.alloc([D], F32), f"m1x{i}") for i in range(2)]
        hb = [TT(A.alloc([D], BF16), f"m1h{i}") for i in range(2)]
        Wsm = TT(A.alloc([16, 1120], BF16), "Wsm")
        wslab(Wsm.v(), w_inp[:, 6144:7264])
        qnb = TT(A.alloc([512], F32), "qnb")
        kvnb = TT(A.alloc([512], F32), "kvnb")
        dma("sp", qnb.v(), V(q_norm.partition_broadcast(128)), (qnb.buf, 0))
        dma("sp", kvnb.v(), V(kv_norm.partition_broadcast(128)), (kvnb.buf, 0))
        junk5 = TT(A.alloc([512], BF16), "junk5")
        cqb = [TT(A.alloc([512], BF16), f"cqb{i}") for i in range(2)]
        ckf = [TT(A.alloc([512], F32), f"ckf{i}") for i in range(2)]
        ckb = [TT(A.alloc([512], BF16), f"ckb{i}") for i in range(2)]
        krf = [TT(A.alloc([64], F32), f"krf{i}") for i in range(2)]
        krb = [TT(A.alloc([64], BF16), f"krb{i}") for i in range(2)]
        kt1 = [TT(A.alloc([64], F32), f"kt1{i}") for i in range(2)]
        kt2 = [TT(A.alloc([64], F32), f"kt2{i}") for i in range(2)]
        cosk = [TT(A.alloc([64], F32), f"cosk{i}") for i in range(2)]
        sink = [TT(A.alloc([64], F32), f"sink{i}") for i in range(2)]
        dma("pool", krT.v(NKT, np.s_[65:81, :]), V(mk_in), (krT.buf, NKT))
        for c2 in range(2):
            k = c2 % 2
            dma("sp", ckf[k].v(), V(cckv[c2 * 128:(c2 + 1) * 128, :]), (ckf[k].buf, 0))
            cp("act", ckb[k].v(), ckf[k].v())
            b = nb()
            for q in range(4):
                tr(V(bank_bf(b)[:, q * 128:(q + 1) * 128], (PB, b)), ckb[k].v(None, np.s_[:, q * 128:(q + 1) * 128]))
            cp("dve", ckvT.v(c2, np.s_[:, 0:4, c2 * 128:(c2 + 1) * 128]),
               V(bank_bf(b)[:, 0:512].rearrange("p (a b) -> p a b", b=128), (PB, b)))
            dma("sp", krf[k].v(), V(ckr[c2 * 128:(c2 + 1) * 128, :]), (krf[k].buf, 0))
            cp("act", krb[k].v(), krf[k].v())
            b = nb()
            tr(V(bank_bf(b)[0:64, 0:128], (PB, b)), krb[k].v())
            cp("dve", krT.v(c2, np.s_[0:64, c2 * 128:(c2 + 1) * 128]), V(bank_bf(b)[0:64, 0:128], (PB, b)))
        pgen = prenorm_gen(X1, False, list(range(NT)), At, Bt, xt, hb, h2T, lambda t_: t_, lambda t_: t_ * 128)
        def m1a_A(t_):
            k = t_ % 2
            tc = np.s_[t_ * 128:(t_ + 1) * 128]
            dma("sp", cosk[k].v(), V(cosK_in[tc, :]), (cosk[k].buf, 0))
            dma("sp", sink[k].v(), V(sinK_in[tc, :]), (sink[k].buf, 0))
            b1 = nb()
            for kc in range(16):
                mm(bank(b1), h2T.v(t_, np.s_[:, kc, tc]), Wsm.v(None, np.s_[:, kc, 32:544]), start=(kc == 0), stop=(kc == 15))
            b3 = nb()
            for kc in range(16):
                mm(bank(b3), h2T.v(t_, np.s_[:, kc, tc]), Wsm.v(None, np.s_[:, kc, 544:1056]), start=(kc == 0), stop=(kc == 15))
            b5 = nb()
            for kc in range(16):
                mm(bank(b5, 64), h2T.v(t_, np.s_[:, kc, tc]), Wsm.v(None, np.s_[:, kc, 1056:1120]), start=(kc == 0), stop=(kc == 15))
            r = rstd_from(bank(b1), 512, junk5.v())
            stt("dve", cqb[k].v(), bank(b1), r, qnb.v(), ALU.mult, ALU.mult)
            r = rstd_from(bank(b3), 512, junk5.v())
            stt("dve", ckf[k].v(), bank(b3), r, kvnb.v(), ALU.mult, ALU.mult)
            dma("sp", V(nckv_out[tc, :]), ckf[k].v(), (ckf[k].buf, 0))
            cp("act", ckb[k].v(), ckf[k].v())
            cp("act", krf[k].v(), bank(b5, 64))
            dma("sp", V(nkr_out[tc, :]), krf[k].v(), (krf[k].buf, 0))
            tt("dve", kt1[k].v(), krf[k].v(), cosk[k].v(), ALU.mult)
            x4 = krf[k].ap.rearrange("p (a h j) -> p a h j", a=2, h=2)
            s4 = sink[k].ap.rearrange("p (a h j) -> p a h j", a=2, h=2)
            o4 = kt2[k].ap.rearrange("p (a h j) -> p a h j", a=2, h=2)
            for hh in range(2):
                tt("dve", V(o4[:, :, hh, :], (kt2[k].buf, None)), V(x4[:, :, 1 - hh, :], (krf[k].buf, None)),
                   V(s4[:, :, hh, :], (sink[k].buf, None)), ALU.mult)
            tt("dve", krb[k].v(), kt1[k].v(), kt2[k].v(), ALU.add)

        def m1a_B(t_):
            k = t_ % 2
            tc = np.s_[t_ * 128:(t_ + 1) * 128]
            kc0 = 256 + t_ * 128
            b2 = nb()
            for q in range(4):
                tr(V(bank_bf(b2)[:, q * 128:(q + 1) * 128], (PB, b2)), cqb[k].v(None, np.s_[:, q * 128:(q + 1) * 128]))
            for q in range(4):
                tr(V(bank_bf(b2)[:, 512 + q * 128:512 + (q + 1) * 128], (PB, b2)), ckb[k].v(None, np.s_[:, q * 128:(q + 1) * 128]))
            b6 = nb()
            tr(V(bank_bf(b6)[0:64, 0:128], (PB, b6)), krb[k].v())
            cp("act", cqnT.v(t_, np.s_[:, 0:4, tc]), V(bank_bf(b2)[:, 0:512].rearrange("p (a b) -> p a b", b=128), (PB, b2)))
            cp("dve", ckvT.v(2 + t_, np.s_[:, 0:4, kc0:kc0 + 128]), V(bank_bf(b2)[:, 512:1024].rearrange("p (a b) -> p a b", b=128), (PB, b2)))
            cp("dve", krT.v(2 + t_, np.s_[0:64, kc0:kc0 + 128]), V(bank_bf(b6)[0:64, 0:128], (PB, b6)))

        next(pgen)
        m1a_A(0)
        for t_ in range(NT):
            next(pgen, None)
            if t_ + 1 < NT:
                m1a_A(t_ + 1)
            m1a_B(t_)
        for _ in pgen:
            pass
        for blk in range(NBLK):
            cols = np.s_[blk * BW:(blk + 1) * BW]
            tparts = tuple(range(blk * (BW // 128), (blk + 1) * (BW // 128)))
            for d_ in range(2):
                b = nb()
                for kc in range(16):
                    mm(bank(b, BW, rows=16), Wsm.v(None, np.s_[:, kc, d_ * 16:(d_ + 1) * 16]),
                       V(h2T.ap[:, kc, cols], (h2T.buf, tparts)), start=(kc == 0), stop=(kc == 15))
                cp("act", aT2[d_].v(None, np.s_[0:16, cols]), bank(b, BW, rows=16))
        stage_end()
        A.pop()

        A.push()
        Wqk = TT(A.alloc([16, 1024], BF16), "Wqk", 2)
        wal = TT(A.alloc([2, 1024], BF16), "wal")
        bal = TT(A.alloc([2, 1024], BF16), "bal")
        dma("pool", wal.v(None, np.s_[0:16]), V(w_alpha.rearrange("d r n -> r d n")), (wal.buf, 0))
        dma("pool", bal.v(None, np.s_[0:1]), V(b_alpha.rearrange("(o d) n -> o d n", o=1)), (bal.buf, 0))
        qf = [TT(A.alloc([512], F32), f"qf{i}") for i in range(3)]
        kf = [TT(A.alloc([512], F32), f"kf{i}") for i in range(3)]
        ef = TT(A.alloc([512], F32), "ef")
        spf = [TT(A.alloc([512], F32), f"spf{i}") for i in range(6)]
        ebf = [TT(A.alloc([512], F32), f"ebf{i}") for i in range(2)]
        eif = [TT(A.alloc([512], F32), f"eif{i}") for i in range(2)]
        eef = [TT(A.alloc([512], F32), f"eef{i}") for i in range(2)]
        qdb = [TT(A.alloc([512], BF16), f"qdb{i}") for i in range(4)]
        kib = [TT(A.alloc([512], BF16), f"kib{i}") for i in range(4)]
        keb = [TT(A.alloc([512], BF16), f"keb{i}") for i in range(2)]
        qdT = [TT(A.alloc([4, 128], BF16), f"qdT{i}") for i in range(2)]
        kiT = [TT(A.alloc([4, 128], BF16), f"kiT{i}") for i in range(2)]
        edc = [TT(A.alloc([4], F32), f"edc{i}") for i in range(2)]
        items = [(ch, t_) for ch in range(2) for t_ in range(NT)]

        def ld_wqk(ch):
            c0 = ch * 512
            dma("pool", Wqk.v(0, np.s_[:, :, 0:512]), V(w_inp[:, c0:c0 + 512].rearrange("(kc p) n -> p kc n", p=128)), (Wqk.buf, 0))
            dma("pool", Wqk.v(1, np.s_[:, :, 512:1024]), V(w_inp[:, 1024 + c0:1024 + c0 + 512].rearrange("(kc p) n -> p kc n", p=128)), (Wqk.buf, 1))

        def p1(n):
            ch, t_ = items[n]
            c0 = ch * 512
            if t_ == 0:
                ld_wqk(ch)
            k = n % 3
            tc = np.s_[t_ * 128:(t_ + 1) * 128]
            bq = nb()
            for kc in range(16):
                mm(bank(bq), h2T.v(t_, np.s_[:, kc, tc]), Wqk.v(0, np.s_[:, kc, 0:512]), start=(kc == 0), stop=(kc == 15))
            cp("act", qf[k].v(), bank(bq), scale=1.0 / 16.0)
            bk = nb()
            for kc in range(16):
                mm(bank(bk), h2T.v(t_, np.s_[:, kc, tc]), Wqk.v(1, np.s_[:, kc, 512:1024]), start=(kc == 0), stop=(kc == 15))
            cp("act", kf[k].v(), bank(bk))
            for d_ in range(2):
                sp_ = spf[(n % 3) * 2 + d_]
                bz = nb()
                mm(bank(bz), aT2[d_].v(None, np.s_[0:16, tc]), wal.v(None, np.s_[0:16, d_, c0:c0 + 512]), start=True, stop=False)
                mm(bank(bz), onesb.v(None, np.s_[0:1, :]), bal.v(None, np.s_[0:1, d_, c0:c0 + 512]), start=False, stop=True)
                act(ef.v(), bank(bz), AF.Exp, scale=-1.0)
                act(sp_.v(), ef.v(), AF.Ln, bias=1.0)

        def p2(n):
            ch, t_ = items[n]
            c0 = ch * 512
            k = n % 3
            for d_ in range(2):
                sp_ = spf[(n % 3) * 2 + d_]
                kk = d_
                k4 = (n % 2) * 2 + d_
                part = (d_ * NT + t_) * 2 + ch
                Lm = Lf if d_ == 0 else Lb
                Um = Uf if d_ == 0 else Ub
                bB = nb()
                mm(bank(bB), Lm, sp_.v())
                bE = nb()
                mm(bank(bE), Um, sp_.v())
                bl = nb()
                for c in range(4):
                    mm(bank(bl, 1, off=c), sp_.v(None, np.s_[:, c * 128:(c + 1) * 128]), ones_col)
                act(ebf[kk].v(), bank(bB), AF.Exp, scale=-1.0 / 16.0)
                act(eif[kk].v(), bank(bB), AF.Exp, scale=1.0 / 16.0)
                act(eef[kk].v(), bank(bE), AF.Exp, scale=-1.0 / 16.0)
                act(edc[kk].v(), bank(bl, 4), AF.Exp, scale=-1.0 / 16.0)
                dma("sp", ED.v(part, np.s_[d_, t_, :, ch * 4:(ch + 1) * 4]), edc[kk].v(), (edc[kk].buf, 0))
                tt("dve", qdb[k4].v(), qf[k].v(), ebf[kk].v(), ALU.mult)
                tt("dve", kib[k4].v(), kf[k].v(), eif[kk].v(), ALU.mult)
                tt("pool", keb[kk].v(), kf[k].v(), eef[kk].v(), ALU.mult)
                dma("sp", KE.v(part, np.s_[d_, t_, :, c0:c0 + 512]), keb[kk].v(), (keb[kk].buf, 0))

        def p3(n):
            ch, t_ = items[n]
            for d_ in range(2):
                kk = d_
                k4 = (n % 2) * 2 + d_
                part = (d_ * NT + t_) * 2 + ch
                for (srcb, dstT, DR, eng) in ((qdb[k4], qdT[kk], QD, "act"), (kib[k4], kiT[kk], KI, "dve")):
                    bt = nb()
                    for q in range(4):
                        tr(V(bank_bf(bt)[:, q * 128:(q + 1) * 128], (PB, bt)), srcb.v(None, np.s_[:, q * 128:(q + 1) * 128]))
                    cp(eng, dstT.v(), V(bank_bf(bt)[:, 0:512].rearrange("p (a b) -> p a b", b=128), (PB, bt)))
                    dma("sp", DR.v(part, np.s_[d_, t_, :, ch * 4:(ch + 1) * 4, :]), dstT.v(), (dstT.buf, 0))

        NI = len(items)
        for n in range(NI + 2):
            if n < NI:
                p1(n)
            if 0 <= n - 1 < NI:
                p2(n - 1)
            if 0 <= n - 2 < NI:
                p3(n - 2)
        stage_end()
        A.pop()

        A.push()
        wsl = [TT(A.alloc([16, 512], BF16), f"m1cw{i}") for i in range(2)]
        vb = [TT(A.alloc([512], BF16), f"m1cv{i}") for i in range(4)]
        gw = [TT(A.alloc([16, 128], BF16), f"m1cg{i}") for i in range(2)]
        gtb = [TT(A.alloc([BW], BF16), f"m1cgt{i}") for i in range(4)]
        bg_w = [TT(A.alloc([16, 256], BF16), f"bgw{i}") for i in range(2)]
        bg_b = [TT(A.alloc([256], BF16, parts=1), f"bgb{i}") for i in range(2)]
        bg_g = [TT(A.alloc([D], F32), "bgg")]
        bg_m = [TT(A.alloc([D], F32), "bgm0")]
        bg = s0_units(S0_LATE, 256, bg_w, bg_b, bg_g, bg_m, make_crep())
        kq = 0
        kv_ = 0
        for (cbase, DR, fn) in ((2048, VV, AF.Copy), (4096, RR, AF.Silu)):
            for s4 in range(4):
                w = wsl[kq % 2]
                kq += 1
                wslab(w.v(), w_inp[:, cbase + s4 * 512:cbase + (s4 + 1) * 512])
                for t_ in range(NT):
                    tc = np.s_[t_ * 128:(t_ + 1) * 128]
                    b = nb()
                    for kc in range(16):
                        mm(bank(b), h2T.v(t_, np.s_[:, kc, tc]), w.v(None, np.s_[:, kc, :]), start=(kc == 0), stop=(kc == 15))
                    o = vb[kv_ % 4]
                    kv_ += 1
                    if fn == AF.Copy and (kv_ % 2 == 0):
                        cp("dve", o.v(), bank(b))
                    else:
                        act(o.v(), bank(b), fn)
                    dma("sp", DR.v(t_, np.s_[tc, s4 * 512:(s4 + 1) * 512]), o.v(), (o.buf, 0))
                    if t_ % 4 == 3:
                        next(bg, None)
        kg = 0
        ko = 0
        for (cbase, DR) in ((7264, GAT), (9312, GBT)):
            for dc in range(16):
                w = gw[kg % 2]
                kg += 1
                wslab(w.v(), w_inp[:, cbase + dc * 128:cbase + (dc + 1) * 128])
                for blk in range(NBLK):
                    cols = np.s_[blk * BW:(blk + 1) * BW]
                    tparts = tuple(range(blk * (BW // 128), (blk + 1) * (BW // 128)))
                    b = nb()
                    for kc in range(16):
                        mm(bank(b, BW), w.v(None, np.s_[:, kc, :]), V(h2T.ap[:, kc, cols], (h2T.buf, tparts)), start=(kc == 0), stop=(kc == 15))
                    o = gtb[ko % 4]
                    ko += 1
                    act(o.v(), bank(b, BW), AF.Sigmoid)
                    dma("sp", DR.v(dc * NBLK + blk, np.s_[dc, :, cols]), o.v(), (o.buf, 0))
                next(bg, None)
        for _ in bg:
            pass
        stage_end()
        A.pop()
        A.pop()

        A.push()
        Stl = [TT(A.alloc([8, 512], F32), f"St{i}", 8) for i in range(2)]
        edf = TT(A.alloc([8], F32), "edf")
        Sbf = TT(A.alloc([8, 512], BF16), "Sbf", 8)
        qd = [TT(A.alloc([8, 128], BF16), f"qd{i}") for i in range(3)]
        ki = [TT(A.alloc([8, 128], BF16), f"ki{i}") for i in range(3)]
        ke = [TT(A.alloc([1024], BF16), f"ke{i}") for i in range(3)]
        vt = [TT(A.alloc([D], BF16), f"vt{i}") for i in range(3)]
        ed = [TT(A.alloc([8], F32), f"ed{i}") for i in range(3)]
        ATs = [TT(A.alloc([512], BF16), f"ATs{i}") for i in range(2)]
        Acp = [TT(A.alloc([512], BF16), f"Acp{i}") for i in range(2)]
        maskf = [TT(A.alloc([512], F32), f"maskf{i}") for i in range(2)]
        for d_ in range(2):
            for h in range(4):
                cp("dve", maskf[d_].v(None, np.s_[:, h * 128:(h + 1) * 128]), Lf if d_ == 0 else Lb)
        ot = [TT(A.alloc([D], F32), f"ot{i}", 4) for i in range(2)]
        oft = [TT(A.alloc([D], F32), f"oft{i}") for i in range(3)]
        rt = [TT(A.alloc([D], BF16), f"rt{i}") for i in range(4)]
        ogb = [TT(A.alloc([D], BF16), f"ogb{i}", 4) for i in range(2)]
        ogTt = [TT(A.alloc([16, 128], BF16), f"ogTt{i}") for i in range(2)]
        gnb = TT(A.alloc([512], F32), "gnb")
        dma("sp", gnb.v(), V(gla_norm.partition_broadcast(128)), (gnb.buf, 0))
        tmpf = [TT(A.alloc([512], F32), f"m2tmp{i}") for i in range(4)]
        junk5 = TT(A.alloc([512], BF16), "m2junk")
        for d_ in range(2):
            cur = 0
            cross = False
            dma("sp", Stl[cur].v(None), V(s0[d_].rearrange("h (kc p) v -> p (h kc) v", p=128)), (Stl[cur].buf, 0))
            cp("act", Sbf.v(None), Stl[cur].v(None))
            order = list(range(NT)) if d_ == 0 else list(range(NT - 1, -1, -1))

            def load(i):
                t_ = order[i]
                k = i % 3
                tc = np.s_[t_ * 128:(t_ + 1) * 128]
                pr = ((d_ * NT + t_) * 2, (d_ * NT + t_) * 2 + 1)
                dma("sp", qd[k].v(), V(QD.ap[d_, t_], (QD.buf, pr)), (qd[k].buf, 0))
                dma("sp", ki[k].v(), V(KI.ap[d_, t_], (KI.buf, pr)), (ki[k].buf, 0))
                dma("sp", ke[k].v(), V(KE.ap[d_, t_], (KE.buf, pr)), (ke[k].buf, 0))
                dma("sp", ed[k].v(), V(ED.ap[d_, t_], (ED.buf, pr)), (ed[k].buf, 0))
                dma("sp", vt[k].v(), VV.v(t_, np.s_[tc, :]), (vt[k].buf, 0))
                if d_ == 1:
                    dma("sp", oft[k].v(), OF.v(t_, np.s_[tc, :]), (oft[k].buf, 0))
                    dma("sp", rt[i % 4].v(), RR.v(t_, np.s_[tc, :]), (rt[i % 4].buf, 0))

            load(0)
            if NT > 1:
                load(1)
            def emitA(i):
                k = i % 3
                k2 = i % 2
                ba = nb()
                for h in range(4):
                    for kc in range(2):
                        mm(bank(ba, 128, off=h * 128), ki[k].v(None, np.s_[:, h * 2 + kc, :]), qd[k].v(None, np.s_[:, h * 2 + kc, :]), start=(kc == 0), stop=(kc == 1))
                tt("dve", ATs[k2].v(), bank(ba), maskf[d_].v(), ALU.mult)

            emitA(0)
            kcp = 0
            pend_fin = []
            for i, t_ in enumerate(order):
                k = i % 3
                k2 = i % 2
                tc = np.s_[t_ * 128:(t_ + 1) * 128]
                if i + 2 < NT:
                    load(i + 2)
                if i + 1 < NT:
                    emitA(i + 1)
                for hp in range(2):
                    bss = {}
                    for h in (2 * hp, 2 * hp + 1):
                        hc = np.s_[:, h * 512:(h + 1) * 512]
                        for kc in range(2):
                            c = h * 2 + kc
                            bs = nb()
                            bss[c] = bs
                            mm(bank(bs), ke[k].v(None, np.s_[:, c * 128:(c + 1) * 128]), vt[k].v(None, hc))
                    for h in (2 * hp, 2 * hp + 1):
                        hc = np.s_[:, h * 512:(h + 1) * 512]
                        bo = nb()
                        mm(bank(bo), ATs[k2].v(None, np.s_[:, h * 128:(h + 1) * 128]), vt[k].v(None, hc), start=True, stop=False)
                        for kc in range(2):
                            c = h * 2 + kc
                            mm(bank(bo), qd[k].v(None, np.s_[:, c, :]), Sbf.v(c, np.s_[:, c, :]), start=False, stop=(kc == 1))
                        if d_ == 0:
                            cp("act", ot[k2].v(h, hc), bank(bo))
                        else:
                            tt("dve", ot[k2].v(h, hc), bank(bo), oft[k].v(None, hc), ALU.add)
                    for h in (2 * hp, 2 * hp + 1):
                        for kc in range(2):
                            c = h * 2 + kc
                            Ssrc = Stl[cur]
                            Sdst = Stl[1 - cur] if cross else Stl[cur]
                            edv = edf.v(None, np.s_[:, c:c + 1]) if cross else ed[k].v(None, np.s_[:, c:c + 1])
                            stt("dve", Sdst.v(c, np.s_[:, c, :]), Ssrc.v(c, np.s_[:, c, :]), edv, bank(bss[c]), ALU.mult, ALU.add)
                            cp("act" if (d_ == 1 or kcp % 2 == 0) else "dve", Sbf.v(c, np.s_[:, c, :]), Sdst.v(c, np.s_[:, c, :]))
                            kcp += 1
                if d_ == 0:
                    dma("act", OF.v(t_, np.s_[tc, :]), ot[k2].v(None), (ot[k2].buf, 0))
                else:
                    while pend_fin:
                        pend_fin.pop(0)()
                    rs_ = [rstd_from(ot[k2].v(h, np.s_[:, h * 512:(h + 1) * 512]), 512, junk5.v()) for h in range(4)]

                    def fin(k2=k2, k4=i % 4, t_=t_, tc=tc, rs_=rs_):
                        for h in range(4):
                            hc = np.s_[:, h * 512:(h + 1) * 512]
                            tm = tmpf[h % 4]
                            stt("dve", tm.v(), ot[k2].v(h, hc), rs_[h], gnb.v(), ALU.mult, ALU.mult)
                            tt("pool", ogb[k2].v(h, hc), tm.v(), rt[k4].v(None, hc), ALU.mult)
                        for half in range(2):
                            bt = nb()
                            for q in range(8):
                                kc = half * 8 + q
                                tr(V(bank_bf(bt)[:, q * 128:(q + 1) * 128], (PB, bt)), ogb[k2].v(kc // 4, np.s_[:, kc * 128:(kc + 1) * 128]))
                            cp("act" if half == 0 else "dve", ogTt[k2].v(None, np.s_[:, half * 8:half * 8 + 8, :]),
                               V(bank_bf(bt).rearrange("p (a b) -> p a b", b=128), (PB, bt)))
                        dma("pool", OGT.v(t_, np.s_[:, :, tc]), ogTt[k2].v(), (ogTt[k2].buf, 0))
                    pend_fin.append(fin)
                end_seg = (t_ % 2 == 1) if d_ == 0 else (t_ % 2 == 0)
                if cross:
                    cur = 1 - cur
                    cross = False
                if end_seg:
                    seg = t_ // 2
                    dst = (sf_out if d_ == 0 else sb_out)[seg].rearrange("h (kc p) v -> p (h kc) v", p=128)
                    dma("act", V(dst), Stl[cur].v(None), (Stl[cur].buf, 1))
                    if i != NT - 1:
                        ts("dve", edf.v(), ed[(i + 1) % 3].v(), flag.v(None, np.s_[:, 0:1]), ALU.mult)
                        act(Sbf.v(None), Stl[cur].v(None), AF.Copy, scale=flag.v(None, np.s_[:, 0:1]))
                        cross = True
        while pend_fin:
            pend_fin.pop(0)()
        stage_end()
        A.pop()

        A.push()
        QB = min(512, T)
        NQB = T // QB
        NQ4 = QB // 128
        kblocks = []
        ks = 0
        while ks < NK:
            kblocks.append((ks, min(512, NK - ks)))
            ks += 512
        omT = TT(A.alloc([16, T], BF16), "omT", 16)
        cosT = TT(A.alloc([T], F32), "cosT")
        sinT = TT(A.alloc([T], F32), "sinT")
        dma("sp", cosT.v(None, np.s_[0:64]), V(cosT_in), (cosT.buf, 0))
        dma("sp", sinT.v(None, np.s_[0:64]), V(sinT_in), (sinT.buf, 0))
        QrT = [TT(A.alloc([T], BF16), f"QrT{i}", 2 + NT) for i in range(2)]
        for i in range(2):
            dma("pool", QrT[i].v(1, np.s_[65:81, :]), V(mq_in), (QrT[i].buf, 1))
        S.op("dve", lambda e: e.memset(krT.ap[64:65, :], 1.0), [], [(krT.buf, NKT)])
        QnT = [TT(A.alloc([T], BF16), f"QnT{i}") for i in range(2)]
        KT = [TT(A.alloc([NK], BF16), f"KT{i}") for i in range(2)]
        Vh = [TT(A.alloc([NKT, 129], BF16), f"Vh{i}") for i in range(2)]
        for i in range(2):
            S.op("dve", lambda e, i=i: e.memset(Vh[i].ap[:, :, 128:129], 1.0), [], [(Vh[i].buf, None)])
        sel64 = TT(A.alloc([65], BF16), "sel64")
        S.op("dve", lambda e: e.memset(sel64.ap, 0.0), [], [(sel64.buf, None)])
        S.op("dve", lambda e: e.memset(sel64.ap[:, 64:65], 1.0), [], [(sel64.buf, None)])
        wkv = [TT(A.alloc([4, 256], BF16), f"wkv{i}") for i in range(2)]
        wq = [TT(A.alloc([4, 192], BF16), f"wq{i}") for i in range(2)]
        wqs = [TT(A.alloc([4, 64], BF16), f"wqs{i}") for i in range(2)]
        NPT = 4
        PT = [TT(A.alloc([QB], BF16), f"PT{i}") for i in range(NPT)]
        obt = [TT(A.alloc([128], BF16), f"obt{i}") for i in range(4)]
        dgb = [TT(A.alloc([128], BF16), f"dgb{i}") for i in range(4)]
        rt1 = TT(A.alloc([QB], F32), "rt1")
        rt2 = TT(A.alloc([QB], F32), "rt2")
        n4 = [0]
        n2 = [0]

        def nbs():
            i = n4[0]
            n4[0] = (i + 1) % 3
            return i

        def nbm():
            i = n2[0]
            n2[0] = (i + 1) % 4
            return i

        MB = 3

        def prep(h):
            k = h % 2
            dma("pool", wkv[k].v(), V(w_ukv[:, h * 256:(h + 1) * 256].rearrange("(kc p) n -> p kc n", p=128)), (wkv[k].buf, 0))
            dma("pool", wq[k].v(), V(w_uq[:, h * 192:(h + 1) * 192].rearrange("(kc p) n -> p kc n", p=128)), (wq[k].buf, 0))
            for a in range(2):
                for hh in range(2):
                    o0 = a * 32 + hh * 16
                    i0 = 128 + a * 32 + (1 - hh) * 16
                    cp("pool", wqs[k].v(None, np.s_[:, :, o0:o0 + 16]), wq[k].v(None, np.s_[:, :, i0:i0 + 16]))
            yield
            for (ks, kw) in kblocks:
                b = MB
                kparts = tuple(range(ks // 128, (ks + kw) // 128))
                for kc in range(4):
                    mm(bank(b, kw), wkv[k].v(None, np.s_[:, kc, 0:128]), V(ckvT.ap[:, kc, ks:ks + kw], (ckvT.buf, kparts)), start=(kc == 0), stop=(kc == 3))
                cp("dve", KT[k].v(None, np.s_[:, ks:ks + kw]), bank(b, kw))
                yield
            for g0 in range(0, NKT, 4):
                g = min(4, NKT - g0)
                b = MB
                for a in range(g):
                    kt_ = g0 + a
                    for kc in range(4):
                        mm(bank(b, 128, off=a * 128), ckvT.v(kt_, np.s_[:, kc, kt_ * 128:(kt_ + 1) * 128]), wkv[k].v(None, np.s_[:, kc, 128:256]), start=(kc == 0), stop=(kc == 3))
                cp("dve", Vh[k].v(None, np.s_[:, g0:g0 + g, 0:128]), V(ps[:, b * 512:b * 512 + g * 128].rearrange("p (a b) -> p a b", b=128), (PB, b)))
                yield
            for qb in range(NQB):
                cols = np.s_[qb * QB:(qb + 1) * QB]
                tparts = tuple(range(qb * NQ4, (qb + 1) * NQ4))
                b = MB
                for kc in range(4):
                    mm(bank(b, QB), wq[k].v(None, np.s_[:, kc, 0:128]), V(cqnT.ap[:, kc, cols], (cqnT.buf, tparts)), start=(kc == 0), stop=(kc == 3))
                cp("dve", QnT[k].v(None, np.s_[:, cols]), bank(b, QB))
                yield
                for kc in range(4):
                    mm(bank(b, QB, rows=64), wq[k].v(None, np.s_[:, kc, 128:192]), V(cqnT.ap[:, kc, cols], (cqnT.buf, tparts)), start=(kc == 0), stop=(kc == 3))
                tt("dve", rt1.v(None, np.s_[0:64, :]), bank(b, QB, rows=64), cosT.v(None, np.s_[0:64, cols]), ALU.mult)
                yield
                for kc in range(4):
                    mm(bank(b, QB, rows=64), wqs[k].v(None, np.s_[:, kc, :]), V(cqnT.ap[:, kc, cols], (cqnT.buf, tparts)), start=(kc == 0), stop=(kc == 3))
                tt("dve", rt2.v(None, np.s_[0:64, :]), bank(b, QB, rows=64), sinT.v(None, np.s_[0:64, cols]), ALU.mult)
                tt("dve", QrT[k].v(0, np.s_[0:64, cols]), rt1.v(None, np.s_[0:64, :]), rt2.v(None, np.s_[0:64, :]), ALU.add)
                yield
            for qb in range(NQB):
                cols = np.s_[qb * QB:(qb + 1) * QB]
                b = MB
                for q4 in range(NQ4):
                    qt = qb * NQ4 + q4
                    tc = np.s_[qt * 128:(qt + 1) * 128]
                    kd = np.s_[(2 + qt) * 128:(3 + qt) * 128]
                    mm(bank(b, 128, off=q4 * 128), QnT[k].v(None, np.s_[:, tc]), KT[k].v(None, np.s_[:, kd]), start=True, stop=False)
                    mm(bank(b, 128, off=q4 * 128), QrT[k].v(0, np.s_[0:64, tc]), krT.v(2 + qt, np.s_[0:64, kd]), start=False, stop=True)
                c0 = [scol() for _ in range(NQ4)]
                while c0[-1] != c0[0] + NQ4 - 1:
                    c0 = [scol() for _ in range(NQ4)]
                S.op("dve", lambda e, b=b, c=c0[0]: e.reduce_max(out=stat.ap[:, c:c + NQ4], in_=ps[:, b * 512:b * 512 + NQ4 * 128].rearrange("p (a b) -> p a b", b=128), axis=AX.X),
                     [(PB, b)], [(stat.buf, tuple(c0))])
                for q4 in range(NQ4):
                    ts("dve", dgb[q4].v(), identb.v(), sv(c0[q4]), ALU.mult, -1.0, ALU.mult)
                yield
                for q4 in range(NQ4):
                    mm(bank(b, 128, rows=65, off=q4 * 128), sel64.v(), dgb[q4].v())
                sparts = tuple(2 + qb * NQ4 + i for i in range(NQ4))
                cp("dve", V(QrT[k].ap[64:65, cols], (QrT[k].buf, sparts)), V(ps[64:65, b * 512:b * 512 + QB], (PB, b)))
                yield

        def run_all(gen):
            for _ in gen:
                pass

        kpt = 0
        LAG = 2
        run_all(prep(0))
        for h in range(16):
            k = h % 2
            bg = prep(h + 1) if h + 1 < 16 else iter(())
            tiles = [(qb, kt_) for qb in range(NQB) for kt_ in range(NKT)]
            pbuf = {}
            deferred = []

            def emit_qk(i):
                qb, kt_ = tiles[i]
                cols = np.s_[qb * QB:(qb + 1) * QB]
                qparts = (0, 1) + tuple(2 + qb * NQ4 + q for q in range(NQ4))
                kcs = np.s_[kt_ * 128:(kt_ + 1) * 128]
                b = nbs()
                mm(bank(b, QB), KT[k].v(None, np.s_[:, kcs]), QnT[k].v(None, np.s_[:, cols]), start=True, stop=False)
                mm(bank(b, QB), V(krT.ap[0:81, kcs], (krT.buf, (kt_, NKT))), V(QrT[k].ap[0:81, cols], (QrT[k].buf, qparts)), start=False, stop=True)
                pbuf[i] = b

            def emit_pv(i):
                nonlocal kpt
                qb, kt_ = tiles[i]
                b = pbuf.pop(i)
                p_ = PT[kpt % NPT]
                kpt += 1
                act(p_.v(), bank(b, QB), AF.Exp, scale=ATT_SCALE)
                for q4 in range(NQ4):
                    ob_ = 4 + q4
                    mm(V(ps[:, ob_ * 512:ob_ * 512 + 129], (PB, ob_)),
                       p_.v(None, np.s_[:, q4 * 128:(q4 + 1) * 128]), Vh[k].v(None, np.s_[:, kt_, :]),
                       start=(kt_ == 0), stop=(kt_ == NKT - 1))
                if kt_ == NKT - 1:
                    obs = []
                    for q4 in range(NQ4):
                        ob_ = 4 + q4
                        o0 = ob_ * 512
                        c_ri = scol()
                        recip(sv(c_ri), V(ps[:, o0 + 128:o0 + 129], (PB, ob_)))
                        ts("dve", obt[q4].v(), V(ps[:, o0:o0 + 128], (PB, ob_)), sv(c_ri), ALU.mult)

                    def fin(qb=qb):
                        for q4 in range(NQ4):
                            tr(V(bank_bf(MB)[:, q4 * 128:(q4 + 1) * 128], (PB, MB)), obt[q4].v())
                        c0_ = qb * QB
                        cp("dve", V(omT.ap[:, h, c0_:c0_ + QB], (omT.buf, h)), V(bank_bf(MB)[:, 0:QB], (PB, MB)))
                    deferred.append((i + 5, fin))

            n = len(tiles)
            for i in range(n + LAG):
                if i < n:
                    emit_qk(i)
                j = i - LAG
                if j >= 0:
                    emit_pv(j)
                while deferred and deferred[0][0] <= i:
                    deferred.pop(0)[1]()
                if i % 2 == 1:
                    next(bg, None)
            while deferred:
                deferred.pop(0)[1]()
            run_all(bg)
        for h in range(16):
            dma("sp", OMT.v(h, np.s_[:, h, :]), omT.v(h, np.s_[:, h, :]), (omT.buf, h))
        stage_end()
        A.pop()
        A.pop()

        A.push()
        FB = min(1024, T)
        NFB = T // FB
        HB = min(512, FB)
        NH = FB // HB
        nt4 = HB // 128
        Rw = A.alloc([8192], F32)
        Rbuf = Buf("m4R", 4)
        ogT_ap = Rw.bitcast(BF16)[:, 0:16 * FB].rearrange("p (a b) -> p a b", b=FB)
        y_ap = Rw.rearrange("p (a b) -> p a b", b=D)
        O2w = A.alloc([8192], F32)
        omTb = TT(O2w.bitcast(BF16)[:, 0:16 * FB].rearrange("p (a b) -> p a b", b=FB), "omTb", 4)
        y2_ap = O2w.rearrange("p (a b) -> p a b", b=D)
        mT = TT(A.alloc([16, FB], BF16), "mT", 16)
        wg = [TT(A.alloc([16, 128], BF16), f"wg{i}") for i in range(2)]
        wm = [TT(A.alloc([16, 128], BF16), f"wm{i}") for i in range(2)]
        wo = [TT(A.alloc([16, 512], BF16), f"wo{i}") for i in range(2)]
        gaT = [TT(A.alloc([FB], BF16), f"gaT{i}") for i in range(2)]
        gbT = [TT(A.alloc([FB], BF16), f"gbT{i}") for i in range(2)]
        G2 = TT(A.alloc([D], F32), "G2")
        dma("sp", G2.v(), modS.v(5, np.s_[5]), (G2.buf, 0))
        xt = [TT(A.alloc([D], F32), f"m4x{i}") for i in range(2)]
        junk = TT(A.alloc([D], BF16), "m4junk")
        t1 = [TT(A.alloc([HB], F32), f"m4t1{i}") for i in range(2)]
        t2 = [TT(A.alloc([HB], F32), f"m4t2{i}") for i in range(2)]
        kwo = 0
        for fb in range(NFB):
            cols = np.s_[fb * FB:(fb + 1) * FB]
            ttiles = tuple(range(fb * (FB // 128), (fb + 1) * (FB // 128)))
            gparts_of = lambda dc: tuple(dc * NBLK + bb for bb in range(fb * (FB // BW), (fb + 1) * (FB // BW)))
            dma("sp", V(ogT_ap, (Rbuf, None)), V(OGT.ap[:, :, cols], (OGT.buf, ttiles)), (Rbuf, 0))
            dma("sp", omTb.v(), V(OMT.ap[:, :, cols], (OMT.buf, None)), (omTb.buf, 0))
            for dc in range(16):
                k = dc % 2
                wslab(wg[k].v(), w_go[:, dc * 128:(dc + 1) * 128])
                wslab(wm[k].v(), w_mo[:, dc * 128:(dc + 1) * 128])
                dma("sp", gaT[k].v(), V(GAT.ap[dc, :, cols], (GAT.buf, gparts_of(dc))), (gaT[k].buf, 0))
                dma("sp", gbT[k].v(), V(GBT.ap[dc, :, cols], (GBT.buf, gparts_of(dc))), (gbT[k].buf, 0))
                for hh in range(NH):
                    hcs = np.s_[hh * HB:(hh + 1) * HB]
                    bg = nb()
                    for kc in range(16):
                        mm(bank(bg, HB), wg[k].v(None, np.s_[:, kc, :]), V(ogT_ap[:, kc, hcs], (Rbuf, None)), start=(kc == 0), stop=(kc == 15))
                    bm = nb()
                    for kc in range(16):
                        mm(bank(bm, HB), wm[k].v(None, np.s_[:, kc, :]), omTb.v(None, np.s_[:, kc, hcs]), start=(kc == 0), stop=(kc == 15))
                    tt("dve", t1[hh % 2].v(), bank(bg, HB), gaT[k].v(None, np.s_[:, hcs]), ALU.mult)
                    tt("dve", t2[hh % 2].v(), bank(bm, HB), gbT[k].v(None, np.s_[:, hcs]), ALU.mult)
                    tt("dve", mT.v(dc, np.s_[:, dc, hcs]), t1[hh % 2].v(), t2[hh % 2].v(), ALU.add)
            TBk = FB // 128

            def ytile(t8, sl=np.s_[:]):
                if t8 < 4:
                    return V(y_ap[:, t8, sl], (Rbuf, t8))
                return V(y2_ap[:, t8 - 4, sl], (omTb.buf, t8 - 4))

            for ct in range(4):
                w = wo[kwo % 2]
                kwo += 1
                wslab(w.v(), w_o[:, ct * 512:(ct + 1) * 512])
                pys = [nb() for _ in range(TBk)]
                for t8 in range(TBk):
                    c0 = t8 * 128
                    for kc in range(16):
                        mm(bank(pys[t8]), mT.v(kc, np.s_[:, kc, c0:c0 + 128]), w.v(None, np.s_[:, kc, :]), start=(kc == 0), stop=(kc == 15))
                    cp("act" if t8 % 2 == 0 else "dve", ytile(t8, np.s_[ct * 512:(ct + 1) * 512]), bank(pys[t8]))
            tiles_b = [fb * TBk + t8 for t8 in range(TBk)]
            postnorm_tiles(lambda i: ytile(i), tiles_b, G2, X1, False, X2, False, xt, junk, add_eng="pool")
        stage_end()
        A.pop()

    stage0()
    if upto >= 1:
        ffn_stage(x_in, True, X1, False, 0, w_f1i, w_f1o)
    if upto >= 2:
        mixer_stages()
    if upto >= 3:
        ffn_stage(X2, False, y_out, True, 6, w_f2i, w_f2o)
    S.barrier()
    S.emit(nc, block, esems, dsems)
    print("ops:", S.stats(), "arena max", A.off)
    es.close()
    return nc


WNAMES = ["w_ada", "b_ada", "norm_gains", "w_ffn1_in", "w_ffn1_out", "w_ffn2_in", "w_ffn2_out", "w_in",
          "w_gla_alpha", "b_gla_alpha", "gla_norm", "w_gla_out", "q_norm", "kv_norm", "w_uq", "w_ukv",
          "w_mla_out", "w_out"]


def consts_array():
    s = np.arange(128)[:, None]
    t = np.arange(128)[None, :]
    c = np.zeros((128, 6, 128), np.float32)
    c[:, 0] = (s == t)
    c[:, 1] = (s <= t)
    c[:, 2] = (s >= t)
    c[:, 3] = (s > t)
    c[:, 4] = (s < t)
    c[:, 5] = 1.0
    return c


def rope_tables(T, real):
    cosT = np.ones((64, T), np.float32)
    sinT = np.zeros((64, T), np.float32)
    if real:
        t = np.arange(T)
        pos = [(t // 64).astype(np.float32), (t % 64).astype(np.float32)]
        inv = (np.float32(10000.0) ** (-np.arange(0, 32, 2, dtype=np.float32) / np.float32(32))).astype(np.float32)
        for a in range(2):
            ang = pos[a][None, :] * inv[:, None]
            for hh in range(2):
                r0 = a * 32 + hh * 16
                cosT[r0:r0 + 16] = np.cos(ang)
                sinT[r0:r0 + 16] = np.sin(ang) * (-1.0 if hh == 0 else 1.0)
    return cosT, sinT, np.ascontiguousarray(cosT.T), np.ascontiguousarray(sinT.T)


def mask_rows(NSEG, kind):
    T = NSEG * 256
    NK = T + 256
    mq = np.zeros((16, T), np.float32)
    mk = np.zeros((16, NK), np.float32)
    for m in range(NSEG + 1):
        mk[m, m * 256:(m + 1) * 256] = 1.0
    if kind == "P":
        for s in range(NSEG):
            mq[:NSEG + 1, s * 256:(s + 1) * 256] = NEG_BIG
            mq[1 + s, s * 256:(s + 1) * 256] = 0.0
    return mq, mk


def core_inputs(full, NSEG, kind):
    T = NSEG * 256
    m = {}
    m["x"] = np.ascontiguousarray(full["x"], dtype=np.float32)
    m["cvec"] = np.ascontiguousarray(full["c"].reshape(16, 128).T)
    m["cckv"] = np.ascontiguousarray(full["cache_ckv"])
    m["ckr"] = np.ascontiguousarray(full["cache_krope"])
    m["s0"] = np.ascontiguousarray(np.stack([full["s0f"], full["s0b"]], axis=0))
    m["flag"] = np.full((128, 1), 1.0 if kind == "S" else 0.0, np.float32)
    cosT, sinT, cosK, sinK = rope_tables(T, kind == "S")
    m["cosT"], m["sinT"], m["cosK"], m["sinK"] = cosT, sinT, cosK, sinK
    mq, mk = mask_rows(NSEG, kind)
    m["mq"], m["mk"] = mq, mk
    m["consts"] = consts_array()
    for k in WNAMES:
        m[k] = full[k]
    return m


def make_test_inputs(rng, NSEG, kind):
    T = NSEG * 256
    f32 = np.float32

    def nrm(shape, scale):
        return (rng.standard_normal(shape, dtype=f32) * f32(scale)).astype(f32)

    DFF = 5504
    full = {
        "x": nrm((T, D), 1.0),
        "c": nrm((D,), 1.0),
        "w_ada": nrm((D, 9 * D), 0.5 * D ** -0.5),
        "b_ada": nrm((1, 9 * D), 0.01),
        "norm_gains": 1.0 + nrm((6, D), 0.05),
        "w_ffn1_in": nrm((D, 2 * DFF), D ** -0.5),
        "w_ffn1_out": nrm((DFF, D), DFF ** -0.5),
        "w_ffn2_in": nrm((D, 2 * DFF), D ** -0.5),
        "w_ffn2_out": nrm((DFF, D), DFF ** -0.5),
        "w_in": nrm((D, 11360), D ** -0.5),
        "w_gla_alpha": nrm((2, 16, 1024), 16 ** -0.5),
        "b_gla_alpha": nrm((2, 1024), 0.1),
        "gla_norm": 1.0 + nrm((1, 512), 0.05),
        "w_gla_out": nrm((D, D), D ** -0.5),
        "q_norm": 1.0 + nrm((1, 512), 0.05),
        "kv_norm": 1.0 + nrm((1, 512), 0.05),
        "w_uq": nrm((512, 3072), 512 ** -0.5),
        "w_ukv": nrm((512, 4096), 512 ** -0.5),
        "w_mla_out": nrm((D, D), D ** -0.5),
        "w_out": nrm((D, D), D ** -0.5),
    }
    if kind == "S":
        full["cache_ckv"] = nrm((256, 512), 1.0)
        full["cache_krope"] = nrm((256, 64), 1.0)
        full["s0f"] = nrm((4, 256, 512), 0.5)
        full["s0b"] = nrm((4, 256, 512), 0.5)
    else:
        full["cache_ckv"] = np.zeros((256, 512), f32)
        full["cache_krope"] = np.zeros((256, 64), f32)
        full["s0f"] = np.zeros((4, 256, 512), f32)
        full["s0b"] = np.zeros((4, 256, 512), f32)
    return full


_NC_CACHE = {}


def kernel(**inputs):
    NSEG = 8
    f32 = np.float32
    inp = {k: np.asarray(v) for k, v in inputs.items()}
    W = {
        "w_ada": inp["w_ada"][0], "b_ada": inp["b_ada"], "norm_gains": inp["norm_gains"][0],
        "w_ffn1_in": inp["w_ffn1_in"][0], "w_ffn1_out": inp["w_ffn1_out"][0],
        "w_ffn2_in": inp["w_ffn2_in"][0], "w_ffn2_out": inp["w_ffn2_out"][0],
        "w_in": inp["w_in"][0], "w_gla_alpha": inp["w_gla_alpha"][0], "b_gla_alpha": inp["b_gla_alpha"][0],
        "gla_norm": inp["gla_norm"], "w_gla_out": inp["w_gla_out"][0],
        "q_norm": inp["q_norm"], "kv_norm": inp["kv_norm"], "w_uq": inp["w_uq"][0], "w_ukv": inp["w_ukv"][0],
        "w_mla_out": inp["w_mla_out"][0], "w_out": inp["w_out"][0],
    }
    W = {k: np.ascontiguousarray(v, dtype=f32) for k, v in W.items()}
    in_maps = []
    for core in range(8):
        if core < 4:
            b = core
            full = dict(x=inp["x_sample"][b], c=inp["c"][b], cache_ckv=inp["cache_ckv"][b, 0],
                        cache_krope=inp["cache_krope"][b, 0], s0f=inp["state_gla_fwd"][b, 0],
                        s0b=inp["state_gla_bwd"][b, 0], **W)
            in_maps.append(core_inputs(full, NSEG, "S"))
        else:
            s0_ = 4 * (core - 4)
            xp = inp["x_prompt"][s0_:s0_ + 4].reshape(1024, D)
            x = np.concatenate([xp, np.zeros((1024, D), f32)], axis=0)
            full = dict(x=x, c=inp["c_ctx"], cache_ckv=np.zeros((256, 512), f32),
                        cache_krope=np.zeros((256, 64), f32), s0f=np.zeros((4, 256, 512), f32),
                        s0b=np.zeros((4, 256, 512), f32), **W)
            in_maps.append(core_inputs(full, NSEG, "P"))
    if NSEG not in _NC_CACHE:
        _NC_CACHE[NSEG] = build(NSEG)
    nc = _NC_CACHE[NSEG]
    res = run_bass_kernel_spmd(nc, in_maps, core_ids=list(range(8)))
    R = res.results
    y_prompt = np.zeros((16, 256, D), f32)
    y_sample = np.zeros((4, 2048, D), f32)
    new_ckv = np.zeros((16, 1, 256, 512), f32)
    new_krope = np.zeros((16, 1, 256, 64), f32)
    new_sf = np.zeros((16, 1, 4, 256, 512), f32)
    new_sb = np.zeros((16, 1, 4, 256, 512), f32)
    for core in range(8):
        r = R[core]
        if core < 4:
            y_sample[core] = np.asarray(r["y"])
        else:
            s0_ = 4 * (core - 4)
            y = np.asarray(r["y"]); ck = np.asarray(r["nckv"]); kr = np.asarray(r["nkr"])
            sf = np.asarray(r["sf"]); sb = np.asarray(r["sb"])
            for s in range(4):
                y_prompt[s0_ + s] = y[s * 256:(s + 1) * 256]
                new_ckv[s0_ + s, 0] = ck[s * 256:(s + 1) * 256]
                new_krope[s0_ + s, 0] = kr[s * 256:(s + 1) * 256]
                new_sf[s0_ + s, 0] = sf[s]
                new_sb[s0_ + s, 0] = sb[s]
    return (y_prompt, y_sample, new_ckv, new_krope, new_sf, new_sb)
```

```python
import numpy as np
from contextlib import ExitStack
import concourse.bass as bass
import concourse.mybir as mybir
from concourse.bass_utils import run_bass_kernel_spmd

F32 = mybir.dt.float32
BF16 = mybir.dt.bfloat16
AF = mybir.ActivationFunctionType
ALU = mybir.AluOpType
AX = mybir.AxisListType

ENGINES = ["pe", "act", "dve", "pool", "sp"]


class Buf:
    def __init__(self, name, nparts=1):
        self.name = name
        self.n = nparts
        self.w = [None] * nparts
        self.r = [[] for _ in range(nparts)]
        self.sem = [None] * nparts

    def parts(self, p):
        if p is None:
            return range(self.n)
        if isinstance(p, int):
            return (p,)
        return p


class V:
    def __init__(self, ap, *deps):
        self.ap = ap
        self.deps = list(deps)


class _Op:
    __slots__ = ("fn", "waits", "is_dma", "dsem", "clock", "marked")

    def __init__(self, fn, waits, is_dma, dsem):
        self.fn = fn
        self.waits = waits
        self.is_dma = is_dma
        self.dsem = dsem
        self.clock = None
        self.marked = False


class Sched:
    def __init__(self, n_dma_sems=80):
        self.ops = {e: [] for e in ENGINES}
        self.known = {e: {} for e in ENGINES}
        self.n_dma = n_dma_sems
        self.dma_count = [0] * n_dma_sems
        self.n_sw = 24
        self.dma_next_sw = 0
        self.dma_next = self.n_sw
        self.dma_clock = {}
        self.snap = {e: None for e in ENGINES}
        self.sem_bufs = []

    def _sem_for(self, buf, part, queue):
        if buf.sem[part] is None:
            buf.sem[part] = {}
            self.sem_bufs.append(buf)
        d = buf.sem[part]
        kind = "sw" if queue == "pool" else "hw"
        if kind not in d:
            if kind == "sw":
                if self.dma_next_sw >= self.n_sw:
                    raise RuntimeError("out of sw dma semaphores")
                d[kind] = self.dma_next_sw
                self.dma_next_sw += 1
            else:
                if self.dma_next >= self.n_dma:
                    raise RuntimeError("out of hw dma semaphores")
                d[kind] = self.dma_next
                self.dma_next += 1
        return d[kind]

    def _collect(self, engine, reads, writes):
        deps = {}

        def add(ev):
            if ev is None:
                return
            k, v = ev
            if k == "pe" and engine == "pe":
                return
            if deps.get(k, -1) < v:
                deps[k] = v

        for b, p in reads:
            for i in b.parts(p):
                add(b.w[i])
        for b, p in writes:
            for i in b.parts(p):
                add(b.w[i])
                for ev in b.r[i]:
                    add(ev)
        return deps

    def _update(self, ev, reads, writes):
        for b, p in reads:
            for i in b.parts(p):
                lst = b.r[i]
                for j, (k, v) in enumerate(lst):
                    if k == ev[0]:
                        lst[j] = ev
                        break
                else:
                    lst.append(ev)
        for b, p in writes:
            for i in b.parts(p):
                b.w[i] = ev
                b.r[i] = []

    def _merge_clock(self, engine, clock):
        kn = self.known[engine]
        for k, v in clock.items():
            if kn.get(k, -1) < v:
                kn[k] = v
        self.snap[engine] = None

    def _snapshot(self, engine):
        if self.snap[engine] is None:
            self.snap[engine] = dict(self.known[engine])
        return self.snap[engine]

    def _make_waits(self, engine, deps):
        waits = []
        kn = self.known[engine]
        for k, v in deps.items():
            if isinstance(k, tuple):
                if kn.get(k, -1) >= v:
                    continue
                v = max(v, self.dma_count[k[1]])
                waits.append((k, v))
                kn[k] = v
                self.snap[engine] = None
                clk = self.dma_clock.get((k[1], v))
                if clk:
                    self._merge_clock(engine, clk)
            else:
                if kn.get(k, -1) >= v:
                    continue
                waits.append((k, v))
                kn[k] = v
                self.snap[engine] = None
                op = self.ops[k][v]
                op.marked = True
                if op.clock:
                    self._merge_clock(engine, op.clock)
        return waits

    def op(self, engine, fn, reads=(), writes=()):
        deps = self._collect(engine, reads, writes)
        waits = self._make_waits(engine, deps)
        o = _Op(fn, waits, False, None)
        o.clock = self._snapshot(engine)
        idx = len(self.ops[engine])
        self.ops[engine].append(o)
        self._update((engine, idx), reads, writes)
        return idx

    def dma(self, queue, fn, reads, writes, semof):
        deps = self._collect(queue, reads, writes)
        waits = self._make_waits(queue, deps)
        si = self._sem_for(semof[0], semof[1], queue)
        self.dma_count[si] += 16
        val = self.dma_count[si]
        o = _Op(fn, waits, True, si)
        self.ops[queue].append(o)
        self.dma_clock[(si, val)] = self._snapshot(queue)
        self._update((("d", si), val), reads, writes)

    def barrier(self):
        last = {}
        for e in ("pe", "act", "dve", "pool"):
            for i in range(len(self.ops[e]) - 1, -1, -1):
                if not self.ops[e][i].is_dma and self.ops[e][i].fn is not None:
                    last[e] = i
                    break
        for e in ENGINES:
            deps = {}
            for f, i in last.items():
                if f != e:
                    deps[f] = i
                elif e in ("act", "dve", "pool"):
                    deps[f] = i
            for si in range(self.n_dma):
                if self.dma_count[si] > 0:
                    deps[("d", si)] = self.dma_count[si]
            waits = self._make_waits(e, deps)
            if waits:
                o = _Op(None, waits, False, None)
                o.clock = self._snapshot(e)
                self.ops[e].append(o)

    def reset_sems(self):
        self.dma_next = self.n_sw
        self.dma_next_sw = 0
        for b in self.sem_bufs:
            b.sem = [None] * b.n
        self.sem_bufs = []

    def emit(self, nc, block, esems, dsems):
        vals = {}
        for e in ENGINES:
            c = 0
            m = {}
            for i, o in enumerate(self.ops[e]):
                if o.marked:
                    c += 1
                    m[i] = c
            vals[e] = m

        def run(e, eng):
            for i, o in enumerate(self.ops[e]):
                for k, v in o.waits:
                    if isinstance(k, tuple):
                        eng.wait_ge(dsems[k[1]], v)
                    else:
                        eng.wait_ge(esems[k], vals[k][v])
                if o.fn is None:
                    continue
                ins = o.fn(eng)
                if o.is_dma:
                    ins.then_inc(dsems[o.dsem], 16)
                elif o.marked:
                    ins.then_inc(esems[e], 1)

        @block.tensor
        def _(eng):
            run("pe", eng)

        @block.scalar
        def _(eng):
            run("act", eng)

        @block.vector
        def _(eng):
            run("dve", eng)

        @block.gpsimd
        def _(eng):
            run("pool", eng)

        @block.sync
        def _(eng):
            run("sp", eng)

    def stats(self):
        return {e: len(self.ops[e]) for e in ENGINES}


class Arena:
    def __init__(self, t, nwords):
        self.t = t
        self.n = nwords
        self.off = 0
        self.marks = []

    def alloc(self, free_shape, dtype, parts=128):
        nel = int(np.prod(free_shape))
        nbytes = nel * (2 if dtype == BF16 else 4)
        nw = (nbytes + 3) // 4
        nw = (nw + 15) // 16 * 16
        if self.off + nw > self.n:
            raise RuntimeError(f"arena overflow: need {nw} words at {self.off} of {self.n}")
        ap = self.t[0:parts, self.off:self.off + nw]
        self.off += nw
        if dtype == BF16:
            ap = ap.bitcast(BF16)[:, 0:nel]
        else:
            ap = ap[:, 0:nel]
        if len(free_shape) == 2:
            ap = ap.rearrange("p (a b) -> p a b", b=free_shape[1])
        elif len(free_shape) == 3:
            ap = ap.rearrange("p (a b c) -> p a b c", b=free_shape[1], c=free_shape[2])
        return ap

    def push(self):
        self.marks.append(self.off)

    def pop(self):
        self.off = self.marks.pop()


D = 2048
DFF = 5504
NFC = 43
ATT_SCALE = 192 ** -0.5
EPS = 1e-6
NEG_BIG = -30000.0
ARENA_WORDS = 53184


class TT:
    def __init__(self, ap, name, nparts=1):
        self.ap = ap
        self.buf = Buf(name, nparts)

    def v(self, part=None, idx=None):
        ap = self.ap if idx is None else self.ap[idx]
        return V(ap, (self.buf, part))


def build(NSEG, upto=99, debug=False):
    T = NSEG * 256
    NT = T // 128
    NK = T + 256
    NKT = NK // 128
    nc = bass.Bass("TRN2", target_bir_lowering=False)

    def din(name, shape):
        return nc.dram_tensor(name, list(shape), F32, kind="ExternalInput").ap()

    def dout(name, shape):
        return nc.dram_tensor(name, list(shape), F32, kind="ExternalOutput").ap()

    def dscr(name, shape, dt):
        if debug:
            return nc.dram_tensor(name, list(shape), dt, kind="ExternalOutput").ap()
        return nc.dram_tensor(name, list(shape), dt).ap()

    x_in = din("x", [T, D])
    cvec = din("cvec", [128, 16])
    cckv = din("cckv", [256, 512])
    ckr = din("ckr", [256, 64])
    s0 = din("s0", [2, 4, 256, 512])
    flag_in = din("flag", [128, 1])
    cosT_in = din("cosT", [64, T])
    sinT_in = din("sinT", [64, T])
    cosK_in = din("cosK", [T, 64])
    sinK_in = din("sinK", [T, 64])
    mq_in = din("mq", [16, T])
    mk_in = din("mk", [16, NK])
    consts_in = din("consts", [128, 6, 128])
    w_ada = din("w_ada", [D, 9 * D])
    b_ada = din("b_ada", [1, 9 * D])
    gains = din("norm_gains", [6, D])
    w_f1i = din("w_ffn1_in", [D, 2 * DFF])
    w_f1o = din("w_ffn1_out", [DFF, D])
    w_f2i = din("w_ffn2_in", [D, 2 * DFF])
    w_f2o = din("w_ffn2_out", [DFF, D])
    w_inp = din("w_in", [D, 11360])
    w_alpha = din("w_gla_alpha", [2, 16, 1024])
    b_alpha = din("b_gla_alpha", [2, 1024])
    gla_norm = din("gla_norm", [1, 512])
    w_go = din("w_gla_out", [D, D])
    q_norm = din("q_norm", [1, 512])
    kv_norm = din("kv_norm", [1, 512])
    w_uq = din("w_uq", [512, 3072])
    w_ukv = din("w_ukv", [512, 4096])
    w_mo = din("w_mla_out", [D, D])
    w_o = din("w_out", [D, D])

    y_out = dout("y", [T, D])
    nckv_out = dout("nckv", [T, 512])
    nkr_out = dout("nkr", [T, 64])
    sf_out = dout("sf", [NSEG, 4, 256, 512])
    sb_out = dout("sb", [NSEG, 4, 256, 512])

    modS = TT(dscr("modS", [9, 128, D], F32), "modS", 9)
    X1 = TT(dscr("X1", [T, D], F32), "X1", NT)
    X2 = TT(dscr("X2", [T, D], F32), "X2", NT)
    QD = TT(dscr("QD", [2, NT, 128, 8, 128], BF16), "QD", 2 * NT)
    KI = TT(dscr("KI", [2, NT, 128, 8, 128], BF16), "KI", 2 * NT)
    KE = TT(dscr("KE", [2, NT, 128, 1024], BF16), "KE", 2 * NT)
    ED = TT(dscr("ED", [2, NT, 128, 8], F32), "ED", 2 * NT)
    VV = TT(dscr("VV", [T, D], BF16), "VV", NT)
    RR = TT(dscr("RR", [T, D], BF16), "RR", NT)
    GAT = TT(dscr("GAT", [16, 128, T], BF16), "GAT", 1)
    GBT = TT(dscr("GBT", [16, 128, T], BF16), "GBT", 1)
    OF = TT(dscr("OF", [T, D], F32), "OF", NT)
    OGT = TT(dscr("OGT", [128, 16, T], BF16), "OGT", 1)
    OMT = TT(dscr("OMT", [128, 16, T], BF16), "OMT", 1)

    S = Sched(90)
    es = ExitStack()
    arena_t = es.enter_context(nc.sbuf_tensor("arena", [128, ARENA_WORDS], F32))
    ps = es.enter_context(nc.psum_tensor("ps", [128, 4096], F32))
    esems = {e: es.enter_context(nc.semaphore(f"s_{e}")) for e in ["pe", "act", "dve", "pool"]}
    dsems = [es.enter_context(nc.semaphore(f"d{i}")) for i in range(90)]
    block = es.enter_context(nc.Block())
    A = Arena(arena_t, ARENA_WORDS)
    PB = Buf("psum", 8)
    bank_rr = [0]

    def nb():
        i = bank_rr[0]
        bank_rr[0] = (i + 1) % 8
        return i

    def bank(i, w=512, rows=128, off=0):
        return V(ps[0:rows, i * 512 + off:i * 512 + off + w], (PB, i))

    def bank_bf(i):
        return ps[:, i * 512:(i + 1) * 512].bitcast(BF16)

    def mm(o, l, r, start=True, stop=True):
        S.op("pe", lambda e: e.matmul(o.ap, lhsT=l.ap, rhs=r.ap, start=start, stop=stop), l.deps + r.deps, o.deps)

    def act(o, i, func, bias=None, scale=None, accum=None):
        kw = {}
        reads = list(i.deps)
        writes = list(o.deps)
        if bias is not None:
            if isinstance(bias, V):
                kw["bias"] = bias.ap
                reads += bias.deps
            else:
                kw["bias"] = bias
        if scale is not None:
            if isinstance(scale, V):
                kw["scale"] = scale.ap
                reads += scale.deps
            else:
                kw["scale"] = scale
        if accum is not None:
            kw["accum_out"] = accum.ap
            writes += accum.deps
        S.op("act", lambda e: e.activation(out=o.ap, in_=i.ap, func=func, **kw), reads, writes)

    def tt(eng, o, a, b, op):
        S.op(eng, lambda e: e.tensor_tensor(out=o.ap, in0=a.ap, in1=b.ap, op=op), a.deps + b.deps, o.deps)

    def stt(eng, o, a, sc, b, op0, op1):
        reads = a.deps + b.deps
        if isinstance(sc, V):
            reads = reads + sc.deps
            scv = sc.ap
        else:
            scv = sc
        S.op(eng, lambda e: e.scalar_tensor_tensor(out=o.ap, in0=a.ap, scalar=scv, in1=b.ap, op0=op0, op1=op1), reads, o.deps)

    def ts(eng, o, a, s1, op0, s2=None, op1=None):
        reads = list(a.deps)
        if isinstance(s1, V):
            reads += s1.deps
            s1 = s1.ap
        if isinstance(s2, V):
            reads += s2.deps
            s2 = s2.ap
        if op1 is None:
            S.op(eng, lambda e: e.tensor_scalar(out=o.ap, in0=a.ap, scalar1=s1, scalar2=None, op0=op0), reads, o.deps)
        else:
            S.op(eng, lambda e: e.tensor_scalar(out=o.ap, in0=a.ap, scalar1=s1, scalar2=s2, op0=op0, op1=op1), reads, o.deps)

    def cp(eng, o, i, scale=None):
        if eng == "act":
            act(o, i, AF.Copy, scale=scale)
        else:
            S.op(eng, lambda e: e.tensor_copy(out=o.ap, in_=i.ap), i.deps, o.deps)

    def dma(q, o, i, semof):
        S.dma(q, lambda e: e.dma_start(out=o.ap, in_=i.ap), i.deps, o.deps, semof)

    def rmax(o, i):
        S.op("dve", lambda e: e.reduce_max(out=o.ap, in_=i.ap, axis=AX.X), i.deps, o.deps)

    def rsum(o, i):
        S.op("dve", lambda e: e.reduce_sum(out=o.ap, in_=i.ap, axis=AX.X), i.deps, o.deps)

    def recip(o, i):
        S.op("dve", lambda e: e.reciprocal(out=o.ap, in_=i.ap), i.deps, o.deps)

    def stage_end():
        S.barrier()
        S.reset_sems()

    cst = TT(A.alloc([6, 128], F32), "cst")
    dma("sp", cst.v(), V(consts_in), (cst.buf, 0))
    identb = TT(A.alloc([128], BF16), "identb")
    onesb = TT(A.alloc([128], BF16), "onesb")
    cp("dve", identb.v(), cst.v(None, np.s_[:, 0, :]))
    cp("dve", onesb.v(), cst.v(None, np.s_[:, 5, :]))
    Lf = cst.v(None, np.s_[:, 1, :])
    Lb = cst.v(None, np.s_[:, 2, :])
    Uf = cst.v(None, np.s_[:, 3, :])
    Ub = cst.v(None, np.s_[:, 4, :])
    ones_col = cst.v(None, np.s_[:, 5, 0:1])
    flag = TT(A.alloc([1], F32), "flag")
    dma("sp", flag.v(), V(flag_in), (flag.buf, 0))
    stat = TT(A.alloc([64], F32), "stat", 64)
    stat_rr = [0]

    def scol():
        i = stat_rr[0]
        stat_rr[0] = (i + 1) % 64
        return i

    def sv(i, w=1):
        return V(stat.ap[:, i:i + w], (stat.buf, tuple(range(i, i + w))))

    def tr(o, i):
        S.op("pe", lambda e: e.transpose(o.ap, i.ap, identb.ap), i.deps + [(identb.buf, None)], o.deps)

    def rstd_from(src, n, junk):
        c0, c1, c2 = scol(), scol(), scol()
        act(junk, src, AF.Square, accum=sv(c0))
        act(sv(c1), sv(c0), AF.Ln, scale=1.0 / n, bias=EPS)
        act(sv(c2), sv(c1), AF.Exp, scale=-0.5)
        return sv(c2)

    def wslab(dst, src_cols):
        dma("pool", dst, V(src_cols.rearrange("(kc p) n -> p kc n", p=128)), (dst.deps[0][0], 0))

    cs_p = TT(A.alloc([16], F32), "cs_p")

    def make_crep():
        crep = TT(A.alloc([16, 128], BF16), "crep")
        for kc in range(16):
            ts("dve", crep.v(None, np.s_[:, kc, :]), onesb.v(), cs_p.v(None, np.s_[:, kc:kc + 1]), ALU.mult)
        return crep

    def s0_units(ms, ncol, wsl, brow, gt_, mt, crep):
        k = 0
        nct = D // ncol
        for mi, m in enumerate(ms):
            sub = m // 3
            kind = m % 3
            g = gt_[mi % len(gt_)]
            if kind == 1:
                dma("sp", g.v(), V(gains[2 * sub:2 * sub + 1, :].partition_broadcast(128)), (g.buf, 0))
            if kind == 2:
                dma("sp", g.v(), V(gains[2 * sub + 1:2 * sub + 2, :].partition_broadcast(128)), (g.buf, 0))
            mtile = mt[mi % len(mt)]
            for ct in range(nct):
                c0 = m * D + ct * ncol
                w = wsl[k % len(wsl)]
                br = brow[k % len(brow)]
                k += 1
                wslab(w.v(None, np.s_[:, :, 0:ncol]), w_ada[:, c0:c0 + ncol])
                dma("pool", br.v(None, np.s_[:, 0:ncol]), V(b_ada[0:1, c0:c0 + ncol]), (br.buf, 0))
                bi = nb()
                for kc in range(16):
                    mm(bank(bi, ncol), crep.v(None, np.s_[:, kc, :]), w.v(None, np.s_[:, kc, 0:ncol]), start=(kc == 0), stop=False)
                mm(bank(bi, ncol), onesb.v(None, np.s_[0:1, :]), br.v(None, np.s_[:, 0:ncol]), start=False, stop=True)
                cs_ = np.s_[:, ct * ncol:(ct + 1) * ncol]
                o = mtile.v(None, cs_)
                if kind == 0:
                    cp("act", o, bank(bi, ncol))
                elif kind == 1:
                    stt("dve", o, bank(bi, ncol), 1.0, g.v(None, cs_), ALU.add, ALU.mult)
                else:
                    coef = 1.0 if sub == 1 else 0.5
                    stt("dve", o, bank(bi, ncol), coef, g.v(None, cs_), ALU.mult, ALU.mult)
                yield
            dma("sp", modS.v(m, np.s_[m]), mtile.v(), (mtile.buf, 0))
            yield

    S0_FIRST = [0, 1, 2, 3, 4]
    S0_LATE = [5, 6, 7, 8]

    def stage0():
        A.push()
        cv = TT(A.alloc([16], F32), "cv")
        wsl = [TT(A.alloc([16, 512], BF16), f"s0w{i}") for i in range(3)]
        brow = [TT(A.alloc([512], BF16, parts=1), f"s0b{i}") for i in range(3)]
        gt_ = [TT(A.alloc([D], F32), f"s0g{i}") for i in range(2)]
        mt = [TT(A.alloc([D], F32), f"s0m{i}") for i in range(2)]
        dma("sp", cv.v(), V(cvec), (cv.buf, 0))
        act(cs_p.v(), cv.v(), AF.Silu)
        crep = make_crep()
        for _ in s0_units(S0_FIRST if upto >= 2 else list(range(9)), 512, wsl, brow, gt_, mt, crep):
            pass
        stage_end()
        A.pop()

    def prenorm_tiles(*a, **kw):
        for _ in prenorm_gen(*a, **kw):
            pass

    def prenorm_gen(src, src_is_input, tiles, At, Bt, xt, hb, hT, hT_part_of, col_of, extra_reads=(), first_writes=()):
        n = len(tiles)

        def load(i):
            t_ = tiles[i]
            xv = xt[i % 2].v()
            if src_is_input:
                dma("sp", xv, V(src[t_ * 128:(t_ + 1) * 128, :]), (xt[i % 2].buf, 0))
            else:
                dma("sp", xv, src.v(t_, np.s_[t_ * 128:(t_ + 1) * 128, :]), (xt[i % 2].buf, 0))

        def s1(i):
            xv = xt[i % 2].v()
            hv = hb[i % 2].v()
            r = rstd_from(xv, D, hv)
            stt("dve", xv, xv, r, At.v(), ALU.mult, ALU.mult)
            tt("dve" if i % 2 == 0 else "pool", hv, xv, Bt.v(), ALU.add)

        def s2(i):
            t_ = tiles[i]
            c0 = col_of(t_)
            for half in range(2):
                bi = nb()
                pv = bank_bf(bi)
                for q in range(8):
                    kc = half * 8 + q
                    tr(V(pv[:, q * 128:(q + 1) * 128], (PB, bi)), hb[i % 2].v(None, np.s_[:, kc * 128:(kc + 1) * 128]))
                dst = V(hT.ap[:, half * 8:half * 8 + 8, c0:c0 + 128], (hT.buf, hT_part_of(t_)), *(first_writes if i == 0 else ()))
                srcv = V(pv.rearrange("p (a b) -> p a b", b=128), (PB, bi), *extra_reads)
                cp("act" if half == 0 else "dve", dst, srcv)

        load(0)
        if n > 1:
            load(1)
        s1(0)
        for i in range(n):
            if i + 1 < n:
                s1(i + 1)
            s2(i)
            if i + 2 < n:
                load(i + 2)
            yield

    def postnorm_tiles(ytile_of, tiles, Gt, res_src, res_is_input, dst, dst_is_output, xt, junk, add_eng="dve"):
        n = len(tiles)

        def load(i):
            t_ = tiles[i]
            xv = xt[i % 2].v()
            if res_is_input:
                dma("sp", xv, V(res_src[t_ * 128:(t_ + 1) * 128, :]), (xt[i % 2].buf, 0))
            else:
                dma("sp", xv, res_src.v(t_, np.s_[t_ * 128:(t_ + 1) * 128, :]), (xt[i % 2].buf, 0))

        load(0)
        if n > 1:
            load(1)
        for i, t_ in enumerate(tiles):
            yv = ytile_of(i)
            xv = xt[i % 2].v()
            r = rstd_from(yv, D, junk.v())
            stt("dve", yv, yv, r, Gt.v(), ALU.mult, ALU.mult)
            tt(add_eng if i % 2 == 1 else "dve", xv, yv, xv, ALU.add)
            if dst_is_output:
                dma("sp", V(dst[t_ * 128:(t_ + 1) * 128, :]), xv, (xt[i % 2].buf, 0))
            else:
                dma("sp", dst.v(t_, np.s_[t_ * 128:(t_ + 1) * 128, :]), xv, (xt[i % 2].buf, 0))
            if i + 2 < n:
                load(i + 2)

    def ffn_stage(src, src_is_input, dst, dst_is_output, mbase, w1, w2):
        A.push()
        FB = min(1024, T)
        NFB = T // FB
        TB = FB // 128
        HB = min(512, FB)
        NH = FB // HB
        At = TT(A.alloc([D], F32), "ffA")
        Bt = TT(A.alloc([D], F32), "ffB")
        Gt = TT(A.alloc([D], F32), "ffG")
        dma("sp", Bt.v(), modS.v(mbase, np.s_[mbase]), (Bt.buf, 0))
        dma("sp", At.v(), modS.v(mbase + 1, np.s_[mbase + 1]), (At.buf, 0))
        dma("sp", Gt.v(), modS.v(mbase + 2, np.s_[mbase + 2]), (Gt.buf, 0))
        Rw = A.alloc([8192], F32)
        Rbuf = Buf("ffR", 4)
        hT = TT(Rw.bitcast(BF16)[:, 0:16 * FB].rearrange("p (a b) -> p a b", b=FB), "ffhT", TB)
        yv_ap = Rw.rearrange("p (a b) -> p a b", b=D)
        aT = TT(A.alloc([NFC, FB], BF16), "ffaT", NFC)
        w1s = [TT(A.alloc([16, 2, 128], BF16), f"ffw1_{i}") for i in range(2)]
        w2s = [TT(A.alloc([4, 512], BF16), f"ffw2_{i}") for i in range(3)]
        xt = [TT(A.alloc([D], F32), f"ffx{i}") for i in range(2)]
        hb = [TT(A.alloc([D], BF16), f"ffh{i}") for i in range(2)]
        sg = [TT(A.alloc([512], F32), f"ffsg{i}") for i in range(2)]
        junk = TT(A.alloc([D], BF16), "ffjunk")
        kw2 = 0
        for fb in range(NFB):
            tiles = [fb * TB + i for i in range(TB)]

            def ld_w1(j):
                w = w1s[j % 2]
                dma("pool", w.v(None, np.s_[:, :, 0, :]), V(w1[:, j * 128:(j + 1) * 128].rearrange("(kc p) n -> p kc n", p=128)), (w.buf, 0))
                dma("pool", w.v(None, np.s_[:, :, 1, :]), V(w1[:, DFF + j * 128:DFF + (j + 1) * 128].rearrange("(kc p) n -> p kc n", p=128)), (w.buf, 0))

            ld_w1(0)
            ld_w1(1)
            prenorm_tiles(src, src_is_input, tiles, At, Bt, xt, hb, hT, lambda t_: t_ - fb * TB, lambda t_: (t_ - fb * TB) * 128, extra_reads=((Rbuf, None),), first_writes=((Rbuf, None),))
            for j in range(NFC):
                w = w1s[j % 2]
                pg = [nb() for _ in range(NH)]
                pu = [nb() for _ in range(NH)]
                for gi, pbs in ((0, pg), (1, pu)):
                    for kc in range(16):
                        for h_ in range(NH):
                            mm(bank(pbs[h_], HB), w.v(None, np.s_[:, kc, gi, :]), hT.v(None, np.s_[:, kc, h_ * HB:(h_ + 1) * HB]), start=(kc == 0), stop=(kc == 15))
                if j + 2 < NFC:
                    ld_w1(j + 2)
                for h_ in range(NH):
                    sgv = sg[h_ % 2].v(None, np.s_[:, 0:HB])
                    act(sgv, bank(pg[h_], HB), AF.Silu)
                    tt("dve", aT.v(j, np.s_[:, j, h_ * HB:(h_ + 1) * HB]), sgv, bank(pu[h_], HB), ALU.mult)
            for th in range(NH):
                nt4 = HB // 128
                for dt in range(4):
                    pys = [nb() for _ in range(nt4)]
                    j = 0
                    while j < NFC:
                        g = min(4, NFC - j)
                        w = w2s[kw2 % 3]
                        kw2 += 1
                        dma("pool", w.v(None, np.s_[:, 0:g, :]), V(w2[j * 128:(j + g) * 128, dt * 512:(dt + 1) * 512].rearrange("(a p) n -> p a n", p=128)), (w.buf, 0))
                        for a in range(g):
                            for t4 in range(nt4):
                                c0 = (th * nt4 + t4) * 128
                                mm(bank(pys[t4]), aT.v(j + a, np.s_[:, j + a, c0:c0 + 128]), w.v(None, np.s_[:, a, :]), start=(j + a == 0), stop=(j + a == NFC - 1))
                        j += g
                    for t4 in range(nt4):
                        cp("act", V(yv_ap[:, t4, dt * 512:(dt + 1) * 512], (Rbuf, t4)), bank(pys[t4]))
                tiles_h = [fb * TB + th * nt4 + t4 for t4 in range(nt4)]
                postnorm_tiles(lambda i: V(yv_ap[:, i, :], (Rbuf, i)), tiles_h, Gt, src, src_is_input, dst, dst_is_output, xt, junk,
                               add_eng=("pool" if th == NH - 1 else "dve"))
        stage_end()
        A.pop()

    def mixer_stages():
        BW = min(512, T)
        NBLK = T // BW
        A.push()
        cqnT = TT(A.alloc([4, T], BF16), "cqnT", NT)
        ckvT = TT(A.alloc([4, NK], BF16), "ckvT", NKT)
        krT = TT(A.alloc([NK], BF16), "krT", NKT + 1)
        GAT.buf = Buf("GAT", 16 * NBLK)
        GBT.buf = Buf("GBT", 16 * NBLK)
        OGT.buf = Buf("OGT", NT)
        OMT.buf = Buf("OMT", 16)
        QD.buf = Buf("QD", 4 * NT)
        KI.buf = Buf("KI", 4 * NT)
        KE.buf = Buf("KE", 4 * NT)
        ED.buf = Buf("ED", 4 * NT)

        A.push()
        h2T = TT(A.alloc([16, T], BF16), "h2T", NT)
        aT2 = [TT(A.alloc([T], BF16), "aTf"), TT(A.alloc([T], BF16), "aTb")]

        A.push()
        At = TT(A.alloc([D], F32), "m1A")
        Bt = TT(A.alloc([D], F32), "m1B")
        dma("sp", Bt.v(), modS.v(3, np.s_[3]), (Bt.buf, 0))
        dma("sp", At.v(), modS.v(4, np.s_[4]), (At.buf, 0))
        xt = [TT(A.alloc([D], F32), f"m1x{i}") for i in range(2)]
        hb = [TT(A.alloc([D], BF16), f"m1h{i}") for i in range(2)]
        Wsm = TT(A.alloc([16, 1120], BF16), "Wsm")
        wslab(Wsm.v(), w_inp[:, 6144:7264])
        qnb = TT(A.alloc([512], F32), "qnb")
        kvnb = TT(A.alloc([512], F32), "kvnb")
        dma("sp", qnb.v(), V(q_norm.partition_broadcast(128)), (qnb.buf, 0))
        dma("sp", kvnb.v(), V(kv_norm.partition_broadcast(128)), (kvnb.buf, 0))
        junk5 = TT(A.alloc([512], BF16), "junk5")
        cqb = [TT(A.alloc([512], BF16), f"cqb{i}") for i in range(2)]
        ckf = [TT(A.alloc([512], F32), f"ckf{i}") for i in range(2)]
        ckb = [TT(A.alloc([512], BF16), f"ckb{i}") for i in range(2)]
        krf = [TT(A.alloc([64], F32), f"krf{i}") for i in range(2)]
        krb = [TT(A.alloc([64], BF16), f"krb{i}") for i in range(2)]
        kt1 = [TT(A.alloc([64], F32), f"kt1{i}") for i in range(2)]
        kt2 = [TT(A.alloc([64], F32), f"kt2{i}") for i in range(2)]
        cosk = [TT(A.alloc([64], F32), f"cosk{i}") for i in range(2)]
        sink = [TT(A.alloc([64], F32), f"sink{i}") for i in range(2)]
        dma("pool", krT.v(NKT, np.s_[65:81, :]), V(mk_in), (krT.buf, NKT))
        for c2 in range(2):
            k = c2 % 2
            dma("sp", ckf[k].v(), V(cckv[c2 * 128:(c2 + 1) * 128, :]), (ckf[k].buf, 0))
            cp("act", ckb[k].v(), ckf[k].v())
            b = nb()
            for q in range(4):
                tr(V(bank_bf(b)[:, q * 128:(q + 1) * 128], (PB, b)), ckb[k].v(None, np.s_[:, q * 128:(q + 1) * 128]))
            cp("dve", ckvT.v(c2, np.s_[:, 0:4, c2 * 128:(c2 + 1) * 128]),
               V(bank_bf(b)[:, 0:512].rearrange("p (a b) -> p a b", b=128), (PB, b)))
            dma("sp", krf[k].v(), V(ckr[c2 * 128:(c2 + 1) * 128, :]), (krf[k].buf, 0))
            cp("act", krb[k].v(), krf[k].v())
            b = nb()
            tr(V(bank_bf(b)[0:64, 0:128], (PB, b)), krb[k].v())
            cp("dve", krT.v(c2, np.s_[0:64, c2 * 128:(c2 + 1) * 128]), V(bank_bf(b)[0:64, 0:128], (PB, b)))
        pgen = prenorm_gen(X1, False, list(range(NT)), At, Bt, xt, hb, h2T, lambda t_: t_, lambda t_: t_ * 128)
        def m1a_A(t_):
            k = t_ % 2
            tc = np.s_[t_ * 128:(t_ + 1) * 128]
            dma("sp", cosk[k].v(), V(cosK_in[tc, :]), (cosk[k].buf, 0))
            dma("sp", sink[k].v(), V(sinK_in[tc, :]), (sink[k].buf, 0))
            b1 = nb()
            for kc in range(16):
                mm(bank(b1), h2T.v(t_, np.s_[:, kc, tc]), Wsm.v(None, np.s_[:, kc, 32:544]), start=(kc == 0), stop=(kc == 15))
            b3 = nb()
            for kc in range(16):
                mm(bank(b3), h2T.v(t_, np.s_[:, kc, tc]), Wsm.v(None, np.s_[:, kc, 544:1056]), start=(kc == 0), stop=(kc == 15))
            b5 = nb()
            for kc in range(16):
                mm(bank(b5, 64), h2T.v(t_, np.s_[:, kc, tc]), Wsm.v(None, np.s_[:, kc, 1056:1120]), start=(kc == 0), stop=(kc == 15))
            r = rstd_from(bank(b1), 512, junk5.v())
            stt("dve", cqb[k].v(), bank(b1), r, qnb.v(), ALU.mult, ALU.mult)
            r = rstd_from(bank(b3), 512, junk5.v())
            stt("dve", ckf[k].v(), bank(b3), r, kvnb.v(), ALU.mult, ALU.mult)
            dma("sp", V(nckv_out[tc, :]), ckf[k].v(), (ckf[k].buf, 0))
            cp("act", ckb[k].v(), ckf[k].v())
            cp("act", krf[k].v(), bank(b5, 64))
            dma("sp", V(nkr_out[tc, :]), krf[k].v(), (krf[k].buf, 0))
            tt("dve", kt1[k].v(), krf[k].v(), cosk[k].v(), ALU.mult)
            x4 = krf[k].ap.rearrange("p (a h j) -> p a h j", a=2, h=2)
            s4 = sink[k].ap.rearrange("p (a h j) -> p a h j", a=2, h=2)
            o4 = kt2[k].ap.rearrange("p (a h j) -> p a h j", a=2, h=2)
            for hh in range(2):
                tt("dve", V(o4[:, :, hh, :], (kt2[k].buf, None)), V(x4[:, :, 1 - hh, :], (krf[k].buf, None)),
                   V(s4[:, :, hh, :], (sink[k].buf, None)), ALU.mult)
            tt("dve", krb[k].v(), kt1[k].v(), kt2[k].v(), ALU.add)

        def m1a_B(t_):
            k = t_ % 2
            tc = np.s_[t_ * 128:(t_ + 1) * 128]
            kc0 = 256 + t_ * 128
            b2 = nb()
            for q in range(4):
                tr(V(bank_bf(b2)[:, q * 128:(q + 1) * 128], (PB, b2)), cqb[k].v(None, np.s_[:, q * 128:(q + 1) * 128]))
            for q in range(4):
                tr(V(bank_bf(b2)[:, 512 + q * 128:512 + (q + 1) * 128], (PB, b2)), ckb[k].v(None, np.s_[:, q * 128:(q + 1) * 128]))
            b6 = nb()
            tr(V(bank_bf(b6)[0:64, 0:128], (PB, b6)), krb[k].v())
            cp("act", cqnT.v(t_, np.s_[:, 0:4, tc]), V(bank_bf(b2)[:, 0:512].rearrange("p (a b) -> p a b", b=128), (PB, b2)))
            cp("dve", ckvT.v(2 + t_, np.s_[:, 0:4, kc0:kc0 + 128]), V(bank_bf(b2)[:, 512:1024].rearrange("p (a b) -> p a b", b=128), (PB, b2)))
            cp("dve", krT.v(2 + t_, np.s_[0:64, kc0:kc0 + 128]), V(bank_bf(b6)[0:64, 0:128], (PB, b6)))

        for _ in pgen:
            pass
        m1a_A(0)
        for t_ in range(NT):
            if t_ + 1 < NT:
                m1a_A(t_ + 1)
            m1a_B(t_)
        for blk in range(NBLK):
            cols = np.s_[blk * BW:(blk + 1) * BW]
            tparts = tuple(range(blk * (BW // 128), (blk + 1) * (BW // 128)))
            for d_ in range(2):
                b = nb()
                for kc in range(16):
                    mm(bank(b, BW, rows=16), Wsm.v(None, np.s_[:, kc, d_ * 16:(d_ + 1) * 16]),
                       V(h2T.ap[:, kc, cols], (h2T.buf, tparts)), start=(kc == 0), stop=(kc == 15))
                cp("act", aT2[d_].v(None, np.s_[0:16, cols]), bank(b, BW, rows=16))
        stage_end()
        A.pop()

        A.push()
        Wqk = TT(A.alloc([16, 1024], BF16), "Wqk", 2)
        wal = TT(A.alloc([2, 1024], BF16), "wal")
        bal = TT(A.alloc([2, 1024], BF16), "bal")
        dma("pool", wal.v(None, np.s_[0:16]), V(w_alpha.rearrange("d r n -> r d n")), (wal.buf, 0))
        dma("pool", bal.v(None, np.s_[0:1]), V(b_alpha.rearrange("(o d) n -> o d n", o=1)), (bal.buf, 0))
        qf = [TT(A.alloc([512], F32), f"qf{i}") for i in range(3)]
        kf = [TT(A.alloc([512], F32), f"kf{i}") for i in range(3)]
        ef = TT(A.alloc([512], F32), "ef")
        spf = [TT(A.alloc([512], F32), f"spf{i}") for i in range(6)]
        ebf = [TT(A.alloc([512], F32), f"ebf{i}") for i in range(2)]
        eif = [TT(A.alloc([512], F32), f"eif{i}") for i in range(2)]
        eef = [TT(A.alloc([512], F32), f"eef{i}") for i in range(2)]
        qdb = [TT(A.alloc([512], BF16), f"qdb{i}") for i in range(4)]
        kib = [TT(A.alloc([512], BF16), f"kib{i}") for i in range(4)]
        keb = [TT(A.alloc([512], BF16), f"keb{i}") for i in range(2)]
        qdT = [TT(A.alloc([4, 128], BF16), f"qdT{i}") for i in range(2)]
        kiT = [TT(A.alloc([4, 128], BF16), f"kiT{i}") for i in range(2)]
        edc = [TT(A.alloc([4], F32), f"edc{i}") for i in range(2)]
        items = [(ch, t_) for ch in range(2) for t_ in range(NT)]

        def ld_wqk(ch):
            c0 = ch * 512
            dma("pool", Wqk.v(0, np.s_[:, :, 0:512]), V(w_inp[:, c0:c0 + 512].rearrange("(kc p) n -> p kc n", p=128)), (Wqk.buf, 0))
            dma("pool", Wqk.v(1, np.s_[:, :, 512:1024]), V(w_inp[:, 1024 + c0:1024 + c0 + 512].rearrange("(kc p) n -> p kc n", p=128)), (Wqk.buf, 1))

        def p1(n):
            ch, t_ = items[n]
            c0 = ch * 512
            if t_ == 0:
                ld_wqk(ch)
            k = n % 3
            tc = np.s_[t_ * 128:(t_ + 1) * 128]
            bq = nb()
            for kc in range(16):
                mm(bank(bq), h2T.v(t_, np.s_[:, kc, tc]), Wqk.v(0, np.s_[:, kc, 0:512]), start=(kc == 0), stop=(kc == 15))
            cp("act", qf[k].v(), bank(bq), scale=1.0 / 16.0)
            bk = nb()
            for kc in range(16):
                mm(bank(bk), h2T.v(t_, np.s_[:, kc, tc]), Wqk.v(1, np.s_[:, kc, 512:1024]), start=(kc == 0), stop=(kc == 15))
            cp("act", kf[k].v(), bank(bk))
            for d_ in range(2):
                sp_ = spf[(n % 3) * 2 + d_]
                bz = nb()
                mm(bank(bz), aT2[d_].v(None, np.s_[0:16, tc]), wal.v(None, np.s_[0:16, d_, c0:c0 + 512]), start=True, stop=False)
                mm(bank(bz), onesb.v(None, np.s_[0:1, :]), bal.v(None, np.s_[0:1, d_, c0:c0 + 512]), start=False, stop=True)
                act(ef.v(), bank(bz), AF.Exp, scale=-1.0)
                act(sp_.v(), ef.v(), AF.Ln, bias=1.0)

        def p2(n):
            ch, t_ = items[n]
            c0 = ch * 512
            k = n % 3
            for d_ in range(2):
                sp_ = spf[(n % 3) * 2 + d_]
                kk = d_
                k4 = (n % 2) * 2 + d_
                part = (d_ * NT + t_) * 2 + ch
                Lm = Lf if d_ == 0 else Lb
                Um = Uf if d_ == 0 else Ub
                bB = nb()
                mm(bank(bB), Lm, sp_.v())
                bE = nb()
                mm(bank(bE), Um, sp_.v())
                bl = nb()
                for c in range(4):
                    mm(bank(bl, 1, off=c), sp_.v(None, np.s_[:, c * 128:(c + 1) * 128]), ones_col)
                act(ebf[kk].v(), bank(bB), AF.Exp, scale=-1.0 / 16.0)
                act(eif[kk].v(), bank(bB), AF.Exp, scale=1.0 / 16.0)
                act(eef[kk].v(), bank(bE), AF.Exp, scale=-1.0 / 16.0)
                act(edc[kk].v(), bank(bl, 4), AF.Exp, scale=-1.0 / 16.0)
                dma("sp", ED.v(part, np.s_[d_, t_, :, ch * 4:(ch + 1) * 4]), edc[kk].v(), (edc[kk].buf, 0))
                tt("dve", qdb[k4].v(), qf[k].v(), ebf[kk].v(), ALU.mult)
                tt("dve", kib[k4].v(), kf[k].v(), eif[kk].v(), ALU.mult)
                tt("pool", keb[kk].v(), kf[k].v(), eef[kk].v(), ALU.mult)
                dma("sp", KE.v(part, np.s_[d_, t_, :, c0:c0 + 512]), keb[kk].v(), (keb[kk].buf, 0))

        def p3(n):
            ch, t_ = items[n]
            for d_ in range(2):
                kk = d_
                k4 = (n % 2) * 2 + d_
                part = (d_ * NT + t_) * 2 + ch
                for (srcb, dstT, DR, eng) in ((qdb[k4], qdT[kk], QD, "act"), (kib[k4], kiT[kk], KI, "dve")):
                    bt = nb()
                    for q in range(4):
                        tr(V(bank_bf(bt)[:, q * 128:(q + 1) * 128], (PB, bt)), srcb.v(None, np.s_[:, q * 128:(q + 1) * 128]))
                    cp(eng, dstT.v(), V(bank_bf(bt)[:, 0:512].rearrange("p (a b) -> p a b", b=128), (PB, bt)))
                    dma("sp", DR.v(part, np.s_[d_, t_, :, ch * 4:(ch + 1) * 4, :]), dstT.v(), (dstT.buf, 0))

        NI = len(items)
        for n in range(NI + 2):
            if n < NI:
                p1(n)
            if 0 <= n - 1 < NI:
                p2(n - 1)
            if 0 <= n - 2 < NI:
                p3(n - 2)
        stage_end()
        A.pop()

        A.push()
        wsl = [TT(A.alloc([16, 512], BF16), f"m1cw{i}") for i in range(2)]
        vb = [TT(A.alloc([512], BF16), f"m1cv{i}") for i in range(4)]
        gw = [TT(A.alloc([16, 128], BF16), f"m1cg{i}") for i in range(2)]
        gtb = [TT(A.alloc([BW], BF16), f"m1cgt{i}") for i in range(4)]
        bg_w = [TT(A.alloc([16, 256], BF16), f"bgw{i}") for i in range(2)]
        bg_b = [TT(A.alloc([256], BF16, parts=1), f"bgb{i}") for i in range(2)]
        bg_g = [TT(A.alloc([D], F32), "bgg")]
        bg_m = [TT(A.alloc([D], F32), "bgm0")]
        bg = s0_units(S0_LATE, 256, bg_w, bg_b, bg_g, bg_m, make_crep())
        kq = 0
        kv_ = 0
        for (cbase, DR, fn) in ((2048, VV, AF.Copy), (4096, RR, AF.Silu)):
            for s4 in range(4):
                w = wsl[kq % 2]
                kq += 1
                wslab(w.v(), w_inp[:, cbase + s4 * 512:cbase + (s4 + 1) * 512])
                for t_ in range(NT):
                    tc = np.s_[t_ * 128:(t_ + 1) * 128]
                    b = nb()
                    for kc in range(16):
                        mm(bank(b), h2T.v(t_, np.s_[:, kc, tc]), w.v(None, np.s_[:, kc, :]), start=(kc == 0), stop=(kc == 15))
                    o = vb[kv_ % 4]
                    kv_ += 1
                    if fn == AF.Copy and (kv_ % 2 == 0):
                        cp("dve", o.v(), bank(b))
                    else:
                        act(o.v(), bank(b), fn)
                    dma("sp", DR.v(t_, np.s_[tc, s4 * 512:(s4 + 1) * 512]), o.v(), (o.buf, 0))
                    if t_ % 4 == 3:
                        next(bg, None)
        kg = 0
        ko = 0
        for (cbase, DR) in ((7264, GAT), (9312, GBT)):
            for dc in range(16):
                w = gw[kg % 2]
                kg += 1
                wslab(w.v(), w_inp[:, cbase + dc * 128:cbase + (dc + 1) * 128])
                for blk in range(NBLK):
                    cols = np.s_[blk * BW:(blk + 1) * BW]
                    tparts = tuple(range(blk * (BW // 128), (blk + 1) * (BW // 128)))
                    b = nb()
                    for kc in range(16):
                        mm(bank(b, BW), w.v(None, np.s_[:, kc, :]), V(h2T.ap[:, kc, cols], (h2T.buf, tparts)), start=(kc == 0), stop=(kc == 15))
                    o = gtb[ko % 4]
                    ko += 1
                    act(o.v(), bank(b, BW), AF.Sigmoid)
                    dma("sp", DR.v(dc * NBLK + blk, np.s_[dc, :, cols]), o.v(), (o.buf, 0))
                next(bg, None)
        for _ in bg:
            pass
        stage_end()
        A.pop()
        A.pop()

        A.push()
        Stl = [TT(A.alloc([8, 512], F32), f"St{i}", 8) for i in range(2)]
        edf = TT(A.alloc([8], F32), "edf")
        Sbf = TT(A.alloc([8, 512], BF16), "Sbf", 8)
        qd = [TT(A.alloc([8, 128], BF16), f"qd{i}") for i in range(3)]
        ki = [TT(A.alloc([8, 128], BF16), f"ki{i}") for i in range(3)]
        ke = [TT(A.alloc([1024], BF16), f"ke{i}") for i in range(3)]
        vt = [TT(A.alloc([D], BF16), f"vt{i}") for i in range(3)]
        ed = [TT(A.alloc([8], F32), f"ed{i}") for i in range(3)]
        ATs = [TT(A.alloc([512], BF16), f"ATs{i}") for i in range(2)]
        Acp = [TT(A.alloc([512], BF16), f"Acp{i}") for i in range(2)]
        maskf = [TT(A.alloc([512], F32), f"maskf{i}") for i in range(2)]
        for d_ in range(2):
            for h in range(4):
                cp("dve", maskf[d_].v(None, np.s_[:, h * 128:(h + 1) * 128]), Lf if d_ == 0 else Lb)
        ot = [TT(A.alloc([D], F32), f"ot{i}", 4) for i in range(2)]
        oft = [TT(A.alloc([D], F32), f"oft{i}") for i in range(3)]
        rt = [TT(A.alloc([D], BF16), f"rt{i}") for i in range(4)]
        ogb = [TT(A.alloc([D], BF16), f"ogb{i}", 4) for i in range(2)]
        ogTt = [TT(A.alloc([16, 128], BF16), f"ogTt{i}") for i in range(2)]
        gnb = TT(A.alloc([512], F32), "gnb")
        dma("sp", gnb.v(), V(gla_norm.partition_broadcast(128)), (gnb.buf, 0))
        tmpf = [TT(A.alloc([512], F32), f"m2tmp{i}") for i in range(4)]
        junk5 = TT(A.alloc([512], BF16), "m2junk")
        for d_ in range(2):
            cur = 0
            cross = False
            dma("sp", Stl[cur].v(None), V(s0[d_].rearrange("h (kc p) v -> p (h kc) v", p=128)), (Stl[cur].buf, 0))
            cp("act", Sbf.v(None), Stl[cur].v(None))
            order = list(range(NT)) if d_ == 0 else list(range(NT - 1, -1, -1))

            def load(i):
                t_ = order[i]
                k = i % 3
                tc = np.s_[t_ * 128:(t_ + 1) * 128]
                pr = ((d_ * NT + t_) * 2, (d_ * NT + t_) * 2 + 1)
                dma("sp", qd[k].v(), V(QD.ap[d_, t_], (QD.buf, pr)), (qd[k].buf, 0))
                dma("sp", ki[k].v(), V(KI.ap[d_, t_], (KI.buf, pr)), (ki[k].buf, 0))
                dma("sp", ke[k].v(), V(KE.ap[d_, t_], (KE.buf, pr)), (ke[k].buf, 0))
                dma("sp", ed[k].v(), V(ED.ap[d_, t_], (ED.buf, pr)), (ed[k].buf, 0))
                dma("sp", vt[k].v(), VV.v(t_, np.s_[tc, :]), (vt[k].buf, 0))
                if d_ == 1:
                    dma("sp", oft[k].v(), OF.v(t_, np.s_[tc, :]), (oft[k].buf, 0))
                    dma("sp", rt[i % 4].v(), RR.v(t_, np.s_[tc, :]), (rt[i % 4].buf, 0))

            load(0)
            if NT > 1:
                load(1)
            def emitA(i):
                k = i % 3
                k2 = i % 2
                ba = nb()
                for h in range(4):
                    for kc in range(2):
                        mm(bank(ba, 128, off=h * 128), ki[k].v(None, np.s_[:, h * 2 + kc, :]), qd[k].v(None, np.s_[:, h * 2 + kc, :]), start=(kc == 0), stop=(kc == 1))
                tt("dve", ATs[k2].v(), bank(ba), maskf[d_].v(), ALU.mult)

            emitA(0)
            kcp = 0
            pend_fin = []
            for i, t_ in enumerate(order):
                k = i % 3
                k2 = i % 2
                tc = np.s_[t_ * 128:(t_ + 1) * 128]
                if i + 2 < NT:
                    load(i + 2)
                if i + 1 < NT:
                    emitA(i + 1)
                for hp in range(2):
                    bss = {}
                    for h in (2 * hp, 2 * hp + 1):
                        hc = np.s_[:, h * 512:(h + 1) * 512]
                        for kc in range(2):
                            c = h * 2 + kc
                            bs = nb()
                            bss[c] = bs
                            mm(bank(bs), ke[k].v(None, np.s_[:, c * 128:(c + 1) * 128]), vt[k].v(None, hc))
                    for h in (2 * hp, 2 * hp + 1):
                        hc = np.s_[:, h * 512:(h + 1) * 512]
                        bo = nb()
                        mm(bank(bo), ATs[k2].v(None, np.s_[:, h * 128:(h + 1) * 128]), vt[k].v(None, hc), start=True, stop=False)
                        for kc in range(2):
                            c = h * 2 + kc
                            mm(bank(bo), qd[k].v(None, np.s_[:, c, :]), Sbf.v(c, np.s_[:, c, :]), start=False, stop=(kc == 1))
                        if d_ == 0:
                            cp("act", ot[k2].v(h, hc), bank(bo))
                        else:
                            tt("dve", ot[k2].v(h, hc), bank(bo), oft[k].v(None, hc), ALU.add)
                    for h in (2 * hp, 2 * hp + 1):
                        for kc in range(2):
                            c = h * 2 + kc
                            Ssrc = Stl[cur]
                            Sdst = Stl[1 - cur] if cross else Stl[cur]
                            edv = edf.v(None, np.s_[:, c:c + 1]) if cross else ed[k].v(None, np.s_[:, c:c + 1])
                            stt("dve", Sdst.v(c, np.s_[:, c, :]), Ssrc.v(c, np.s_[:, c, :]), edv, bank(bss[c]), ALU.mult, ALU.add)
                            cp("act" if (d_ == 1 or kcp % 2 == 0) else "dve", Sbf.v(c, np.s_[:, c, :]), Sdst.v(c, np.s_[:, c, :]))
                            kcp += 1
                if d_ == 0:
                    dma("act", OF.v(t_, np.s_[tc, :]), ot[k2].v(None), (ot[k2].buf, 0))
                else:
                    while pend_fin:
                        pend_fin.pop(0)()
                    rs_ = [rstd_from(ot[k2].v(h, np.s_[:, h * 512:(h + 1) * 512]), 512, junk5.v()) for h in range(4)]

                    def fin(k2=k2, k4=i % 4, t_=t_, tc=tc, rs_=rs_):
                        for h in range(4):
                            hc = np.s_[:, h * 512:(h + 1) * 512]
                            tm = tmpf[h % 4]
                            stt("dve", tm.v(), ot[k2].v(h, hc), rs_[h], gnb.v(), ALU.mult, ALU.mult)
                            tt("pool", ogb[k2].v(h, hc), tm.v(), rt[k4].v(None, hc), ALU.mult)
                        for half in range(2):
                            bt = nb()
                            for q in range(8):
                                kc = half * 8 + q
                                tr(V(bank_bf(bt)[:, q * 128:(q + 1) * 128], (PB, bt)), ogb[k2].v(kc // 4, np.s_[:, kc * 128:(kc + 1) * 128]))
                            cp("act" if half == 0 else "dve", ogTt[k2].v(None, np.s_[:, half * 8:half * 8 + 8, :]),
                               V(bank_bf(bt).rearrange("p (a b) -> p a b", b=128), (PB, bt)))
                        dma("pool", OGT.v(t_, np.s_[:, :, tc]), ogTt[k2].v(), (ogTt[k2].buf, 0))
                    pend_fin.append(fin)
                end_seg = (t_ % 2 == 1) if d_ == 0 else (t_ % 2 == 0)
                if cross:
                    cur = 1 - cur
                    cross = False
                if end_seg:
                    seg = t_ // 2
                    dst = (sf_out if d_ == 0 else sb_out)[seg].rearrange("h (kc p) v -> p (h kc) v", p=128)
                    dma("act", V(dst), Stl[cur].v(None), (Stl[cur].buf, 1))
                    if i != NT - 1:
                        ts("dve", edf.v(), ed[(i + 1) % 3].v(), flag.v(None, np.s_[:, 0:1]), ALU.mult)
                        act(Sbf.v(None), Stl[cur].v(None), AF.Copy, scale=flag.v(None, np.s_[:, 0:1]))
                        cross = True
        while pend_fin:
            pend_fin.pop(0)()
        stage_end()
        A.pop()

        A.push()
        QB = min(512, T)
        NQB = T // QB
        NQ4 = QB // 128
        kblocks = []
        ks = 0
        while ks < NK:
            kblocks.append((ks, min(512, NK - ks)))
            ks += 512
        omT = TT(A.alloc([16, T], BF16), "omT", 16)
        cosT = TT(A.alloc([T], F32), "cosT")
        sinT = TT(A.alloc([T], F32), "sinT")
        dma("sp", cosT.v(None, np.s_[0:64]), V(cosT_in), (cosT.buf, 0))
        dma("sp", sinT.v(None, np.s_[0:64]), V(sinT_in), (sinT.buf, 0))
        QrT = [TT(A.alloc([T], BF16), f"QrT{i}", 2 + NT) for i in range(2)]
        for i in range(2):
            dma("pool", QrT[i].v(1, np.s_[65:81, :]), V(mq_in), (QrT[i].buf, 1))
        S.op("dve", lambda e: e.memset(krT.ap[64:65, :], 1.0), [], [(krT.buf, NKT)])
        QnT = [TT(A.alloc([T], BF16), f"QnT{i}") for i in range(2)]
        KT = [TT(A.alloc([NK], BF16), f"KT{i}") for i in range(2)]
        Vh = [TT(A.alloc([NKT, 129], BF16), f"Vh{i}") for i in range(2)]
        for i in range(2):
            S.op("dve", lambda e, i=i: e.memset(Vh[i].ap[:, :, 128:129], 1.0), [], [(Vh[i].buf, None)])
        sel64 = TT(A.alloc([65], BF16), "sel64")
        S.op("dve", lambda e: e.memset(sel64.ap, 0.0), [], [(sel64.buf, None)])
        S.op("dve", lambda e: e.memset(sel64.ap[:, 64:65], 1.0), [], [(sel64.buf, None)])
        wkv = [TT(A.alloc([4, 256], BF16), f"wkv{i}") for i in range(2)]
        wq = [TT(A.alloc([4, 192], BF16), f"wq{i}") for i in range(2)]
        wqs = [TT(A.alloc([4, 64], BF16), f"wqs{i}") for i in range(2)]
        NPT = 4
        PT = [TT(A.alloc([QB], BF16), f"PT{i}") for i in range(NPT)]
        obt = [TT(A.alloc([128], BF16), f"obt{i}") for i in range(4)]
        dgb = [TT(A.alloc([128], BF16), f"dgb{i}") for i in range(4)]
        rt1 = TT(A.alloc([QB], F32), "rt1")
        rt2 = TT(A.alloc([QB], F32), "rt2")
        n4 = [0]
        n2 = [0]

        def nbs():
            i = n4[0]
            n4[0] = (i + 1) % 3
            return i

        def nbm():
            i = n2[0]
            n2[0] = (i + 1) % 4
            return i

        MB = 3

        def prep(h):
            k = h % 2
            dma("pool", wkv[k].v(), V(w_ukv[:, h * 256:(h + 1) * 256].rearrange("(kc p) n -> p kc n", p=128)), (wkv[k].buf, 0))
            dma("pool", wq[k].v(), V(w_uq[:, h * 192:(h + 1) * 192].rearrange("(kc p) n -> p kc n", p=128)), (wq[k].buf, 0))
            for a in range(2):
                for hh in range(2):
                    o0 = a * 32 + hh * 16
                    i0 = 128 + a * 32 + (1 - hh) * 16
                    cp("pool", wqs[k].v(None, np.s_[:, :, o0:o0 + 16]), wq[k].v(None, np.s_[:, :, i0:i0 + 16]))
            yield
            for (ks, kw) in kblocks:
                b = MB
                kparts = tuple(range(ks // 128, (ks + kw) // 128))
                for kc in range(4):
                    mm(bank(b, kw), wkv[k].v(None, np.s_[:, kc, 0:128]), V(ckvT.ap[:, kc, ks:ks + kw], (ckvT.buf, kparts)), start=(kc == 0), stop=(kc == 3))
                cp("dve", KT[k].v(None, np.s_[:, ks:ks + kw]), bank(b, kw))
                yield
            for g0 in range(0, NKT, 4):
                g = min(4, NKT - g0)
                b = MB
                for a in range(g):
                    kt_ = g0 + a
                    for kc in range(4):
                        mm(bank(b, 128, off=a * 128), ckvT.v(kt_, np.s_[:, kc, kt_ * 128:(kt_ + 1) * 128]), wkv[k].v(None, np.s_[:, kc, 128:256]), start=(kc == 0), stop=(kc == 3))
                cp("dve", Vh[k].v(None, np.s_[:, g0:g0 + g, 0:128]), V(ps[:, b * 512:b * 512 + g * 128].rearrange("p (a b) -> p a b", b=128), (PB, b)))
                yield
            for qb in range(NQB):
                cols = np.s_[qb * QB:(qb + 1) * QB]
                tparts = tuple(range(qb * NQ4, (qb + 1) * NQ4))
                b = MB
                for kc in range(4):
                    mm(bank(b, QB), wq[k].v(None, np.s_[:, kc, 0:128]), V(cqnT.ap[:, kc, cols], (cqnT.buf, tparts)), start=(kc == 0), stop=(kc == 3))
                cp("dve", QnT[k].v(None, np.s_[:, cols]), bank(b, QB))
                yield
                for kc in range(4):
                    mm(bank(b, QB, rows=64), wq[k].v(None, np.s_[:, kc, 128:192]), V(cqnT.ap[:, kc, cols], (cqnT.buf, tparts)), start=(kc == 0), stop=(kc == 3))
                tt("dve", rt1.v(None, np.s_[0:64, :]), bank(b, QB, rows=64), cosT.v(None, np.s_[0:64, cols]), ALU.mult)
                yield
                for kc in range(4):
                    mm(bank(b, QB, rows=64), wqs[k].v(None, np.s_[:, kc, :]), V(cqnT.ap[:, kc, cols], (cqnT.buf, tparts)), start=(kc == 0), stop=(kc == 3))
                tt("dve", rt2.v(None, np.s_[0:64, :]), bank(b, QB, rows=64), sinT.v(None, np.s_[0:64, cols]), ALU.mult)
                tt("dve", QrT[k].v(0, np.s_[0:64, cols]), rt1.v(None, np.s_[0:64, :]), rt2.v(None, np.s_[0:64, :]), ALU.add)
                yield
            for qb in range(NQB):
                cols = np.s_[qb * QB:(qb + 1) * QB]
                b = MB
                for q4 in range(NQ4):
                    qt = qb * NQ4 + q4
                    tc = np.s_[qt * 128:(qt + 1) * 128]
                    kd = np.s_[(2 + qt) * 128:(3 + qt) * 128]
                    mm(bank(b, 128, off=q4 * 128), QnT[k].v(None, np.s_[:, tc]), KT[k].v(None, np.s_[:, kd]), start=True, stop=False)
                    mm(bank(b, 128, off=q4 * 128), QrT[k].v(0, np.s_[0:64, tc]), krT.v(2 + qt, np.s_[0:64, kd]), start=False, stop=True)
                c0 = [scol() for _ in range(NQ4)]
                while c0[-1] != c0[0] + NQ4 - 1:
                    c0 = [scol() for _ in range(NQ4)]
                S.op("dve", lambda e, b=b, c=c0[0]: e.reduce_max(out=stat.ap[:, c:c + NQ4], in_=ps[:, b * 512:b * 512 + NQ4 * 128].rearrange("p (a b) -> p a b", b=128), axis=AX.X),
                     [(PB, b)], [(stat.buf, tuple(c0))])
                for q4 in range(NQ4):
                    ts("dve", dgb[q4].v(), identb.v(), sv(c0[q4]), ALU.mult, -1.0, ALU.mult)
                yield
                for q4 in range(NQ4):
                    mm(bank(b, 128, rows=65, off=q4 * 128), sel64.v(), dgb[q4].v())
                sparts = tuple(2 + qb * NQ4 + i for i in range(NQ4))
                cp("dve", V(QrT[k].ap[64:65, cols], (QrT[k].buf, sparts)), V(ps[64:65, b * 512:b * 512 + QB], (PB, b)))
                yield

        def run_all(gen):
            for _ in gen:
                pass

        kpt = 0
        LAG = 2
        run_all(prep(0))
        for h in range(16):
            k = h % 2
            bg = prep(h + 1) if h + 1 < 16 else iter(())
            tiles = [(qb, kt_) for qb in range(NQB) for kt_ in range(NKT)]
            pbuf = {}
            deferred = []

            def emit_qk(i):
                qb, kt_ = tiles[i]
                cols = np.s_[qb * QB:(qb + 1) * QB]
                qparts = (0, 1) + tuple(2 + qb * NQ4 + q for q in range(NQ4))
                kcs = np.s_[kt_ * 128:(kt_ + 1) * 128]
                b = nbs()
                mm(bank(b, QB), KT[k].v(None, np.s_[:, kcs]), QnT[k].v(None, np.s_[:, cols]), start=True, stop=False)
                mm(bank(b, QB), V(krT.ap[0:81, kcs], (krT.buf, (kt_, NKT))), V(QrT[k].ap[0:81, cols], (QrT[k].buf, qparts)), start=False, stop=True)
                pbuf[i] = b

            def emit_pv(i):
                nonlocal kpt
                qb, kt_ = tiles[i]
                b = pbuf.pop(i)
                p_ = PT[kpt % NPT]
                kpt += 1
                act(p_.v(), bank(b, QB), AF.Exp, scale=ATT_SCALE)
                for q4 in range(NQ4):
                    ob_ = 4 + q4
                    mm(V(ps[:, ob_ * 512:ob_ * 512 + 129], (PB, ob_)),
                       p_.v(None, np.s_[:, q4 * 128:(q4 + 1) * 128]), Vh[k].v(None, np.s_[:, kt_, :]),
                       start=(kt_ == 0), stop=(kt_ == NKT - 1))
                if kt_ == NKT - 1:
                    obs = []
                    for q4 in range(NQ4):
                        ob_ = 4 + q4
                        o0 = ob_ * 512
                        c_ri = scol()
                        recip(sv(c_ri), V(ps[:, o0 + 128:o0 + 129], (PB, ob_)))
                        ts("dve", obt[q4].v(), V(ps[:, o0:o0 + 128], (PB, ob_)), sv(c_ri), ALU.mult)

                    def fin(qb=qb):
                        for q4 in range(NQ4):
                            tr(V(bank_bf(MB)[:, q4 * 128:(q4 + 1) * 128], (PB, MB)), obt[q4].v())
                        c0_ = qb * QB
                        cp("dve", V(omT.ap[:, h, c0_:c0_ + QB], (omT.buf, h)), V(bank_bf(MB)[:, 0:QB], (PB, MB)))
                    deferred.append((i + 5, fin))

            n = len(tiles)
            for i in range(n + LAG):
                if i < n:
                    emit_qk(i)
                j = i - LAG
                if j >= 0:
                    emit_pv(j)
                while deferred and deferred[0][0] <= i:
                    deferred.pop(0)[1]()
                if i % 2 == 1:
                    next(bg, None)
            while deferred:
                deferred.pop(0)[1]()
            run_all(bg)
        for h in range(16):
            dma("sp", OMT.v(h, np.s_[:, h, :]), omT.v(h, np.s_[:, h, :]), (omT.buf, h))
        stage_end()
        A.pop()
        A.pop()

        A.push()
        FB = min(1024, T)
        NFB = T // FB
        HB = min(512, FB)
        NH = FB // HB
        nt4 = HB // 128
        Rw = A.alloc([8192], F32)
        Rbuf = Buf("m4R", 4)
        ogT_ap = Rw.bitcast(BF16)[:, 0:16 * FB].rearrange("p (a b) -> p a b", b=FB)
        y_ap = Rw.rearrange("p (a b) -> p a b", b=D)
        O2w = A.alloc([8192], F32)
        omTb = TT(O2w.bitcast(BF16)[:, 0:16 * FB].rearrange("p (a b) -> p a b", b=FB), "omTb", 4)
        y2_ap = O2w.rearrange("p (a b) -> p a b", b=D)
        mT = TT(A.alloc([16, FB], BF16), "mT", 16)
        wg = [TT(A.alloc([16, 128], BF16), f"wg{i}") for i in range(2)]
        wm = [TT(A.alloc([16, 128], BF16), f"wm{i}") for i in range(2)]
        wo = [TT(A.alloc([16, 512], BF16), f"wo{i}") for i in range(2)]
        gaT = [TT(A.alloc([FB], BF16), f"gaT{i}") for i in range(2)]
        gbT = [TT(A.alloc([FB], BF16), f"gbT{i}") for i in range(2)]
        G2 = TT(A.alloc([D], F32), "G2")
        dma("sp", G2.v(), modS.v(5, np.s_[5]), (G2.buf, 0))
        xt = [TT(A.alloc([D], F32), f"m4x{i}") for i in range(2)]
        junk = TT(A.alloc([D], BF16), "m4junk")
        t1 = [TT(A.alloc([HB], F32), f"m4t1{i}") for i in range(2)]
        t2 = [TT(A.alloc([HB], F32), f"m4t2{i}") for i in range(2)]
        kwo = 0
        for fb in range(NFB):
            cols = np.s_[fb * FB:(fb + 1) * FB]
            ttiles = tuple(range(fb * (FB // 128), (fb + 1) * (FB // 128)))
            gparts_of = lambda dc: tuple(dc * NBLK + bb for bb in range(fb * (FB // BW), (fb + 1) * (FB // BW)))
            dma("sp", V(ogT_ap, (Rbuf, None)), V(OGT.ap[:, :, cols], (OGT.buf, ttiles)), (Rbuf, 0))
            dma("sp", omTb.v(), V(OMT.ap[:, :, cols], (OMT.buf, None)), (omTb.buf, 0))
            for dc in range(16):
                k = dc % 2
                wslab(wg[k].v(), w_go[:, dc * 128:(dc + 1) * 128])
                wslab(wm[k].v(), w_mo[:, dc * 128:(dc + 1) * 128])
                dma("sp", gaT[k].v(), V(GAT.ap[dc, :, cols], (GAT.buf, gparts_of(dc))), (gaT[k].buf, 0))
                dma("sp", gbT[k].v(), V(GBT.ap[dc, :, cols], (GBT.buf, gparts_of(dc))), (gbT[k].buf, 0))
                for hh in range(NH):
                    hcs = np.s_[hh * HB:(hh + 1) * HB]
                    bg = nb()
                    for kc in range(16):
                        mm(bank(bg, HB), wg[k].v(None, np.s_[:, kc, :]), V(ogT_ap[:, kc, hcs], (Rbuf, None)), start=(kc == 0), stop=(kc == 15))
                    bm = nb()
                    for kc in range(16):
                        mm(bank(bm, HB), wm[k].v(None, np.s_[:, kc, :]), omTb.v(None, np.s_[:, kc, hcs]), start=(kc == 0), stop=(kc == 15))
                    tt("dve", t1[hh % 2].v(), bank(bg, HB), gaT[k].v(None, np.s_[:, hcs]), ALU.mult)
                    tt("dve", t2[hh % 2].v(), bank(bm, HB), gbT[k].v(None, np.s_[:, hcs]), ALU.mult)
                    tt("dve", mT.v(dc, np.s_[:, dc, hcs]), t1[hh % 2].v(), t2[hh % 2].v(), ALU.add)
            TBk = FB // 128

            def ytile(t8, sl=np.s_[:]):
                if t8 < 4:
                    return V(y_ap[:, t8, sl], (Rbuf, t8))
                return V(y2_ap[:, t8 - 4, sl], (omTb.buf, t8 - 4))

            for ct in range(4):
                w = wo[kwo % 2]
                kwo += 1
                wslab(w.v(), w_o[:, ct * 512:(ct + 1) * 512])
                pys = [nb() for _ in range(TBk)]
                for t8 in range(TBk):
                    c0 = t8 * 128
                    for kc in range(16):
                        mm(bank(pys[t8]), mT.v(kc, np.s_[:, kc, c0:c0 + 128]), w.v(None, np.s_[:, kc, :]), start=(kc == 0), stop=(kc == 15))
                    cp("act" if t8 % 2 == 0 else "dve", ytile(t8, np.s_[ct * 512:(ct + 1) * 512]), bank(pys[t8]))
            tiles_b = [fb * TBk + t8 for t8 in range(TBk)]
            postnorm_tiles(lambda i: ytile(i), tiles_b, G2, X1, False, X2, False, xt, junk, add_eng="pool")
        stage_end()
        A.pop()

    stage0()
    if upto >= 1:
        ffn_stage(x_in, True, X1, False, 0, w_f1i, w_f1o)
    if upto >= 2:
        mixer_stages()
    if upto >= 3:
        ffn_stage(X2, False, y_out, True, 6, w_f2i, w_f2o)
    S.barrier()
    S.emit(nc, block, esems, dsems)
    print("ops:", S.stats(), "arena max", A.off)
    es.close()
    return nc


WNAMES = ["w_ada", "b_ada", "norm_gains", "w_ffn1_in", "w_ffn1_out", "w_ffn2_in", "w_ffn2_out", "w_in",
          "w_gla_alpha", "b_gla_alpha", "gla_norm", "w_gla_out", "q_norm", "kv_norm", "w_uq", "w_ukv",
          "w_mla_out", "w_out"]


def consts_array():
    s = np.arange(128)[:, None]
    t = np.arange(128)[None, :]
    c = np.zeros((128, 6, 128), np.float32)
    c[:, 0] = (s == t)
    c[:, 1] = (s <= t)
    c[:, 2] = (s >= t)
    c[:, 3] = (s > t)
    c[:, 4] = (s < t)
    c[:, 5] = 1.0
    return c


def rope_tables(T, real):
    cosT = np.ones((64, T), np.float32)
    sinT = np.zeros((64, T), np.float32)
    if real:
        t = np.arange(T)
        pos = [(t // 64).astype(np.float32), (t % 64).astype(np.float32)]
        inv = (np.float32(10000.0) ** (-np.arange(0, 32, 2, dtype=np.float32) / np.float32(32))).astype(np.float32)
        for a in range(2):
            ang = pos[a][None, :] * inv[:, None]
            for hh in range(2):
                r0 = a * 32 + hh * 16
                cosT[r0:r0 + 16] = np.cos(ang)
                sinT[r0:r0 + 16] = np.sin(ang) * (-1.0 if hh == 0 else 1.0)
    return cosT, sinT, np.ascontiguousarray(cosT.T), np.ascontiguousarray(sinT.T)


def mask_rows(NSEG, kind):
    T = NSEG * 256
    NK = T + 256
    mq = np.zeros((16, T), np.float32)
    mk = np.zeros((16, NK), np.float32)
    for m in range(NSEG + 1):
        mk[m, m * 256:(m + 1) * 256] = 1.0
    if kind == "P":
        for s in range(NSEG):
            mq[:NSEG + 1, s * 256:(s + 1) * 256] = NEG_BIG
            mq[1 + s, s * 256:(s + 1) * 256] = 0.0
    return mq, mk


def core_inputs(full, NSEG, kind):
    T = NSEG * 256
    m = {}
    m["x"] = np.ascontiguousarray(full["x"], dtype=np.float32)
    m["cvec"] = np.ascontiguousarray(full["c"].reshape(16, 128).T)
    m["cckv"] = np.ascontiguousarray(full["cache_ckv"])
    m["ckr"] = np.ascontiguousarray(full["cache_krope"])
    m["s0"] = np.ascontiguousarray(np.stack([full["s0f"], full["s0b"]], axis=0))
    m["flag"] = np.full((128, 1), 1.0 if kind == "S" else 0.0, np.float32)
    cosT, sinT, cosK, sinK = rope_tables(T, kind == "S")
    m["cosT"], m["sinT"], m["cosK"], m["sinK"] = cosT, sinT, cosK, sinK
    mq, mk = mask_rows(NSEG, kind)
    m["mq"], m["mk"] = mq, mk
    m["consts"] = consts_array()
    for k in WNAMES:
        m[k] = full[k]
    return m


def make_test_inputs(rng, NSEG, kind):
    T = NSEG * 256
    f32 = np.float32

    def nrm(shape, scale):
        return (rng.standard_normal(shape, dtype=f32) * f32(scale)).astype(f32)

    DFF = 5504
    full = {
        "x": nrm((T, D), 1.0),
        "c": nrm((D,), 1.0),
        "w_ada": nrm((D, 9 * D), 0.5 * D ** -0.5),
        "b_ada": nrm((1, 9 * D), 0.01),
        "norm_gains": 1.0 + nrm((6, D), 0.05),
        "w_ffn1_in": nrm((D, 2 * DFF), D ** -0.5),
        "w_ffn1_out": nrm((DFF, D), DFF ** -0.5),
        "w_ffn2_in": nrm((D, 2 * DFF), D ** -0.5),
        "w_ffn2_out": nrm((DFF, D), DFF ** -0.5),
        "w_in": nrm((D, 11360), D ** -0.5),
        "w_gla_alpha": nrm((2, 16, 1024), 16 ** -0.5),
        "b_gla_alpha": nrm((2, 1024), 0.1),
        "gla_norm": 1.0 + nrm((1, 512), 0.05),
        "w_gla_out": nrm((D, D), D ** -0.5),
        "q_norm": 1.0 + nrm((1, 512), 0.05),
        "kv_norm": 1.0 + nrm((1, 512), 0.05),
        "w_uq": nrm((512, 3072), 512 ** -0.5),
        "w_ukv": nrm((512, 4096), 512 ** -0.5),
        "w_mla_out": nrm((D, D), D ** -0.5),
        "w_out": nrm((D, D), D ** -0.5),
    }
    if kind == "S":
        full["cache_ckv"] = nrm((256, 512), 1.0)
        full["cache_krope"] = nrm((256, 64), 1.0)
        full["s0f"] = nrm((4, 256, 512), 0.5)
        full["s0b"] = nrm((4, 256, 512), 0.5)
    else:
        full["cache_ckv"] = np.zeros((256, 512), f32)
        full["cache_krope"] = np.zeros((256, 64), f32)
        full["s0f"] = np.zeros((4, 256, 512), f32)
        full["s0b"] = np.zeros((4, 256, 512), f32)
    return full


_NC_CACHE = {}


def kernel(**inputs):
    NSEG = 8
    f32 = np.float32
    inp = {k: np.asarray(v) for k, v in inputs.items()}
    W = {
        "w_ada": inp["w_ada"][0], "b_ada": inp["b_ada"], "norm_gains": inp["norm_gains"][0],
        "w_ffn1_in": inp["w_ffn1_in"][0], "w_ffn1_out": inp["w_ffn1_out"][0],
        "w_ffn2_in": inp["w_ffn2_in"][0], "w_ffn2_out": inp["w_ffn2_out"][0],
        "w_in": inp["w_in"][0], "w_gla_alpha": inp["w_gla_alpha"][0], "b_gla_alpha": inp["b_gla_alpha"][0],
        "gla_norm": inp["gla_norm"], "w_gla_out": inp["w_gla_out"][0],
        "q_norm": inp["q_norm"], "kv_norm": inp["kv_norm"], "w_uq": inp["w_uq"][0], "w_ukv": inp["w_ukv"][0],
        "w_mla_out": inp["w_mla_out"][0], "w_out": inp["w_out"][0],
    }
    W = {k: np.ascontiguousarray(v, dtype=f32) for k, v in W.items()}
    in_maps = []
    for core in range(8):
        if core < 4:
            b = core
            full = dict(x=inp["x_sample"][b], c=inp["c"][b], cache_ckv=inp["cache_ckv"][b, 0],
                        cache_krope=inp["cache_krope"][b, 0], s0f=inp["state_gla_fwd"][b, 0],
                        s0b=inp["state_gla_bwd"][b, 0], **W)
            in_maps.append(core_inputs(full, NSEG, "S"))
        else:
            s0_ = 4 * (core - 4)
            xp = inp["x_prompt"][s0_:s0_ + 4].reshape(1024, D)
            x = np.concatenate([xp, np.zeros((1024, D), f32)], axis=0)
            full = dict(x=x, c=inp["c_ctx"], cache_ckv=np.zeros((256, 512), f32),
                        cache_krope=np.zeros((256, 64), f32), s0f=np.zeros((4, 256, 512), f32),
                        s0b=np.zeros((4, 256, 512), f32), **W)
            in_maps.append(core_inputs(full, NSEG, "P"))
    if NSEG not in _NC_CACHE:
        _NC_CACHE[NSEG] = build(NSEG)
    nc = _NC_CACHE[NSEG]
    res = run_bass_kernel_spmd(nc, in_maps, core_ids=list(range(8)))
    R = res.results
    y_prompt = np.zeros((16, 256, D), f32)
    y_sample = np.zeros((4, 2048, D), f32)
    new_ckv = np.zeros((16, 1, 256, 512), f32)
    new_krope = np.zeros((16, 1, 256, 64), f32)
    new_sf = np.zeros((16, 1, 4, 256, 512), f32)
    new_sb = np.zeros((16, 1, 4, 256, 512), f32)
    for core in range(8):
        r = R[core]
        if core < 4:
            y_sample[core] = np.asarray(r["y"])
        else:
            s0_ = 4 * (core - 4)
            y = np.asarray(r["y"]); ck = np.asarray(r["nckv"]); kr = np.asarray(r["nkr"])
            sf = np.asarray(r["sf"]); sb = np.asarray(r["sb"])
            for s in range(4):
                y_prompt[s0_ + s] = y[s * 256:(s + 1) * 256]
                new_ckv[s0_ + s, 0] = ck[s * 256:(s + 1) * 256]
                new_krope[s0_ + s, 0] = kr[s * 256:(s + 1) * 256]
                new_sf[s0_ + s, 0] = sf[s]
                new_sb[s0_ + s, 0] = sb[s]
    return (y_prompt, y_sample, new_ckv, new_krope, new_sf, new_sb)
```

```python
import numpy as np
from contextlib import ExitStack
import concourse.bass as bass
import concourse.mybir as mybir
from concourse.bass_utils import run_bass_kernel_spmd

F32 = mybir.dt.float32
BF16 = mybir.dt.bfloat16
AF = mybir.ActivationFunctionType
ALU = mybir.AluOpType
AX = mybir.AxisListType

ENGINES = ["pe", "act", "dve", "pool", "sp"]


class Buf:
    def __init__(self, name, nparts=1):
        self.name = name
        self.n = nparts
        self.w = [None] * nparts
        self.r = [[] for _ in range(nparts)]
        self.sem = [None] * nparts

    def parts(self, p):
        if p is None:
            return range(self.n)
        if isinstance(p, int):
            return (p,)
        return p


class V:
    def __init__(self, ap, *deps):
        self.ap = ap
        self.deps = list(deps)


class _Op:
    __slots__ = ("fn", "waits", "is_dma", "dsem", "clock", "marked")

    def __init__(self, fn, waits, is_dma, dsem):
        self.fn = fn
        self.waits = waits
        self.is_dma = is_dma
        self.dsem = dsem
        self.clock = None
        self.marked = False


class Sched:
    def __init__(self, n_dma_sems=80):
        self.ops = {e: [] for e in ENGINES}
        self.known = {e: {} for e in ENGINES}
        self.n_dma = n_dma_sems
        self.dma_count = [0] * n_dma_sems
        self.n_sw = 24
        self.dma_next_sw = 0
        self.dma_next = self.n_sw
        self.dma_clock = {}
        self.snap = {e: None for e in ENGINES}
        self.sem_bufs = []

    def _sem_for(self, buf, part, queue):
        if buf.sem[part] is None:
            buf.sem[part] = {}
            self.sem_bufs.append(buf)
        d = buf.sem[part]
        kind = "sw" if queue == "pool" else "hw"
        if kind not in d:
            if kind == "sw":
                if self.dma_next_sw >= self.n_sw:
                    raise RuntimeError("out of sw dma semaphores")
                d[kind] = self.dma_next_sw
                self.dma_next_sw += 1
            else:
                if self.dma_next >= self.n_dma:
                    raise RuntimeError("out of hw dma semaphores")
                d[kind] = self.dma_next
                self.dma_next += 1
        return d[kind]

    def _collect(self, engine, reads, writes):
        deps = {}

        def add(ev):
            if ev is None:
                return
            k, v = ev
            if k == "pe" and engine == "pe":
                return
            if deps.get(k, -1) < v:
                deps[k] = v

        for b, p in reads:
            for i in b.parts(p):
                add(b.w[i])
        for b, p in writes:
            for i in b.parts(p):
                add(b.w[i])
                for ev in b.r[i]:
                    add(ev)
        return deps

    def _update(self, ev, reads, writes):
        for b, p in reads:
            for i in b.parts(p):
                lst = b.r[i]
                for j, (k, v) in enumerate(lst):
                    if k == ev[0]:
                        lst[j] = ev
                        break
                else:
                    lst.append(ev)
        for b, p in writes:
            for i in b.parts(p):
                b.w[i] = ev
                b.r[i] = []

    def _merge_clock(self, engine, clock):
        kn = self.known[engine]
        for k, v in clock.items():
            if kn.get(k, -1) < v:
                kn[k] = v
        self.snap[engine] = None

    def _snapshot(self, engine):
        if self.snap[engine] is None:
            self.snap[engine] = dict(self.known[engine])
        return self.snap[engine]

    def _make_waits(self, engine, deps):
        waits = []
        kn = self.known[engine]
        for k, v in deps.items():
            if isinstance(k, tuple):
                if kn.get(k, -1) >= v:
                    continue
                v = max(v, self.dma_count[k[1]])
                waits.append((k, v))
                kn[k] = v
                self.snap[engine] = None
                clk = self.dma_clock.get((k[1], v))
                if clk:
                    self._merge_clock(engine, clk)
            else:
                if kn.get(k, -1) >= v:
                    continue
                waits.append((k, v))
                kn[k] = v
                self.snap[engine] = None
                op = self.ops[k][v]
                op.marked = True
                if op.clock:
                    self._merge_clock(engine, op.clock)
        return waits

    def op(self, engine, fn, reads=(), writes=()):
        deps = self._collect(engine, reads, writes)
        waits = self._make_waits(engine, deps)
        o = _Op(fn, waits, False, None)
        o.clock = self._snapshot(engine)
        idx = len(self.ops[engine])
        self.ops[engine].append(o)
        self._update((engine, idx), reads, writes)
        return idx

    def dma(self, queue, fn, reads, writes, semof):
        deps = self._collect(queue, reads, writes)
        waits = self._make_waits(queue, deps)
        si = self._sem_for(semof[0], semof[1], queue)
        self.dma_count[si] += 16
        val = self.dma_count[si]
        o = _Op(fn, waits, True, si)
        self.ops[queue].append(o)
        self.dma_clock[(si, val)] = self._snapshot(queue)
        self._update((("d", si), val), reads, writes)

    def barrier(self):
        last = {}
        for e in ("pe", "act", "dve", "pool"):
            for i in range(len(self.ops[e]) - 1, -1, -1):
                if not self.ops[e][i].is_dma and self.ops[e][i].fn is not None:
                    last[e] = i
                    break
        for e in ENGINES:
            deps = {}
            for f, i in last.items():
                if f != e:
                    deps[f] = i
                elif e in ("act", "dve", "pool"):
                    deps[f] = i
            for si in range(self.n_dma):
                if self.dma_count[si] > 0:
                    deps[("d", si)] = self.dma_count[si]
            waits = self._make_waits(e, deps)
            if waits:
                o = _Op(None, waits, False, None)
                o.clock = self._snapshot(e)
                self.ops[e].append(o)

    def reset_sems(self):
        self.dma_next = self.n_sw
        self.dma_next_sw = 0
        for b in self.sem_bufs:
            b.sem = [None] * b.n
        self.sem_bufs = []

    def emit(self, nc, block, esems, dsems):
        vals = {}
        for e in ENGINES:
            c = 0
            m = {}
            for i, o in enumerate(self.ops[e]):
                if o.marked:
                    c += 1
                    m[i] = c
            vals[e] = m

        def run(e, eng):
            for i, o in enumerate(self.ops[e]):
                for k, v in o.waits:
                    if isinstance(k, tuple):
                        eng.wait_ge(dsems[k[1]], v)
                    else:
                        eng.wait_ge(esems[k], vals[k][v])
                if o.fn is None:
                    continue
                ins = o.fn(eng)
                if o.is_dma:
                    ins.then_inc(dsems[o.dsem], 16)
                elif o.marked:
                    ins.then_inc(esems[e], 1)

        @block.tensor
        def _(eng):
            run("pe", eng)

        @block.scalar
        def _(eng):
            run("act", eng)

        @block.vector
        def _(eng):
            run("dve", eng)

        @block.gpsimd
        def _(eng):
            run("pool", eng)

        @block.sync
        def _(eng):
            run("sp", eng)

    def stats(self):
        return {e: len(self.ops[e]) for e in ENGINES}


class Arena:
    def __init__(self, t, nwords):
        self.t = t
        self.n = nwords
        self.off = 0
        self.marks = []

    def alloc(self, free_shape, dtype, parts=128):
        nel = int(np.prod(free_shape))
        nbytes = nel * (2 if dtype == BF16 else 4)
        nw = (nbytes + 3) // 4
        nw = (nw + 15) // 16 * 16
        if self.off + nw > self.n:
            raise RuntimeError(f"arena overflow: need {nw} words at {self.off} of {self.n}")
        ap = self.t[0:parts, self.off:self.off + nw]
        self.off += nw
        if dtype == BF16:
            ap = ap.bitcast(BF16)[:, 0:nel]
        else:
            ap = ap[:, 0:nel]
        if len(free_shape) == 2:
            ap = ap.rearrange("p (a b) -> p a b", b=free_shape[1])
        elif len(free_shape) == 3:
            ap = ap.rearrange("p (a b c) -> p a b c", b=free_shape[1], c=free_shape[2])
        return ap

    def push(self):
        self.marks.append(self.off)

    def pop(self):
        self.off = self.marks.pop()


D = 2048
DFF = 5504
NFC = 43
ATT_SCALE = 192 ** -0.5
EPS = 1e-6
NEG_BIG = -30000.0
ARENA_WORDS = 53184


class TT:
    def __init__(self, ap, name, nparts=1):
        self.ap = ap
        self.buf = Buf(name, nparts)

    def v(self, part=None, idx=None):
        ap = self.ap if idx is None else self.ap[idx]
        return V(ap, (self.buf, part))


def build(NSEG, upto=99, debug=False):
    T = NSEG * 256
    NT = T // 128
    NK = T + 256
    NKT = NK // 128
    nc = bass.Bass("TRN2", target_bir_lowering=False)

    def din(name, shape):
        return nc.dram_tensor(name, list(shape), F32, kind="ExternalInput").ap()

    def dout(name, shape):
        return nc.dram_tensor(name, list(shape), F32, kind="ExternalOutput").ap()

    def dscr(name, shape, dt):
        if debug:
            return nc.dram_tensor(name, list(shape), dt, kind="ExternalOutput").ap()
        return nc.dram_tensor(name, list(shape), dt).ap()

    x_in = din("x", [T, D])
    cvec = din("cvec", [128, 16])
    cckv = din("cckv", [256, 512])
    ckr = din("ckr", [256, 64])
    s0 = din("s0", [2, 4, 256, 512])
    flag_in = din("flag", [128, 1])
    cosT_in = din("cosT", [64, T])
    sinT_in = din("sinT", [64, T])
    cosK_in = din("cosK", [T, 64])
    sinK_in = din("sinK", [T, 64])
    mq_in = din("mq", [16, T])
    mk_in = din("mk", [16, NK])
    consts_in = din("consts", [128, 6, 128])
    w_ada = din("w_ada", [D, 9 * D])
    b_ada = din("b_ada", [1, 9 * D])
    gains = din("norm_gains", [6, D])
    w_f1i = din("w_ffn1_in", [D, 2 * DFF])
    w_f1o = din("w_ffn1_out", [DFF, D])
    w_f2i = din("w_ffn2_in", [D, 2 * DFF])
    w_f2o = din("w_ffn2_out", [DFF, D])
    w_inp = din("w_in", [D, 11360])
    w_alpha = din("w_gla_alpha", [2, 16, 1024])
    b_alpha = din("b_gla_alpha", [2, 1024])
    gla_norm = din("gla_norm", [1, 512])
    w_go = din("w_gla_out", [D, D])
    q_norm = din("q_norm", [1, 512])
    kv_norm = din("kv_norm", [1, 512])
    w_uq = din("w_uq", [512, 3072])
    w_ukv = din("w_ukv", [512, 4096])
    w_mo = din("w_mla_out", [D, D])
    w_o = din("w_out", [D, D])

    y_out = dout("y", [T, D])
    nckv_out = dout("nckv", [T, 512])
    nkr_out = dout("nkr", [T, 64])
    sf_out = dout("sf", [NSEG, 4, 256, 512])
    sb_out = dout("sb", [NSEG, 4, 256, 512])

    modS = TT(dscr("modS", [9, 128, D], F32), "modS", 9)
    X1 = TT(dscr("X1", [T, D], F32), "X1", NT)
    X2 = TT(dscr("X2", [T, D], F32), "X2", NT)
    QD = TT(dscr("QD", [2, NT, 128, 8, 128], BF16), "QD", 2 * NT)
    KI = TT(dscr("KI", [2, NT, 128, 8, 128], BF16), "KI", 2 * NT)
    KE = TT(dscr("KE", [2, NT, 128, 1024], BF16), "KE", 2 * NT)
    ED = TT(dscr("ED", [2, NT, 128, 8], F32), "ED", 2 * NT)
    VV = TT(dscr("VV", [T, D], BF16), "VV", NT)
    RR = TT(dscr("RR", [T, D], BF16), "RR", NT)
    GAT = TT(dscr("GAT", [16, 128, T], BF16), "GAT", 1)
    GBT = TT(dscr("GBT", [16, 128, T], BF16), "GBT", 1)
    OF = TT(dscr("OF", [T, D], F32), "OF", NT)
    OGT = TT(dscr("OGT", [128, 16, T], BF16), "OGT", 1)
    OMT = TT(dscr("OMT", [128, 16, T], BF16), "OMT", 1)

    S = Sched(90)
    es = ExitStack()
    arena_t = es.enter_context(nc.sbuf_tensor("arena", [128, ARENA_WORDS], F32))
    ps = es.enter_context(nc.psum_tensor("ps", [128, 4096], F32))
    esems = {e: es.enter_context(nc.semaphore(f"s_{e}")) for e in ["pe", "act", "dve", "pool"]}
    dsems = [es.enter_context(nc.semaphore(f"d{i}")) for i in range(90)]
    block = es.enter_context(nc.Block())
    A = Arena(arena_t, ARENA_WORDS)
    PB = Buf("psum", 8)
    bank_rr = [0]

    def nb():
        i = bank_rr[0]
        bank_rr[0] = (i + 1) % 8
        return i

    def bank(i, w=512, rows=128, off=0):
        return V(ps[0:rows, i * 512 + off:i * 512 + off + w], (PB, i))

    def bank_bf(i):
        return ps[:, i * 512:(i + 1) * 512].bitcast(BF16)

    def mm(o, l, r, start=True, stop=True):
        S.op("pe", lambda e: e.matmul(o.ap, lhsT=l.ap, rhs=r.ap, start=start, stop=stop), l.deps + r.deps, o.deps)

    def act(o, i, func, bias=None, scale=None, accum=None):
        kw = {}
        reads = list(i.deps)
        writes = list(o.deps)
        if bias is not None:
            if isinstance(bias, V):
                kw["bias"] = bias.ap
                reads += bias.deps
            else:
                kw["bias"] = bias
        if scale is not None:
            if isinstance(scale, V):
                kw["scale"] = scale.ap
                reads += scale.deps
            else:
                kw["scale"] = scale
        if accum is not None:
            kw["accum_out"] = accum.ap
            writes += accum.deps
        S.op("act", lambda e: e.activation(out=o.ap, in_=i.ap, func=func, **kw), reads, writes)

    def tt(eng, o, a, b, op):
        S.op(eng, lambda e: e.tensor_tensor(out=o.ap, in0=a.ap, in1=b.ap, op=op), a.deps + b.deps, o.deps)

    def stt(eng, o, a, sc, b, op0, op1):
        reads = a.deps + b.deps
        if isinstance(sc, V):
            reads = reads + sc.deps
            scv = sc.ap
        else:
            scv = sc
        S.op(eng, lambda e: e.scalar_tensor_tensor(out=o.ap, in0=a.ap, scalar=scv, in1=b.ap, op0=op0, op1=op1), reads, o.deps)

    def ts(eng, o, a, s1, op0, s2=None, op1=None):
        reads = list(a.deps)
        if isinstance(s1, V):
            reads += s1.deps
            s1 = s1.ap
        if isinstance(s2, V):
            reads += s2.deps
            s2 = s2.ap
        if op1 is None:
            S.op(eng, lambda e: e.tensor_scalar(out=o.ap, in0=a.ap, scalar1=s1, scalar2=None, op0=op0), reads, o.deps)
        else:
            S.op(eng, lambda e: e.tensor_scalar(out=o.ap, in0=a.ap, scalar1=s1, scalar2=s2, op0=op0, op1=op1), reads, o.deps)

    def cp(eng, o, i, scale=None):
        if eng == "act":
            act(o, i, AF.Copy, scale=scale)
        else:
            S.op(eng, lambda e: e.tensor_copy(out=o.ap, in_=i.ap), i.deps, o.deps)

    def dma(q, o, i, semof):
        S.dma(q, lambda e: e.dma_start(out=o.ap, in_=i.ap), i.deps, o.deps, semof)

    def rmax(o, i):
        S.op("dve", lambda e: e.reduce_max(out=o.ap, in_=i.ap, axis=AX.X), i.deps, o.deps)

    def rsum(o, i):
        S.op("dve", lambda e: e.reduce_sum(out=o.ap, in_=i.ap, axis=AX.X), i.deps, o.deps)

    def recip(o, i):
        S.op("dve", lambda e: e.reciprocal(out=o.ap, in_=i.ap), i.deps, o.deps)

    def stage_end():
        S.barrier()
        S.reset_sems()

    cst = TT(A.alloc([6, 128], F32), "cst")
    dma("sp", cst.v(), V(consts_in), (cst.buf, 0))
    identb = TT(A.alloc([128], BF16), "identb")
    onesb = TT(A.alloc([128], BF16), "onesb")
    cp("dve", identb.v(), cst.v(None, np.s_[:, 0, :]))
    cp("dve", onesb.v(), cst.v(None, np.s_[:, 5, :]))
    Lf = cst.v(None, np.s_[:, 1, :])
    Lb = cst.v(None, np.s_[:, 2, :])
    Uf = cst.v(None, np.s_[:, 3, :])
    Ub = cst.v(None, np.s_[:, 4, :])
    ones_col = cst.v(None, np.s_[:, 5, 0:1])
    flag = TT(A.alloc([1], F32), "flag")
    dma("sp", flag.v(), V(flag_in), (flag.buf, 0))
    stat = TT(A.alloc([64], F32), "stat", 64)
    stat_rr = [0]

    def scol():
        i = stat_rr[0]
        stat_rr[0] = (i + 1) % 64
        return i

    def sv(i, w=1):
        return V(stat.ap[:, i:i + w], (stat.buf, tuple(range(i, i + w))))

    def tr(o, i):
        S.op("pe", lambda e: e.transpose(o.ap, i.ap, identb.ap), i.deps + [(identb.buf, None)], o.deps)

    def rstd_from(src, n, junk):
        c0, c1, c2 = scol(), scol(), scol()
        act(junk, src, AF.Square, accum=sv(c0))
        act(sv(c1), sv(c0), AF.Ln, scale=1.0 / n, bias=EPS)
        act(sv(c2), sv(c1), AF.Exp, scale=-0.5)
        return sv(c2)

    def wslab(dst, src_cols):
        dma("pool", dst, V(src_cols.rearrange("(kc p) n -> p kc n", p=128)), (dst.deps[0][0], 0))

    cs_p = TT(A.alloc([16], F32), "cs_p")

    def make_crep():
        crep = TT(A.alloc([16, 128], BF16), "crep")
        for kc in range(16):
            ts("dve", crep.v(None, np.s_[:, kc, :]), onesb.v(), cs_p.v(None, np.s_[:, kc:kc + 1]), ALU.mult)
        return crep

    def s0_units(ms, ncol, wsl, brow, gt_, mt, crep):
        k = 0
        nct = D // ncol
        for mi, m in enumerate(ms):
            sub = m // 3
            kind = m % 3
            g = gt_[mi % len(gt_)]
            if kind == 1:
                dma("sp", g.v(), V(gains[2 * sub:2 * sub + 1, :].partition_broadcast(128)), (g.buf, 0))
            if kind == 2:
                dma("sp", g.v(), V(gains[2 * sub + 1:2 * sub + 2, :].partition_broadcast(128)), (g.buf, 0))
            mtile = mt[mi % len(mt)]
            for ct in range(nct):
                c0 = m * D + ct * ncol
                w = wsl[k % len(wsl)]
                br = brow[k % len(brow)]
                k += 1
                wslab(w.v(None, np.s_[:, :, 0:ncol]), w_ada[:, c0:c0 + ncol])
                dma("pool", br.v(None, np.s_[:, 0:ncol]), V(b_ada[0:1, c0:c0 + ncol]), (br.buf, 0))
                bi = nb()
                for kc in range(16):
                    mm(bank(bi, ncol), crep.v(None, np.s_[:, kc, :]), w.v(None, np.s_[:, kc, 0:ncol]), start=(kc == 0), stop=False)
                mm(bank(bi, ncol), onesb.v(None, np.s_[0:1, :]), br.v(None, np.s_[:, 0:ncol]), start=False, stop=True)
                cs_ = np.s_[:, ct * ncol:(ct + 1) * ncol]
                o = mtile.v(None, cs_)
                if kind == 0:
                    cp("act", o, bank(bi, ncol))
                elif kind == 1:
                    stt("dve", o, bank(bi, ncol), 1.0, g.v(None, cs_), ALU.add, ALU.mult)
                else:
                    coef = 1.0 if sub == 1 else 0.5
                    stt("dve", o, bank(bi, ncol), coef, g.v(None, cs_), ALU.mult, ALU.mult)
                yield
            dma("sp", modS.v(m, np.s_[m]), mtile.v(), (mtile.buf, 0))
            yield

    S0_FIRST = [0, 1, 2, 3, 4]
    S0_LATE = [5, 6, 7, 8]

    def stage0():
        A.push()
        cv = TT(A.alloc([16], F32), "cv")
        wsl = [TT(A.alloc([16, 512], BF16), f"s0w{i}") for i in range(5)]
        brow = [TT(A.alloc([512], BF16, parts=1), f"s0b{i}") for i in range(5)]
        gt_ = [TT(A.alloc([D], F32), f"s0g{i}") for i in range(2)]
        mt = [TT(A.alloc([D], F32), f"s0m{i}") for i in range(2)]
        dma("sp", cv.v(), V(cvec), (cv.buf, 0))
        act(cs_p.v(), cv.v(), AF.Silu)
        crep = make_crep()
        for _ in s0_units(S0_FIRST if upto >= 2 else list(range(9)), 512, wsl, brow, gt_, mt, crep):
            pass
        stage_end()
        A.pop()

    def prenorm_tiles(*a, **kw):
        for _ in prenorm_gen(*a, **kw):
            pass

    def prenorm_gen(src, src_is_input, tiles, At, Bt, xt, hb, hT, hT_part_of, col_of, extra_reads=(), first_writes=()):
        n = len(tiles)

        def load(i):
            t_ = tiles[i]
            xv = xt[i % 2].v()
            if src_is_input:
                dma("sp", xv, V(src[t_ * 128:(t_ + 1) * 128, :]), (xt[i % 2].buf, 0))
            else:
                dma("sp", xv, src.v(t_, np.s_[t_ * 128:(t_ + 1) * 128, :]), (xt[i % 2].buf, 0))

        def s1(i):
            xv = xt[i % 2].v()
            hv = hb[i % 2].v()
            r = rstd_from(xv, D, hv)
            stt("dve", xv, xv, r, At.v(), ALU.mult, ALU.mult)
            tt("dve" if i % 2 == 0 else "pool", hv, xv, Bt.v(), ALU.add)

        def s2(i):
            t_ = tiles[i]
            c0 = col_of(t_)
            for half in range(2):
                bi = nb()
                pv = bank_bf(bi)
                for q in range(8):
                    kc = half * 8 + q
                    tr(V(pv[:, q * 128:(q + 1) * 128], (PB, bi)), hb[i % 2].v(None, np.s_[:, kc * 128:(kc + 1) * 128]))
                dst = V(hT.ap[:, half * 8:half * 8 + 8, c0:c0 + 128], (hT.buf, hT_part_of(t_)), *(first_writes if i == 0 else ()))
                srcv = V(pv.rearrange("p (a b) -> p a b", b=128), (PB, bi), *extra_reads)
                cp("act" if half == 0 else "dve", dst, srcv)

        load(0)
        if n > 1:
            load(1)
        s1(0)
        for i in range(n):
            if i + 1 < n:
                s1(i + 1)
            s2(i)
            if i + 2 < n:
                load(i + 2)
            yield

    def postnorm_tiles(ytile_of, tiles, Gt, res_src, res_is_input, dst, dst_is_output, xt, junk, add_eng="dve"):
        n = len(tiles)

        def load(i):
            t_ = tiles[i]
            xv = xt[i % 2].v()
            if res_is_input:
                dma("sp", xv, V(res_src[t_ * 128:(t_ + 1) * 128, :]), (xt[i % 2].buf, 0))
            else:
                dma("sp", xv, res_src.v(t_, np.s_[t_ * 128:(t_ + 1) * 128, :]), (xt[i % 2].buf, 0))

        load(0)
        if n > 1:
            load(1)
        for i, t_ in enumerate(tiles):
            yv = ytile_of(i)
            xv = xt[i % 2].v()
            r = rstd_from(yv, D, junk.v())
            stt("dve", yv, yv, r, Gt.v(), ALU.mult, ALU.mult)
            tt(add_eng if i % 2 == 1 else "dve", xv, yv, xv, ALU.add)
            if dst_is_output:
                dma("sp", V(dst[t_ * 128:(t_ + 1) * 128, :]), xv, (xt[i % 2].buf, 0))
            else:
                dma("sp", dst.v(t_, np.s_[t_ * 128:(t_ + 1) * 128, :]), xv, (xt[i % 2].buf, 0))
            if i + 2 < n:
                load(i + 2)

    def ffn_stage(src, src_is_input, dst, dst_is_output, mbase, w1, w2):
        A.push()
        FB = min(1024, T)
        NFB = T // FB
        TB = FB // 128
        HB = min(512, FB)
        NH = FB // HB
        At = TT(A.alloc([D], F32), "ffA")
        Bt = TT(A.alloc([D], F32), "ffB")
        Gt = TT(A.alloc([D], F32), "ffG")
        dma("sp", Bt.v(), modS.v(mbase, np.s_[mbase]), (Bt.buf, 0))
        dma("sp", At.v(), modS.v(mbase + 1, np.s_[mbase + 1]), (At.buf, 0))
        dma("sp", Gt.v(), modS.v(mbase + 2, np.s_[mbase + 2]), (Gt.buf, 0))
        Rw = A.alloc([8192], F32)
        Rbuf = Buf("ffR", 4)
        hT = TT(Rw.bitcast(BF16)[:, 0:16 * FB].rearrange("p (a b) -> p a b", b=FB), "ffhT", TB)
        yv_ap = Rw.rearrange("p (a b) -> p a b", b=D)
        aT = TT(A.alloc([NFC, FB], BF16), "ffaT", NFC)
        w1s = [TT(A.alloc([16, 2, 128], BF16), f"ffw1_{i}") for i in range(2)]
        w2s = [TT(A.alloc([4, 512], BF16), f"ffw2_{i}") for i in range(3)]
        xt = [TT(A.alloc([D], F32), f"ffx{i}") for i in range(2)]
        hb = [TT(A.alloc([D], BF16), f"ffh{i}") for i in range(2)]
        sg = [TT(A.alloc([512], F32), f"ffsg{i}") for i in range(2)]
        junk = TT(A.alloc([D], BF16), "ffjunk")
        kw2 = 0
        for fb in range(NFB):
            tiles = [fb * TB + i for i in range(TB)]

            def ld_w1(j):
                w = w1s[j % 2]
                dma("pool", w.v(None, np.s_[:, :, 0, :]), V(w1[:, j * 128:(j + 1) * 128].rearrange("(kc p) n -> p kc n", p=128)), (w.buf, 0))
                dma("pool", w.v(None, np.s_[:, :, 1, :]), V(w1[:, DFF + j * 128:DFF + (j + 1) * 128].rearrange("(kc p) n -> p kc n", p=128)), (w.buf, 0))

            ld_w1(0)
            ld_w1(1)
            prenorm_tiles(src, src_is_input, tiles, At, Bt, xt, hb, hT, lambda t_: t_ - fb * TB, lambda t_: (t_ - fb * TB) * 128, extra_reads=((Rbuf, None),), first_writes=((Rbuf, None),))
            for j in range(NFC):
                w = w1s[j % 2]
                pg = [nb() for _ in range(NH)]
                pu = [nb() for _ in range(NH)]
                for gi, pbs in ((0, pg), (1, pu)):
                    for kc in range(16):
                        for h_ in range(NH):
                            mm(bank(pbs[h_], HB), w.v(None, np.s_[:, kc, gi, :]), hT.v(None, np.s_[:, kc, h_ * HB:(h_ + 1) * HB]), start=(kc == 0), stop=(kc == 15))
                if j + 2 < NFC:
                    ld_w1(j + 2)
                for h_ in range(NH):
                    sgv = sg[h_ % 2].v(None, np.s_[:, 0:HB])
                    act(sgv, bank(pg[h_], HB), AF.Silu)
                    tt("dve", aT.v(j, np.s_[:, j, h_ * HB:(h_ + 1) * HB]), sgv, bank(pu[h_], HB), ALU.mult)
            for th in range(NH):
                nt4 = HB // 128
                for dt in range(4):
                    pys = [nb() for _ in range(nt4)]
                    j = 0
                    while j < NFC:
                        g = min(4, NFC - j)
                        w = w2s[kw2 % 3]
                        kw2 += 1
                        dma("pool", w.v(None, np.s_[:, 0:g, :]), V(w2[j * 128:(j + g) * 128, dt * 512:(dt + 1) * 512].rearrange("(a p) n -> p a n", p=128)), (w.buf, 0))
                        for a in range(g):
                            for t4 in range(nt4):
                                c0 = (th * nt4 + t4) * 128
                                mm(bank(pys[t4]), aT.v(j + a, np.s_[:, j + a, c0:c0 + 128]), w.v(None, np.s_[:, a, :]), start=(j + a == 0), stop=(j + a == NFC - 1))
                        j += g
                    for t4 in range(nt4):
                        cp("act", V(yv_ap[:, t4, dt * 512:(dt + 1) * 512], (Rbuf, t4)), bank(pys[t4]))
                tiles_h = [fb * TB + th * nt4 + t4 for t4 in range(nt4)]
                postnorm_tiles(lambda i: V(yv_ap[:, i, :], (Rbuf, i)), tiles_h, Gt, src, src_is_input, dst, dst_is_output, xt, junk,
                               add_eng=("pool" if th == NH - 1 else "dve"))
        stage_end()
        A.pop()

    def mixer_stages():
        BW = min(512, T)
        NBLK = T // BW
        A.push()
        cqnT = TT(A.alloc([4, T], BF16), "cqnT", NT)
        ckvT = TT(A.alloc([4, NK], BF16), "ckvT", NKT)
        krT = TT(A.alloc([NK], BF16), "krT", NKT + 1)
        GAT.buf = Buf("GAT", 16 * NBLK)
        GBT.buf = Buf("GBT", 16 * NBLK)
        OGT.buf = Buf("OGT", NT)
        OMT.buf = Buf("OMT", 16)
        QD.buf = Buf("QD", 4 * NT)
        KI.buf = Buf("KI", 4 * NT)
        KE.buf = Buf("KE", 4 * NT)
        ED.buf = Buf("ED", 4 * NT)

        A.push()
        h2T = TT(A.alloc([16, T], BF16), "h2T", NT)
        aT2 = [TT(A.alloc([T], BF16), "aTf"), TT(A.alloc([T], BF16), "aTb")]

        A.push()
        At = TT(A.alloc([D], F32), "m1A")
        Bt = TT(A.alloc([D], F32), "m1B")
        dma("sp", Bt.v(), modS.v(3, np.s_[3]), (Bt.buf, 0))
        dma("sp", At.v(), modS.v(4, np.s_[4]), (At.buf, 0))
        xt = [TT(A.alloc([D], F32), f"m1x{i}") for i in range(2)]
        hb = [TT(A.alloc([D], BF16), f"m1h{i}") for i in range(2)]
        Wsm = TT(A.alloc([16, 1120], BF16), "Wsm")
        wslab(Wsm.v(), w_inp[:, 6144:7264])
        qnb = TT(A.alloc([512], F32), "qnb")
        kvnb = TT(A.alloc([512], F32), "kvnb")
        dma("sp", qnb.v(), V(q_norm.partition_broadcast(128)), (qnb.buf, 0))
        dma("sp", kvnb.v(), V(kv_norm.partition_broadcast(128)), (kvnb.buf, 0))
        junk5 = TT(A.alloc([512], BF16), "junk5")
        cqb = [TT(A.alloc([512], BF16), f"cqb{i}") for i in range(2)]
        ckf = [TT(A.alloc([512], F32), f"ckf{i}") for i in range(2)]
        ckb = [TT(A.alloc([512], BF16), f"ckb{i}") for i in range(2)]
        krf = [TT(A.alloc([64], F32), f"krf{i}") for i in range(2)]
        krb = [TT(A.alloc([64], BF16), f"krb{i}") for i in range(2)]
        kt1 = [TT(A.alloc([64], F32), f"kt1{i}") for i in range(2)]
        kt2 = [TT(A.alloc([64], F32), f"kt2{i}") for i in range(2)]
        cosk = [TT(A.alloc([64], F32), f"cosk{i}") for i in range(2)]
        sink = [TT(A.alloc([64], F32), f"sink{i}") for i in range(2)]
        dma("pool", krT.v(NKT, np.s_[65:81, :]), V(mk_in), (krT.buf, NKT))
        for c2 in range(2):
            k = c2 % 2
            dma("sp", ckf[k].v(), V(cckv[c2 * 128:(c2 + 1) * 128, :]), (ckf[k].buf, 0))
            cp("act", ckb[k].v(), ckf[k].v())
            b = nb()
            for q in range(4):
                tr(V(bank_bf(b)[:, q * 128:(q + 1) * 128], (PB, b)), ckb[k].v(None, np.s_[:, q * 128:(q + 1) * 128]))
            cp("dve", ckvT.v(c2, np.s_[:, 0:4, c2 * 128:(c2 + 1) * 128]),
               V(bank_bf(b)[:, 0:512].rearrange("p (a b) -> p a b", b=128), (PB, b)))
            dma("sp", krf[k].v(), V(ckr[c2 * 128:(c2 + 1) * 128, :]), (krf[k].buf, 0))
            cp("act", krb[k].v(), krf[k].v())
            b = nb()
            tr(V(bank_bf(b)[0:64, 0:128], (PB, b)), krb[k].v())
            cp("dve", krT.v(c2, np.s_[0:64, c2 * 128:(c2 + 1) * 128]), V(bank_bf(b)[0:64, 0:128], (PB, b)))
        pgen = prenorm_gen(X1, False, list(range(NT)), At, Bt, xt, hb, h2T, lambda t_: t_, lambda t_: t_ * 128)
        def m1a_A(t_):
            k = t_ % 2
            tc = np.s_[t_ * 128:(t_ + 1) * 128]
            dma("sp", cosk[k].v(), V(cosK_in[tc, :]), (cosk[k].buf, 0))
            dma("sp", sink[k].v(), V(sinK_in[tc, :]), (sink[k].buf, 0))
            b1 = nb()
            for kc in range(16):
                mm(bank(b1), h2T.v(t_, np.s_[:, kc, tc]), Wsm.v(None, np.s_[:, kc, 32:544]), start=(kc == 0), stop=(kc == 15))
            b3 = nb()
            for kc in range(16):
                mm(bank(b3), h2T.v(t_, np.s_[:, kc, tc]), Wsm.v(None, np.s_[:, kc, 544:1056]), start=(kc == 0), stop=(kc == 15))
            b5 = nb()
            for kc in range(16):
                mm(bank(b5, 64), h2T.v(t_, np.s_[:, kc, tc]), Wsm.v(None, np.s_[:, kc, 1056:1120]), start=(kc == 0), stop=(kc == 15))
            r = rstd_from(bank(b1), 512, junk5.v())
            stt("dve", cqb[k].v(), bank(b1), r, qnb.v(), ALU.mult, ALU.mult)
            r = rstd_from(bank(b3), 512, junk5.v())
            stt("dve", ckf[k].v(), bank(b3), r, kvnb.v(), ALU.mult, ALU.mult)
            dma("sp", V(nckv_out[tc, :]), ckf[k].v(), (ckf[k].buf, 0))
            cp("act", ckb[k].v(), ckf[k].v())
            cp("act", krf[k].v(), bank(b5, 64))
            dma("sp", V(nkr_out[tc, :]), krf[k].v(), (krf[k].buf, 0))
            tt("dve", kt1[k].v(), krf[k].v(), cosk[k].v(), ALU.mult)
            x4 = krf[k].ap.rearrange("p (a h j) -> p a h j", a=2, h=2)
            s4 = sink[k].ap.rearrange("p (a h j) -> p a h j", a=2, h=2)
            o4 = kt2[k].ap.rearrange("p (a h j) -> p a h j", a=2, h=2)
            for hh in range(2):
                tt("dve", V(o4[:, :, hh, :], (kt2[k].buf, None)), V(x4[:, :, 1 - hh, :], (krf[k].buf, None)),
                   V(s4[:, :, hh, :], (sink[k].buf, None)), ALU.mult)
            tt("dve", krb[k].v(), kt1[k].v(), kt2[k].v(), ALU.add)

        def m1a_B(t_):
            k = t_ % 2
            tc = np.s_[t_ * 128:(t_ + 1) * 128]
            kc0 = 256 + t_ * 128
            b2 = nb()
            for q in range(4):
                tr(V(bank_bf(b2)[:, q * 128:(q + 1) * 128], (PB, b2)), cqb[k].v(None, np.s_[:, q * 128:(q + 1) * 128]))
            for q in range(4):
                tr(V(bank_bf(b2)[:, 512 + q * 128:512 + (q + 1) * 128], (PB, b2)), ckb[k].v(None, np.s_[:, q * 128:(q + 1) * 128]))
            b6 = nb()
            tr(V(bank_bf(b6)[0:64, 0:128], (PB, b6)), krb[k].v())
            cp("act", cqnT.v(t_, np.s_[:, 0:4, tc]), V(bank_bf(b2)[:, 0:512].rearrange("p (a b) -> p a b", b=128), (PB, b2)))
            cp("dve", ckvT.v(2 + t_, np.s_[:, 0:4, kc0:kc0 + 128]), V(bank_bf(b2)[:, 512:1024].rearrange("p (a b) -> p a b", b=128), (PB, b2)))
            cp("dve", krT.v(2 + t_, np.s_[0:64, kc0:kc0 + 128]), V(bank_bf(b6)[0:64, 0:128], (PB, b6)))

        for _ in pgen:
            pass
        m1a_A(0)
        for t_ in range(NT):
            if t_ + 1 < NT:
                m1a_A(t_ + 1)
            m1a_B(t_)
        for blk in range(NBLK):
            cols = np.s_[blk * BW:(blk + 1) * BW]
            tparts = tuple(range(blk * (BW // 128), (blk + 1) * (BW // 128)))
            for d_ in range(2):
                b = nb()
                for kc in range(16):
                    mm(bank(b, BW, rows=16), Wsm.v(None, np.s_[:, kc, d_ * 16:(d_ + 1) * 16]),
                       V(h2T.ap[:, kc, cols], (h2T.buf, tparts)), start=(kc == 0), stop=(kc == 15))
                cp("act", aT2[d_].v(None, np.s_[0:16, cols]), bank(b, BW, rows=16))
        stage_end()
        A.pop()

        A.push()
        Wqk = TT(A.alloc([16, 1024], BF16), "Wqk", 2)
        wal = TT(A.alloc([2, 1024], BF16), "wal")
        bal = TT(A.alloc([2, 1024], BF16), "bal")
        dma("pool", wal.v(None, np.s_[0:16]), V(w_alpha.rearrange("d r n -> r d n")), (wal.buf, 0))
        dma("pool", bal.v(None, np.s_[0:1]), V(b_alpha.rearrange("(o d) n -> o d n", o=1)), (bal.buf, 0))
        qf = [TT(A.alloc([512], F32), f"qf{i}") for i in range(3)]
        kf = [TT(A.alloc([512], F32), f"kf{i}") for i in range(3)]
        ef = TT(A.alloc([512], F32), "ef")
        spf = [TT(A.alloc([512], F32), f"spf{i}") for i in range(6)]
        ebf = [TT(A.alloc([512], F32), f"ebf{i}") for i in range(2)]
        eif = [TT(A.alloc([512], F32), f"eif{i}") for i in range(2)]
        eef = [TT(A.alloc([512], F32), f"eef{i}") for i in range(2)]
        qdb = [TT(A.alloc([512], BF16), f"qdb{i}") for i in range(4)]
        kib = [TT(A.alloc([512], BF16), f"kib{i}") for i in range(4)]
        keb = [TT(A.alloc([512], BF16), f"keb{i}") for i in range(2)]
        qdT = [TT(A.alloc([4, 128], BF16), f"qdT{i}") for i in range(2)]
        kiT = [TT(A.alloc([4, 128], BF16), f"kiT{i}") for i in range(2)]
        edc = [TT(A.alloc([4], F32), f"edc{i}") for i in range(2)]
        items = [(ch, t_) for ch in range(2) for t_ in range(NT)]

        def ld_wqk(ch):
            c0 = ch * 512
            dma("pool", Wqk.v(0, np.s_[:, :, 0:512]), V(w_inp[:, c0:c0 + 512].rearrange("(kc p) n -> p kc n", p=128)), (Wqk.buf, 0))
            dma("pool", Wqk.v(1, np.s_[:, :, 512:1024]), V(w_inp[:, 1024 + c0:1024 + c0 + 512].rearrange("(kc p) n -> p kc n", p=128)), (Wqk.buf, 1))

        def p1(n):
            ch, t_ = items[n]
            c0 = ch * 512
            if t_ == 0:
                ld_wqk(ch)
            k = n % 3
            tc = np.s_[t_ * 128:(t_ + 1) * 128]
            bq = nb()
            for kc in range(16):
                mm(bank(bq), h2T.v(t_, np.s_[:, kc, tc]), Wqk.v(0, np.s_[:, kc, 0:512]), start=(kc == 0), stop=(kc == 15))
            cp("act", qf[k].v(), bank(bq), scale=1.0 / 16.0)
            bk = nb()
            for kc in range(16):
                mm(bank(bk), h2T.v(t_, np.s_[:, kc, tc]), Wqk.v(1, np.s_[:, kc, 512:1024]), start=(kc == 0), stop=(kc == 15))
            cp("act", kf[k].v(), bank(bk))
            for d_ in range(2):
                sp_ = spf[(n % 3) * 2 + d_]
                bz = nb()
                mm(bank(bz), aT2[d_].v(None, np.s_[0:16, tc]), wal.v(None, np.s_[0:16, d_, c0:c0 + 512]), start=True, stop=False)
                mm(bank(bz), onesb.v(None, np.s_[0:1, :]), bal.v(None, np.s_[0:1, d_, c0:c0 + 512]), start=False, stop=True)
                act(ef.v(), bank(bz), AF.Exp, scale=-1.0)
                act(sp_.v(), ef.v(), AF.Ln, bias=1.0)

        def p2(n):
            ch, t_ = items[n]
            c0 = ch * 512
            k = n % 3
            for d_ in range(2):
                sp_ = spf[(n % 3) * 2 + d_]
                kk = d_
                k4 = (n % 2) * 2 + d_
                part = (d_ * NT + t_) * 2 + ch
                Lm = Lf if d_ == 0 else Lb
                Um = Uf if d_ == 0 else Ub
                bB = nb()
                mm(bank(bB), Lm, sp_.v())
                bE = nb()
                mm(bank(bE), Um, sp_.v())
                bl = nb()
                for c in range(4):
                    mm(bank(bl, 1, off=c), sp_.v(None, np.s_[:, c * 128:(c + 1) * 128]), ones_col)
                act(ebf[kk].v(), bank(bB), AF.Exp, scale=-1.0 / 16.0)
                act(eif[kk].v(), bank(bB), AF.Exp, scale=1.0 / 16.0)
                act(eef[kk].v(), bank(bE), AF.Exp, scale=-1.0 / 16.0)
                act(edc[kk].v(), bank(bl, 4), AF.Exp, scale=-1.0 / 16.0)
                dma("sp", ED.v(part, np.s_[d_, t_, :, ch * 4:(ch + 1) * 4]), edc[kk].v(), (edc[kk].buf, 0))
                tt("dve", qdb[k4].v(), qf[k].v(), ebf[kk].v(), ALU.mult)
                tt("dve", kib[k4].v(), kf[k].v(), eif[kk].v(), ALU.mult)
                tt("pool", keb[kk].v(), kf[k].v(), eef[kk].v(), ALU.mult)
                dma("sp", KE.v(part, np.s_[d_, t_, :, c0:c0 + 512]), keb[kk].v(), (keb[kk].buf, 0))

        def p3(n):
            ch, t_ = items[n]
            for d_ in range(2):
                kk = d_
                k4 = (n % 2) * 2 + d_
                part = (d_ * NT + t_) * 2 + ch
                for (srcb, dstT, DR, eng) in ((qdb[k4], qdT[kk], QD, "act"), (kib[k4], kiT[kk], KI, "dve")):
                    bt = nb()
                    for q in range(4):
                        tr(V(bank_bf(bt)[:, q * 128:(q + 1) * 128], (PB, bt)), srcb.v(None, np.s_[:, q * 128:(q + 1) * 128]))
                    cp(eng, dstT.v(), V(bank_bf(bt)[:, 0:512].rearrange("p (a b) -> p a b", b=128), (PB, bt)))
                    dma("sp", DR.v(part, np.s_[d_, t_, :, ch * 4:(ch + 1) * 4, :]), dstT.v(), (dstT.buf, 0))

        NI = len(items)
        for n in range(NI + 2):
            if n < NI:
                p1(n)
            if 0 <= n - 1 < NI:
                p2(n - 1)
            if 0 <= n - 2 < NI:
                p3(n - 2)
        stage_end()
        A.pop()

        A.push()
        wsl = [TT(A.alloc([16, 512], BF16), f"m1cw{i}") for i in range(2)]
        vb = [TT(A.alloc([512], BF16), f"m1cv{i}") for i in range(4)]
        gw = [TT(A.alloc([16, 128], BF16), f"m1cg{i}") for i in range(2)]
        gtb = [TT(A.alloc([BW], BF16), f"m1cgt{i}") for i in range(4)]
        bg_w = [TT(A.alloc([16, 256], BF16), f"bgw{i}") for i in range(2)]
        bg_b = [TT(A.alloc([256], BF16, parts=1), f"bgb{i}") for i in range(2)]
        bg_g = [TT(A.alloc([D], F32), "bgg")]
        bg_m = [TT(A.alloc([D], F32), "bgm0")]
        bg = s0_units(S0_LATE, 256, bg_w, bg_b, bg_g, bg_m, make_crep())
        kq = 0
        kv_ = 0
        for (cbase, DR, fn) in ((2048, VV, AF.Copy), (4096, RR, AF.Silu)):
            for s4 in range(4):
                w = wsl[kq % 2]
                kq += 1
                wslab(w.v(), w_inp[:, cbase + s4 * 512:cbase + (s4 + 1) * 512])
                for t_ in range(NT):
                    tc = np.s_[t_ * 128:(t_ + 1) * 128]
                    b = nb()
                    for kc in range(16):
                        mm(bank(b), h2T.v(t_, np.s_[:, kc, tc]), w.v(None, np.s_[:, kc, :]), start=(kc == 0), stop=(kc == 15))
                    o = vb[kv_ % 4]
                    kv_ += 1
                    if fn == AF.Copy and (kv_ % 2 == 0):
                        cp("dve", o.v(), bank(b))
                    else:
                        act(o.v(), bank(b), fn)
                    dma("sp", DR.v(t_, np.s_[tc, s4 * 512:(s4 + 1) * 512]), o.v(), (o.buf, 0))
                    if t_ % 4 == 3:
                        next(bg, None)
        kg = 0
        ko = 0
        for (cbase, DR) in ((7264, GAT), (9312, GBT)):
            for dc in range(16):
                w = gw[kg % 2]
                kg += 1
                wslab(w.v(), w_inp[:, cbase + dc * 128:cbase + (dc + 1) * 128])
                for blk in range(NBLK):
                    cols = np.s_[blk * BW:(blk + 1) * BW]
                    tparts = tuple(range(blk * (BW // 128), (blk + 1) * (BW // 128)))
                    b = nb()
                    for kc in range(16):
                        mm(bank(b, BW), w.v(None, np.s_[:, kc, :]), V(h2T.ap[:, kc, cols], (h2T.buf, tparts)), start=(kc == 0), stop=(kc == 15))
                    o = gtb[ko % 4]
                    ko += 1
                    act(o.v(), bank(b, BW), AF.Sigmoid)
                    dma("sp", DR.v(dc * NBLK + blk, np.s_[dc, :, cols]), o.v(), (o.buf, 0))
                next(bg, None)
        for _ in bg:
            pass
        stage_end()
        A.pop()
        A.pop()

        A.push()
        Stl = [TT(A.alloc([8, 512], F32), f"St{i}", 8) for i in range(2)]
        edf = TT(A.alloc([8], F32), "edf")
        Sbf = TT(A.alloc([8, 512], BF16), "Sbf", 8)
        qd = [TT(A.alloc([8, 128], BF16), f"qd{i}") for i in range(3)]
        ki = [TT(A.alloc([8, 128], BF16), f"ki{i}") for i in range(3)]
        ke = [TT(A.alloc([1024], BF16), f"ke{i}") for i in range(3)]
        vt = [TT(A.alloc([D], BF16), f"vt{i}") for i in range(3)]
        ed = [TT(A.alloc([8], F32), f"ed{i}") for i in range(3)]
        ATs = [TT(A.alloc([512], BF16), f"ATs{i}") for i in range(2)]
        Acp = [TT(A.alloc([512], BF16), f"Acp{i}") for i in range(2)]
        maskf = [TT(A.alloc([512], F32), f"maskf{i}") for i in range(2)]
        for d_ in range(2):
            for h in range(4):
                cp("dve", maskf[d_].v(None, np.s_[:, h * 128:(h + 1) * 128]), Lf if d_ == 0 else Lb)
        ot = [TT(A.alloc([D], F32), f"ot{i}", 4) for i in range(2)]
        oft = [TT(A.alloc([D], F32), f"oft{i}") for i in range(3)]
        rt = [TT(A.alloc([D], BF16), f"rt{i}") for i in range(4)]
        ogb = [TT(A.alloc([D], BF16), f"ogb{i}", 4) for i in range(2)]
        ogTt = [TT(A.alloc([16, 128], BF16), f"ogTt{i}") for i in range(2)]
        gnb = TT(A.alloc([512], F32), "gnb")
        dma("sp", gnb.v(), V(gla_norm.partition_broadcast(128)), (gnb.buf, 0))
        tmpf = [TT(A.alloc([512], F32), f"m2tmp{i}") for i in range(4)]
        junk5 = TT(A.alloc([512], BF16), "m2junk")
        for d_ in range(2):
            cur = 0
            cross = False
            dma("sp", Stl[cur].v(None), V(s0[d_].rearrange("h (kc p) v -> p (h kc) v", p=128)), (Stl[cur].buf, 0))
            cp("act", Sbf.v(None), Stl[cur].v(None))
            order = list(range(NT)) if d_ == 0 else list(range(NT - 1, -1, -1))

            def load(i):
                t_ = order[i]
                k = i % 3
                tc = np.s_[t_ * 128:(t_ + 1) * 128]
                pr = ((d_ * NT + t_) * 2, (d_ * NT + t_) * 2 + 1)
                dma("sp", qd[k].v(), V(QD.ap[d_, t_], (QD.buf, pr)), (qd[k].buf, 0))
                dma("sp", ki[k].v(), V(KI.ap[d_, t_], (KI.buf, pr)), (ki[k].buf, 0))
                dma("sp", ke[k].v(), V(KE.ap[d_, t_], (KE.buf, pr)), (ke[k].buf, 0))
                dma("sp", ed[k].v(), V(ED.ap[d_, t_], (ED.buf, pr)), (ed[k].buf, 0))
                dma("sp", vt[k].v(), VV.v(t_, np.s_[tc, :]), (vt[k].buf, 0))
                if d_ == 1:
                    dma("sp", oft[k].v(), OF.v(t_, np.s_[tc, :]), (oft[k].buf, 0))
                    dma("sp", rt[i % 4].v(), RR.v(t_, np.s_[tc, :]), (rt[i % 4].buf, 0))

            load(0)
            if NT > 1:
                load(1)
            def emitA(i):
                k = i % 3
                k2 = i % 2
                ba = nb()
                for h in range(4):
                    for kc in range(2):
                        mm(bank(ba, 128, off=h * 128), ki[k].v(None, np.s_[:, h * 2 + kc, :]), qd[k].v(None, np.s_[:, h * 2 + kc, :]), start=(kc == 0), stop=(kc == 1))
                tt("dve", ATs[k2].v(), bank(ba), maskf[d_].v(), ALU.mult)

            emitA(0)
            kcp = 0
            pend_fin = []
            for i, t_ in enumerate(order):
                k = i % 3
                k2 = i % 2
                tc = np.s_[t_ * 128:(t_ + 1) * 128]
                if i + 2 < NT:
                    load(i + 2)
                if i + 1 < NT:
                    emitA(i + 1)
                for hp in range(2):
                    bss = {}
                    for h in (2 * hp, 2 * hp + 1):
                        hc = np.s_[:, h * 512:(h + 1) * 512]
                        for kc in range(2):
                            c = h * 2 + kc
                            bs = nb()
                            bss[c] = bs
                            mm(bank(bs), ke[k].v(None, np.s_[:, c * 128:(c + 1) * 128]), vt[k].v(None, hc))
                    for h in (2 * hp, 2 * hp + 1):
                        hc = np.s_[:, h * 512:(h + 1) * 512]
                        bo = nb()
                        mm(bank(bo), ATs[k2].v(None, np.s_[:, h * 128:(h + 1) * 128]), vt[k].v(None, hc), start=True, stop=False)
                        for kc in range(2):
                            c = h * 2 + kc
                            mm(bank(bo), qd[k].v(None, np.s_[:, c, :]), Sbf.v(c, np.s_[:, c, :]), start=False, stop=(kc == 1))
                        if d_ == 0:
                            cp("act", ot[k2].v(h, hc), bank(bo))
                        else:
                            tt("dve", ot[k2].v(h, hc), bank(bo), oft[k].v(None, hc), ALU.add)
                    for h in (2 * hp, 2 * hp + 1):
                        for kc in range(2):
                            c = h * 2 + kc
                            Ssrc = Stl[cur]
                            Sdst = Stl[1 - cur] if cross else Stl[cur]
                            edv = edf.v(None, np.s_[:, c:c + 1]) if cross else ed[k].v(None, np.s_[:, c:c + 1])
                            stt("dve", Sdst.v(c, np.s_[:, c, :]), Ssrc.v(c, np.s_[:, c, :]), edv, bank(bss[c]), ALU.mult, ALU.add)
                            cp("act" if (d_ == 1 or kcp % 2 == 0) else "dve", Sbf.v(c, np.s_[:, c, :]), Sdst.v(c, np.s_[:, c, :]))
                            kcp += 1
                if d_ == 0:
                    dma("act", OF.v(t_, np.s_[tc, :]), ot[k2].v(None), (ot[k2].buf, 0))
                else:
                    while pend_fin:
                        pend_fin.pop(0)()
                    rs_ = [rstd_from(ot[k2].v(h, np.s_[:, h * 512:(h + 1) * 512]), 512, junk5.v()) for h in range(4)]

                    def fin(k2=k2, k4=i % 4, t_=t_, tc=tc, rs_=rs_):
                        for h in range(4):
                            hc = np.s_[:, h * 512:(h + 1) * 512]
                            tm = tmpf[h % 4]
                            stt("dve", tm.v(), ot[k2].v(h, hc), rs_[h], gnb.v(), ALU.mult, ALU.mult)
                            tt("pool", ogb[k2].v(h, hc), tm.v(), rt[k4].v(None, hc), ALU.mult)
                        for half in range(2):
                            bt = nb()
                            for q in range(8):
                                kc = half * 8 + q
                                tr(V(bank_bf(bt)[:, q * 128:(q + 1) * 128], (PB, bt)), ogb[k2].v(kc // 4, np.s_[:, kc * 128:(kc + 1) * 128]))
                            cp("act" if half == 0 else "dve", ogTt[k2].v(None, np.s_[:, half * 8:half * 8 + 8, :]),
                               V(bank_bf(bt).rearrange("p (a b) -> p a b", b=128), (PB, bt)))
                        dma("pool", OGT.v(t_, np.s_[:, :, tc]), ogTt[k2].v(), (ogTt[k2].buf, 0))
                    pend_fin.append(fin)
                end_seg = (t_ % 2 == 1) if d_ == 0 else (t_ % 2 == 0)
                if cross:
                    cur = 1 - cur
                    cross = False
                if end_seg:
                    seg = t_ // 2
                    dst = (sf_out if d_ == 0 else sb_out)[seg].rearrange("h (kc p) v -> p (h kc) v", p=128)
                    dma("act", V(dst), Stl[cur].v(None), (Stl[cur].buf, 1))
                    if i != NT - 1:
                        ts("dve", edf.v(), ed[(i + 1) % 3].v(), flag.v(None, np.s_[:, 0:1]), ALU.mult)
                        act(Sbf.v(None), Stl[cur].v(None), AF.Copy, scale=flag.v(None, np.s_[:, 0:1]))
                        cross = True
        while pend_fin:
            pend_fin.pop(0)()
        stage_end()
        A.pop()

        A.push()
        QB = min(512, T)
        NQB = T // QB
        NQ4 = QB // 128
        kblocks = []
        ks = 0
        while ks < NK:
            kblocks.append((ks, min(512, NK - ks)))
            ks += 512
        omT = TT(A.alloc([16, T], BF16), "omT", 16)
        cosT = TT(A.alloc([T], F32), "cosT")
        sinT = TT(A.alloc([T], F32), "sinT")
        dma("sp", cosT.v(None, np.s_[0:64]), V(cosT_in), (cosT.buf, 0))
        dma("sp", sinT.v(None, np.s_[0:64]), V(sinT_in), (sinT.buf, 0))
        QrT = [TT(A.alloc([T], BF16), f"QrT{i}", 2 + NT) for i in range(2)]
        for i in range(2):
            dma("pool", QrT[i].v(1, np.s_[65:81, :]), V(mq_in), (QrT[i].buf, 1))
        S.op("dve", lambda e: e.memset(krT.ap[64:65, :], 1.0), [], [(krT.buf, NKT)])
        QnT = [TT(A.alloc([T], BF16), f"QnT{i}") for i in range(2)]
        KT = [TT(A.alloc([NK], BF16), f"KT{i}") for i in range(2)]
        Vh = [TT(A.alloc([NKT, 129], BF16), f"Vh{i}") for i in range(2)]
        for i in range(2):
            S.op("dve", lambda e, i=i: e.memset(Vh[i].ap[:, :, 128:129], 1.0), [], [(Vh[i].buf, None)])
        sel64 = TT(A.alloc([65], BF16), "sel64")
        S.op("dve", lambda e: e.memset(sel64.ap, 0.0), [], [(sel64.buf, None)])
        S.op("dve", lambda e: e.memset(sel64.ap[:, 64:65], 1.0), [], [(sel64.buf, None)])
        wkv = [TT(A.alloc([4, 256], BF16), f"wkv{i}") for i in range(2)]
        wq = [TT(A.alloc([4, 192], BF16), f"wq{i}") for i in range(2)]
        wqs = [TT(A.alloc([4, 64], BF16), f"wqs{i}") for i in range(2)]
        NPT = 4
        PT = [TT(A.alloc([QB], BF16), f"PT{i}") for i in range(NPT)]
        obt = [TT(A.alloc([128], BF16), f"obt{i}") for i in range(4)]
        dgb = [TT(A.alloc([128], BF16), f"dgb{i}") for i in range(4)]
        rt1 = TT(A.alloc([QB], F32), "rt1")
        rt2 = TT(A.alloc([QB], F32), "rt2")
        n4 = [0]
        n2 = [0]

        def nbs():
            i = n4[0]
            n4[0] = (i + 1) % 3
            return i

        def nbm():
            i = n2[0]
            n2[0] = (i + 1) % 4
            return i

        MB = 3

        def prep(h):
            k = h % 2
            dma("pool", wkv[k].v(), V(w_ukv[:, h * 256:(h + 1) * 256].rearrange("(kc p) n -> p kc n", p=128)), (wkv[k].buf, 0))
            dma("pool", wq[k].v(), V(w_uq[:, h * 192:(h + 1) * 192].rearrange("(kc p) n -> p kc n", p=128)), (wq[k].buf, 0))
            for a in range(2):
                for hh in range(2):
                    o0 = a * 32 + hh * 16
                    i0 = 128 + a * 32 + (1 - hh) * 16
                    cp("pool", wqs[k].v(None, np.s_[:, :, o0:o0 + 16]), wq[k].v(None, np.s_[:, :, i0:i0 + 16]))
            yield
            for (ks, kw) in kblocks:
                b = MB
                kparts = tuple(range(ks // 128, (ks + kw) // 128))
                for kc in range(4):
                    mm(bank(b, kw), wkv[k].v(None, np.s_[:, kc, 0:128]), V(ckvT.ap[:, kc, ks:ks + kw], (ckvT.buf, kparts)), start=(kc == 0), stop=(kc == 3))
                cp("dve", KT[k].v(None, np.s_[:, ks:ks + kw]), bank(b, kw))
                yield
            for g0 in range(0, NKT, 4):
                g = min(4, NKT - g0)
                b = MB
                for a in range(g):
                    kt_ = g0 + a
                    for kc in range(4):
                        mm(bank(b, 128, off=a * 128), ckvT.v(kt_, np.s_[:, kc, kt_ * 128:(kt_ + 1) * 128]), wkv[k].v(None, np.s_[:, kc, 128:256]), start=(kc == 0), stop=(kc == 3))
                cp("dve", Vh[k].v(None, np.s_[:, g0:g0 + g, 0:128]), V(ps[:, b * 512:b * 512 + g * 128].rearrange("p (a b) -> p a b", b=128), (PB, b)))
                yield
            for qb in range(NQB):
                cols = np.s_[qb * QB:(qb + 1) * QB]
                tparts = tuple(range(qb * NQ4, (qb + 1) * NQ4))
                b = MB
                for kc in range(4):
                    mm(bank(b, QB), wq[k].v(None, np.s_[:, kc, 0:128]), V(cqnT.ap[:, kc, cols], (cqnT.buf, tparts)), start=(kc == 0), stop=(kc == 3))
                cp("dve", QnT[k].v(None, np.s_[:, cols]), bank(b, QB))
                yield
                for kc in range(4):
                    mm(bank(b, QB, rows=64), wq[k].v(None, np.s_[:, kc, 128:192]), V(cqnT.ap[:, kc, cols], (cqnT.buf, tparts)), start=(kc == 0), stop=(kc == 3))
                tt("dve", rt1.v(None, np.s_[0:64, :]), bank(b, QB, rows=64), cosT.v(None, np.s_[0:64, cols]), ALU.mult)
                yield
                for kc in range(4):
                    mm(bank(b, QB, rows=64), wqs[k].v(None, np.s_[:, kc, :]), V(cqnT.ap[:, kc, cols], (cqnT.buf, tparts)), start=(kc == 0), stop=(kc == 3))
                tt("dve", rt2.v(None, np.s_[0:64, :]), bank(b, QB, rows=64), sinT.v(None, np.s_[0:64, cols]), ALU.mult)
                tt("dve", QrT[k].v(0, np.s_[0:64, cols]), rt1.v(None, np.s_[0:64, :]), rt2.v(None, np.s_[0:64, :]), ALU.add)
                yield
            for qb in range(NQB):
                cols = np.s_[qb * QB:(qb + 1) * QB]
                b = MB
                for q4 in range(NQ4):
                    qt = qb * NQ4 + q4
                    tc = np.s_[qt * 128:(qt + 1) * 128]
                    kd = np.s_[(2 + qt) * 128:(3 + qt) * 128]
                    mm(bank(b, 128, off=q4 * 128), QnT[k].v(None, np.s_[:, tc]), KT[k].v(None, np.s_[:, kd]), start=True, stop=False)
                    mm(bank(b, 128, off=q4 * 128), QrT[k].v(0, np.s_[0:64, tc]), krT.v(2 + qt, np.s_[0:64, kd]), start=False, stop=True)
                c0 = [scol() for _ in range(NQ4)]
                while c0[-1] != c0[0] + NQ4 - 1:
                    c0 = [scol() for _ in range(NQ4)]
                S.op("dve", lambda e, b=b, c=c0[0]: e.reduce_max(out=stat.ap[:, c:c + NQ4], in_=ps[:, b * 512:b * 512 + NQ4 * 128].rearrange("p (a b) -> p a b", b=128), axis=AX.X),
                     [(PB, b)], [(stat.buf, tuple(c0))])
                for q4 in range(NQ4):
                    ts("dve", dgb[q4].v(), identb.v(), sv(c0[q4]), ALU.mult, -1.0, ALU.mult)
                yield
                for q4 in range(NQ4):
                    mm(bank(b, 128, rows=65, off=q4 * 128), sel64.v(), dgb[q4].v())
                sparts = tuple(2 + qb * NQ4 + i for i in range(NQ4))
                cp("dve", V(QrT[k].ap[64:65, cols], (QrT[k].buf, sparts)), V(ps[64:65, b * 512:b * 512 + QB], (PB, b)))
                yield

        def run_all(gen):
            for _ in gen:
                pass

        kpt = 0
        LAG = 2
        run_all(prep(0))
        for h in range(16):
            k = h % 2
            bg = prep(h + 1) if h + 1 < 16 else iter(())
            tiles = [(qb, kt_) for qb in range(NQB) for kt_ in range(NKT)]
            pbuf = {}
            deferred = []

            def emit_qk(i):
                qb, kt_ = tiles[i]
                cols = np.s_[qb * QB:(qb + 1) * QB]
                qparts = (0, 1) + tuple(2 + qb * NQ4 + q for q in range(NQ4))
                kcs = np.s_[kt_ * 128:(kt_ + 1) * 128]
                b = nbs()
                mm(bank(b, QB), KT[k].v(None, np.s_[:, kcs]), QnT[k].v(None, np.s_[:, cols]), start=True, stop=False)
                mm(bank(b, QB), V(krT.ap[0:81, kcs], (krT.buf, (kt_, NKT))), V(QrT[k].ap[0:81, cols], (QrT[k].buf, qparts)), start=False, stop=True)
                pbuf[i] = b

            def emit_pv(i):
                nonlocal kpt
                qb, kt_ = tiles[i]
                b = pbuf.pop(i)
                p_ = PT[kpt % NPT]
                kpt += 1
                act(p_.v(), bank(b, QB), AF.Exp, scale=ATT_SCALE)
                for q4 in range(NQ4):
                    ob_ = 4 + q4
                    mm(V(ps[:, ob_ * 512:ob_ * 512 + 129], (PB, ob_)),
                       p_.v(None, np.s_[:, q4 * 128:(q4 + 1) * 128]), Vh[k].v(None, np.s_[:, kt_, :]),
                       start=(kt_ == 0), stop=(kt_ == NKT - 1))
                if kt_ == NKT - 1:
                    obs = []
                    for q4 in range(NQ4):
                        ob_ = 4 + q4
                        o0 = ob_ * 512
                        c_ri = scol()
                        recip(sv(c_ri), V(ps[:, o0 + 128:o0 + 129], (PB, ob_)))
                        ts("dve", obt[q4].v(), V(ps[:, o0:o0 + 128], (PB, ob_)), sv(c_ri), ALU.mult)

                    def fin(qb=qb):
                        for q4 in range(NQ4):
                            tr(V(bank_bf(MB)[:, q4 * 128:(q4 + 1) * 128], (PB, MB)), obt[q4].v())
                        c0_ = qb * QB
                        cp("dve", V(omT.ap[:, h, c0_:c0_ + QB], (omT.buf, h)), V(bank_bf(MB)[:, 0:QB], (PB, MB)))
                    deferred.append((i + 5, fin))

            n = len(tiles)
            for i in range(n + LAG):
                if i < n:
                    emit_qk(i)
                j = i - LAG
                if j >= 0:
                    emit_pv(j)
                while deferred and deferred[0][0] <= i:
                    deferred.pop(0)[1]()
                if i % 2 == 1:
                    next(bg, None)
            while deferred:
                deferred.pop(0)[1]()
            run_all(bg)
        for h in range(16):
            dma("sp", OMT.v(h, np.s_[:, h, :]), omT.v(h, np.s_[:, h, :]), (omT.buf, h))
        stage_end()
        A.pop()
        A.pop()

        A.push()
        FB = min(1024, T)
        NFB = T // FB
        HB = min(512, FB)
        NH = FB // HB
        nt4 = HB // 128
        Rw = A.alloc([8192], F32)
        Rbuf = Buf("m4R", 4)
        ogT_ap = Rw.bitcast(BF16)[:, 0:16 * FB].rearrange("p (a b) -> p a b", b=FB)
        y_ap = Rw.rearrange("p (a b) -> p a b", b=D)
        O2w = A.alloc([8192], F32)
        omTb = TT(O2w.bitcast(BF16)[:, 0:16 * FB].rearrange("p (a b) -> p a b", b=FB), "omTb", 4)
        y2_ap = O2w.rearrange("p (a b) -> p a b", b=D)
        mT = TT(A.alloc([16, FB], BF16), "mT", 16)
        wg = [TT(A.alloc([16, 128], BF16), f"wg{i}") for i in range(2)]
        wm = [TT(A.alloc([16, 128], BF16), f"wm{i}") for i in range(2)]
        wo = [TT(A.alloc([16, 512], BF16), f"wo{i}") for i in range(2)]
        gaT = [TT(A.alloc([FB], BF16), f"gaT{i}") for i in range(2)]
        gbT = [TT(A.alloc([FB], BF16), f"gbT{i}") for i in range(2)]
        G2 = TT(A.alloc([D], F32), "G2")
        dma("sp", G2.v(), modS.v(5, np.s_[5]), (G2.buf, 0))
        xt = [TT(A.alloc([D], F32), f"m4x{i}") for i in range(2)]
        junk = TT(A.alloc([D], BF16), "m4junk")
        t1 = [TT(A.alloc([HB], F32), f"m4t1{i}") for i in range(2)]
        t2 = [TT(A.alloc([HB], F32), f"m4t2{i}") for i in range(2)]
        kwo = 0
        for fb in range(NFB):
            cols = np.s_[fb * FB:(fb + 1) * FB]
            ttiles = tuple(range(fb * (FB // 128), (fb + 1) * (FB // 128)))
            gparts_of = lambda dc: tuple(dc * NBLK + bb for bb in range(fb * (FB // BW), (fb + 1) * (FB // BW)))
            dma("sp", V(ogT_ap, (Rbuf, None)), V(OGT.ap[:, :, cols], (OGT.buf, ttiles)), (Rbuf, 0))
            dma("sp", omTb.v(), V(OMT.ap[:, :, cols], (OMT.buf, None)), (omTb.buf, 0))
            for dc in range(16):
                k = dc % 2
                wslab(wg[k].v(), w_go[:, dc * 128:(dc + 1) * 128])
                wslab(wm[k].v(), w_mo[:, dc * 128:(dc + 1) * 128])
                dma("sp", gaT[k].v(), V(GAT.ap[dc, :, cols], (GAT.buf, gparts_of(dc))), (gaT[k].buf, 0))
                dma("sp", gbT[k].v(), V(GBT.ap[dc, :, cols], (GBT.buf, gparts_of(dc))), (gbT[k].buf, 0))
                for hh in range(NH):
                    hcs = np.s_[hh * HB:(hh + 1) * HB]
                    bg = nb()
                    for kc in range(16):
                        mm(bank(bg, HB), wg[k].v(None, np.s_[:, kc, :]), V(ogT_ap[:, kc, hcs], (Rbuf, None)), start=(kc == 0), stop=(kc == 15))
                    bm = nb()
                    for kc in range(16):
                        mm(bank(bm, HB), wm[k].v(None, np.s_[:, kc, :]), omTb.v(None, np.s_[:, kc, hcs]), start=(kc == 0), stop=(kc == 15))
                    tt("dve", t1[hh % 2].v(), bank(bg, HB), gaT[k].v(None, np.s_[:, hcs]), ALU.mult)
                    tt("dve", t2[hh % 2].v(), bank(bm, HB), gbT[k].v(None, np.s_[:, hcs]), ALU.mult)
                    tt("dve", mT.v(dc, np.s_[:, dc, hcs]), t1[hh % 2].v(), t2[hh % 2].v(), ALU.add)
            TBk = FB // 128

            def ytile(t8, sl=np.s_[:]):
                if t8 < 4:
                    return V(y_ap[:, t8, sl], (Rbuf, t8))
                return V(y2_ap[:, t8 - 4, sl], (omTb.buf, t8 - 4))

            for ct in range(4):
                w = wo[kwo % 2]
                kwo += 1
                wslab(w.v(), w_o[:, ct * 512:(ct + 1) * 512])
                pys = [nb() for _ in range(TBk)]
                for t8 in range(TBk):
                    c0 = t8 * 128
                    for kc in range(16):
                        mm(bank(pys[t8]), mT.v(kc, np.s_[:, kc, c0:c0 + 128]), w.v(None, np.s_[:, kc, :]), start=(kc == 0), stop=(kc == 15))
                    cp("act" if t8 % 2 == 0 else "dve", ytile(t8, np.s_[ct * 512:(ct + 1) * 512]), bank(pys[t8]))
            tiles_b = [fb * TBk + t8 for t8 in range(TBk)]
            postnorm_tiles(lambda i: ytile(i), tiles_b, G2, X1, False, X2, False, xt, junk, add_eng="pool")
        stage_end()
        A.pop()

    stage0()
    if upto >= 1:
        ffn_stage(x_in, True, X1, False, 0, w_f1i, w_f1o)
    if upto >= 2:
        mixer_stages()
    if upto >= 3:
        ffn_stage(X2, False, y_out, True, 6, w_f2i, w_f2o)
    S.barrier()
    S.emit(nc, block, esems, dsems)
    print("ops:", S.stats(), "arena max", A.off)
    es.close()
    return nc


WNAMES = ["w_ada", "b_ada", "norm_gains", "w_ffn1_in", "w_ffn1_out", "w_ffn2_in", "w_ffn2_out", "w_in",
          "w_gla_alpha", "b_gla_alpha", "gla_norm", "w_gla_out", "q_norm", "kv_norm", "w_uq", "w_ukv",
          "w_mla_out", "w_out"]


def consts_array():
    s = np.arange(128)[:, None]
    t = np.arange(128)[None, :]
    c = np.zeros((128, 6, 128), np.float32)
    c[:, 0] = (s == t)
    c[:, 1] = (s <= t)
    c[:, 2] = (s >= t)
    c[:, 3] = (s > t)
    c[:, 4] = (s < t)
    c[:, 5] = 1.0
    return c


def rope_tables(T, real):
    cosT = np.ones((64, T), np.float32)
    sinT = np.zeros((64, T), np.float32)
    if real:
        t = np.arange(T)
        pos = [(t // 64).astype(np.float32), (t % 64).astype(np.float32)]
        inv = (np.float32(10000.0) ** (-np.arange(0, 32, 2, dtype=np.float32) / np.float32(32))).astype(np.float32)
        for a in range(2):
            ang = pos[a][None, :] * inv[:, None]
            for hh in range(2):
                r0 = a * 32 + hh * 16
                cosT[r0:r0 + 16] = np.cos(ang)
                sinT[r0:r0 + 16] = np.sin(ang) * (-1.0 if hh == 0 else 1.0)
    return cosT, sinT, np.ascontiguousarray(cosT.T), np.ascontiguousarray(sinT.T)


def mask_rows(NSEG, kind):
    T = NSEG * 256
    NK = T + 256
    mq = np.zeros((16, T), np.float32)
    mk = np.zeros((16, NK), np.float32)
    for m in range(NSEG + 1):
        mk[m, m * 256:(m + 1) * 256] = 1.0
    if kind == "P":
        for s in range(NSEG):
            mq[:NSEG + 1, s * 256:(s + 1) * 256] = NEG_BIG
            mq[1 + s, s * 256:(s + 1) * 256] = 0.0
    return mq, mk


def core_inputs(full, NSEG, kind):
    T = NSEG * 256
    m = {}
    m["x"] = np.ascontiguousarray(full["x"], dtype=np.float32)
    m["cvec"] = np.ascontiguousarray(full["c"].reshape(16, 128).T)
    m["cckv"] = np.ascontiguousarray(full["cache_ckv"])
    m["ckr"] = np.ascontiguousarray(full["cache_krope"])
    m["s0"] = np.ascontiguousarray(np.stack([full["s0f"], full["s0b"]], axis=0))
    m["flag"] = np.full((128, 1), 1.0 if kind == "S" else 0.0, np.float32)
    cosT, sinT, cosK, sinK = rope_tables(T, kind == "S")
    m["cosT"], m["sinT"], m["cosK"], m["sinK"] = cosT, sinT, cosK, sinK
    mq, mk = mask_rows(NSEG, kind)
    m["mq"], m["mk"] = mq, mk
    m["consts"] = consts_array()
    for k in WNAMES:
        m[k] = full[k]
    return m


def make_test_inputs(rng, NSEG, kind):
    T = NSEG * 256
    f32 = np.float32

    def nrm(shape, scale):
        return (rng.standard_normal(shape, dtype=f32) * f32(scale)).astype(f32)

    DFF = 5504
    full = {
        "x": nrm((T, D), 1.0),
        "c": nrm((D,), 1.0),
        "w_ada": nrm((D, 9 * D), 0.5 * D ** -0.5),
        "b_ada": nrm((1, 9 * D), 0.01),
        "norm_gains": 1.0 + nrm((6, D), 0.05),
        "w_ffn1_in": nrm((D, 2 * DFF), D ** -0.5),
        "w_ffn1_out": nrm((DFF, D), DFF ** -0.5),
        "w_ffn2_in": nrm((D, 2 * DFF), D ** -0.5),
        "w_ffn2_out": nrm((DFF, D), DFF ** -0.5),
        "w_in": nrm((D, 11360), D ** -0.5),
        "w_gla_alpha": nrm((2, 16, 1024), 16 ** -0.5),
        "b_gla_alpha": nrm((2, 1024), 0.1),
        "gla_norm": 1.0 + nrm((1, 512), 0.05),
        "w_gla_out": nrm((D, D), D ** -0.5),
        "q_norm": 1.0 + nrm((1, 512), 0.05),
        "kv_norm": 1.0 + nrm((1, 512), 0.05),
        "w_uq": nrm((512, 3072), 512 ** -0.5),
        "w_ukv": nrm((512, 4096), 512 ** -0.5),
        "w_mla_out": nrm((D, D), D ** -0.5),
        "w_out": nrm((D, D), D ** -0.5),
    }
    if kind == "S":
        full["cache_ckv"] = nrm((256, 512), 1.0)
        full["cache_krope"] = nrm((256, 64), 1.0)
        full["s0f"] = nrm((4, 256, 512), 0.5)
        full["s0b"] = nrm((4, 256, 512), 0.5)
    else:
        full["cache_ckv"] = np.zeros((256, 512), f32)
        full["cache_krope"] = np.zeros((256, 64), f32)
        full["s0f"] = np.zeros((4, 256, 512), f32)
        full["s0b"] = np.zeros((4, 256, 512), f32)
    return full


_NC_CACHE = {}


def kernel(**inputs):
    NSEG = 8
    f32 = np.float32
    inp = {k: np.asarray(v) for k, v in inputs.items()}
    W = {
        "w_ada": inp["w_ada"][0], "b_ada": inp["b_ada"], "norm_gains": inp["norm_gains"][0],
        "w_ffn1_in": inp["w_ffn1_in"][0], "w_ffn1_out": inp["w_ffn1_out"][0],
        "w_ffn2_in": inp["w_ffn2_in"][0], "w_ffn2_out": inp["w_ffn2_out"][0],
        "w_in": inp["w_in"][0], "w_gla_alpha": inp["w_gla_alpha"][0], "b_gla_alpha": inp["b_gla_alpha"][0],
        "gla_norm": inp["gla_norm"], "w_gla_out": inp["w_gla_out"][0],
        "q_norm": inp["q_norm"], "kv_norm": inp["kv_norm"], "w_uq": inp["w_uq"][0], "w_ukv": inp["w_ukv"][0],
        "w_mla_out": inp["w_mla_out"][0], "w_out": inp["w_out"][0],
    }
    W = {k: np.ascontiguousarray(v, dtype=f32) for k, v in W.items()}
    in_maps = []
    for core in range(8):
        if core < 4:
            b = core
            full = dict(x=inp["x_sample"][b], c=inp["c"][b], cache_ckv=inp["cache_ckv"][b, 0],
                        cache_krope=inp["cache_krope"][b, 0], s0f=inp["state_gla_fwd"][b, 0],
                        s0b=inp["state_gla_bwd"][b, 0], **W)
            in_maps.append(core_inputs(full, NSEG, "S"))
        else:
            s0_ = 4 * (core - 4)
            xp = inp["x_prompt"][s0_:s0_ + 4].reshape(1024, D)
            x = np.concatenate([xp, np.zeros((1024, D), f32)], axis=0)
            full = dict(x=x, c=inp["c_ctx"], cache_ckv=np.zeros((256, 512), f32),
                        cache_krope=np.zeros((256, 64), f32), s0f=np.zeros((4, 256, 512), f32),
                        s0b=np.zeros((4, 256, 512), f32), **W)
            in_maps.append(core_inputs(full, NSEG, "P"))
    if NSEG not in _NC_CACHE:
        _NC_CACHE[NSEG] = build(NSEG)
    nc = _NC_CACHE[NSEG]
    res = run_bass_kernel_spmd(nc, in_maps, core_ids=list(range(8)))
    R = res.results
    y_prompt = np.zeros((16, 256, D), f32)
    y_sample = np.zeros((4, 2048, D), f32)
    new_ckv = np.zeros((16, 1, 256, 512), f32)
    new_krope = np.zeros((16, 1, 256, 64), f32)
    new_sf = np.zeros((16, 1, 4, 256, 512), f32)
    new_sb = np.zeros((16, 1, 4, 256, 512), f32)
    for core in range(8):
        r = R[core]
        if core < 4:
            y_sample[core] = np.asarray(r["y"])
        else:
            s0_ = 4 * (core - 4)
            y = np.asarray(r["y"]); ck = np.asarray(r["nckv"]); kr = np.asarray(r["nkr"])
            sf = np.asarray(r["sf"]); sb = np.asarray(r["sb"])
            for s in range(4):
                y_prompt[s0_ + s] = y[s * 256:(s + 1) * 256]
                new_ckv[s0_ + s, 0] = ck[s * 256:(s + 1) * 256]
                new_krope[s0_ + s, 0] = kr[s * 256:(s + 1) * 256]
                new_sf[s0_ + s, 0] = sf[s]
                new_sb[s0_ + s, 0] = sb[s]
    return (y_prompt, y_sample, new_ckv, new_krope, new_sf, new_sb)
```

```python
import numpy as np
from contextlib import ExitStack
import concourse.bass as bass
import concourse.mybir as mybir
from concourse.bass_utils import run_bass_kernel_spmd

F32 = mybir.dt.float32
BF16 = mybir.dt.bfloat16
AF = mybir.ActivationFunctionType
ALU = mybir.AluOpType
AX = mybir.AxisListType

ENGINES = ["pe", "act", "dve", "pool", "sp"]


class Buf:
    def __init__(self, name, nparts=1):
        self.name = name
        self.n = nparts
        self.w = [None] * nparts
        self.r = [[] for _ in range(nparts)]
        self.sem = [None] * nparts

    def parts(self, p):
        if p is None:
            return range(self.n)
        if isinstance(p, int):
            return (p,)
        return p


class V:
    def __init__(self, ap, *deps):
        self.ap = ap
        self.deps = list(deps)


class _Op:
    __slots__ = ("fn", "waits", "is_dma", "dsem", "clock", "marked")

    def __init__(self, fn, waits, is_dma, dsem):
        self.fn = fn
        self.waits = waits
        self.is_dma = is_dma
        self.dsem = dsem
        self.clock = None
        self.marked = False


class Sched:
    def __init__(self, n_dma_sems=80):
        self.ops = {e: [] for e in ENGINES}
        self.known = {e: {} for e in ENGINES}
        self.n_dma = n_dma_sems
        self.dma_count = [0] * n_dma_sems
        self.n_sw = 24
        self.dma_next_sw = 0
        self.dma_next = self.n_sw
        self.dma_clock = {}
        self.snap = {e: None for e in ENGINES}
        self.sem_bufs = []

    def _sem_for(self, buf, part, queue):
        if buf.sem[part] is None:
            buf.sem[part] = {}
            self.sem_bufs.append(buf)
        d = buf.sem[part]
        kind = "sw" if queue == "pool" else "hw"
        if kind not in d:
            if kind == "sw":
                if self.dma_next_sw >= self.n_sw:
                    raise RuntimeError("out of sw dma semaphores")
                d[kind] = self.dma_next_sw
                self.dma_next_sw += 1
            else:
                if self.dma_next >= self.n_dma:
                    raise RuntimeError("out of hw dma semaphores")
                d[kind] = self.dma_next
                self.dma_next += 1
        return d[kind]

    def _collect(self, engine, reads, writes):
        deps = {}

        def add(ev):
            if ev is None:
                return
            k, v = ev
            if k == "pe" and engine == "pe":
                return
            if deps.get(k, -1) < v:
                deps[k] = v

        for b, p in reads:
            for i in b.parts(p):
                add(b.w[i])
        for b, p in writes:
            for i in b.parts(p):
                add(b.w[i])
                for ev in b.r[i]:
                    add(ev)
        return deps

    def _update(self, ev, reads, writes):
        for b, p in reads:
            for i in b.parts(p):
                lst = b.r[i]
                for j, (k, v) in enumerate(lst):
                    if k == ev[0]:
                        lst[j] = ev
                        break
                else:
                    lst.append(ev)
        for b, p in writes:
            for i in b.parts(p):
                b.w[i] = ev
                b.r[i] = []

    def _merge_clock(self, engine, clock):
        kn = self.known[engine]
        for k, v in clock.items():
            if kn.get(k, -1) < v:
                kn[k] = v
        self.snap[engine] = None

    def _snapshot(self, engine):
        if self.snap[engine] is None:
            self.snap[engine] = dict(self.known[engine])
        return self.snap[engine]

    def _make_waits(self, engine, deps):
        waits = []
        kn = self.known[engine]
        for k, v in deps.items():
            if isinstance(k, tuple):
                if kn.get(k, -1) >= v:
                    continue
                v = max(v, self.dma_count[k[1]])
                waits.append((k, v))
                kn[k] = v
                self.snap[engine] = None
                clk = self.dma_clock.get((k[1], v))
                if clk:
                    self._merge_clock(engine, clk)
            else:
                if kn.get(k, -1) >= v:
                    continue
                waits.append((k, v))
                kn[k] = v
                self.snap[engine] = None
                op = self.ops[k][v]
                op.marked = True
                if op.clock:
                    self._merge_clock(engine, op.clock)
        return waits

    def op(self, engine, fn, reads=(), writes=()):
        deps = self._collect(engine, reads, writes)
        waits = self._make_waits(engine, deps)
        o = _Op(fn, waits, False, None)
        o.clock = self._snapshot(engine)
        idx = len(self.ops[engine])
        self.ops[engine].append(o)
        self._update((engine, idx), reads, writes)
        return idx

    def dma(self, queue, fn, reads, writes, semof):
        deps = self._collect(queue, reads, writes)
        waits = self._make_waits(queue, deps)
        si = self._sem_for(semof[0], semof[1], queue)
        self.dma_count[si] += 16
        val = self.dma_count[si]
        o = _Op(fn, waits, True, si)
        self.ops[queue].append(o)
        self.dma_clock[(si, val)] = self._snapshot(queue)
        self._update((("d", si), val), reads, writes)

    def barrier(self):
        last = {}
        for e in ("pe", "act", "dve", "pool"):
            for i in range(len(self.ops[e]) - 1, -1, -1):
                if not self.ops[e][i].is_dma and self.ops[e][i].fn is not None:
                    last[e] = i
                    break
        for e in ENGINES:
            deps = {}
            for f, i in last.items():
                if f != e:
                    deps[f] = i
                elif e in ("act", "dve", "pool"):
                    deps[f] = i
            for si in range(self.n_dma):
                if self.dma_count[si] > 0:
                    deps[("d", si)] = self.dma_count[si]
            waits = self._make_waits(e, deps)
            if waits:
                o = _Op(None, waits, False, None)
                o.clock = self._snapshot(e)
                self.ops[e].append(o)

    def reset_sems(self):
        self.dma_next = self.n_sw
        self.dma_next_sw = 0
        for b in self.sem_bufs:
            b.sem = [None] * b.n
        self.sem_bufs = []

    def emit(self, nc, block, esems, dsems):
        vals = {}
        for e in ENGINES:
            c = 0
            m = {}
            for i, o in enumerate(self.ops[e]):
                if o.marked:
                    c += 1
                    m[i] = c
            vals[e] = m

        def run(e, eng):
            for i, o in enumerate(self.ops[e]):
                for k, v in o.waits:
                    if isinstance(k, tuple):
                        eng.wait_ge(dsems[k[1]], v)
                    else:
                        eng.wait_ge(esems[k], vals[k][v])
                if o.fn is None:
                    continue
                ins = o.fn(eng)
                if o.is_dma:
                    ins.then_inc(dsems[o.dsem], 16)
                elif o.marked:
                    ins.then_inc(esems[e], 1)

        @block.tensor
        def _(eng):
            run("pe", eng)

        @block.scalar
        def _(eng):
            run("act", eng)

        @block.vector
        def _(eng):
            run("dve", eng)

        @block.gpsimd
        def _(eng):
            run("pool", eng)

        @block.sync
        def _(eng):
            run("sp", eng)

    def stats(self):
        return {e: len(self.ops[e]) for e in ENGINES}


class Arena:
    def __init__(self, t, nwords):
        self.t = t
        self.n = nwords
        self.off = 0
        self.marks = []

    def alloc(self, free_shape, dtype, parts=128):
        nel = int(np.prod(free_shape))
        nbytes = nel * (2 if dtype == BF16 else 4)
        nw = (nbytes + 3) // 4
        nw = (nw + 15) // 16 * 16
        if self.off + nw > self.n:
            raise RuntimeError(f"arena overflow: need {nw} words at {self.off} of {self.n}")
        ap = self.t[0:parts, self.off:self.off + nw]
        self.off += nw
        if dtype == BF16:
            ap = ap.bitcast(BF16)[:, 0:nel]
        else:
            ap = ap[:, 0:nel]
        if len(free_shape) == 2:
            ap = ap.rearrange("p (a b) -> p a b", b=free_shape[1])
        elif len(free_shape) == 3:
            ap = ap.rearrange("p (a b c) -> p a b c", b=free_shape[1], c=free_shape[2])
        return ap

    def push(self):
        self.marks.append(self.off)

    def pop(self):
        self.off = self.marks.pop()


D = 2048
DFF = 5504
NFC = 43
ATT_SCALE = 192 ** -0.5
EPS = 1e-6
NEG_BIG = -30000.0
ARENA_WORDS = 53184


class TT:
    def __init__(self, ap, name, nparts=1):
        self.ap = ap
        self.buf = Buf(name, nparts)

    def v(self, part=None, idx=None):
        ap = self.ap if idx is None else self.ap[idx]
        return V(ap, (self.buf, part))


def build(NSEG, upto=99, debug=False):
    T = NSEG * 256
    NT = T // 128
    NK = T + 256
    NKT = NK // 128
    nc = bass.Bass("TRN2", target_bir_lowering=False)

    def din(name, shape):
        return nc.dram_tensor(name, list(shape), F32, kind="ExternalInput").ap()

    def dout(name, shape):
        return nc.dram_tensor(name, list(shape), F32, kind="ExternalOutput").ap()

    def dscr(name, shape, dt):
        if debug:
            return nc.dram_tensor(name, list(shape), dt, kind="ExternalOutput").ap()
        return nc.dram_tensor(name, list(shape), dt).ap()

    x_in = din("x", [T, D])
    cvec = din("cvec", [128, 16])
    cckv = din("cckv", [256, 512])
    ckr = din("ckr", [256, 64])
    s0 = din("s0", [2, 4, 256, 512])
    flag_in = din("flag", [128, 1])
    cosT_in = din("cosT", [64, T])
    sinT_in = din("sinT", [64, T])
    cosK_in = din("cosK", [T, 64])
    sinK_in = din("sinK", [T, 64])
    mq_in = din("mq", [16, T])
    mk_in = din("mk", [16, NK])
    consts_in = din("consts", [128, 6, 128])
    w_ada = din("w_ada", [D, 9 * D])
    b_ada = din("b_ada", [1, 9 * D])
    gains = din("norm_gains", [6, D])
    w_f1i = din("w_ffn1_in", [D, 2 * DFF])
    w_f1o = din("w_ffn1_out", [DFF, D])
    w_f2i = din("w_ffn2_in", [D, 2 * DFF])
    w_f2o = din("w_ffn2_out", [DFF, D])
    w_inp = din("w_in", [D, 11360])
    w_alpha = din("w_gla_alpha", [2, 16, 1024])
    b_alpha = din("b_gla_alpha", [2, 1024])
    gla_norm = din("gla_norm", [1, 512])
    w_go = din("w_gla_out", [D, D])
    q_norm = din("q_norm", [1, 512])
    kv_norm = din("kv_norm", [1, 512])
    w_uq = din("w_uq", [512, 3072])
    w_ukv = din("w_ukv", [512, 4096])
    w_mo = din("w_mla_out", [D, D])
    w_o = din("w_out", [D, D])

    y_out = dout("y", [T, D])
    nckv_out = dout("nckv", [T, 512])
    nkr_out = dout("nkr", [T, 64])
    sf_out = dout("sf", [NSEG, 4, 256, 512])
    sb_out = dout("sb", [NSEG, 4, 256, 512])

    modS = TT(dscr("modS", [9, 128, D], F32), "modS", 9)
    X1 = TT(dscr("X1", [T, D], F32), "X1", NT)
    X2 = TT(dscr("X2", [T, D], F32), "X2", NT)
    QD = TT(dscr("QD", [2, NT, 128, 8, 128], BF16), "QD", 2 * NT)
    KI = TT(dscr("KI", [2, NT, 128, 8, 128], BF16), "KI", 2 * NT)
    KE = TT(dscr("KE", [2, NT, 128, 1024], BF16), "KE", 2 * NT)
    ED = TT(dscr("ED", [2, NT, 128, 8], F32), "ED", 2 * NT)
    VV = TT(dscr("VV", [T, D], BF16), "VV", NT)
    RR = TT(dscr("RR", [T, D], BF16), "RR", NT)
    GAT = TT(dscr("GAT", [16, 128, T], BF16), "GAT", 1)
    GBT = TT(dscr("GBT", [16, 128, T], BF16), "GBT", 1)
    OF = TT(dscr("OF", [T, D], F32), "OF", NT)
    OGT = TT(dscr("OGT", [128, 16, T], BF16), "OGT", 1)
    OMT = TT(dscr("OMT", [128, 16, T], BF16), "OMT", 1)

    S = Sched(90)
    es = ExitStack()
    arena_t = es.enter_context(nc.sbuf_tensor("arena", [128, ARENA_WORDS], F32))
    ps = es.enter_context(nc.psum_tensor("ps", [128, 4096], F32))
    esems = {e: es.enter_context(nc.semaphore(f"s_{e}")) for e in ["pe", "act", "dve", "pool"]}
    dsems = [es.enter_context(nc.semaphore(f"d{i}")) for i in range(90)]
    block = es.enter_context(nc.Block())
    A = Arena(arena_t, ARENA_WORDS)
    PB = Buf("psum", 8)
    bank_rr = [0]

    def nb():
        i = bank_rr[0]
        bank_rr[0] = (i + 1) % 8
        return i

    def bank(i, w=512, rows=128, off=0):
        return V(ps[0:rows, i * 512 + off:i * 512 + off + w], (PB, i))

    def bank_bf(i):
        return ps[:, i * 512:(i + 1) * 512].bitcast(BF16)

    def mm(o, l, r, start=True, stop=True):
        S.op("pe", lambda e: e.matmul(o.ap, lhsT=l.ap, rhs=r.ap, start=start, stop=stop), l.deps + r.deps, o.deps)

    def act(o, i, func, bias=None, scale=None, accum=None):
        kw = {}
        reads = list(i.deps)
        writes = list(o.deps)
        if bias is not None:
            if isinstance(bias, V):
                kw["bias"] = bias.ap
                reads += bias.deps
            else:
                kw["bias"] = bias
        if scale is not None:
            if isinstance(scale, V):
                kw["scale"] = scale.ap
                reads += scale.deps
            else:
                kw["scale"] = scale
        if accum is not None:
            kw["accum_out"] = accum.ap
            writes += accum.deps
        S.op("act", lambda e: e.activation(out=o.ap, in_=i.ap, func=func, **kw), reads, writes)

    def tt(eng, o, a, b, op):
        S.op(eng, lambda e: e.tensor_tensor(out=o.ap, in0=a.ap, in1=b.ap, op=op), a.deps + b.deps, o.deps)

    def stt(eng, o, a, sc, b, op0, op1):
        reads = a.deps + b.deps
        if isinstance(sc, V):
            reads = reads + sc.deps
            scv = sc.ap
        else:
            scv = sc
        S.op(eng, lambda e: e.scalar_tensor_tensor(out=o.ap, in0=a.ap, scalar=scv, in1=b.ap, op0=op0, op1=op1), reads, o.deps)

    def ts(eng, o, a, s1, op0, s2=None, op1=None):
        reads = list(a.deps)
        if isinstance(s1, V):
            reads += s1.deps
            s1 = s1.ap
        if isinstance(s2, V):
            reads += s2.deps
            s2 = s2.ap
        if op1 is None:
            S.op(eng, lambda e: e.tensor_scalar(out=o.ap, in0=a.ap, scalar1=s1, scalar2=None, op0=op0), reads, o.deps)
        else:
            S.op(eng, lambda e: e.tensor_scalar(out=o.ap, in0=a.ap, scalar1=s1, scalar2=s2, op0=op0, op1=op1), reads, o.deps)

    def cp(eng, o, i, scale=None):
        if eng == "act":
            act(o, i, AF.Copy, scale=scale)
        else:
            S.op(eng, lambda e: e.tensor_copy(out=o.ap, in_=i.ap), i.deps, o.deps)

    def dma(q, o, i, semof):
        S.dma(q, lambda e: e.dma_start(out=o.ap, in_=i.ap), i.deps, o.deps, semof)

    def rmax(o, i):
        S.op("dve", lambda e: e.reduce_max(out=o.ap, in_=i.ap, axis=AX.X), i.deps, o.deps)

    def rsum(o, i):
        S.op("dve", lambda e: e.reduce_sum(out=o.ap, in_=i.ap, axis=AX.X), i.deps, o.deps)

    def recip(o, i):
        S.op("dve", lambda e: e.reciprocal(out=o.ap, in_=i.ap), i.deps, o.deps)

    def stage_end():
        S.barrier()
        S.reset_sems()

    cst = TT(A.alloc([6, 128], F32), "cst")
    dma("sp", cst.v(), V(consts_in), (cst.buf, 0))
    identb = TT(A.alloc([128], BF16), "identb")
    onesb = TT(A.alloc([128], BF16), "onesb")
    cp("dve", identb.v(), cst.v(None, np.s_[:, 0, :]))
    cp("dve", onesb.v(), cst.v(None, np.s_[:, 5, :]))
    Lf = cst.v(None, np.s_[:, 1, :])
    Lb = cst.v(None, np.s_[:, 2, :])
    Uf = cst.v(None, np.s_[:, 3, :])
    Ub = cst.v(None, np.s_[:, 4, :])
    ones_col = cst.v(None, np.s_[:, 5, 0:1])
    flag = TT(A.alloc([1], F32), "flag")
    dma("sp", flag.v(), V(flag_in), (flag.buf, 0))
    stat = TT(A.alloc([64], F32), "stat", 64)
    stat_rr = [0]

    def scol():
        i = stat_rr[0]
        stat_rr[0] = (i + 1) % 64
        return i

    def sv(i, w=1):
        return V(stat.ap[:, i:i + w], (stat.buf, tuple(range(i, i + w))))

    def tr(o, i):
        S.op("pe", lambda e: e.transpose(o.ap, i.ap, identb.ap), i.deps + [(identb.buf, None)], o.deps)

    def rstd_from(src, n, junk):
        c0, c1, c2 = scol(), scol(), scol()
        act(junk, src, AF.Square, accum=sv(c0))
        act(sv(c1), sv(c0), AF.Ln, scale=1.0 / n, bias=EPS)
        act(sv(c2), sv(c1), AF.Exp, scale=-0.5)
        return sv(c2)

    def wslab(dst, src_cols):
        dma("pool", dst, V(src_cols.rearrange("(kc p) n -> p kc n", p=128)), (dst.deps[0][0], 0))

    cs_p = TT(A.alloc([16], F32), "cs_p")

    def make_crep():
        crep = TT(A.alloc([16, 128], BF16), "crep")
        for kc in range(16):
            ts("dve", crep.v(None, np.s_[:, kc, :]), onesb.v(), cs_p.v(None, np.s_[:, kc:kc + 1]), ALU.mult)
        return crep

    def s0_units(ms, ncol, wsl, brow, gt_, mt, crep):
        k = 0
        nct = D // ncol
        for mi, m in enumerate(ms):
            sub = m // 3
            kind = m % 3
            g = gt_[mi % len(gt_)]
            if kind == 1:
                dma("sp", g.v(), V(gains[2 * sub:2 * sub + 1, :].partition_broadcast(128)), (g.buf, 0))
            if kind == 2:
                dma("sp", g.v(), V(gains[2 * sub + 1:2 * sub + 2, :].partition_broadcast(128)), (g.buf, 0))
            mtile = mt[mi % len(mt)]
            for ct in range(nct):
                c0 = m * D + ct * ncol
                w = wsl[k % len(wsl)]
                br = brow[k % len(brow)]
                k += 1
                wslab(w.v(None, np.s_[:, :, 0:ncol]), w_ada[:, c0:c0 + ncol])
                dma("pool", br.v(None, np.s_[:, 0:ncol]), V(b_ada[0:1, c0:c0 + ncol]), (br.buf, 0))
                bi = nb()
                for kc in range(16):
                    mm(bank(bi, ncol), crep.v(None, np.s_[:, kc, :]), w.v(None, np.s_[:, kc, 0:ncol]), start=(kc == 0), stop=False)
                mm(bank(bi, ncol), onesb.v(None, np.s_[0:1, :]), br.v(None, np.s_[:, 0:ncol]), start=False, stop=True)
                cs_ = np.s_[:, ct * ncol:(ct + 1) * ncol]
                o = mtile.v(None, cs_)
                if kind == 0:
                    cp("act", o, bank(bi, ncol))
                elif kind == 1:
                    stt("dve", o, bank(bi, ncol), 1.0, g.v(None, cs_), ALU.add, ALU.mult)
                else:
                    coef = 1.0 if sub == 1 else 0.5
                    stt("dve", o, bank(bi, ncol), coef, g.v(None, cs_), ALU.mult, ALU.mult)
                yield
            dma("sp", modS.v(m, np.s_[m]), mtile.v(), (mtile.buf, 0))
            yield

    S0_FIRST = [0, 1, 2, 3, 4]
    S0_LATE = [5, 6, 7, 8]

    def stage0():
        A.push()
        cv = TT(A.alloc([16], F32), "cv")
        wsl = [TT(A.alloc([16, 512], BF16), f"s0w{i}") for i in range(5)]
        brow = [TT(A.alloc([512], BF16, parts=1), f"s0b{i}") for i in range(5)]
        gt_ = [TT(A.alloc([D], F32), f"s0g{i}") for i in range(2)]
        mt = [TT(A.alloc([D], F32), f"s0m{i}") for i in range(2)]
        dma("sp", cv.v(), V(cvec), (cv.buf, 0))
        act(cs_p.v(), cv.v(), AF.Silu)
        crep = make_crep()
        for _ in s0_units(S0_FIRST if upto >= 2 else list(range(9)), 512, wsl, brow, gt_, mt, crep):
            pass
        stage_end()
        A.pop()

    def prenorm_tiles(*a, **kw):
        for _ in prenorm_gen(*a, **kw):
            pass

    def prenorm_gen(src, src_is_input, tiles, At, Bt, xt, hb, hT, hT_part_of, col_of, extra_reads=(), first_writes=()):
        n = len(tiles)

        def load(i):
            t_ = tiles[i]
            xv = xt[i % 2].v()
            if src_is_input:
                dma("sp", xv, V(src[t_ * 128:(t_ + 1) * 128, :]), (xt[i % 2].buf, 0))
            else:
                dma("sp", xv, src.v(t_, np.s_[t_ * 128:(t_ + 1) * 128, :]), (xt[i % 2].buf, 0))

        def s1(i):
            xv = xt[i % 2].v()
            hv = hb[i % 2].v()
            r = rstd_from(xv, D, hv)
            stt("dve", xv, xv, r, At.v(), ALU.mult, ALU.mult)
            tt("dve" if i % 2 == 0 else "pool", hv, xv, Bt.v(), ALU.add)

        def s2(i):
            t_ = tiles[i]
            c0 = col_of(t_)
            for half in range(2):
                bi = nb()
                pv = bank_bf(bi)
                for q in range(8):
                    kc = half * 8 + q
                    tr(V(pv[:, q * 128:(q + 1) * 128], (PB, bi)), hb[i % 2].v(None, np.s_[:, kc * 128:(kc + 1) * 128]))
                dst = V(hT.ap[:, half * 8:half * 8 + 8, c0:c0 + 128], (hT.buf, hT_part_of(t_)), *(first_writes if i == 0 else ()))
                srcv = V(pv.rearrange("p (a b) -> p a b", b=128), (PB, bi), *extra_reads)
                cp("act" if half == 0 else "dve", dst, srcv)

        load(0)
        if n > 1:
            load(1)
        s1(0)
        for i in range(n):
            if i + 1 < n:
                s1(i + 1)
            s2(i)
            if i + 2 < n:
                load(i + 2)
            yield

    def postnorm_tiles(ytile_of, tiles, Gt, res_src, res_is_input, dst, dst_is_output, xt, junk, add_eng="dve"):
        n = len(tiles)

        def load(i):
            t_ = tiles[i]
            xv = xt[i % 2].v()
            if res_is_input:
                dma("sp", xv, V(res_src[t_ * 128:(t_ + 1) * 128, :]), (xt[i % 2].buf, 0))
            else:
                dma("sp", xv, res_src.v(t_, np.s_[t_ * 128:(t_ + 1) * 128, :]), (xt[i % 2].buf, 0))

        load(0)
        if n > 1:
            load(1)
        for i, t_ in enumerate(tiles):
            yv = ytile_of(i)
            xv = xt[i % 2].v()
            r = rstd_from(yv, D, junk.v())
            stt("dve", yv, yv, r, Gt.v(), ALU.mult, ALU.mult)
            tt(add_eng if i % 2 == 1 else "dve", xv, yv, xv, ALU.add)
            if dst_is_output:
                dma("sp", V(dst[t_ * 128:(t_ + 1) * 128, :]), xv, (xt[i % 2].buf, 0))
            else:
                dma("sp", dst.v(t_, np.s_[t_ * 128:(t_ + 1) * 128, :]), xv, (xt[i % 2].buf, 0))
            if i + 2 < n:
                load(i + 2)

    def ffn_stage(src, src_is_input, dst, dst_is_output, mbase, w1, w2):
        A.push()
        FB = min(1024, T)
        NFB = T // FB
        TB = FB // 128
        HB = min(512, FB)
        NH = FB // HB
        At = TT(A.alloc([D], F32), "ffA")
        Bt = TT(A.alloc([D], F32), "ffB")
        Gt = TT(A.alloc([D], F32), "ffG")
        dma("sp", Bt.v(), modS.v(mbase, np.s_[mbase]), (Bt.buf, 0))
        dma("sp", At.v(), modS.v(mbase + 1, np.s_[mbase + 1]), (At.buf, 0))
        dma("sp", Gt.v(), modS.v(mbase + 2, np.s_[mbase + 2]), (Gt.buf, 0))
        Rw = A.alloc([8192], F32)
        Rbuf = Buf("ffR", 4)
        hT = TT(Rw.bitcast(BF16)[:, 0:16 * FB].rearrange("p (a b) -> p a b", b=FB), "ffhT", TB)
        yv_ap = Rw.rearrange("p (a b) -> p a b", b=D)
        aT = TT(A.alloc([NFC, FB], BF16), "ffaT", NFC)
        w1s = [TT(A.alloc([16, 2, 128], BF16), f"ffw1_{i}") for i in range(2)]
        w2s = [TT(A.alloc([4, 512], BF16), f"ffw2_{i}") for i in range(3)]
        xt = [TT(A.alloc([D], F32), f"ffx{i}") for i in range(2)]
        hb = [TT(A.alloc([D], BF16), f"ffh{i}") for i in range(2)]
        sg = [TT(A.alloc([512], F32), f"ffsg{i}") for i in range(2)]
        junk = TT(A.alloc([D], BF16), "ffjunk")
        kw2 = 0
        for fb in range(NFB):
            tiles = [fb * TB + i for i in range(TB)]

            def ld_w1(j):
                w = w1s[j % 2]
                dma("pool", w.v(None, np.s_[:, :, 0, :]), V(w1[:, j * 128:(j + 1) * 128].rearrange("(kc p) n -> p kc n", p=128)), (w.buf, 0))
                dma("pool", w.v(None, np.s_[:, :, 1, :]), V(w1[:, DFF + j * 128:DFF + (j + 1) * 128].rearrange("(kc p) n -> p kc n", p=128)), (w.buf, 0))

            ld_w1(0)
            ld_w1(1)
            prenorm_tiles(src, src_is_input, tiles, At, Bt, xt, hb, hT, lambda t_: t_ - fb * TB, lambda t_: (t_ - fb * TB) * 128, extra_reads=((Rbuf, None),), first_writes=((Rbuf, None),))
            for j in range(NFC):
                w = w1s[j % 2]
                pg = [nb() for _ in range(NH)]
                pu = [nb() for _ in range(NH)]
                for gi, pbs in ((0, pg), (1, pu)):
                    for kc in range(16):
                        for h_ in range(NH):
                            mm(bank(pbs[h_], HB), w.v(None, np.s_[:, kc, gi, :]), hT.v(None, np.s_[:, kc, h_ * HB:(h_ + 1) * HB]), start=(kc == 0), stop=(kc == 15))
                if j + 2 < NFC:
                    ld_w1(j + 2)
                for h_ in range(NH):
                    sgv = sg[h_ % 2].v(None, np.s_[:, 0:HB])
                    act(sgv, bank(pg[h_], HB), AF.Silu)
                    tt("dve", aT.v(j, np.s_[:, j, h_ * HB:(h_ + 1) * HB]), sgv, bank(pu[h_], HB), ALU.mult)
            for th in range(NH):
                nt4 = HB // 128
                for dt in range(4):
                    pys = [nb() for _ in range(nt4)]
                    j = 0
                    while j < NFC:
                        g = min(4, NFC - j)
                        w = w2s[kw2 % 3]
                        kw2 += 1
                        dma("pool", w.v(None, np.s_[:, 0:g, :]), V(w2[j * 128:(j + g) * 128, dt * 512:(dt + 1) * 512].rearrange("(a p) n -> p a n", p=128)), (w.buf, 0))
                        for a in range(g):
                            for t4 in range(nt4):
                                c0 = (th * nt4 + t4) * 128
                                mm(bank(pys[t4]), aT.v(j + a, np.s_[:, j + a, c0:c0 + 128]), w.v(None, np.s_[:, a, :]), start=(j + a == 0), stop=(j + a == NFC - 1))
                        j += g
                    for t4 in range(nt4):
                        cp("act", V(yv_ap[:, t4, dt * 512:(dt + 1) * 512], (Rbuf, t4)), bank(pys[t4]))
                tiles_h = [fb * TB + th * nt4 + t4 for t4 in range(nt4)]
                postnorm_tiles(lambda i: V(yv_ap[:, i, :], (Rbuf, i)), tiles_h, Gt, src, src_is_input, dst, dst_is_output, xt, junk,
                               add_eng="dve")
        stage_end()
        A.pop()

    def mixer_stages():
        BW = min(512, T)
        NBLK = T // BW
        A.push()
        cqnT = TT(A.alloc([4, T], BF16), "cqnT", NT)
        ckvT = TT(A.alloc([4, NK], BF16), "ckvT", NKT)
        krT = TT(A.alloc([NK], BF16), "krT", NKT + 1)
        GAT.buf = Buf("GAT", 16 * NBLK)
        GBT.buf = Buf("GBT", 16 * NBLK)
        OGT.buf = Buf("OGT", NT)
        OMT.buf = Buf("OMT", 16)
        QD.buf = Buf("QD", 4 * NT)
        KI.buf = Buf("KI", 4 * NT)
        KE.buf = Buf("KE", 4 * NT)
        ED.buf = Buf("ED", 4 * NT)

        A.push()
        h2T = TT(A.alloc([16, T], BF16), "h2T", NT)
        aT2 = [TT(A.alloc([T], BF16), "aTf"), TT(A.alloc([T], BF16), "aTb")]

        A.push()
        At = TT(A.alloc([D], F32), "m1A")
        Bt = TT(A.alloc([D], F32), "m1B")
        dma("sp", Bt.v(), modS.v(3, np.s_[3]), (Bt.buf, 0))
        dma("sp", At.v(), modS.v(4, np.s_[4]), (At.buf, 0))
        xt = [TT(A.alloc([D], F32), f"m1x{i}") for i in range(2)]
        hb = [TT(A.alloc([D], BF16), f"m1h{i}") for i in range(2)]
        Wsm = TT(A.alloc([16, 1120], BF16), "Wsm")
        wslab(Wsm.v(), w_inp[:, 6144:7264])
        qnb = TT(A.alloc([512], F32), "qnb")
        kvnb = TT(A.alloc([512], F32), "kvnb")
        dma("sp", qnb.v(), V(q_norm.partition_broadcast(128)), (qnb.buf, 0))
        dma("sp", kvnb.v(), V(kv_norm.partition_broadcast(128)), (kvnb.buf, 0))
        junk5 = TT(A.alloc([512], BF16), "junk5")
        cqb = [TT(A.alloc([512], BF16), f"cqb{i}") for i in range(2)]
        ckf = [TT(A.alloc([512], F32), f"ckf{i}") for i in range(2)]
        ckb = [TT(A.alloc([512], BF16), f"ckb{i}") for i in range(2)]
        krf = [TT(A.alloc([64], F32), f"krf{i}") for i in range(2)]
        krb = [TT(A.alloc([64], BF16), f"krb{i}") for i in range(2)]
        kt1 = [TT(A.alloc([64], F32), f"kt1{i}") for i in range(2)]
        kt2 = [TT(A.alloc([64], F32), f"kt2{i}") for i in range(2)]
        cosk = [TT(A.alloc([64], F32), f"cosk{i}") for i in range(2)]
        sink = [TT(A.alloc([64], F32), f"sink{i}") for i in range(2)]
        dma("pool", krT.v(NKT, np.s_[65:81, :]), V(mk_in), (krT.buf, NKT))
        for c2 in range(2):
            k = c2 % 2
            dma("sp", ckf[k].v(), V(cckv[c2 * 128:(c2 + 1) * 128, :]), (ckf[k].buf, 0))
            cp("act", ckb[k].v(), ckf[k].v())
            b = nb()
            for q in range(4):
                tr(V(bank_bf(b)[:, q * 128:(q + 1) * 128], (PB, b)), ckb[k].v(None, np.s_[:, q * 128:(q + 1) * 128]))
            cp("dve", ckvT.v(c2, np.s_[:, 0:4, c2 * 128:(c2 + 1) * 128]),
               V(bank_bf(b)[:, 0:512].rearrange("p (a b) -> p a b", b=128), (PB, b)))
            dma("sp", krf[k].v(), V(ckr[c2 * 128:(c2 + 1) * 128, :]), (krf[k].buf, 0))
            cp("act", krb[k].v(), krf[k].v())
            b = nb()
            tr(V(bank_bf(b)[0:64, 0:128], (PB, b)), krb[k].v())
            cp("dve", krT.v(c2, np.s_[0:64, c2 * 128:(c2 + 1) * 128]), V(bank_bf(b)[0:64, 0:128], (PB, b)))
        pgen = prenorm_gen(X1, False, list(range(NT)), At, Bt, xt, hb, h2T, lambda t_: t_, lambda t_: t_ * 128)
        def m1a_A(t_):
            k = t_ % 2
            tc = np.s_[t_ * 128:(t_ + 1) * 128]
            dma("sp", cosk[k].v(), V(cosK_in[tc, :]), (cosk[k].buf, 0))
            dma("sp", sink[k].v(), V(sinK_in[tc, :]), (sink[k].buf, 0))
            b1 = nb()
            for kc in range(16):
                mm(bank(b1), h2T.v(t_, np.s_[:, kc, tc]), Wsm.v(None, np.s_[:, kc, 32:544]), start=(kc == 0), stop=(kc == 15))
            b3 = nb()
            for kc in range(16):
                mm(bank(b3), h2T.v(t_, np.s_[:, kc, tc]), Wsm.v(None, np.s_[:, kc, 544:1056]), start=(kc == 0), stop=(kc == 15))
            b5 = nb()
            for kc in range(16):
                mm(bank(b5, 64), h2T.v(t_, np.s_[:, kc, tc]), Wsm.v(None, np.s_[:, kc, 1056:1120]), start=(kc == 0), stop=(kc == 15))
            r = rstd_from(bank(b1), 512, junk5.v())
            stt("dve", cqb[k].v(), bank(b1), r, qnb.v(), ALU.mult, ALU.mult)
            r = rstd_from(bank(b3), 512, junk5.v())
            stt("dve", ckf[k].v(), bank(b3), r, kvnb.v(), ALU.mult, ALU.mult)
            dma("sp", V(nckv_out[tc, :]), ckf[k].v(), (ckf[k].buf, 0))
            cp("act", ckb[k].v(), ckf[k].v())
            cp("act", krf[k].v(), bank(b5, 64))
            dma("sp", V(nkr_out[tc, :]), krf[k].v(), (krf[k].buf, 0))
            tt("dve", kt1[k].v(), krf[k].v(), cosk[k].v(), ALU.mult)
            x4 = krf[k].ap.rearrange("p (a h j) -> p a h j", a=2, h=2)
            s4 = sink[k].ap.rearrange("p (a h j) -> p a h j", a=2, h=2)
            o4 = kt2[k].ap.rearrange("p (a h j) -> p a h j", a=2, h=2)
            for hh in range(2):
                tt("dve", V(o4[:, :, hh, :], (kt2[k].buf, None)), V(x4[:, :, 1 - hh, :], (krf[k].buf, None)),
                   V(s4[:, :, hh, :], (sink[k].buf, None)), ALU.mult)
            tt("dve", krb[k].v(), kt1[k].v(), kt2[k].v(), ALU.add)

        def m1a_B(t_):
            k = t_ % 2
            tc = np.s_[t_ * 128:(t_ + 1) * 128]
            kc0 = 256 + t_ * 128
            b2 = nb()
            for q in range(4):
                tr(V(bank_bf(b2)[:, q * 128:(q + 1) * 128], (PB, b2)), cqb[k].v(None, np.s_[:, q * 128:(q + 1) * 128]))
            for q in range(4):
                tr(V(bank_bf(b2)[:, 512 + q * 128:512 + (q + 1) * 128], (PB, b2)), ckb[k].v(None, np.s_[:, q * 128:(q + 1) * 128]))
            b6 = nb()
            tr(V(bank_bf(b6)[0:64, 0:128], (PB, b6)), krb[k].v())
            cp("act", cqnT.v(t_, np.s_[:, 0:4, tc]), V(bank_bf(b2)[:, 0:512].rearrange("p (a b) -> p a b", b=128), (PB, b2)))
            cp("dve", ckvT.v(2 + t_, np.s_[:, 0:4, kc0:kc0 + 128]), V(bank_bf(b2)[:, 512:1024].rearrange("p (a b) -> p a b", b=128), (PB, b2)))
            cp("dve", krT.v(2 + t_, np.s_[0:64, kc0:kc0 + 128]), V(bank_bf(b6)[0:64, 0:128], (PB, b6)))

        for _ in pgen:
            pass
        m1a_A(0)
        for t_ in range(NT):
            if t_ + 1 < NT:
                m1a_A(t_ + 1)
            m1a_B(t_)
        for blk in range(NBLK):
            cols = np.s_[blk * BW:(blk + 1) * BW]
            tparts = tuple(range(blk * (BW // 128), (blk + 1) * (BW // 128)))
            for d_ in range(2):
                b = nb()
                for kc in range(16):
                    mm(bank(b, BW, rows=16), Wsm.v(None, np.s_[:, kc, d_ * 16:(d_ + 1) * 16]),
                       V(h2T.ap[:, kc, cols], (h2T.buf, tparts)), start=(kc == 0), stop=(kc == 15))
                cp("act", aT2[d_].v(None, np.s_[0:16, cols]), bank(b, BW, rows=16))
        stage_end()
        A.pop()

        A.push()
        Wqk = TT(A.alloc([16, 1024], BF16), "Wqk", 2)
        wal = TT(A.alloc([2, 1024], BF16), "wal")
        bal = TT(A.alloc([2, 1024], BF16), "bal")
        dma("pool", wal.v(None, np.s_[0:16]), V(w_alpha.rearrange("d r n -> r d n")), (wal.buf, 0))
        dma("pool", bal.v(None, np.s_[0:1]), V(b_alpha.rearrange("(o d) n -> o d n", o=1)), (bal.buf, 0))
        qf = [TT(A.alloc([512], F32), f"qf{i}") for i in range(3)]
        kf = [TT(A.alloc([512], F32), f"kf{i}") for i in range(3)]
        ef = TT(A.alloc([512], F32), "ef")
        spf = [TT(A.alloc([512], F32), f"spf{i}") for i in range(6)]
        ebf = [TT(A.alloc([512], F32), f"ebf{i}") for i in range(2)]
        eif = [TT(A.alloc([512], F32), f"eif{i}") for i in range(2)]
        eef = [TT(A.alloc([512], F32), f"eef{i}") for i in range(2)]
        qdb = [TT(A.alloc([512], BF16), f"qdb{i}") for i in range(4)]
        kib = [TT(A.alloc([512], BF16), f"kib{i}") for i in range(4)]
        keb = [TT(A.alloc([512], BF16), f"keb{i}") for i in range(2)]
        qdT = [TT(A.alloc([4, 128], BF16), f"qdT{i}") for i in range(2)]
        kiT = [TT(A.alloc([4, 128], BF16), f"kiT{i}") for i in range(2)]
        edc = [TT(A.alloc([4], F32), f"edc{i}") for i in range(2)]
        items = [(ch, t_) for ch in range(2) for t_ in range(NT)]

        def ld_wqk(ch):
            c0 = ch * 512
            dma("pool", Wqk.v(0, np.s_[:, :, 0:512]), V(w_inp[:, c0:c0 + 512].rearrange("(kc p) n -> p kc n", p=128)), (Wqk.buf, 0))
            dma("pool", Wqk.v(1, np.s_[:, :, 512:1024]), V(w_inp[:, 1024 + c0:1024 + c0 + 512].rearrange("(kc p) n -> p kc n", p=128)), (Wqk.buf, 1))

        def p1(n):
            ch, t_ = items[n]
            c0 = ch * 512
            if t_ == 0:
                ld_wqk(ch)
            k = n % 3
            tc = np.s_[t_ * 128:(t_ + 1) * 128]
            bq = nb()
            for kc in range(16):
                mm(bank(bq), h2T.v(t_, np.s_[:, kc, tc]), Wqk.v(0, np.s_[:, kc, 0:512]), start=(kc == 0), stop=(kc == 15))
            cp("act", qf[k].v(), bank(bq), scale=1.0 / 16.0)
            bk = nb()
            for kc in range(16):
                mm(bank(bk), h2T.v(t_, np.s_[:, kc, tc]), Wqk.v(1, np.s_[:, kc, 512:1024]), start=(kc == 0), stop=(kc == 15))
            cp("act", kf[k].v(), bank(bk))
            for d_ in range(2):
                sp_ = spf[(n % 3) * 2 + d_]
                bz = nb()
                mm(bank(bz), aT2[d_].v(None, np.s_[0:16, tc]), wal.v(None, np.s_[0:16, d_, c0:c0 + 512]), start=True, stop=False)
                mm(bank(bz), onesb.v(None, np.s_[0:1, :]), bal.v(None, np.s_[0:1, d_, c0:c0 + 512]), start=False, stop=True)
                act(ef.v(), bank(bz), AF.Exp, scale=-1.0)
                act(sp_.v(), ef.v(), AF.Ln, bias=1.0)

        def p2(n):
            ch, t_ = items[n]
            c0 = ch * 512
            k = n % 3
            for d_ in range(2):
                sp_ = spf[(n % 3) * 2 + d_]
                kk = d_
                k4 = (n % 2) * 2 + d_
                part = (d_ * NT + t_) * 2 + ch
                Lm = Lf if d_ == 0 else Lb
                Um = Uf if d_ == 0 else Ub
                bB = nb()
                mm(bank(bB), Lm, sp_.v())
                bE = nb()
                mm(bank(bE), Um, sp_.v())
                bl = nb()
                for c in range(4):
                    mm(bank(bl, 1, off=c), sp_.v(None, np.s_[:, c * 128:(c + 1) * 128]), ones_col)
                act(ebf[kk].v(), bank(bB), AF.Exp, scale=-1.0 / 16.0)
                act(eif[kk].v(), bank(bB), AF.Exp, scale=1.0 / 16.0)
                act(eef[kk].v(), bank(bE), AF.Exp, scale=-1.0 / 16.0)
                act(edc[kk].v(), bank(bl, 4), AF.Exp, scale=-1.0 / 16.0)
                dma("sp", ED.v(part, np.s_[d_, t_, :, ch * 4:(ch + 1) * 4]), edc[kk].v(), (edc[kk].buf, 0))
                tt("dve", qdb[k4].v(), qf[k].v(), ebf[kk].v(), ALU.mult)
                tt("dve", kib[k4].v(), kf[k].v(), eif[kk].v(), ALU.mult)
                tt("pool", keb[kk].v(), kf[k].v(), eef[kk].v(), ALU.mult)
                dma("sp", KE.v(part, np.s_[d_, t_, :, c0:c0 + 512]), keb[kk].v(), (keb[kk].buf, 0))

        def p3(n):
            ch, t_ = items[n]
            for d_ in range(2):
                kk = d_
                k4 = (n % 2) * 2 + d_
                part = (d_ * NT + t_) * 2 + ch
                for (srcb, dstT, DR, eng) in ((qdb[k4], qdT[kk], QD, "act"), (kib[k4], kiT[kk], KI, "dve")):
                    bt = nb()
                    for q in range(4):
                        tr(V(bank_bf(bt)[:, q * 128:(q + 1) * 128], (PB, bt)), srcb.v(None, np.s_[:, q * 128:(q + 1) * 128]))
                    cp(eng, dstT.v(), V(bank_bf(bt)[:, 0:512].rearrange("p (a b) -> p a b", b=128), (PB, bt)))
                    dma("sp", DR.v(part, np.s_[d_, t_, :, ch * 4:(ch + 1) * 4, :]), dstT.v(), (dstT.buf, 0))

        NI = len(items)
        for n in range(NI + 2):
            if n < NI:
                p1(n)
            if 0 <= n - 1 < NI:
                p2(n - 1)
            if 0 <= n - 2 < NI:
                p3(n - 2)
        stage_end()
        A.pop()

        A.push()
        wsl = [TT(A.alloc([16, 512], BF16), f"m1cw{i}") for i in range(2)]
        vb = [TT(A.alloc([512], BF16), f"m1cv{i}") for i in range(4)]
        gw = [TT(A.alloc([16, 128], BF16), f"m1cg{i}") for i in range(2)]
        gtb = [TT(A.alloc([BW], BF16), f"m1cgt{i}") for i in range(4)]
        bg_w = [TT(A.alloc([16, 256], BF16), f"bgw{i}") for i in range(2)]
        bg_b = [TT(A.alloc([256], BF16, parts=1), f"bgb{i}") for i in range(2)]
        bg_g = [TT(A.alloc([D], F32), "bgg")]
        bg_m = [TT(A.alloc([D], F32), "bgm0")]
        bg = s0_units(S0_LATE, 256, bg_w, bg_b, bg_g, bg_m, make_crep())
        kq = 0
        kv_ = 0
        for (cbase, DR, fn) in ((2048, VV, AF.Copy), (4096, RR, AF.Silu)):
            for s4 in range(4):
                w = wsl[kq % 2]
                kq += 1
                wslab(w.v(), w_inp[:, cbase + s4 * 512:cbase + (s4 + 1) * 512])
                for t_ in range(NT):
                    tc = np.s_[t_ * 128:(t_ + 1) * 128]
                    b = nb()
                    for kc in range(16):
                        mm(bank(b), h2T.v(t_, np.s_[:, kc, tc]), w.v(None, np.s_[:, kc, :]), start=(kc == 0), stop=(kc == 15))
                    o = vb[kv_ % 4]
                    kv_ += 1
                    if fn == AF.Copy and (kv_ % 2 == 0):
                        cp("dve", o.v(), bank(b))
                    else:
                        act(o.v(), bank(b), fn)
                    dma("sp", DR.v(t_, np.s_[tc, s4 * 512:(s4 + 1) * 512]), o.v(), (o.buf, 0))
                    if t_ % 4 == 3:
                        next(bg, None)
        kg = 0
        ko = 0
        for (cbase, DR) in ((7264, GAT), (9312, GBT)):
            for dc in range(16):
                w = gw[kg % 2]
                kg += 1
                wslab(w.v(), w_inp[:, cbase + dc * 128:cbase + (dc + 1) * 128])
                for blk in range(NBLK):
                    cols = np.s_[blk * BW:(blk + 1) * BW]
                    tparts = tuple(range(blk * (BW // 128), (blk + 1) * (BW // 128)))
                    b = nb()
                    for kc in range(16):
                        mm(bank(b, BW), w.v(None, np.s_[:, kc, :]), V(h2T.ap[:, kc, cols], (h2T.buf, tparts)), start=(kc == 0), stop=(kc == 15))
                    o = gtb[ko % 4]
                    ko += 1
                    act(o.v(), bank(b, BW), AF.Sigmoid)
                    dma("sp", DR.v(dc * NBLK + blk, np.s_[dc, :, cols]), o.v(), (o.buf, 0))
                next(bg, None)
        for _ in bg:
            pass
        stage_end()
        A.pop()
        A.pop()

        A.push()
        Stl = [TT(A.alloc([8, 512], F32), f"St{i}", 8) for i in range(2)]
        edf = TT(A.alloc([8], F32), "edf")
        Sbf = TT(A.alloc([8, 512], BF16), "Sbf", 8)
        qd = [TT(A.alloc([8, 128], BF16), f"qd{i}") for i in range(3)]
        ki = [TT(A.alloc([8, 128], BF16), f"ki{i}") for i in range(3)]
        ke = [TT(A.alloc([1024], BF16), f"ke{i}") for i in range(3)]
        vt = [TT(A.alloc([D], BF16), f"vt{i}") for i in range(3)]
        ed = [TT(A.alloc([8], F32), f"ed{i}") for i in range(3)]
        ATs = [TT(A.alloc([512], BF16), f"ATs{i}") for i in range(2)]
        Acp = [TT(A.alloc([512], BF16), f"Acp{i}") for i in range(2)]
        maskf = [TT(A.alloc([512], F32), f"maskf{i}") for i in range(2)]
        for d_ in range(2):
            for h in range(4):
                cp("dve", maskf[d_].v(None, np.s_[:, h * 128:(h + 1) * 128]), Lf if d_ == 0 else Lb)
        ot = [TT(A.alloc([D], F32), f"ot{i}", 4) for i in range(2)]
        oft = [TT(A.alloc([D], F32), f"oft{i}") for i in range(3)]
        rt = [TT(A.alloc([D], BF16), f"rt{i}") for i in range(4)]
        ogb = [TT(A.alloc([D], BF16), f"ogb{i}", 4) for i in range(2)]
        ogTt = [TT(A.alloc([16, 128], BF16), f"ogTt{i}") for i in range(2)]
        gnb = TT(A.alloc([512], F32), "gnb")
        dma("sp", gnb.v(), V(gla_norm.partition_broadcast(128)), (gnb.buf, 0))
        tmpf = [TT(A.alloc([512], F32), f"m2tmp{i}") for i in range(4)]
        junk5 = TT(A.alloc([512], BF16), "m2junk")
        for d_ in range(2):
            cur = 0
            cross = False
            dma("sp", Stl[cur].v(None), V(s0[d_].rearrange("h (kc p) v -> p (h kc) v", p=128)), (Stl[cur].buf, 0))
            cp("act", Sbf.v(None), Stl[cur].v(None))
            order = list(range(NT)) if d_ == 0 else list(range(NT - 1, -1, -1))

            def load(i):
                t_ = order[i]
                k = i % 3
                tc = np.s_[t_ * 128:(t_ + 1) * 128]
                pr = ((d_ * NT + t_) * 2, (d_ * NT + t_) * 2 + 1)
                dma("sp", qd[k].v(), V(QD.ap[d_, t_], (QD.buf, pr)), (qd[k].buf, 0))
                dma("sp", ki[k].v(), V(KI.ap[d_, t_], (KI.buf, pr)), (ki[k].buf, 0))
                dma("sp", ke[k].v(), V(KE.ap[d_, t_], (KE.buf, pr)), (ke[k].buf, 0))
                dma("sp", ed[k].v(), V(ED.ap[d_, t_], (ED.buf, pr)), (ed[k].buf, 0))
                dma("sp", vt[k].v(), VV.v(t_, np.s_[tc, :]), (vt[k].buf, 0))
                if d_ == 1:
                    dma("sp", oft[k].v(), OF.v(t_, np.s_[tc, :]), (oft[k].buf, 0))
                    dma("sp", rt[i % 4].v(), RR.v(t_, np.s_[tc, :]), (rt[i % 4].buf, 0))

            load(0)
            if NT > 1:
                load(1)
            def emitA(i):
                k = i % 3
                k2 = i % 2
                ba = nb()
                for h in range(4):
                    for kc in range(2):
                        mm(bank(ba, 128, off=h * 128), ki[k].v(None, np.s_[:, h * 2 + kc, :]), qd[k].v(None, np.s_[:, h * 2 + kc, :]), start=(kc == 0), stop=(kc == 1))
                tt("dve", ATs[k2].v(), bank(ba), maskf[d_].v(), ALU.mult)

            emitA(0)
            kcp = 0
            pend_fin = []
            for i, t_ in enumerate(order):
                k = i % 3
                k2 = i % 2
                tc = np.s_[t_ * 128:(t_ + 1) * 128]
                if i + 2 < NT:
                    load(i + 2)
                if i + 1 < NT:
                    emitA(i + 1)
                for hp in range(2):
                    bss = {}
                    for h in (2 * hp, 2 * hp + 1):
                        hc = np.s_[:, h * 512:(h + 1) * 512]
                        for kc in range(2):
                            c = h * 2 + kc
                            bs = nb()
                            bss[c] = bs
                            mm(bank(bs), ke[k].v(None, np.s_[:, c * 128:(c + 1) * 128]), vt[k].v(None, hc))
                    for h in (2 * hp, 2 * hp + 1):
                        hc = np.s_[:, h * 512:(h + 1) * 512]
                        bo = nb()
                        mm(bank(bo), ATs[k2].v(None, np.s_[:, h * 128:(h + 1) * 128]), vt[k].v(None, hc), start=True, stop=False)
                        for kc in range(2):
                            c = h * 2 + kc
                            mm(bank(bo), qd[k].v(None, np.s_[:, c, :]), Sbf.v(c, np.s_[:, c, :]), start=False, stop=(kc == 1))
                        if d_ == 0:
                            cp("act", ot[k2].v(h, hc), bank(bo))
                        else:
                            tt("dve", ot[k2].v(h, hc), bank(bo), oft[k].v(None, hc), ALU.add)
                    for h in (2 * hp, 2 * hp + 1):
                        for kc in range(2):
                            c = h * 2 + kc
                            Ssrc = Stl[cur]
                            Sdst = Stl[1 - cur] if cross else Stl[cur]
                            edv = edf.v(None, np.s_[:, c:c + 1]) if cross else ed[k].v(None, np.s_[:, c:c + 1])
                            stt("dve", Sdst.v(c, np.s_[:, c, :]), Ssrc.v(c, np.s_[:, c, :]), edv, bank(bss[c]), ALU.mult, ALU.add)
                            cp("act" if (d_ == 1 or kcp % 2 == 0) else "dve", Sbf.v(c, np.s_[:, c, :]), Sdst.v(c, np.s_[:, c, :]))
                            kcp += 1
                if d_ == 0:
                    dma("act", OF.v(t_, np.s_[tc, :]), ot[k2].v(None), (ot[k2].buf, 0))
                else:
                    while pend_fin:
                        pend_fin.pop(0)()
                    rs_ = [rstd_from(ot[k2].v(h, np.s_[:, h * 512:(h + 1) * 512]), 512, junk5.v()) for h in range(4)]

                    def fin(k2=k2, k4=i % 4, t_=t_, tc=tc, rs_=rs_):
                        for h in range(4):
                            hc = np.s_[:, h * 512:(h + 1) * 512]
                            tm = tmpf[h % 4]
                            stt("dve", tm.v(), ot[k2].v(h, hc), rs_[h], gnb.v(), ALU.mult, ALU.mult)
                            tt("pool", ogb[k2].v(h, hc), tm.v(), rt[k4].v(None, hc), ALU.mult)
                        for half in range(2):
                            bt = nb()
                            for q in range(8):
                                kc = half * 8 + q
                                tr(V(bank_bf(bt)[:, q * 128:(q + 1) * 128], (PB, bt)), ogb[k2].v(kc // 4, np.s_[:, kc * 128:(kc + 1) * 128]))
                            cp("act" if half == 0 else "dve", ogTt[k2].v(None, np.s_[:, half * 8:half * 8 + 8, :]),
                               V(bank_bf(bt).rearrange("p (a b) -> p a b", b=128), (PB, bt)))
                        dma("pool", OGT.v(t_, np.s_[:, :, tc]), ogTt[k2].v(), (ogTt[k2].buf, 0))
                    pend_fin.append(fin)
                end_seg = (t_ % 2 == 1) if d_ == 0 else (t_ % 2 == 0)
                if cross:
                    cur = 1 - cur
                    cross = False
                if end_seg:
                    seg = t_ // 2
                    dst = (sf_out if d_ == 0 else sb_out)[seg].rearrange("h (kc p) v -> p (h kc) v", p=128)
                    dma("act", V(dst), Stl[cur].v(None), (Stl[cur].buf, 1))
                    if i != NT - 1:
                        ts("dve", edf.v(), ed[(i + 1) % 3].v(), flag.v(None, np.s_[:, 0:1]), ALU.mult)
                        act(Sbf.v(None), Stl[cur].v(None), AF.Copy, scale=flag.v(None, np.s_[:, 0:1]))
                        cross = True
        while pend_fin:
            pend_fin.pop(0)()
        stage_end()
        A.pop()

        A.push()
        QB = min(512, T)
        NQB = T // QB
        NQ4 = QB // 128
        kblocks = []
        ks = 0
        while ks < NK:
            kblocks.append((ks, min(512, NK - ks)))
            ks += 512
        omT = TT(A.alloc([16, T], BF16), "omT", 16)
        cosT = TT(A.alloc([T], F32), "cosT")
        sinT = TT(A.alloc([T], F32), "sinT")
        dma("sp", cosT.v(None, np.s_[0:64]), V(cosT_in), (cosT.buf, 0))
        dma("sp", sinT.v(None, np.s_[0:64]), V(sinT_in), (sinT.buf, 0))
        QrT = [TT(A.alloc([T], BF16), f"QrT{i}", 2 + NT) for i in range(2)]
        for i in range(2):
            dma("pool", QrT[i].v(1, np.s_[65:81, :]), V(mq_in), (QrT[i].buf, 1))
        S.op("dve", lambda e: e.memset(krT.ap[64:65, :], 1.0), [], [(krT.buf, NKT)])
        QnT = [TT(A.alloc([T], BF16), f"QnT{i}") for i in range(2)]
        KT = [TT(A.alloc([NK], BF16), f"KT{i}") for i in range(2)]
        Vh = [TT(A.alloc([NKT, 129], BF16), f"Vh{i}") for i in range(2)]
        for i in range(2):
            S.op("dve", lambda e, i=i: e.memset(Vh[i].ap[:, :, 128:129], 1.0), [], [(Vh[i].buf, None)])
        sel64 = TT(A.alloc([65], BF16), "sel64")
        S.op("dve", lambda e: e.memset(sel64.ap, 0.0), [], [(sel64.buf, None)])
        S.op("dve", lambda e: e.memset(sel64.ap[:, 64:65], 1.0), [], [(sel64.buf, None)])
        wkv = [TT(A.alloc([4, 256], BF16), f"wkv{i}") for i in range(2)]
        wq = [TT(A.alloc([4, 192], BF16), f"wq{i}") for i in range(2)]
        wqs = [TT(A.alloc([4, 64], BF16), f"wqs{i}") for i in range(2)]
        NPT = 4
        PT = [TT(A.alloc([QB], BF16), f"PT{i}") for i in range(NPT)]
        obt = [TT(A.alloc([128], BF16), f"obt{i}") for i in range(4)]
        dgb = [TT(A.alloc([128], BF16), f"dgb{i}") for i in range(4)]
        rt1 = TT(A.alloc([QB], F32), "rt1")
        rt2 = TT(A.alloc([QB], F32), "rt2")
        n4 = [0]
        n2 = [0]

        def nbs():
            i = n4[0]
            n4[0] = (i + 1) % 3
            return i

        def nbm():
            i = n2[0]
            n2[0] = (i + 1) % 4
            return i

        MB = 3

        def prep(h):
            k = h % 2
            dma("pool", wkv[k].v(), V(w_ukv[:, h * 256:(h + 1) * 256].rearrange("(kc p) n -> p kc n", p=128)), (wkv[k].buf, 0))
            dma("pool", wq[k].v(), V(w_uq[:, h * 192:(h + 1) * 192].rearrange("(kc p) n -> p kc n", p=128)), (wq[k].buf, 0))
            for a in range(2):
                for hh in range(2):
                    o0 = a * 32 + hh * 16
                    i0 = 128 + a * 32 + (1 - hh) * 16
                    cp("pool", wqs[k].v(None, np.s_[:, :, o0:o0 + 16]), wq[k].v(None, np.s_[:, :, i0:i0 + 16]))
            yield
            for (ks, kw) in kblocks:
                b = MB
                kparts = tuple(range(ks // 128, (ks + kw) // 128))
                for kc in range(4):
                    mm(bank(b, kw), wkv[k].v(None, np.s_[:, kc, 0:128]), V(ckvT.ap[:, kc, ks:ks + kw], (ckvT.buf, kparts)), start=(kc == 0), stop=(kc == 3))
                cp("dve", KT[k].v(None, np.s_[:, ks:ks + kw]), bank(b, kw))
                yield
            for g0 in range(0, NKT, 4):
                g = min(4, NKT - g0)
                b = MB
                for a in range(g):
                    kt_ = g0 + a
                    for kc in range(4):
                        mm(bank(b, 128, off=a * 128), ckvT.v(kt_, np.s_[:, kc, kt_ * 128:(kt_ + 1) * 128]), wkv[k].v(None, np.s_[:, kc, 128:256]), start=(kc == 0), stop=(kc == 3))
                cp("dve", Vh[k].v(None, np.s_[:, g0:g0 + g, 0:128]), V(ps[:, b * 512:b * 512 + g * 128].rearrange("p (a b) -> p a b", b=128), (PB, b)))
                yield
            for qb in range(NQB):
                cols = np.s_[qb * QB:(qb + 1) * QB]
                tparts = tuple(range(qb * NQ4, (qb + 1) * NQ4))
                b = MB
                for kc in range(4):
                    mm(bank(b, QB), wq[k].v(None, np.s_[:, kc, 0:128]), V(cqnT.ap[:, kc, cols], (cqnT.buf, tparts)), start=(kc == 0), stop=(kc == 3))
                cp("dve", QnT[k].v(None, np.s_[:, cols]), bank(b, QB))
                yield
                for kc in range(4):
                    mm(bank(b, QB, rows=64), wq[k].v(None, np.s_[:, kc, 128:192]), V(cqnT.ap[:, kc, cols], (cqnT.buf, tparts)), start=(kc == 0), stop=(kc == 3))
                tt("dve", rt1.v(None, np.s_[0:64, :]), bank(b, QB, rows=64), cosT.v(None, np.s_[0:64, cols]), ALU.mult)
                yield
                for kc in range(4):
                    mm(bank(b, QB, rows=64), wqs[k].v(None, np.s_[:, kc, :]), V(cqnT.ap[:, kc, cols], (cqnT.buf, tparts)), start=(kc == 0), stop=(kc == 3))
                tt("dve", rt2.v(None, np.s_[0:64, :]), bank(b, QB, rows=64), sinT.v(None, np.s_[0:64, cols]), ALU.mult)
                tt("dve", QrT[k].v(0, np.s_[0:64, cols]), rt1.v(None, np.s_[0:64, :]), rt2.v(None, np.s_[0:64, :]), ALU.add)
                yield
            for qb in range(NQB):
                cols = np.s_[qb * QB:(qb + 1) * QB]
                b = MB
                for q4 in range(NQ4):
                    qt = qb * NQ4 + q4
                    tc = np.s_[qt * 128:(qt + 1) * 128]
                    kd = np.s_[(2 + qt) * 128:(3 + qt) * 128]
                    mm(bank(b, 128, off=q4 * 128), QnT[k].v(None, np.s_[:, tc]), KT[k].v(None, np.s_[:, kd]), start=True, stop=False)
                    mm(bank(b, 128, off=q4 * 128), QrT[k].v(0, np.s_[0:64, tc]), krT.v(2 + qt, np.s_[0:64, kd]), start=False, stop=True)
                c0 = [scol() for _ in range(NQ4)]
                while c0[-1] != c0[0] + NQ4 - 1:
                    c0 = [scol() for _ in range(NQ4)]
                S.op("dve", lambda e, b=b, c=c0[0]: e.reduce_max(out=stat.ap[:, c:c + NQ4], in_=ps[:, b * 512:b * 512 + NQ4 * 128].rearrange("p (a b) -> p a b", b=128), axis=AX.X),
                     [(PB, b)], [(stat.buf, tuple(c0))])
                for q4 in range(NQ4):
                    ts("dve", dgb[q4].v(), identb.v(), sv(c0[q4]), ALU.mult, -1.0, ALU.mult)
                yield
                for q4 in range(NQ4):
                    mm(bank(b, 128, rows=65, off=q4 * 128), sel64.v(), dgb[q4].v())
                sparts = tuple(2 + qb * NQ4 + i for i in range(NQ4))
                cp("dve", V(QrT[k].ap[64:65, cols], (QrT[k].buf, sparts)), V(ps[64:65, b * 512:b * 512 + QB], (PB, b)))
                yield

        def run_all(gen):
            for _ in gen:
                pass

        kpt = 0
        LAG = 2
        run_all(prep(0))
        for h in range(16):
            k = h % 2
            bg = prep(h + 1) if h + 1 < 16 else iter(())
            tiles = [(qb, kt_) for qb in range(NQB) for kt_ in range(NKT)]
            pbuf = {}
            deferred = []

            def emit_qk(i):
                qb, kt_ = tiles[i]
                cols = np.s_[qb * QB:(qb + 1) * QB]
                qparts = (0, 1) + tuple(2 + qb * NQ4 + q for q in range(NQ4))
                kcs = np.s_[kt_ * 128:(kt_ + 1) * 128]
                b = nbs()
                mm(bank(b, QB), KT[k].v(None, np.s_[:, kcs]), QnT[k].v(None, np.s_[:, cols]), start=True, stop=False)
                mm(bank(b, QB), V(krT.ap[0:81, kcs], (krT.buf, (kt_, NKT))), V(QrT[k].ap[0:81, cols], (QrT[k].buf, qparts)), start=False, stop=True)
                pbuf[i] = b

            def emit_pv(i):
                nonlocal kpt
                qb, kt_ = tiles[i]
                b = pbuf.pop(i)
                p_ = PT[kpt % NPT]
                kpt += 1
                act(p_.v(), bank(b, QB), AF.Exp, scale=ATT_SCALE)
                for q4 in range(NQ4):
                    ob_ = 4 + q4
                    mm(V(ps[:, ob_ * 512:ob_ * 512 + 129], (PB, ob_)),
                       p_.v(None, np.s_[:, q4 * 128:(q4 + 1) * 128]), Vh[k].v(None, np.s_[:, kt_, :]),
                       start=(kt_ == 0), stop=(kt_ == NKT - 1))
                if kt_ == NKT - 1:
                    obs = []
                    for q4 in range(NQ4):
                        ob_ = 4 + q4
                        o0 = ob_ * 512
                        c_ri = scol()
                        recip(sv(c_ri), V(ps[:, o0 + 128:o0 + 129], (PB, ob_)))
                        ts("dve", obt[q4].v(), V(ps[:, o0:o0 + 128], (PB, ob_)), sv(c_ri), ALU.mult)

                    def fin(qb=qb):
                        for q4 in range(NQ4):
                            tr(V(bank_bf(MB)[:, q4 * 128:(q4 + 1) * 128], (PB, MB)), obt[q4].v())
                        c0_ = qb * QB
                        cp("dve", V(omT.ap[:, h, c0_:c0_ + QB], (omT.buf, h)), V(bank_bf(MB)[:, 0:QB], (PB, MB)))
                    deferred.append((i + 5, fin))

            n = len(tiles)
            for i in range(n + LAG):
                if i < n:
                    emit_qk(i)
                j = i - LAG
                if j >= 0:
                    emit_pv(j)
                while deferred and deferred[0][0] <= i:
                    deferred.pop(0)[1]()
                if i % 2 == 1:
                    next(bg, None)
            while deferred:
                deferred.pop(0)[1]()
            run_all(bg)
        for h in range(16):
            dma("sp", OMT.v(h, np.s_[:, h, :]), omT.v(h, np.s_[:, h, :]), (omT.buf, h))
        stage_end()
        A.pop()
        A.pop()

        A.push()
        FB = min(1024, T)
        NFB = T // FB
        HB = min(512, FB)
        NH = FB // HB
        nt4 = HB // 128
        Rw = A.alloc([8192], F32)
        Rbuf = Buf("m4R", 4)
        ogT_ap = Rw.bitcast(BF16)[:, 0:16 * FB].rearrange("p (a b) -> p a b", b=FB)
        y_ap = Rw.rearrange("p (a b) -> p a b", b=D)
        O2w = A.alloc([8192], F32)
        omTb = TT(O2w.bitcast(BF16)[:, 0:16 * FB].rearrange("p (a b) -> p a b", b=FB), "omTb", 4)
        y2_ap = O2w.rearrange("p (a b) -> p a b", b=D)
        mT = TT(A.alloc([16, FB], BF16), "mT", 16)
        wg = [TT(A.alloc([16, 128], BF16), f"wg{i}") for i in range(2)]
        wm = [TT(A.alloc([16, 128], BF16), f"wm{i}") for i in range(2)]
        wo = [TT(A.alloc([16, 512], BF16), f"wo{i}") for i in range(2)]
        gaT = [TT(A.alloc([FB], BF16), f"gaT{i}") for i in range(2)]
        gbT = [TT(A.alloc([FB], BF16), f"gbT{i}") for i in range(2)]
        G2 = TT(A.alloc([D], F32), "G2")
        dma("sp", G2.v(), modS.v(5, np.s_[5]), (G2.buf, 0))
        xt = [TT(A.alloc([D], F32), f"m4x{i}") for i in range(2)]
        junk = TT(A.alloc([D], BF16), "m4junk")
        t1 = [TT(A.alloc([HB], F32), f"m4t1{i}") for i in range(2)]
        t2 = [TT(A.alloc([HB], F32), f"m4t2{i}") for i in range(2)]
        kwo = 0
        for fb in range(NFB):
            cols = np.s_[fb * FB:(fb + 1) * FB]
            ttiles = tuple(range(fb * (FB // 128), (fb + 1) * (FB // 128)))
            gparts_of = lambda dc: tuple(dc * NBLK + bb for bb in range(fb * (FB // BW), (fb + 1) * (FB // BW)))
            dma("sp", V(ogT_ap, (Rbuf, None)), V(OGT.ap[:, :, cols], (OGT.buf, ttiles)), (Rbuf, 0))
            dma("sp", omTb.v(), V(OMT.ap[:, :, cols], (OMT.buf, None)), (omTb.buf, 0))
            for dc in range(16):
                k = dc % 2
                wslab(wg[k].v(), w_go[:, dc * 128:(dc + 1) * 128])
                wslab(wm[k].v(), w_mo[:, dc * 128:(dc + 1) * 128])
                dma("sp", gaT[k].v(), V(GAT.ap[dc, :, cols], (GAT.buf, gparts_of(dc))), (gaT[k].buf, 0))
                dma("sp", gbT[k].v(), V(GBT.ap[dc, :, cols], (GBT.buf, gparts_of(dc))), (gbT[k].buf, 0))
                for hh in range(NH):
                    hcs = np.s_[hh * HB:(hh + 1) * HB]
                    bg = nb()
                    for kc in range(16):
                        mm(bank(bg, HB), wg[k].v(None, np.s_[:, kc, :]), V(ogT_ap[:, kc, hcs], (Rbuf, None)), start=(kc == 0), stop=(kc == 15))
                    bm = nb()
                    for kc in range(16):
                        mm(bank(bm, HB), wm[k].v(None, np.s_[:, kc, :]), omTb.v(None, np.s_[:, kc, hcs]), start=(kc == 0), stop=(kc == 15))
                    tt("dve", t1[hh % 2].v(), bank(bg, HB), gaT[k].v(None, np.s_[:, hcs]), ALU.mult)
                    tt("dve", t2[hh % 2].v(), bank(bm, HB), gbT[k].v(None, np.s_[:, hcs]), ALU.mult)
                    tt("dve", mT.v(dc, np.s_[:, dc, hcs]), t1[hh % 2].v(), t2[hh % 2].v(), ALU.add)
            TBk = FB // 128

            def ytile(t8, sl=np.s_[:]):
                if t8 < 4:
                    return V(y_ap[:, t8, sl], (Rbuf, t8))
                return V(y2_ap[:, t8 - 4, sl], (omTb.buf, t8 - 4))

            for ct in range(4):
                w = wo[kwo % 2]
                kwo += 1
                wslab(w.v(), w_o[:, ct * 512:(ct + 1) * 512])
                pys = [nb() for _ in range(TBk)]
                for t8 in range(TBk):
                    c0 = t8 * 128
                    for kc in range(16):
                        mm(bank(pys[t8]), mT.v(kc, np.s_[:, kc, c0:c0 + 128]), w.v(None, np.s_[:, kc, :]), start=(kc == 0), stop=(kc == 15))
                    cp("act" if t8 % 2 == 0 else "dve", ytile(t8, np.s_[ct * 512:(ct + 1) * 512]), bank(pys[t8]))
            tiles_b = [fb * TBk + t8 for t8 in range(TBk)]
            postnorm_tiles(lambda i: ytile(i), tiles_b, G2, X1, False, X2, False, xt, junk, add_eng="dve")
        stage_end()
        A.pop()

    stage0()
    if upto >= 1:
        ffn_stage(x_in, True, X1, False, 0, w_f1i, w_f1o)
    if upto >= 2:
        mixer_stages()
    if upto >= 3:
        ffn_stage(X2, False, y_out, True, 6, w_f2i, w_f2o)
    S.barrier()
    S.emit(nc, block, esems, dsems)
    print("ops:", S.stats(), "arena max", A.off)
    es.close()
    return nc


WNAMES = ["w_ada", "b_ada", "norm_gains", "w_ffn1_in", "w_ffn1_out", "w_ffn2_in", "w_ffn2_out", "w_in",
          "w_gla_alpha", "b_gla_alpha", "gla_norm", "w_gla_out", "q_norm", "kv_norm", "w_uq", "w_ukv",
          "w_mla_out", "w_out"]


def consts_array():
    s = np.arange(128)[:, None]
    t = np.arange(128)[None, :]
    c = np.zeros((128, 6, 128), np.float32)
    c[:, 0] = (s == t)
    c[:, 1] = (s <= t)
    c[:, 2] = (s >= t)
    c[:, 3] = (s > t)
    c[:, 4] = (s < t)
    c[:, 5] = 1.0
    return c


def rope_tables(T, real):
    cosT = np.ones((64, T), np.float32)
    sinT = np.zeros((64, T), np.float32)
    if real:
        t = np.arange(T)
        pos = [(t // 64).astype(np.float32), (t % 64).astype(np.float32)]
        inv = (np.float32(10000.0) ** (-np.arange(0, 32, 2, dtype=np.float32) / np.float32(32))).astype(np.float32)
        for a in range(2):
            ang = pos[a][None, :] * inv[:, None]
            for hh in range(2):
                r0 = a * 32 + hh * 16
                cosT[r0:r0 + 16] = np.cos(ang)
                sinT[r0:r0 + 16] = np.sin(ang) * (-1.0 if hh == 0 else 1.0)
    return cosT, sinT, np.ascontiguousarray(cosT.T), np.ascontiguousarray(sinT.T)


def mask_rows(NSEG, kind):
    T = NSEG * 256
    NK = T + 256
    mq = np.zeros((16, T), np.float32)
    mk = np.zeros((16, NK), np.float32)
    for m in range(NSEG + 1):
        mk[m, m * 256:(m + 1) * 256] = 1.0
    if kind == "P":
        for s in range(NSEG):
            mq[:NSEG + 1, s * 256:(s + 1) * 256] = NEG_BIG
            mq[1 + s, s * 256:(s + 1) * 256] = 0.0
    return mq, mk


def core_inputs(full, NSEG, kind):
    T = NSEG * 256
    m = {}
    m["x"] = np.ascontiguousarray(full["x"], dtype=np.float32)
    m["cvec"] = np.ascontiguousarray(full["c"].reshape(16, 128).T)
    m["cckv"] = np.ascontiguousarray(full["cache_ckv"])
    m["ckr"] = np.ascontiguousarray(full["cache_krope"])
    m["s0"] = np.ascontiguousarray(np.stack([full["s0f"], full["s0b"]], axis=0))
    m["flag"] = np.full((128, 1), 1.0 if kind == "S" else 0.0, np.float32)
    cosT, sinT, cosK, sinK = rope_tables(T, kind == "S")
    m["cosT"], m["sinT"], m["cosK"], m["sinK"] = cosT, sinT, cosK, sinK
    mq, mk = mask_rows(NSEG, kind)
    m["mq"], m["mk"] = mq, mk
    m["consts"] = consts_array()
    for k in WNAMES:
        m[k] = full[k]
    return m


def make_test_inputs(rng, NSEG, kind):
    T = NSEG * 256
    f32 = np.float32

    def nrm(shape, scale):
        return (rng.standard_normal(shape, dtype=f32) * f32(scale)).astype(f32)

    DFF = 5504
    full = {
        "x": nrm((T, D), 1.0),
        "c": nrm((D,), 1.0),
        "w_ada": nrm((D, 9 * D), 0.5 * D ** -0.5),
        "b_ada": nrm((1, 9 * D), 0.01),
        "norm_gains": 1.0 + nrm((6, D), 0.05),
        "w_ffn1_in": nrm((D, 2 * DFF), D ** -0.5),
        "w_ffn1_out": nrm((DFF, D), DFF ** -0.5),
        "w_ffn2_in": nrm((D, 2 * DFF), D ** -0.5),
        "w_ffn2_out": nrm((DFF, D), DFF ** -0.5),
        "w_in": nrm((D, 11360), D ** -0.5),
        "w_gla_alpha": nrm((2, 16, 1024), 16 ** -0.5),
        "b_gla_alpha": nrm((2, 1024), 0.1),
        "gla_norm": 1.0 + nrm((1, 512), 0.05),
        "w_gla_out": nrm((D, D), D ** -0.5),
        "q_norm": 1.0 + nrm((1, 512), 0.05),
        "kv_norm": 1.0 + nrm((1, 512), 0.05),
        "w_uq": nrm((512, 3072), 512 ** -0.5),
        "w_ukv": nrm((512, 4096), 512 ** -0.5),
        "w_mla_out": nrm((D, D), D ** -0.5),
        "w_out": nrm((D, D), D ** -0.5),
    }
    if kind == "S":
        full["cache_ckv"] = nrm((256, 512), 1.0)
        full["cache_krope"] = nrm((256, 64), 1.0)
        full["s0f"] = nrm((4, 256, 512), 0.5)
        full["s0b"] = nrm((4, 256, 512), 0.5)
    else:
        full["cache_ckv"] = np.zeros((256, 512), f32)
        full["cache_krope"] = np.zeros((256, 64), f32)
        full["s0f"] = np.zeros((4, 256, 512), f32)
        full["s0b"] = np.zeros((4, 256, 512), f32)
    return full


_NC_CACHE = {}


def kernel(**inputs):
    NSEG = 8
    f32 = np.float32
    inp = {k: np.asarray(v) for k, v in inputs.items()}
    W = {
        "w_ada": inp["w_ada"][0], "b_ada": inp["b_ada"], "norm_gains": inp["norm_gains"][0],
        "w_ffn1_in": inp["w_ffn1_in"][0], "w_ffn1_out": inp["w_ffn1_out"][0],
        "w_ffn2_in": inp["w_ffn2_in"][0], "w_ffn2_out": inp["w_ffn2_out"][0],
        "w_in": inp["w_in"][0], "w_gla_alpha": inp["w_gla_alpha"][0], "b_gla_alpha": inp["b_gla_alpha"][0],
        "gla_norm": inp["gla_norm"], "w_gla_out": inp["w_gla_out"][0],
        "q_norm": inp["q_norm"], "kv_norm": inp["kv_norm"], "w_uq": inp["w_uq"][0], "w_ukv": inp["w_ukv"][0],
        "w_mla_out": inp["w_mla_out"][0], "w_out": inp["w_out"][0],
    }
    W = {k: np.ascontiguousarray(v, dtype=f32) for k, v in W.items()}
    in_maps = []
    for core in range(8):
        if core < 4:
            b = core
            full = dict(x=inp["x_sample"][b], c=inp["c"][b], cache_ckv=inp["cache_ckv"][b, 0],
                        cache_krope=inp["cache_krope"][b, 0], s0f=inp["state_gla_fwd"][b, 0],
                        s0b=inp["state_gla_bwd"][b, 0], **W)
            in_maps.append(core_inputs(full, NSEG, "S"))
        else:
            s0_ = 4 * (core - 4)
            xp = inp["x_prompt"][s0_:s0_ + 4].reshape(1024, D)
            x = np.concatenate([xp, np.zeros((1024, D), f32)], axis=0)
            full = dict(x=x, c=inp["c_ctx"], cache_ckv=np.zeros((256, 512), f32),
                        cache_krope=np.zeros((256, 64), f32), s0f=np.zeros((4, 256, 512), f32),
                        s0b=np.zeros((4, 256, 512), f32), **W)
            in_maps.append(core_inputs(full, NSEG, "P"))
    if NSEG not in _NC_CACHE:
        _NC_CACHE[NSEG] = build(NSEG)
    nc = _NC_CACHE[NSEG]
    res = run_bass_kernel_spmd(nc, in_maps, core_ids=list(range(8)))
    R = res.results
    y_prompt = np.zeros((16, 256, D), f32)
    y_sample = np.zeros((4, 2048, D), f32)
    new_ckv = np.zeros((16, 1, 256, 512), f32)
    new_krope = np.zeros((16, 1, 256, 64), f32)
    new_sf = np.zeros((16, 1, 4, 256, 512), f32)
    new_sb = np.zeros((16, 1, 4, 256, 512), f32)
    for core in range(8):
        r = R[core]
        if core < 4:
            y_sample[core] = np.asarray(r["y"])
        else:
            s0_ = 4 * (core - 4)
            y = np.asarray(r["y"]); ck = np.asarray(r["nckv"]); kr = np.asarray(r["nkr"])
            sf = np.asarray(r["sf"]); sb = np.asarray(r["sb"])
            for s in range(4):
                y_prompt[s0_ + s] = y[s * 256:(s + 1) * 256]
                new_ckv[s0_ + s, 0] = ck[s * 256:(s + 1) * 256]
                new_krope[s0_ + s, 0] = kr[s * 256:(s + 1) * 256]
                new_sf[s0_ + s, 0] = sf[s]
                new_sb[s0_ + s, 0] = sb[s]
    return (y_prompt, y_sample, new_ckv, new_krope, new_sf, new_sb)
```
